# Optimizing a Trainium2 kernel written in Bass

```python
import math
import jax, jax.numpy as jnp
from jax import lax
import numpy as np

D_MODEL = 2048
BATCH = 16
SEQ = 256
DEPTH = 2
DEC_BATCH = 8
DEC_SEQ = 2048
PAST_LEN = 256

GRID_W = 64
MIX_WIDTH = D_MODEL
GROUP_W = MIX_WIDTH // 4
N_DIR = 2
GLA_HEADS = 4
GLA_DK = GROUP_W // (2 * GLA_HEADS)
GLA_DV = GROUP_W // GLA_HEADS
GLA_RANK = 16
GLA_TAU = 16.0
RET_HEADS = 4
RET_DK = GROUP_W // (2 * RET_HEADS)
RET_DV = GROUP_W // RET_HEADS
S5_CH = GROUP_W
S5_GROUP = 16
S5_GROUPS = S5_CH // S5_GROUP
S5_STATE = 64
HY_CH = GROUP_W
HY_ORDER = 2
HY_BANDS = 16
HY_EMB = 1 + 2 * HY_BANDS
HY_HIDDEN = 64
HY_SHORT = 3
CHUNK = 64
D_FF = 4 * D_MODEL
ALPHA = (2 * DEPTH) ** 0.25
BETA = (8 * DEPTH) ** -0.25
LN_EPS = 1e-5
NORM_EPS = 1e-6
IN_SPLITS = (GLA_HEADS * GLA_DK, GLA_HEADS * GLA_DK, GLA_HEADS * GLA_DV, GROUP_W, N_DIR * GLA_RANK,
             RET_HEADS * RET_DK, RET_HEADS * RET_DK, RET_HEADS * RET_DV, GROUP_W,
             S5_CH, 3 * HY_CH)
D_IN = sum(IN_SPLITS)

kernel_name = 'hybrid_prefix_diffusion_step'


def _layer_norm(x, g=None, b=None):
    xf = x.astype(jnp.float32)
    mu = jnp.mean(xf, -1, keepdims=True)
    var = jnp.mean(jnp.square(xf - mu), -1, keepdims=True)
    y = (xf - mu) * lax.rsqrt(var + LN_EPS)
    if g is not None:
        y = y * g.astype(jnp.float32) + b.astype(jnp.float32)
    return y.astype(x.dtype)


def _split_heads(x, h):
    bsz, l, _ = x.shape
    return x.reshape(bsz, l, h, -1).transpose(0, 2, 1, 3)


def _merge_heads(x):
    bsz, h, l, d = x.shape
    return x.transpose(0, 2, 1, 3).reshape(bsz, l, h * d)


def _chunk(x):
    bsz, h, l, d = x.shape
    return x.reshape(bsz, h, l // CHUNK, CHUNK, d)


def _flip_t(x):
    return jnp.flip(x, axis=2)


def _scan_chunks(decay, update, s0):
    def step(s, inp):
        d, u = inp
        return d[..., None] * s + u, s
    s_fin, s_prev = lax.scan(step, s0, (jnp.moveaxis(decay, 2, 0), jnp.moveaxis(update, 2, 0)))
    return s_fin, jnp.moveaxis(s_prev, 0, 2)


def _gla_chunked(q, k, v, log_a, s0):
    q, k, v, la = _chunk(q), _chunk(k), _chunk(v), _chunk(log_a)
    b = jnp.cumsum(la, axis=3)
    b_last = b[:, :, :, -1:, :]
    q_d = q * jnp.exp(b)
    k_d = k * jnp.exp(-b)
    k_s = k * jnp.exp(b_last - b)
    lower = jnp.tril(jnp.ones((CHUNK, CHUNK), jnp.float32))
    att = jnp.einsum('bhncd,bhnsd->bhncs', q_d, k_d) * lower
    upd = jnp.einsum('bhnsd,bhnsv->bhndv', k_s, v)
    s_fin, s_prev = _scan_chunks(jnp.exp(b_last[:, :, :, 0]), upd, s0)
    o = jnp.einsum('bhncs,bhnsv->bhncv', att, v) + jnp.einsum('bhncd,bhndv->bhncv', q_d, s_prev)
    bsz, h, n, c, dv = o.shape
    return o.reshape(bsz, h, n * c, dv), s_fin


def _retention_chunked(q, k, v, log_g, s0):
    q, k, v = _chunk(q), _chunk(k), _chunk(v)
    bsz, h, n, c, dk = q.shape
    pos = jnp.arange(CHUNK, dtype=jnp.float32)
    diff = pos[:, None] - pos[None, :]
    lg = log_g[:, None, None]
    decay_mat = jnp.exp(jnp.where(diff >= 0, lg * diff, -jnp.inf))
    w_q = jnp.exp(log_g[:, None] * (pos + 1.0))
    w_k = jnp.exp(log_g[:, None] * (CHUNK - 1.0 - pos))
    att = jnp.einsum('bhncd,bhnsd->bhncs', q, k) * decay_mat[None, :, None]
    upd = jnp.einsum('bhnsd,hs,bhnsv->bhndv', k, w_k, v)
    chunk_decay = jnp.broadcast_to(jnp.exp(log_g * CHUNK)[None, :, None, None], (bsz, h, n, dk))
    s_fin, s_prev = _scan_chunks(chunk_decay, upd, s0)
    o = jnp.einsum('bhncs,bhnsv->bhncv', att, v) + jnp.einsum('bhncd,hc,bhndv->bhncv', q, w_q, s_prev)
    return o.reshape(bsz, h, n * c, -1), s_fin


def _gla_mixer(q, k, v, g, lr, p, s0):
    f32 = jnp.float32
    qh = _split_heads(q.astype(f32), GLA_HEADS) * GLA_DK ** -0.5
    kh = _split_heads(k.astype(f32), GLA_HEADS)
    vh = _split_heads(v.astype(f32), GLA_HEADS)
    lr = lr.astype(f32).reshape(lr.shape[0], lr.shape[1], N_DIR, GLA_RANK)
    logits = jnp.einsum('bldr,drk->dblk', lr, p['gla_w_gate'].astype(f32)) + p['gla_b_gate'].astype(f32)[:, None, None, :]
    log_a = jax.nn.log_sigmoid(logits) / GLA_TAU
    s0 = s0.astype(f32)
    o_f, s_f = _gla_chunked(qh, kh, vh, _split_heads(log_a[0], GLA_HEADS), s0[:, 0])
    o_b, s_b = _gla_chunked(_flip_t(qh), _flip_t(kh), _flip_t(vh), _flip_t(_split_heads(log_a[1], GLA_HEADS)), s0[:, 1])
    o = o_f + _flip_t(o_b)
    o = o * lax.rsqrt(jnp.mean(o * o, -1, keepdims=True) + NORM_EPS) * p['gla_norm_w'].astype(f32)
    out = _merge_heads(o) * jax.nn.silu(g.astype(f32))
    return out, jnp.stack([s_f, s_b], axis=1)


def _ret_mixer(q, k, v, g, p, s0):
    f32 = jnp.float32
    qh = _split_heads(q.astype(f32), RET_HEADS)
    kh = _split_heads(k.astype(f32), RET_HEADS) * RET_DK ** -0.5
    vh = _split_heads(v.astype(f32), RET_HEADS)
    log_g = jnp.log1p(-jnp.exp2(-p['ret_decay_exp'].astype(f32)))
    s0 = s0.astype(f32)
    o_f, s_f = _retention_chunked(qh, kh, vh, log_g[0], s0[:, 0])
    o_b, s_b = _retention_chunked(_flip_t(qh), _flip_t(kh), _flip_t(vh), log_g[1], s0[:, 1])
    o = o_f + _flip_t(o_b)
    mu = jnp.mean(o, -1, keepdims=True)
    var = jnp.mean(jnp.square(o - mu), -1, keepdims=True)
    o = (o - mu) * lax.rsqrt(var + LN_EPS)
    out = _merge_heads(o) * jax.nn.silu(g.astype(f32))
    return out, jnp.stack([s_f, s_b], axis=1)


def _linear_combine(e1, e2):
    a1, b1 = e1
    a2, b2 = e2
    return a1 * a2, a2 * b1 + b2


def _s5_direction(u, lam, step, bmat, cmat, h0):
    lam_bar = jnp.exp(lam * step[:, None])
    b_bar = ((lam_bar - 1.0) / lam)[..., None] * bmat
    bu = jnp.einsum('gph,blgh->blgp', b_bar, u)
    a = jnp.broadcast_to(lam_bar, bu.shape)
    a_cum, h = lax.associative_scan(_linear_combine, (a, bu), axis=1)
    h = h + a_cum * h0[:, None]
    y = jnp.real(jnp.einsum('ghp,blgp->blgh', cmat, h))
    return y, h[:, -1]


def _s5_mixer(u, p, h0_re, h0_im):
    f32 = jnp.float32
    f = lambda n: p[n].astype(f32)
    bsz, l, _ = u.shape
    uf = u.astype(f32)
    ug = uf.reshape(bsz, l, S5_GROUPS, S5_GROUP).astype(jnp.complex64)
    h0 = lax.complex(h0_re.astype(f32), h0_im.astype(f32))
    lam = lax.complex(f('s5_a_re'), f('s5_a_im'))
    bmat = lax.complex(f('s5_b_re'), f('s5_b_im'))
    cmat = lax.complex(f('s5_c_re'), f('s5_c_im'))
    step = jnp.exp(f('s5_log_step'))
    y_f, h_f = _s5_direction(ug, lam[0], step[0], bmat[0], cmat[0], h0[:, 0])
    y_b, h_b = _s5_direction(jnp.flip(ug, 1), lam[1], step[1], bmat[1], cmat[1], h0[:, 1])
    y = (y_f + jnp.flip(y_b, 1)).reshape(bsz, l, S5_CH) + uf * f('s5_d')
    z = jax.nn.gelu(y)
    out = z * jax.nn.sigmoid(z @ f('s5_glu_w') + f('s5_glu_b'))
    h_new = jnp.stack([h_f, h_b], axis=1)
    return out, jnp.real(h_new), jnp.imag(h_new)


def _short_conv(x, w, b, rows):
    bsz, l, ch = x.shape
    n = l // rows
    pad = HY_SHORT // 2
    xp = jnp.pad(x.reshape(bsz, rows, n, ch), ((0, 0), (0, 0), (pad, pad), (0, 0)))
    y = xp[:, :, 0:n] * w[0]
    for i in range(1, HY_SHORT):
        y = y + xp[:, :, i:i + n] * w[i]
    return (y + b).reshape(bsz, l, ch)


def _hyena_filters(l, p):
    f32 = jnp.float32
    g = lambda n: p[n].astype(f32)
    t = jnp.linspace(0.0, 1.0, l, dtype=f32)[:, None]
    w = 2.0 * math.pi * jnp.arange(l, dtype=f32)[:, None] / l
    fr = jnp.linspace(1e-4, HY_BANDS - 1.0, HY_BANDS, dtype=f32)[None, :]
    feats = jnp.concatenate([t, jnp.cos(fr * w), -jnp.sin(fr * w)], axis=-1)
    freq = g('hy_f_freq')
    h = jnp.sin(freq * (feats @ g('hy_f_w1') + g('hy_f_b1')))
    h = jnp.sin(freq * (h @ g('hy_f_w2') + g('hy_f_b2')))
    filt = (h @ g('hy_f_w3')) * jnp.exp(-t * jnp.abs(g('hy_decay')))
    filt = filt.reshape(l, N_DIR, HY_ORDER, HY_CH)
    filt = filt / jnp.sum(jnp.abs(filt), axis=0, keepdims=True)
    two_sided = jnp.concatenate([filt[:, 0], jnp.zeros((1, HY_ORDER, HY_CH), f32), jnp.flip(filt[1:, 1], 0)], axis=0)
    return jnp.fft.rfft(two_sided, axis=0)


def _hyena_mixer(z, p, rows):
    f32 = jnp.float32
    l = z.shape[1]
    zc = _short_conv(z.astype(f32), p['hy_conv_w'].astype(f32), p['hy_conv_b'].astype(f32), rows)
    v, x1, x2 = jnp.split(zc, 3, axis=-1)
    filt = _hyena_filters(l, p)
    d = p['hy_d'].astype(f32)
    y = v
    for o, gate in enumerate((x1, x2)):
        conv = jnp.fft.irfft(jnp.fft.rfft(y, n=2 * l, axis=1) * filt[None, :, o], n=2 * l, axis=1)[:, :l]
        y = gate * (conv + d[o] * y)
    return y


def _mixer(h, p, state, rows):
    s_gla, s_ret, s5_re, s5_im = state
    proj = h @ p['w_in']
    idx = np.cumsum(IN_SPLITS)[:-1].tolist()
    gq, gk, gv, gg, glr, rq, rk, rv, rg, su, hy = jnp.split(proj, idx, axis=-1)
    o_gla, ns_gla = _gla_mixer(gq, gk, gv, gg, glr, p, s_gla)
    o_ret, ns_ret = _ret_mixer(rq, rk, rv, rg, p, s_ret)
    o_s5, ns_re, ns_im = _s5_mixer(su, p, s5_re, s5_im)
    o_hy = _hyena_mixer(hy, p, rows)
    mix = jnp.concatenate([o_gla, o_ret, o_s5, o_hy], axis=-1).astype(h.dtype)
    return mix @ p['w_out'], (ns_gla, ns_ret, ns_re, ns_im)


def _layer(x, mod, p, state, rows):
    sh1, sc1, g1, sh2, sc2, g2 = jnp.split(mod[:, None, :], 6, axis=-1)
    h = _layer_norm(x) * (1.0 + sc1) + sh1
    mix, new_state = _mixer(h, p, state, rows)
    x = _layer_norm(ALPHA * x + g1 * mix, p['ln1_g'], p['ln1_b'])
    h = _layer_norm(x) * (1.0 + sc2) + sh2
    ff = jnp.square(jax.nn.relu(h @ p['w_up'])) @ p['w_down']
    x = _layer_norm(ALPHA * x + g2 * ff, p['ln2_g'], p['ln2_b'])
    return x, new_state


def setup_inputs(seed: int = 0) -> dict:
    key = jax.random.key(seed)
    ks = iter(jax.random.split(key, 64))
    f32 = jnp.float32
    nrm = lambda shape, s: s * jax.random.normal(next(ks), shape, f32)
    inp = {}
    inp['x_prompt'] = nrm((BATCH, SEQ, D_MODEL), 1.0)
    inp['x_sample'] = nrm((DEC_BATCH, DEC_SEQ, D_MODEL), 1.0)
    inp['state_gla'] = nrm((DEC_BATCH, DEPTH, N_DIR, GLA_HEADS, GLA_DK, GLA_DV), 0.1)
    inp['state_ret'] = nrm((DEC_BATCH, DEPTH, N_DIR, RET_HEADS, RET_DK, RET_DV), 0.1)
    inp['state_s5_re'] = nrm((DEC_BATCH, DEPTH, N_DIR, S5_GROUPS, S5_STATE), 0.1)
    inp['state_s5_im'] = nrm((DEC_BATCH, DEPTH, N_DIR, S5_GROUPS, S5_STATE), 0.1)
    inp['c'] = nrm((DEC_BATCH, D_MODEL), 1.0)
    inp['c_ctx'] = nrm((D_MODEL,), 1.0)
    inp['ada_w'] = nrm((DEPTH, D_MODEL, 6 * D_MODEL), 0.5 * D_MODEL ** -0.5)
    inp['ada_b'] = nrm((DEPTH, 6 * D_MODEL), 0.1)
    inp['w_in'] = nrm((DEPTH, D_MODEL, D_IN), D_MODEL ** -0.5)
    inp['gla_w_gate'] = nrm((DEPTH, N_DIR, GLA_RANK, GLA_HEADS * GLA_DK), GLA_RANK ** -0.5)
    inp['gla_b_gate'] = 1.0 + nrm((DEPTH, N_DIR, GLA_HEADS * GLA_DK), 0.5)
    inp['gla_norm_w'] = 1.0 + nrm((DEPTH, GLA_DV), 0.05)
    inp['ret_decay_exp'] = (5.0 + jnp.arange(RET_HEADS, dtype=f32)[None, None, :]
                            + 0.5 * jnp.arange(N_DIR, dtype=f32)[None, :, None]
                            + nrm((DEPTH, N_DIR, RET_HEADS), 0.05))
    inp['s5_a_re'] = -0.5 + nrm((DEPTH, N_DIR, S5_GROUPS, S5_STATE), 0.01)
    inp['s5_a_im'] = math.pi * jnp.arange(S5_STATE, dtype=f32) + nrm((DEPTH, N_DIR, S5_GROUPS, S5_STATE), 0.01)
    inp['s5_log_step'] = jax.random.uniform(next(ks), (DEPTH, N_DIR, S5_GROUPS), f32, math.log(1e-3), math.log(1e-1))
    inp['s5_b_re'] = nrm((DEPTH, N_DIR, S5_GROUPS, S5_STATE, S5_GROUP), (2 * S5_GROUP) ** -0.5)
    inp['s5_b_im'] = nrm((DEPTH, N_DIR, S5_GROUPS, S5_STATE, S5_GROUP), (2 * S5_GROUP) ** -0.5)
    inp['s5_c_re'] = nrm((DEPTH, N_DIR, S5_GROUPS, S5_GROUP, S5_STATE), S5_STATE ** -0.5)
    inp['s5_c_im'] = nrm((DEPTH, N_DIR, S5_GROUPS, S5_GROUP, S5_STATE), S5_STATE ** -0.5)
    inp['s5_d'] = nrm((DEPTH, S5_CH), 1.0)
    inp['s5_glu_w'] = nrm((DEPTH, S5_CH, S5_CH), S5_CH ** -0.5)
    inp['s5_glu_b'] = nrm((DEPTH, S5_CH), 0.1)
    inp['hy_conv_w'] = nrm((DEPTH, HY_SHORT, 3 * HY_CH), HY_SHORT ** -0.5)
    inp['hy_conv_b'] = nrm((DEPTH, 3 * HY_CH), 0.1)
    inp['hy_f_w1'] = nrm((DEPTH, HY_EMB, HY_HIDDEN), HY_EMB ** -0.5)
    inp['hy_f_b1'] = nrm((DEPTH, HY_HIDDEN), 0.5)
    inp['hy_f_w2'] = nrm((DEPTH, HY_HIDDEN, HY_HIDDEN), HY_HIDDEN ** -0.5)
    inp['hy_f_b2'] = nrm((DEPTH, HY_HIDDEN), 0.5)
    inp['hy_f_freq'] = 1.0 + nrm((DEPTH, HY_HIDDEN), 0.1)
    inp['hy_f_w3'] = nrm((DEPTH, HY_HIDDEN, N_DIR * HY_ORDER * HY_CH), HY_HIDDEN ** -0.5)
    base_decay = jnp.abs(jnp.linspace(math.log(1e-2) / 1.5, math.log(1e-2) / 0.3, HY_CH, dtype=f32))
    inp['hy_decay'] = jnp.tile(base_decay, N_DIR * HY_ORDER)[None, :] + nrm((DEPTH, N_DIR * HY_ORDER * HY_CH), 0.1)
    inp['hy_d'] = nrm((DEPTH, HY_ORDER, HY_CH), 0.5)
    inp['w_out'] = nrm((DEPTH, MIX_WIDTH, D_MODEL), BETA * MIX_WIDTH ** -0.5)
    inp['ln1_g'] = 1.0 + nrm((DEPTH, D_MODEL), 0.05)
    inp['ln1_b'] = nrm((DEPTH, D_MODEL), 0.05)
    inp['w_up'] = nrm((DEPTH, D_MODEL, D_FF), D_MODEL ** -0.5)
    inp['w_down'] = nrm((DEPTH, D_FF, D_MODEL), BETA * D_FF ** -0.5)
    inp['ln2_g'] = 1.0 + nrm((DEPTH, D_MODEL), 0.05)
    inp['ln2_b'] = nrm((DEPTH, D_MODEL), 0.05)
    return inp


def reference(x_prompt, x_sample, state_gla, state_ret, state_s5_re, state_s5_im, c, c_ctx,
              ada_w, ada_b, w_in, gla_w_gate, gla_b_gate, gla_norm_w, ret_decay_exp,
              s5_a_re, s5_a_im, s5_log_step, s5_b_re, s5_b_im, s5_c_re, s5_c_im, s5_d, s5_glu_w, s5_glu_b,
              hy_conv_w, hy_conv_b, hy_f_w1, hy_f_b1, hy_f_w2, hy_f_b2, hy_f_freq, hy_f_w3, hy_decay, hy_d,
              w_out, ln1_g, ln1_b, w_up, w_down, ln2_g, ln2_b):
    f32 = jnp.float32
    stacked = dict(w_in=w_in, gla_w_gate=gla_w_gate, gla_b_gate=gla_b_gate, gla_norm_w=gla_norm_w,
                   ret_decay_exp=ret_decay_exp, s5_a_re=s5_a_re, s5_a_im=s5_a_im, s5_log_step=s5_log_step,
                   s5_b_re=s5_b_re, s5_b_im=s5_b_im, s5_c_re=s5_c_re, s5_c_im=s5_c_im, s5_d=s5_d,
                   s5_glu_w=s5_glu_w, s5_glu_b=s5_glu_b, hy_conv_w=hy_conv_w, hy_conv_b=hy_conv_b,
                   hy_f_w1=hy_f_w1, hy_f_b1=hy_f_b1, hy_f_w2=hy_f_w2, hy_f_b2=hy_f_b2, hy_f_freq=hy_f_freq,
                   hy_f_w3=hy_f_w3, hy_decay=hy_decay, hy_d=hy_d, w_out=w_out, ln1_g=ln1_g, ln1_b=ln1_b,
                   w_up=w_up, w_down=w_down, ln2_g=ln2_g, ln2_b=ln2_b)
    bp = x_prompt.shape[0]
    zero_state = (jnp.zeros((bp, N_DIR, GLA_HEADS, GLA_DK, GLA_DV), f32),
                  jnp.zeros((bp, N_DIR, RET_HEADS, RET_DK, RET_DV), f32),
                  jnp.zeros((bp, N_DIR, S5_GROUPS, S5_STATE), f32),
                  jnp.zeros((bp, N_DIR, S5_GROUPS, S5_STATE), f32))
    rows = x_sample.shape[1] // GRID_W
    yp, ys = x_prompt, x_sample
    ctx_gla, ctx_ret, ctx_re, ctx_im = [], [], [], []
    for l in range(DEPTH):
        p = {name: arr[l] for name, arr in stacked.items()}
        mod_ctx = jax.nn.silu(c_ctx)[None, :] @ ada_w[l] + ada_b[l]
        mod_lat = jax.nn.silu(c) @ ada_w[l] + ada_b[l]
        yp, (sg, sr, s5r, s5i) = _layer(yp, mod_ctx, p, zero_state, 1)
        ctx_gla.append(sg)
        ctx_ret.append(sr)
        ctx_re.append(s5r)
        ctx_im.append(s5i)
        cached = (state_gla[:, l], state_ret[:, l], state_s5_re[:, l], state_s5_im[:, l])
        ys, _ = _layer(ys, mod_lat, p, cached, rows)
    new_state_gla = jnp.stack(ctx_gla, axis=1)
    new_state_ret = jnp.stack(ctx_ret, axis=1)
    new_state_s5_re = jnp.stack(ctx_re, axis=1)
    new_state_s5_im = jnp.stack(ctx_im, axis=1)
    return (yp, ys, new_state_gla, new_state_ret, new_state_s5_re, new_state_s5_im)
```

```python
import numpy as np
import ml_dtypes
from contextlib import ExitStack
import concourse.bass as bass
import concourse.mybir as mybir
from concourse.bass_utils import run_bass_kernel_spmd

F32 = mybir.dt.float32
BF16 = mybir.dt.bfloat16
AF = mybir.ActivationFunctionType
ALU = mybir.AluOpType
AX = mybir.AxisListType

D = 2048
DEPTH = 2
LS = 2048
LP = 256
NTOK = LS + 2 * LP
NTT = NTOK // 128
NTB = NTOK // 512
DFF = 8192
DIN = 5152
ALPHA = (2 * DEPTH) ** 0.25
LN_EPS = 1e-5
NORM_EPS = 1e-6
OFF = dict(gq=0, gk=256, gv=512, gg=1024, glr=1536, rq=1568, rk=1824, rv=2080, rg=2592, su=3104, hy=3616)
SEQS = [(0, LS), (LS, LP), (LS + LP, LP)]


class Buf:
    __slots__ = ("name", "w", "r", "dsem", "dcnt", "sw")

    def __init__(self, name):
        self.name = name
        self.w = None
        self.r = {}
        self.dsem = None
        self.dcnt = 0


class T:
    def __init__(self, t, name, excl=False):
        self.t = t
        self.b = Buf(name)
        self.excl = excl

    def __getitem__(self, k):
        return self.t[k]


class Eng:
    def __init__(self, P, name, h, selfsync=True):
        self.P = P
        self.name = name
        self.h = h
        self.selfsync = selfsync
        self.sem = P.newsem()
        self.cnt = 0
        self.seen = {}
        self.mysems = {id(self.sem)}


class Prog:
    SEM_LIMIT = 12000

    def __init__(self, nc):
        self.nc = nc
        self.es = ExitStack()
        self.nsem = 0
        self.freesems = []
        self.freesems_sw = []
        self.bufs = []
        self.semobj = {}
        self.pe = Eng(self, "pe", nc.tensor, selfsync=False)
        self.dve = Eng(self, "dve", nc.vector)
        self.act = Eng(self, "act", nc.scalar)
        self.pool = Eng(self, "pool", nc.gpsimd)
        self.sp = Eng(self, "sp", nc.sync)
        self.engs = [self.pe, self.dve, self.act, self.pool, self.sp]
        self.dchans = []
        self.nscope = 0

    def newsem(self):
        self.nsem += 1
        s = self.es.enter_context(self.nc.semaphore(f"sem{self.nsem}"))
        self.semobj[id(s)] = s
        return s

    def reg(self, t):
        self.bufs.append(t.b)
        return t

    def dram(self, name, shape, dt, kind="Internal"):
        h = self.nc.dram_tensor(name, list(shape), dt, kind=kind)
        return self.reg(T(h.ap(), name))

    def _need(self, eng, evs):
        for ev in evs:
            if ev is None:
                continue
            sem, val = ev
            k = id(sem)
            if (not eng.selfsync) and k in eng.mysems:
                continue
            if eng.seen.get(k, 0) >= val:
                continue
            eng.h.wait_ge(sem, val)
            eng.seen[k] = val

    def _deps(self, reads, writes, eng=None):
        evs = []
        for t in reads:
            evs.append(t.b.w)
            if t.excl:
                for k, (s, v) in t.b.r.items():
                    if eng is None or k not in eng.mysems:
                        evs.append((s, v))
        for t in writes:
            evs.append(t.b.w)
            for k, (s, v) in t.b.r.items():
                evs.append((s, v))
        return evs

    def _commit(self, ev, reads, writes):
        for t in reads:
            t.b.r[id(ev[0])] = ev
        for t in writes:
            t.b.w = ev
            t.b.r = {}

    def op(self, eng, fn, reads=(), writes=()):
        self._need(eng, self._deps(reads, writes, eng))
        inst = fn()
        eng.cnt += 1
        inst.then_inc(eng.sem, 1)
        ev = (eng.sem, eng.cnt)
        self._commit(ev, reads, writes)
        if eng.cnt >= self.SEM_LIMIT:
            eng.sem = self.newsem()
            eng.mysems.add(id(eng.sem))
            eng.cnt = 0
        return inst

    def dma(self, q, out, in_, reads, writes, chan, **kw):
        self._need(q, self._deps(reads, writes, q))
        sw = (q is self.pool)
        key = "sw" if sw else "hw"
        if not hasattr(chan, "chs"):
            chan.chs = {}
        if key not in chan.chs:
            pool_ = self.freesems_sw if sw else self.freesems
            b = Buf(chan.b.name + "_" + key)
            if pool_:
                b.dsem, b.dcnt = pool_.pop()
            else:
                b.dsem, b.dcnt = self.newsem(), 0
            b.sw = sw
            chan.chs[key] = b
            self.dchans.append(b)
        b = chan.chs[key]
        inst = q.h.dma_start(out=out, in_=in_, **kw)
        b.dcnt += 16
        inst.then_inc(b.dsem, 16)
        ev = (b.dsem, b.dcnt)
        self._commit(ev, reads, writes)
        return inst

    def barrier(self):
        evs = [(e.sem, e.cnt) for e in self.engs if e.cnt > 0]
        evs += [(b.dsem, b.dcnt) for b in self.dchans if b.dcnt > 0]
        for e in self.engs:
            sv = e.selfsync
            e.selfsync = True
            self._need(e, [ev for ev in evs if sv or id(ev[0]) not in e.mysems])
            e.selfsync = sv
        for b in self.bufs:
            b.w = None
            b.r = {}

    def scope(self):
        return Scope(self)


class Scope:
    def __init__(self, P):
        self.P = P
        self.es = ExitStack()
        self.ts = []
        P.nscope += 1
        self.id = P.nscope

    def sb(self, name, shape, dt=F32):
        t = self.es.enter_context(self.P.nc.sbuf_tensor(f"{name}_{self.id}", list(shape), dt))
        tt = self.P.reg(T(t, name))
        self.ts.append(tt)
        return tt

    def ps(self, name, shape, dt=F32):
        t = self.es.enter_context(self.P.nc.psum_tensor(f"{name}_{self.id}", list(shape), dt))
        tt = self.P.reg(T(t, name, excl=True))
        self.ts.append(tt)
        return tt

    def close(self):
        P = self.P
        P.barrier()
        for tt in self.ts:
            for key, cb in getattr(tt, "chs", {}).items():
                (P.freesems_sw if cb.sw else P.freesems).append((cb.dsem, cb.dcnt))
                P.dchans.remove(cb)
            tt.chs = {}
            P.bufs.remove(tt.b)
        self.es.close()


def build_program(debug=False):
    nc = bass.Bass("TRN2", target_bir_lowering=False, dynamic_dma_scratch_size=8192)
    P = Prog(nc)
    pe, dve, act, pool, sp = P.pe, P.dve, P.act, P.pool, P.sp

    def din(name, shape, dt=F32):
        return P.dram(name, shape, dt, kind="ExternalInput")

    def dout(name, shape, dt=F32):
        return P.dram(name, shape, dt, kind="ExternalOutput")

    x_in = din("x", [NTOK, D])
    cvec = din("cvec", [2, D])
    ada_w = din("ada_w", [DEPTH, D, 6 * D])
    ada_b = din("ada_b", [DEPTH, 6 * D])
    w_in = din("w_in", [DEPTH, D, DIN])
    w_out = din("w_out", [DEPTH, D, D])
    w_up = din("w_up", [DEPTH, D, DFF])
    w_down = din("w_down", [DEPTH, DFF, D])
    ln1_g = din("ln1_g", [DEPTH, D]); ln1_b = din("ln1_b", [DEPTH, D])
    ln2_g = din("ln2_g", [DEPTH, D]); ln2_b = din("ln2_b", [DEPTH, D])
    sg_in = din("sg", [DEPTH, 2, 4, 64, 128]); sr_in = din("sr", [DEPTH, 2, 4, 64, 128])
    gla_w_gate = din("gla_w_gate", [DEPTH, 2, 16, 256]); gla_b_gate = din("gla_b_gate", [DEPTH, 2, 256])
    gla_norm_w = din("gla_norm_w", [DEPTH, 128]); ret_decay_exp = din("ret_decay_exp", [DEPTH, 2, 4])
    s5re_in = din("s5re", [DEPTH, 2, 32, 64]); s5im_in = din("s5im", [DEPTH, 2, 32, 64])
    s5_a_re = din("s5_a_re", [DEPTH, 2, 32, 64]); s5_a_im = din("s5_a_im", [DEPTH, 2, 32, 64])
    s5_log_step = din("s5_log_step", [DEPTH, 2, 32])
    s5_b_re = din("s5_b_re", [DEPTH, 2, 32, 64, 16]); s5_b_im = din("s5_b_im", [DEPTH, 2, 32, 64, 16])
    s5_c_re = din("s5_c_re", [DEPTH, 2, 32, 16, 64]); s5_c_im = din("s5_c_im", [DEPTH, 2, 32, 16, 64])
    s5_d = din("s5_d", [DEPTH, 512]); s5_glu_w = din("s5_glu_w", [DEPTH, 512, 512]); s5_glu_b = din("s5_glu_b", [DEPTH, 512])
    tp1_in = din("tp1", [LS])
    hy_conv_w = din("hy_conv_w", [DEPTH, 3, 1536]); hy_conv_b = din("hy_conv_b", [DEPTH, 1536])
    hy_f_w1 = din("hy_f_w1", [DEPTH, 33, 64]); hy_f_b1 = din("hy_f_b1", [DEPTH, 64])
    hy_f_w2 = din("hy_f_w2", [DEPTH, 64, 64]); hy_f_b2 = din("hy_f_b2", [DEPTH, 64])
    hy_f_freq = din("hy_f_freq", [DEPTH, 64]); hy_f_w3 = din("hy_f_w3", [DEPTH, 64, 2048])
    hy_decay = din("hy_decay", [DEPTH, 2048]); hy_d = din("hy_d", [DEPTH, 2, 512])
    fstash = P.dram("fstash", [2, 2, 128, 16, 512], BF16)
    s5BT = P.dram("s5BT", [32, 128, 2, 128], BF16); s5CP = P.dram("s5CP", [32, 128, 2, 128], BF16)
    hyc = {}
    for L_ in (LS, LP):
        sx = str(L_); ntc_ = L_ // 128; nfk_ = ntc_ + 1; nblk_ = max(1, L_ // 512); bw_ = min(512, L_)
        hyc["tFC" + sx] = din("tFC" + sx, [nfk_, 128, ntc_, 128], BF16); hyc["tFS" + sx] = din("tFS" + sx, [nfk_, 128, ntc_, 128], BF16)
        hyc["tIC" + sx] = din("tIC" + sx, [nblk_, 128, nfk_, bw_], BF16); hyc["tIS" + sx] = din("tIS" + sx, [nblk_, 128, nfk_, bw_], BF16)
        hyc["featsT" + sx] = din("featsT" + sx, [64, L_]); hyc["featsTr" + sx] = din("featsTr" + sx, [64, L_])
        hyc["negtv" + sx] = din("negtv" + sx, [128, ntc_]); hyc["negtvr" + sx] = din("negtvr" + sx, [128, ntc_])
        hyc["wk" + sx] = din("wk" + sx, [128, nfk_]); hyc["sgw" + sx] = din("sgw" + sx, [128, nfk_])
        hyc["Fs" + sx] = P.dram("Fs" + sx, [2, nfk_, 128, 2, 512], F32)
    ident_in = din("ident", [128, 128])
    masks_in = din("masks", [2, 128, 128])
    y_out = dout("y", [NTOK, D])
    nsg = dout("nsg", [2, DEPTH, 2, 4, 64, 128]); nsr = dout("nsr", [2, DEPTH, 2, 4, 64, 128])
    ns5re = dout("ns5re", [2, DEPTH, 2, 32, 64]); ns5im = dout("ns5im", [2, DEPTH, 2, 32, 64])
    kind = "ExternalOutput" if debug else "Internal"
    mixonly = isinstance(debug, str) and debug.startswith("mix")
    pkind = "ExternalInput" if mixonly else kind
    xcur = P.dram("xcur", [NTOK, D], F32, kind=kind)
    modvec = P.dram("modvec", [DEPTH, 2, 6 * D], F32)
    wo_bf = P.dram("wo_bf", [DEPTH, 4, 128, 16, 512], BF16)
    wu_bf = P.dram("wu_bf", [DEPTH, 16, 128, 16, 512], BF16)
    wd_bf = P.dram("wd_bf", [DEPTH, 4, 8, 128, 8, 512], BF16)
    pF = {n: P.dram("pF_" + n, [c, 128, NTOK], F32, kind=pkind) for n, c in
          dict(qkg=4, qkr=4, su=4, hy=12).items()}
    pLr = P.dram("pF_lr", [32, NTOK], F32, kind=pkind)
    pT = {n: P.dram("pT_" + n, [NTOK, 512], F32, kind=pkind) for n in ("gv", "gg", "rv", "rg")}
    mixT = P.dram("mixT", [16, 128, NTOK], BF16, kind=kind)

    G = P.scope()
    ident_f = G.sb("ident_f", [128, 128], F32)
    ident_b = G.sb("ident_b", [128, 128], BF16)
    modT = G.sb("modT", [128, DEPTH, 96, 2], F32)
    P.dma(sp, ident_f[:], ident_in[:, :], [ident_in], [ident_f], ident_f)
    P.op(dve, lambda: nc.vector.tensor_copy(out=ident_b[:], in_=ident_f[:]), [ident_f], [ident_b])

    def load_slab(q, dst, src_ap, srcT):
        P.dma(q, dst[:], src_ap.rearrange("(k p) n -> p k n", p=128), [srcT], [dst], dst)

    cast_rr = [0]

    def cast(dst, src, nk):
        engs = [(act, lambda o, i: nc.scalar.copy(out=o, in_=i)),
                (dve, lambda o, i: nc.vector.tensor_copy(out=o, in_=i))]
        h = (nk * 5) // 8
        for (k0, k1) in ((0, h), (h, nk)):
            e, f = engs[cast_rr[0] % 2]
            cast_rr[0] += 1
            P.op(e, lambda f=f, k0=k0, k1=k1: f(dst[:, k0:k1, :], src[:, k0:k1, :]), [src], [dst])

    if not mixonly:
        S0 = P.scope()
        cT = S0.sb("cT", [128, 16, 2])
        abT = S0.sb("abT", [128, DEPTH, 96])
        stg = [S0.sb(f"stg{i}", [128, 16, 512]) for i in range(2)]
        stb = [S0.sb(f"stb{i}", [128, 16, 512], BF16) for i in range(2)]
        modps = [S0.ps(f"modps{l}", [128, 96, 2]) for l in range(DEPTH)]
        with nc.allow_non_contiguous_dma(reason="tiny param transposes"):
            for r in range(2):
                P.dma(sp, cT[:, :, r], cvec[r].rearrange("(k p) -> p k", p=128), [cvec], [cT], cT)
            for l in range(DEPTH):
                P.dma(sp, abT[:, l, :], ada_b[l].rearrange("(j p) -> p j", p=128), [ada_b], [abT], abT)
        P.op(act, lambda: nc.scalar.activation(out=cT[:], in_=cT[:], func=AF.Silu), [cT], [cT])
        cTb = S0.sb("cTb", [128, 16, 2], BF16)
        P.op(dve, lambda: nc.vector.tensor_copy(out=cTb[:], in_=cT[:]), [cT], [cTb])
        it = 0
        for l in range(DEPTH):
            for s in range(24):
                buf0 = stg[it % 2]
                buf = stb[it % 2]
                it += 1
                if s == 0:
                    load_slab(sp, buf0, ada_w[l, :, 0:512], ada_w)
                if s + 1 < 24:
                    load_slab(sp, stg[it % 2], ada_w[l, :, (s + 1) * 512:(s + 2) * 512], ada_w)
                cast(buf, buf0, 16)
                for j in range(4):
                    for k in range(16):
                        P.op(pe, lambda buf=buf, j=j, k=k, s=s, l=l: nc.tensor.matmul(
                            modps[l][:, s * 4 + j, :], lhsT=buf[:, k, j * 128:(j + 1) * 128], rhs=cTb[:, k, :],
                            start=(k == 0), stop=(k == 15)), [buf, cTb], [modps[l]])
            for r in range(2):
                P.op(dve, lambda l=l, r=r: nc.vector.tensor_tensor(out=modT[:, l, :, r], in0=modps[l][:, :, r],
                                                                   in1=abT[:, l, :], op=ALU.add),
                     [modps[l], abT], [modT])
        with nc.allow_non_contiguous_dma(reason="mod vectors to DRAM rows"):
            for l in range(DEPTH):
                for r in range(2):
                    P.dma(sp, modvec[l, r].rearrange("(j p) -> p j", p=128), modT[:, l, :, r], [modT], [modvec], modT)
        for l in range(DEPTH):
            for c0 in (16, 64):
                P.op(dve, lambda l=l, c0=c0: nc.vector.tensor_scalar_add(out=modT[:, l, c0:c0 + 16, :],
                                                                         in0=modT[:, l, c0:c0 + 16, :], scalar1=1.0),
                     [modT], [modT])
        S0.close()

    convjobs = {}
    for l in range(DEPTH):
        jobs = []
        for s_ in range(4):
            jobs.append((w_out, w_out[l, :, s_ * 512:(s_ + 1) * 512], wo_bf, wo_bf[l, s_], 16))
        for s_ in range(16):
            jobs.append((w_up, w_up[l, :, s_ * 512:(s_ + 1) * 512], wu_bf, wu_bf[l, s_], 16))
        for c in range(4):
            for kg in range(0, 8, 2):
                jobs.append((w_down, w_down[l, kg * 1024:(kg + 2) * 1024, c * 512:(c + 1) * 512], wd_bf,
                             wd_bf[l, c, kg:kg + 2].rearrange("g p k n -> p g k n"), 16))
        convjobs[l] = jobs if not mixonly else []

    def ln_stats(S, xt, eps, name):
        st = S.sb(name + "_st", [128, 4, 6])
        mv = S.sb(name + "_mv", [128, 2])
        rs = S.sb(name + "_rs", [128, 1])
        return st, mv, rs

    def do_ln_stats(st, mv, rs, xt, eps):
        for c in range(4):
            P.op(dve, lambda c=c: nc.vector.bn_stats(out=st[:, c, :], in_=xt[:, c * 512:(c + 1) * 512]), [xt], [st])
        P.op(dve, lambda: nc.vector.bn_aggr(out=mv[:], in_=st[:].rearrange("p c s -> p (c s)")), [st], [mv])
        P.op(act, lambda: nc.scalar.activation(out=rs[:], in_=mv[:, 1:2], func=AF.Ln, bias=eps), [mv], [rs])
        P.op(act, lambda: nc.scalar.activation(out=rs[:], in_=rs[:], func=AF.Exp, scale=-0.5), [rs], [rs])

    for l in range(DEPTH):
        xsrc = x_in if l == 0 else xcur
        xdst = xcur if l == 0 else y_out
        if not mixonly:
            SA = P.scope()
            hT = SA.sb("hT", [128, 16, NTOK], BF16)
            SA1 = P.scope()
            xin = [SA1.sb(f"xin{i}", [128, D]) for i in range(2)]
            xn = [SA1.sb(f"xn{i}", [128, D], BF16) for i in range(2)]
            stq = [ln_stats(SA1, None, LN_EPS, f"lnA{i}") for i in range(2)]
            psT = [SA1.ps(f"psT{i}", [128, 16, 128], BF16) for i in range(2)]
            for tt in range(NTT):
                typ = 0 if tt < LS // 128 else 1
                xi, xb_, (st, mv, rs), pt = xin[tt % 2], xn[tt % 2], stq[tt % 2], psT[tt % 2]
                P.dma(sp, xi[:], xsrc[tt * 128:(tt + 1) * 128, :], [xsrc], [xi], xi)
                do_ln_stats(st, mv, rs, xi, LN_EPS)
                P.op(dve, lambda xi=xi, xb_=xb_, mv=mv, rs=rs: nc.vector.tensor_scalar(
                    out=xb_[:], in0=xi[:], scalar1=mv[:, 0:1], scalar2=rs[:, 0:1], op0=ALU.subtract, op1=ALU.mult),
                     [xi, mv, rs], [xb_])
                for j in range(16):
                    P.op(pe, lambda j=j, pt=pt, xb_=xb_: nc.tensor.transpose(pt[:, j, :], xb_[:, j * 128:(j + 1) * 128], ident_b[:]),
                         [xb_, ident_b], [pt])
                for j in range(16):
                    if j % 2 == 0:
                        P.op(dve, lambda j=j, pt=pt, typ=typ, tt=tt: nc.vector.tensor_scalar(
                            out=hT[:, j, tt * 128:(tt + 1) * 128], in0=pt[:, j, :], scalar1=modT[:, l, 16 + j, typ:typ + 1],
                            scalar2=modT[:, l, j, typ:typ + 1], op0=ALU.mult, op1=ALU.add), [pt, modT], [hT])
                    else:
                        P.op(act, lambda j=j, pt=pt, typ=typ, tt=tt: nc.scalar.activation(
                            out=hT[:, j, tt * 128:(tt + 1) * 128], in_=pt[:, j, :], func=AF.Identity,
                            scale=modT[:, l, 16 + j, typ:typ + 1], bias=modT[:, l, j, typ:typ + 1]), [pt, modT], [hT])
            SA1.close()
            SB = P.scope()
            wst = [SB.sb(f"wst{i}", [128, 16, 512]) for i in range(2)]
            wsb = [SB.sb(f"wsb{i}", [128, 16, 512], BF16) for i in range(2)]
            ost = [SB.sb(f"ost{i}", [128, 512]) for i in range(4)]
            psB = [SB.ps(f"psB{i}", [128, 512]) for i in range(4)]
            oi = [0]

            def evac(ps_ap, psT_, dst_dram_ap, dstT, np_, scale=1.0, func=None):
                o = ost[oi[0] % 4]
                e = oi[0] % 2
                oi[0] += 1
                if func is not None or e == 0:
                    P.op(act, lambda: nc.scalar.activation(out=o[0:np_, :], in_=ps_ap, func=(func or AF.Copy), scale=scale),
                         [psT_], [o])
                else:
                    P.op(dve, lambda: nc.vector.tensor_scalar_mul(out=o[0:np_, :], in0=ps_ap, scalar1=scale), [psT_], [o])
                P.dma(pool, dst_dram_ap, o[0:np_, :], [o], [dstT], o)

            slabs = [("F", OFF["gq"], pF["qkg"], (0.125, 0.125, 1.0, 1.0)), ("T", OFF["gv"], pT["gv"], None),
                     ("T", OFF["gg"], pT["gg"], AF.Silu), ("L", OFF["glr"], pLr, None),
                     ("F", OFF["rq"], pF["qkr"], (1.0, 1.0, 0.125, 0.125)), ("T", OFF["rv"], pT["rv"], None),
                     ("T", OFF["rg"], pT["rg"], AF.Silu), ("F", OFF["su"], pF["su"], (1.0,) * 4),
                     ("F3", OFF["hy"], pF["hy"], 0), ("F3", OFF["hy"] + 512, pF["hy"], 4), ("F3", OFF["hy"] + 1024, pF["hy"], 8)]
            for si, (kind_, c0, dstT, extra) in enumerate(slabs):
                a, b = wst[si % 2], wsb[si % 2]
                ncol = 32 if kind_ == "L" else 512
                P.dma(sp, a[:, :, 0:ncol], w_in[l, :, c0:c0 + ncol].rearrange("(k p) n -> p k n", p=128), [w_in], [a], a)
                h8 = 8
                P.op(act, lambda a=a, b=b, ncol=ncol: nc.scalar.copy(out=b[:, 0:8, 0:ncol], in_=a[:, 0:8, 0:ncol]), [a], [b])
                P.op(dve, lambda a=a, b=b, ncol=ncol: nc.vector.tensor_copy(out=b[:, 8:16, 0:ncol], in_=a[:, 8:16, 0:ncol]), [a], [b])
                if kind_ in ("F", "F3"):
                    for tb in range(NTB):
                        for j in range(4):
                            ps = psB[(tb * 4 + j) % 4]
                            for k in range(16):
                                P.op(pe, lambda ps=ps, b=b, j=j, k=k, tb=tb: nc.tensor.matmul(
                                    ps[:], lhsT=b[:, k, j * 128:(j + 1) * 128], rhs=hT[:, k, tb * 512:(tb + 1) * 512],
                                    start=(k == 0), stop=(k == 15)), [b, hT], [ps])
                            if kind_ == "F":
                                evac(ps[:], ps, dstT[j, :, tb * 512:(tb + 1) * 512], dstT, 128, scale=extra[j])
                            else:
                                evac(ps[:], ps, dstT[extra + j, :, tb * 512:(tb + 1) * 512], dstT, 128)
                elif kind_ == "L":
                    for tb in range(NTB):
                        ps = psB[tb % 4]
                        for k in range(16):
                            P.op(pe, lambda ps=ps, b=b, k=k, tb=tb: nc.tensor.matmul(
                                ps[0:32, :], lhsT=b[:, k, 0:32], rhs=hT[:, k, tb * 512:(tb + 1) * 512],
                                start=(k == 0), stop=(k == 15)), [b, hT], [ps])
                        evac(ps[0:32, :], ps, dstT[:, tb * 512:(tb + 1) * 512], dstT, 32)
                else:
                    for tt in range(NTT):
                        ps = psB[tt % 4]
                        for k in range(16):
                            P.op(pe, lambda ps=ps, b=b, k=k, tt=tt: nc.tensor.matmul(
                                ps[:], lhsT=hT[:, k, tt * 128:(tt + 1) * 128], rhs=b[:, k, :],
                                start=(k == 0), stop=(k == 15)), [b, hT], [ps])
                        evac(ps[:], ps, dstT[tt * 128:(tt + 1) * 128, :], dstT, 128, func=extra)
            SB.close()
            SA.close()

        env_ = dict(locals()); env_.update(hyc); env_["convjobs"] = convjobs[l]
        mixers(P, nc, l, env_)
        if debug == 1 or mixonly:
            break

        if not mixonly:
            SD = P.scope()
            mxb = SD.sb("mxb", [128, 16, 512], BF16)
            h2T = mxb
            uT = SD.sb("uT", [128, 64, 512], BF16)
            xa = [[SD.sb(f"xa{s_}{i}", [128, D]) for i in range(4)] for s_ in range(2)]
            bc = {n: SD.sb("bc_" + n, [128, D]) for n in ("g1", "g2", "lg", "lb")}
            wsl = [SD.sb(f"wsl{i}", [128, 8, 512], BF16) for i in range(3)]
            tmp = [SD.sb(f"tmpD{i}", [128, 512]) for i in range(2)]
            xnb = SD.sb("xnb", [128, D], BF16)
            stD = ln_stats(SD, None, LN_EPS, "lnD")
            pb8 = [SD.ps(f"pb8_{i}", [128, 512]) for i in range(8)]
            psAcc = pb8[0:4]
            def load_lnp(gT, bT):
                P.dma(sp, bc["lg"][:], gT[l].partition_broadcast(128), [gT], [bc["lg"]], bc["lg"])
                P.dma(sp, bc["lb"][:], bT[l].partition_broadcast(128), [bT], [bc["lb"]], bc["lb"])
            wi = [0]

            def wload(src_ap, srcT, nk=8):
                w = wsl[wi[0] % 3]
                wi[0] += 1
                P.dma(sp, w[:, 0:nk, :], src_ap, [srcT], [w], w)
                return w

            def resid(xt, c, ps, gname, ti):
                t_ = tmp[ti % 2]
                P.op(dve, lambda: nc.vector.tensor_tensor(out=t_[:], in0=ps[:], in1=bc[gname][:, c * 512:(c + 1) * 512], op=ALU.mult),
                     [ps, bc[gname]], [t_])
                P.op(dve, lambda: nc.vector.scalar_tensor_tensor(out=xt[:, c * 512:(c + 1) * 512], in0=xt[:, c * 512:(c + 1) * 512],
                                                                 scalar=ALPHA, in1=t_[:], op0=ALU.mult, op1=ALU.add),
                     [t_, xt], [xt])

            nmr = SD.sb("nmr", [128, 1])

            def act_norm(dst, src, srcT, dstT):
                st, mv, rs = stD
                P.op(dve, lambda: nc.vector.scalar_tensor_tensor(out=nmr[:], in0=mv[:, 0:1], scalar=-1.0, in1=rs[:], op0=ALU.mult, op1=ALU.mult),
                     [mv, rs], [nmr])
                P.op(act, lambda: nc.scalar.activation(out=dst, in_=src, func=AF.Identity, scale=rs[:, 0:1], bias=nmr[:, 0:1]), [srcT, rs, nmr], [dstT])

            def ln_affine(xt, gn, bn):
                st, mv, rs = stD
                do_ln_stats(st, mv, rs, xt, LN_EPS)
                act_norm(xt[:], xt[:], xt, xt)
                P.op(dve, lambda: nc.vector.tensor_tensor(out=xt[:], in0=xt[:], in1=bc[gn][:], op=ALU.mult), [xt, bc[gn]], [xt])
                P.op(dve, lambda: nc.vector.tensor_tensor(out=xt[:], in0=xt[:], in1=bc[bn][:], op=ALU.add), [xt, bc[bn]], [xt])

            tic = [0]

            def typ_of(tb):
                return 0 if tb < 4 else 1

            def st_load(tb):
                typ = typ_of(tb)
                if tb == 0 or tb == 4:
                    P.dma(sp, bc["g1"][:], modvec[l, typ, 2 * D:3 * D].partition_broadcast(128), [modvec], [bc["g1"]], bc["g1"])
                P.dma(sp, mxb[:], mixT[:, :, tb * 512:(tb + 1) * 512].rearrange("k p n -> p k n"), [mixT], [mxb], mxb)
                for tt in range(4):
                    t0 = tb * 512 + tt * 128
                    P.dma(sp, xa[tb % 2][tt][:], xsrc[t0:t0 + 128, :], [xsrc], [xa[tb % 2][tt]], xa[tb % 2][tt])

            def st_D1(tb):
                X = xa[tb % 2]
                for c in range(4):
                    for hf in range(2):
                        w = wload(wo_bf[l, c, :, hf * 8:(hf + 1) * 8, :], wo_bf)
                        for tt in range(4):
                            for k8 in range(8):
                                k = hf * 8 + k8
                                P.op(pe, lambda w=w, tt=tt, k=k, k8=k8: nc.tensor.matmul(
                                    psAcc[tt][:], lhsT=mxb[:, k, tt * 128:(tt + 1) * 128], rhs=w[:, k8, :],
                                    start=(k == 0), stop=(k == 15)), [w, mxb], [psAcc[tt]])
                    for tt in range(4):
                        resid(X[tt], c, psAcc[tt], "g1", tic[0]); tic[0] += 1

            def st_D2(tb, tt):
                X = xa[tb % 2]
                typ = typ_of(tb)
                if tt == 0:
                    load_lnp(ln1_g, ln1_b)
                ln_affine(X[tt], "lg", "lb")
                st, mv, rs = stD
                do_ln_stats(st, mv, rs, X[tt], LN_EPS)
                act_norm(xnb[:], X[tt][:], X[tt], xnb)
                def tview(j):
                    bank = pb8[6 + j // 8]
                    return bank, bank[:].bitcast(BF16)[:, (j % 8) * 128:(j % 8 + 1) * 128]
                for j in range(16):
                    bank, v = tview(j)
                    P.op(pe, lambda j=j, v=v: nc.tensor.transpose(v, xnb[:, j * 128:(j + 1) * 128], ident_b[:]),
                         [xnb, ident_b], [bank])
                for j in range(16):
                    bank, v = tview(j)
                    if j < 8:
                        P.op(dve, lambda j=j, v=v: nc.vector.tensor_scalar(
                            out=h2T[:, j, tt * 128:(tt + 1) * 128], in0=v, scalar1=modT[:, l, 64 + j, typ:typ + 1],
                            scalar2=modT[:, l, 48 + j, typ:typ + 1], op0=ALU.mult, op1=ALU.add), [bank, modT], [h2T])
                    else:
                        P.op(act, lambda j=j, v=v: nc.scalar.activation(
                            out=h2T[:, j, tt * 128:(tt + 1) * 128], in_=v, func=AF.Identity,
                            scale=modT[:, l, 64 + j, typ:typ + 1], bias=modT[:, l, 48 + j, typ:typ + 1]), [bank, modT], [h2T])

            def st_D3(tb):
                ei = 0
                for s_ in range(16):
                    banks = pb8[(s_ % 2) * 4:(s_ % 2) * 4 + 4]
                    for hf in range(2):
                        w = wload(wu_bf[l, s_, :, hf * 8:(hf + 1) * 8, :], wu_bf)
                        for k8 in range(8):
                            k = hf * 8 + k8
                            for j in range(4):
                                P.op(pe, lambda w=w, j=j, k=k, k8=k8, banks=banks: nc.tensor.matmul(
                                    banks[j][:], lhsT=w[:, k8, j * 128:(j + 1) * 128], rhs=h2T[:, k, :], start=(k == 0), stop=(k == 15)),
                                     [w, h2T], [banks[j]])
                    for j in range(4):
                        t_ = tmp[ei % 2]
                        ei += 1
                        P.op(act, lambda j=j, t_=t_, banks=banks: nc.scalar.activation(out=t_[:], in_=banks[j][:], func=AF.Relu), [banks[j]], [t_])
                        P.op(pool, lambda t_=t_, s_=s_, j=j: nc.gpsimd.tensor_tensor(out=uT[:, s_ * 4 + j, :], in0=t_[:], in1=t_[:], op=ALU.mult),
                             [t_], [uT])

            def st_D4(tb, c):
                X = xa[tb % 2]
                if c == 0 and (tb == 0 or tb == 4):
                    typ = typ_of(tb)
                    P.dma(sp, bc["g2"][:], modvec[l, typ, 5 * D:6 * D].partition_broadcast(128), [modvec], [bc["g2"]], bc["g2"])
                for kg in range(8):
                    w = wload(wd_bf[l, c, kg], wd_bf)
                    for k in range(8):
                        kk = kg * 8 + k
                        for tt in range(4):
                            P.op(pe, lambda w=w, tt=tt, k=k, kk=kk: nc.tensor.matmul(
                                psAcc[tt][:], lhsT=uT[:, kk, tt * 128:(tt + 1) * 128], rhs=w[:, k, :],
                                start=(kk == 0), stop=(kk == 63)), [w, uT], [psAcc[tt]])

            def st_D4r(tb, c):
                X = xa[tb % 2]
                for tt in range(4):
                    resid(X[tt], c, psAcc[tt], "g2", tic[0]); tic[0] += 1

            def st_D5(tb):
                X = xa[tb % 2]
                load_lnp(ln2_g, ln2_b)
                for tt in range(4):
                    ln_affine(X[tt], "lg", "lb")
                    t0 = tb * 512 + tt * 128
                    P.dma(pool, xdst[t0:t0 + 128, :], X[tt][:], [X[tt]], [xdst], X[tt])

            st_load(0)
            st_D1(0)
            for tt in range(4):
                st_D2(0, tt)
            for tb in range(NTB):
                st_D3(tb)
                nxt = tb + 1 < NTB
                if nxt:
                    st_load(tb + 1)
                st_D4(tb, 0)
                st_D4r(tb, 0)
                if nxt:
                    st_D1(tb + 1)
                for c in range(1, 4):
                    st_D4(tb, c)
                    if nxt:
                        st_D2(tb + 1, c - 1)
                        if c == 3:
                            st_D2(tb + 1, 3)
                    st_D4r(tb, c)
                st_D5(tb)
            SD.close()

    G.close()
    P.barrier()
    P.es.close()
    return nc


def mixers(P, nc, l, env):
    sel = env.get("debug")
    sel = sel[4:] if isinstance(sel, str) and sel.startswith("mix:") else "hy,s5,gla,ret"
    if "hy" in sel:
        mixer_hy(P, nc, l, env)
    if "s5" in sel:
        mixer_s5(P, nc, l, env)
    for gi in range(2):
        if ("gla", "ret")[gi] in sel:
            mixer_la(P, nc, l, env, gi)


def mixer_la(P, nc, l, env, gi):
    pe, dve, act, pool, sp = P.pe, P.dve, P.act, P.pool, P.sp
    qk = env["pF"]["qkg" if gi == 0 else "qkr"]
    vT = env["pT"]["gv" if gi == 0 else "rv"]
    gT = env["pT"]["gg" if gi == 0 else "rg"]
    pLr = env["pLr"]
    mixT = env["mixT"]
    ident_b = env["ident_b"]
    st_in = env["sg_in"] if gi == 0 else env["sr_in"]
    st_out = env["nsg"] if gi == 0 else env["nsr"]
    S = P.scope()
    rmask = S.sb("rmask", [128, LS])
    masks = S.sb("masks", [128, 2, 128], F32)
    P.dma(sp, masks[:], env["masks_in"][:].rearrange("m s c -> s m c"), [env["masks_in"]], [masks], masks)
    qdm = [[S.sb(f"qdm{d}{a}", [128, LS], BF16) for a in range(2)] for d in range(2)]
    kd = [S.sb(f"kd{d}", [128, LS], BF16) for d in range(2)]
    dec = S.sb("dec", [128, 2, 32])
    vb = S.sb("vb", [128, 16, 256], BF16)
    vst = S.sb("vst", [128, 4, 256])
    o_all = S.sb("o_all", [128, 16, 256])
    Rf = [S.sb(f"Rf{d}", [128, 256]) for d in range(2)]
    Sbf = [[S.sb(f"Sbf{d}{i}", [128, 256], BF16) for i in range(2)] for d in range(2)]
    kdt = [S.sb(f"kdt{d}", [128, 128], BF16) for d in range(2)]
    att = [S.sb(f"att{d}", [128, 2, 128], BF16) for d in range(2)]
    par = S.sb("par", [128, 2, 2])
    nw = S.sb("nw", [128, 128])
    psA = [S.ps(f"psA{d}", [128, 2, 128]) for d in range(2)]
    psO = [S.ps(f"psO{d}", [128, 256]) for d in range(2)]
    psUp = [S.ps(f"psUp{d}", [128, 256]) for d in range(2)]
    for d in range(2):
        P.op(dve, lambda d=d: nc.vector.memset(kdt[d][:], 0.0), [], [kdt[d]])
        P.op(dve, lambda d=d: nc.vector.memset(att[d][:], 0.0), [], [att[d]])
        for a in range(2):
            P.op(pool, lambda d=d, a=a: nc.gpsimd.memset(qdm[d][a][:], 0.0), [], [qdm[d][a]])
    P.op(pool, lambda: nc.gpsimd.memset(vb[:], 0.0), [], [vb])
    psKt = S.ps("psK", [128, 2, 128], BF16)
    psK = [psKt, psKt]
    P.op(dve, lambda: nc.vector.memset(rmask[:], 1.0), [], [rmask])
    P.op(dve, lambda: nc.vector.memset(rmask[:].rearrange("p (n t) -> p n t", t=128)[:, :, 0:1], 0.0), [], [rmask])
    if gi == 0:
        wg = S.sb("wg", [32, 2, 256])
        lrT = S.sb("lrT", [32, LS])
        P.op(dve, lambda: nc.vector.memset(wg[:], 0.0), [], [wg])
        for d in range(2):
            P.dma(sp, wg[d * 16:(d + 1) * 16, d, :], env["gla_w_gate"][l, d], [env["gla_w_gate"]], [wg], wg)
        with nc.allow_non_contiguous_dma(reason="tiny"):
            for d in range(2):
                P.dma(sp, par[:, d, :], env["gla_b_gate"][l, d].rearrange("(h p) -> p h", p=128), [env["gla_b_gate"]], [par], par)
        P.op(dve, lambda: nc.vector.tensor_scalar_mul(out=par[:], in0=par[:], scalar1=-1.0), [par], [par])
        P.dma(sp, nw[:], env["gla_norm_w"][l].partition_broadcast(128), [env["gla_norm_w"]], [nw], nw)
    else:
        for d in range(2):
            for hp in range(2):
                for a in range(2):
                    P.dma(sp, par[a * 64:(a + 1) * 64, d, hp:hp + 1],
                          env["ret_decay_exp"][l, d, 2 * hp + a:2 * hp + a + 1].partition_broadcast(64),
                          [env["ret_decay_exp"]], [par], par)
        P.op(act, lambda: nc.scalar.activation(out=par[:], in_=par[:], func=AF.Exp, scale=-float(np.log(2.0))), [par], [par])
        P.op(act, lambda: nc.scalar.activation(out=par[:], in_=par[:], func=AF.Ln, scale=-1.0, bias=1.0), [par], [par])
        P.op(dve, lambda: nc.vector.tensor_scalar_mul(out=par[:], in0=par[:], scalar1=-16.0), [par], [par])

    for si, (off, L) in enumerate(SEQS):
        nch = L // 128
        for hp in range(2):
            SP = P.scope()
            qk_f = SP.sb("qk_f", [128, 2, LS])
            lsp = SP.sb("lsp", [128, LS]); cs = SP.sb("cs", [128, LS]); bb = SP.sb("bb", [128, LS])
            Eb = SP.sb("Eb", [128, LS]); Enb = SP.sb("Enb", [128, LS])
            psL = SP.ps("psL", [128, 512])
            P.dma(sp, qk_f[:, 0, 0:L], qk[hp, :, off:off + L], [qk], [qk_f], qk_f)
            P.dma(sp, qk_f[:, 1, 0:L], qk[2 + hp, :, off:off + L], [qk], [qk_f], qk_f)
            if gi == 0:
                P.dma(sp, lrT[:, 0:L], pLr[:, off:off + L], [pLr], [lrT], lrT)
            for d in range(2):
                if gi == 0:
                    for t0 in range(0, L, 512):
                        n_ = min(512, L - t0)
                        P.op(pe, lambda d=d, t0=t0, n_=n_: nc.tensor.matmul(
                            psL[:, 0:n_], lhsT=wg[:, d, hp * 128:(hp + 1) * 128], rhs=lrT[:, t0:t0 + n_], start=True, stop=True),
                             [wg, lrT], [psL])
                        P.op(act, lambda d=d, t0=t0, n_=n_: nc.scalar.activation(
                            out=lsp[:, t0:t0 + n_], in_=psL[:, 0:n_], func=AF.Exp, scale=-1.0, bias=par[:, d, hp:hp + 1]),
                             [psL, par], [lsp])
                    P.op(act, lambda: nc.scalar.activation(out=lsp[:, 0:L], in_=lsp[:, 0:L], func=AF.Ln, bias=1.0), [lsp], [lsp])
                else:
                    P.op(dve, lambda d=d: nc.vector.tensor_scalar_mul(out=lsp[:, 0:L], in0=rmask[:, 0:L], scalar1=0.0), [rmask], [lsp])
                    P.op(dve, lambda d=d: nc.vector.tensor_scalar_add(out=lsp[:, 0:L], in0=lsp[:, 0:L], scalar1=par[:, d, hp:hp + 1]),
                         [lsp, par], [lsp])
                P.op(dve, lambda: nc.vector.tensor_tensor_scan(out=cs[:, 0:L], data0=rmask[:, 0:L], data1=lsp[:, 0:L], initial=0.0,
                                                               op0=ALU.mult, op1=ALU.add), [rmask, lsp], [cs])
                cs3 = cs[:, 0:L].rearrange("p (n t) -> p n t", t=128)
                if d == 0:
                    bsrc = cs
                else:
                    bsrc = bb
                    P.op(dve, lambda: nc.vector.tensor_tensor(out=bb[:, 0:L], in0=lsp[:, 0:L], in1=cs[:, 0:L], op=ALU.subtract), [lsp, cs], [bb])
                    P.op(dve, lambda cs3=cs3: nc.vector.tensor_tensor(
                        out=bb[:, 0:L].rearrange("p (n t) -> p n t", t=128), in0=bb[:, 0:L].rearrange("p (n t) -> p n t", t=128),
                        in1=cs3[:, :, 127:128].to_broadcast([128, nch, 128]), op=ALU.add), [bb, cs], [bb])
                P.op(act, lambda bsrc=bsrc: nc.scalar.activation(out=Eb[:, 0:L], in_=bsrc[:, 0:L], func=AF.Exp, scale=-1.0 / 16.0), [bsrc], [Eb])
                P.op(act, lambda bsrc=bsrc: nc.scalar.activation(out=Enb[:, 0:L], in_=bsrc[:, 0:L], func=AF.Exp, scale=1.0 / 16.0), [bsrc], [Enb])
                for a in range(2):
                    pa = slice(a * 64, (a + 1) * 64)
                    P.op(dve, lambda d=d, a=a, pa=pa: nc.vector.tensor_tensor(out=qdm[d][a][pa, 0:L], in0=qk_f[pa, 0, 0:L], in1=Eb[pa, 0:L], op=ALU.mult),
                         [qk_f, Eb], [qdm[d][a]])
                P.op(pool, lambda d=d: nc.gpsimd.tensor_tensor(out=kd[d][:, 0:L], in0=qk_f[:, 1, 0:L], in1=Enb[:, 0:L], op=ALU.mult), [qk_f, Enb], [kd[d]])
                col = 127 if d == 0 else 0
                P.op(dve, lambda d=d, col=col: nc.vector.tensor_copy(
                    out=dec[:, d, 0:nch], in_=Eb[:, 0:L].rearrange("p (n t) -> p n t", t=128)[:, :, col]), [Eb], [dec])
            SP.close()
            npc = max(1, nch // 4)
            cpp = min(4, nch)
            for pc in range(npc):
                t0 = off + pc * 512
                P.dma(sp, vst[:, 0:cpp, :], vT[t0:t0 + cpp * 128, hp * 256:(hp + 1) * 256].rearrange("(n t) c -> t n c", t=128),
                      [vT], [vst], vst)
                P.op(act, lambda pc=pc: nc.scalar.copy(out=vb[:, pc * 4:pc * 4 + cpp, :], in_=vst[:, 0:cpp, :]), [vst], [vb])
            P.op(dve, lambda: nc.vector.memset(o_all[:, 0:nch, :], 0.0), [], [o_all])
            for d in range(2):
                P.op(dve, lambda d=d: nc.vector.memset(Rf[d][:], 0.0), [], [Rf[d]])
                if si == 0:
                    for a in range(2):
                        P.dma(sp, Rf[d][a * 64:(a + 1) * 64, a * 128:(a + 1) * 128], st_in[l, d, 2 * hp + a], [st_in], [Rf[d]], Rf[d])
                P.op(act, lambda d=d: nc.scalar.copy(out=Sbf[d][0][:], in_=Rf[d][:]), [Rf[d]], [Sbf[d][0]])
            for i in range(nch):
                for d in range(2):
                    n = i if d == 0 else nch - 1 - i
                    npv = (i - 1) if d == 0 else nch - i
                    sl = slice(n * 128, (n + 1) * 128)
                    Scur = Sbf[d][i % 2]
                    Snxt = Sbf[d][(i + 1) % 2]
                    P.op(pe, lambda d=d, sl=sl: nc.tensor.transpose(psK[d][:, d, :], kd[d][:, sl], ident_b[:]), [kd[d], ident_b], [psK[d]])
                    P.op(act, lambda d=d: nc.scalar.copy(out=kdt[d][:], in_=psK[d][:, d, :]), [psK[d]], [kdt[d]])
                    for a in range(2):
                        P.op(pe, lambda d=d, a=a, sl=sl: nc.tensor.matmul(
                            psA[d][:, a, :], lhsT=kd[d][:, sl], rhs=qdm[d][a][:, sl], start=True, stop=True),
                             [kd[d], qdm[d][a]], [psA[d]])
                    P.op(pe, lambda d=d, n=n: nc.tensor.matmul(psUp[d][:], lhsT=kdt[d][:], rhs=vb[:, n, :], start=True, stop=True),
                         [kdt[d], vb], [psUp[d]])
                    P.op(dve, lambda d=d: nc.vector.tensor_tensor(
                        out=att[d][:], in0=psA[d][:], in1=masks[:, d, :].unsqueeze(1).to_broadcast([128, 2, 128]), op=ALU.mult),
                         [psA[d], masks], [att[d]])
                    if i == 0:
                        P.op(dve, lambda d=d: nc.vector.tensor_tensor(out=Rf[d][:], in0=Rf[d][:], in1=psUp[d][:], op=ALU.add),
                             [Rf[d], psUp[d]], [Rf[d]])
                    else:
                        P.op(dve, lambda d=d, npv=npv: nc.vector.scalar_tensor_tensor(
                            out=Rf[d][:], in0=Rf[d][:], scalar=dec[:, d, npv:npv + 1], in1=psUp[d][:], op0=ALU.mult, op1=ALU.add),
                             [Rf[d], psUp[d], dec], [Rf[d]])
                    if i < nch - 1:
                        P.op(act, lambda d=d, n=n, Snxt=Snxt: nc.scalar.activation(out=Snxt[:], in_=Rf[d][:], func=AF.Copy, scale=dec[:, d, n:n + 1]),
                             [Rf[d], dec], [Snxt])
                    for a in range(2):
                        P.op(pe, lambda d=d, a=a, n=n: nc.tensor.matmul(
                            psO[d][:, a * 128:(a + 1) * 128], lhsT=att[d][:, a, :], rhs=vb[:, n, a * 128:(a + 1) * 128], start=True, stop=False),
                             [att[d], vb], [psO[d]])
                        P.op(pe, lambda d=d, a=a, sl=sl, Scur=Scur: nc.tensor.matmul(
                            psO[d][:, a * 128:(a + 1) * 128], lhsT=qdm[d][a][:, sl], rhs=Scur[:, a * 128:(a + 1) * 128],
                            start=False, stop=True), [qdm[d][a], Scur], [psO[d]])
                    P.op(dve, lambda d=d, n=n: nc.vector.tensor_tensor(out=o_all[:, n, :], in0=psO[d][:], in1=o_all[:, n, :], op=ALU.add),
                         [psO[d], o_all], [o_all])
                    if i == nch - 1 and si > 0:
                        P.op(dve, lambda d=d, n=n: nc.vector.tensor_scalar_mul(out=Rf[d][:], in0=Rf[d][:], scalar1=dec[:, d, n:n + 1]),
                             [Rf[d], dec], [Rf[d]])
                        for a in range(2):
                            P.dma(pool, st_out[si - 1, l, d, 2 * hp + a], Rf[d][a * 64:(a + 1) * 64, a * 128:(a + 1) * 128], [Rf[d]], [st_out], Rf[d])
            SQ = P.scope()
            tsq = SQ.sb("tsq", [128, 8, 128]); gst = SQ.sb("gst", [128, 4, 256])
            s1 = SQ.sb("s1", [128, 8]); s2 = SQ.sb("s2", [128, 8]); rs = SQ.sb("rs", [128, 8])
            yb = SQ.sb("yb", [128, 4, 256], BF16); ysb = SQ.sb("ysb", [128, 2, 512], BF16)
            psY = SQ.ps("psY", [128, 2, 512], BF16)
            for pc in range(npc):
                t0 = off + pc * 512
                ntk = cpp * 128
                P.dma(sp, gst[:, 0:cpp, :], gT[t0:t0 + ntk, hp * 256:(hp + 1) * 256].rearrange("(n t) c -> t n c", t=128), [gT], [gst], gst)
                o3 = o_all[:, pc * 4:pc * 4 + cpp, :].rearrange("p n (a v) -> p (n a) v", a=2)
                na = cpp * 2
                P.op(dve, lambda o3=o3: nc.vector.tensor_reduce(out=s1[:, 0:na], in_=o3, axis=AX.X, op=ALU.add), [o_all], [s1])
                P.op(act, lambda o3=o3: nc.scalar.activation(out=tsq[:, 0:na, :], in_=o3, func=AF.Square), [o_all], [tsq])
                P.op(dve, lambda: nc.vector.tensor_reduce(out=s2[:, 0:na], in_=tsq[:, 0:na, :], axis=AX.X, op=ALU.add), [tsq], [s2])
                if gi == 0:
                    P.op(act, lambda: nc.scalar.activation(out=rs[:, 0:na], in_=s2[:, 0:na], func=AF.Ln, scale=1.0 / 128.0, bias=NORM_EPS), [s2], [rs])
                else:
                    P.op(dve, lambda: nc.vector.tensor_scalar_mul(out=s1[:, 0:na], in0=s1[:, 0:na], scalar1=1.0 / 128.0), [s1], [s1])
                    P.op(dve, lambda: nc.vector.tensor_tensor(out=rs[:, 0:na], in0=s1[:, 0:na], in1=s1[:, 0:na], op=ALU.mult), [s1], [rs])
                    P.op(dve, lambda: nc.vector.scalar_tensor_tensor(out=rs[:, 0:na], in0=s2[:, 0:na], scalar=1.0 / 128.0, in1=rs[:, 0:na],
                                                                     op0=ALU.mult, op1=ALU.subtract), [s2, rs], [rs])
                    P.op(act, lambda: nc.scalar.activation(out=rs[:, 0:na], in_=rs[:, 0:na], func=AF.Ln, bias=LN_EPS), [rs], [rs])
                    P.op(dve, lambda o3=o3: nc.vector.tensor_tensor(out=o3, in0=o3, in1=s1[:, 0:na].unsqueeze(2).to_broadcast([128, na, 128]),
                                                                    op=ALU.subtract), [o_all, s1], [o_all])
                P.op(act, lambda: nc.scalar.activation(out=rs[:, 0:na], in_=rs[:, 0:na], func=AF.Exp, scale=-0.5), [rs], [rs])
                P.op(dve, lambda o3=o3: nc.vector.tensor_tensor(out=o3, in0=o3, in1=rs[:, 0:na].unsqueeze(2).to_broadcast([128, na, 128]),
                                                                op=ALU.mult), [o_all, rs], [o_all])
                if gi == 0:
                    P.op(dve, lambda o3=o3: nc.vector.tensor_tensor(out=o3, in0=o3, in1=nw[:, :].unsqueeze(1).to_broadcast([128, na, 128]),
                                                                    op=ALU.mult), [o_all, nw], [o_all])
                P.op(dve, lambda pc=pc: nc.vector.tensor_tensor(out=yb[:, 0:cpp, :], in0=o_all[:, pc * 4:pc * 4 + cpp, :], in1=gst[:, 0:cpp, :],
                                                                op=ALU.mult), [o_all, gst], [yb])
                for n8 in range(cpp):
                    for a in range(2):
                        P.op(pe, lambda n8=n8, a=a: nc.tensor.transpose(psY[:, a, n8 * 128:(n8 + 1) * 128], yb[:, n8, a * 128:(a + 1) * 128],
                                                                        ident_b[:]), [yb, ident_b], [psY])
                P.op(act, lambda: nc.scalar.copy(out=ysb[:, :, 0:ntk], in_=psY[:, :, 0:ntk]), [psY], [ysb])
                for a in range(2):
                    P.dma(pool, mixT[gi * 4 + 2 * hp + a, :, t0:t0 + ntk], ysb[:, a, 0:ntk], [ysb], [mixT], ysb)
            SQ.close()
    S.close()


TWO_PI = float(2.0 * np.pi)
MAGIC = 12582912.0
CW1 = 6.28125
CW2 = float(2.0 * np.pi - 6.28125)


def mixer_s5(P, nc, l, env):
    pe, dve, act, pool, sp = P.pe, P.dve, P.act, P.pool, P.sp
    su = env["pF"]["su"]; mixT = env["mixT"]; ident_f = env["ident_f"]
    E = env
    S = P.scope()
    tp1 = S.sb("tp1", [128, LS])
    bt = [[S.sb(f"bt{i}{j}", [128, 512]) for j in range(6)] for i in range(1)]
    zb = [[S.sb(f"zb{d}{c}", [128, LS], BF16) for c in range(2)] for d in range(2)]
    ust = [S.sb("ust0", [32, LS])] * 2; ugp = [S.sb(f"ugp{i}", [128, LS], BF16) for i in range(2)]
    BTt = [S.sb(f"BTt{d}", [128, 2, 128], BF16) for d in range(2)]
    CPt = [S.sb(f"CPt{d}", [128, 2, 128], BF16) for d in range(2)]
    zz = S.sb("zz", [128, 4, LS], BF16)
    gw = S.sb("gw", [128, 4, 512], BF16)
    pv = {n: S.sb("pv_" + n, [128, 32]) for n in ("are", "aim", "st", "r", "th", "s", "c", "kre", "kim", "t0", "t1", "t2",
                                                   "h0re", "h0im", "hfre", "hfim", "zero", "ph")}
    dvec = S.sb("dvec", [128, 4]); gbv = S.sb("gbv", [128, 4])
    lastc = S.sb("lastc", [128, 32, 4])
    psB = [[S.ps(f"psB{i}{c}", [128, 512]) for c in range(2)] for i in range(2)]
    psY = [S.ps(f"psY{i}", [128, 512]) for i in range(4)]
    s5BT = env["s5BT"]; s5CP = env["s5CP"]

    def V(name, fn, reads, writes):
        P.op(dve, fn, reads, writes)

    P.dma(sp, tp1[:], E["tp1_in"][:].partition_broadcast(128), [E["tp1_in"]], [tp1], tp1)
    for i_ in range(2):
        P.op(pool, lambda i_=i_: nc.gpsimd.memset(ugp[i_][:], 0.0), [], [ugp[i_]])
    P.op(dve, lambda: nc.vector.memset(pv["zero"][:], 0.0), [], [pv["zero"]])
    SP = P.scope()
    BT = SP.sb("BT", [128, 32, 2, 128], BF16)
    Cpad = SP.sb("Cpad", [128, 32, 2, 128], BF16)
    P.op(pool, lambda: nc.gpsimd.memset(BT[:], 0.0), [], [BT])
    P.op(pool, lambda: nc.gpsimd.memset(Cpad[:], 0.0), [], [Cpad])
    Bc = [SP.sb(f"Bc{c}", [128, 32, 16]) for c in range(2)]
    Bb = [SP.sb(f"Bb{c}", [128, 32, 16]) for c in range(2)]
    Bsm = SP.sb("Bsm", [128, 32, 2, 32])
    Cc = [SP.sb(f"Cc{c}", [128, 32, 16]) for c in range(2)]
    tB = SP.sb("tB", [128, 32, 16])
    gwf = SP.sb("gwf", [128, 4, 512])
    with nc.allow_non_contiguous_dma(reason="small parameter layout transforms"):
        for g2 in range(2):
            pa = slice(g2 * 64, (g2 + 1) * 64)
            for d in range(2):
                ds_ = slice(d * 16, (d + 1) * 16)
                P.dma(sp, pv["are"][pa, ds_], E["s5_a_re"][l, d, g2::2].rearrange("gp p -> p gp"), [E["s5_a_re"]], [pv["are"]], pv["are"])
                P.dma(sp, pv["aim"][pa, ds_], E["s5_a_im"][l, d, g2::2].rearrange("gp p -> p gp"), [E["s5_a_im"]], [pv["aim"]], pv["aim"])
                P.dma(sp, pv["st"][pa, ds_], E["s5_log_step"][l, d, g2::2].partition_broadcast(64), [E["s5_log_step"]], [pv["st"]], pv["st"])
                P.dma(sp, pv["h0re"][pa, ds_], E["s5re_in"][l, d, g2::2].rearrange("gp p -> p gp"), [E["s5re_in"]], [pv["h0re"]], pv["h0re"])
                P.dma(sp, pv["h0im"][pa, ds_], E["s5im_in"][l, d, g2::2].rearrange("gp p -> p gp"), [E["s5im_in"]], [pv["h0im"]], pv["h0im"])
                P.dma(sp, Bc[0][pa, ds_, :], E["s5_b_re"][l, d, g2::2].rearrange("gp p i -> p gp i"), [E["s5_b_re"]], [Bc[0]], Bc[0])
                P.dma(sp, Bc[1][pa, ds_, :], E["s5_b_im"][l, d, g2::2].rearrange("gp p i -> p gp i"), [E["s5_b_im"]], [Bc[1]], Bc[1])
                for gp_ in range(16):
                    P.dma(sp, Cc[0][pa, d * 16 + gp_, :], E["s5_c_re"][l, d, 2 * gp_ + g2].rearrange("o p -> p o"), [E["s5_c_re"]], [Cc[0]], Cc[0])
                    P.dma(sp, Cc[1][pa, d * 16 + gp_, :], E["s5_c_im"][l, d, 2 * gp_ + g2].rearrange("o p -> p o"), [E["s5_c_im"]], [Cc[1]], Cc[1])
        P.dma(sp, dvec[:], E["s5_d"][l].rearrange("(c p) -> p c", p=128), [E["s5_d"]], [dvec], dvec)
        P.dma(sp, gbv[:], E["s5_glu_b"][l].rearrange("(c p) -> p c", p=128), [E["s5_glu_b"]], [gbv], gbv)
    P.dma(sp, gwf[:], E["s5_glu_w"][l].rearrange("(c p) n -> p c n", p=128), [E["s5_glu_w"]], [gwf], gwf)
    P.op(act, lambda: nc.scalar.copy(out=gw[:], in_=gwf[:]), [gwf], [gw])
    a = pv
    P.op(act, lambda: nc.scalar.activation(out=a["st"][:], in_=a["st"][:], func=AF.Exp), [a["st"]], [a["st"]])
    V("ar", lambda: nc.vector.tensor_tensor(out=a["r"][:], in0=a["are"][:], in1=a["st"][:], op=ALU.mult), [a["are"], a["st"]], [a["r"]])
    P.op(act, lambda: nc.scalar.activation(out=a["r"][:], in_=a["r"][:], func=AF.Exp), [a["r"]], [a["r"]])
    V("th", lambda: nc.vector.tensor_tensor(out=a["th"][:], in0=a["aim"][:], in1=a["st"][:], op=ALU.mult), [a["aim"], a["st"]], [a["th"]])

    def range_reduce(y, k, n, Y, K):
        V("rr1", lambda: nc.vector.tensor_scalar(out=k[:, 0:n], in0=y[:, 0:n], scalar1=1.0 / TWO_PI, scalar2=MAGIC, op0=ALU.mult, op1=ALU.add), [Y], [K])
        V("rr2", lambda: nc.vector.tensor_scalar_add(out=k[:, 0:n], in0=k[:, 0:n], scalar1=-MAGIC), [K], [K])
        V("rr3", lambda: nc.vector.scalar_tensor_tensor(out=y[:, 0:n], in0=k[:, 0:n], scalar=-CW1, in1=y[:, 0:n], op0=ALU.mult, op1=ALU.add), [K, Y], [Y])
        V("rr4", lambda: nc.vector.scalar_tensor_tensor(out=y[:, 0:n], in0=k[:, 0:n], scalar=-CW2, in1=y[:, 0:n], op0=ALU.mult, op1=ALU.add), [K, Y], [Y])

    def sincos(y, n, sn, cs, Y, SN, CS):
        P.op(act, lambda: nc.scalar.activation(out=sn[:, 0:n], in_=y[:, 0:n], func=AF.Sin, scale=0.999998), [Y], [SN])
        P.op(act, lambda: nc.scalar.activation(out=cs[:, 0:n], in_=y[:, 0:n], func=AF.Sin, scale=0.5), [Y], [CS])
        P.op(act, lambda: nc.scalar.activation(out=cs[:, 0:n], in_=cs[:, 0:n], func=AF.Square), [CS], [CS])
        P.op(pool, lambda: nc.gpsimd.tensor_scalar(out=cs[:, 0:n], in0=cs[:, 0:n], scalar1=-2.0, scalar2=1.0, op0=ALU.mult, op1=ALU.add), [CS], [CS])

    range_reduce(a["th"], a["t0"], 32, a["th"], a["t0"])
    sincos(a["th"], 32, a["s"], a["c"], a["th"], a["s"], a["c"])
    V("lre", lambda: nc.vector.tensor_tensor(out=a["t0"][:], in0=a["r"][:], in1=a["c"][:], op=ALU.mult), [a["r"], a["c"]], [a["t0"]])
    V("lre1", lambda: nc.vector.tensor_scalar_add(out=a["t0"][:], in0=a["t0"][:], scalar1=-1.0), [a["t0"]], [a["t0"]])
    V("lim", lambda: nc.vector.tensor_tensor(out=a["t1"][:], in0=a["r"][:], in1=a["s"][:], op=ALU.mult), [a["r"], a["s"]], [a["t1"]])
    V("den", lambda: nc.vector.tensor_tensor(out=a["t2"][:], in0=a["are"][:], in1=a["are"][:], op=ALU.mult), [a["are"]], [a["t2"]])
    V("den2", lambda: nc.vector.tensor_tensor(out=a["kre"][:], in0=a["aim"][:], in1=a["aim"][:], op=ALU.mult), [a["aim"]], [a["kre"]])
    V("den3", lambda: nc.vector.tensor_tensor(out=a["t2"][:], in0=a["t2"][:], in1=a["kre"][:], op=ALU.add), [a["t2"], a["kre"]], [a["t2"]])
    V("rden", lambda: nc.vector.reciprocal(out=a["t2"][:], in_=a["t2"][:]), [a["t2"]], [a["t2"]])
    V("k1", lambda: nc.vector.tensor_tensor(out=a["kre"][:], in0=a["t0"][:], in1=a["are"][:], op=ALU.mult), [a["t0"], a["are"]], [a["kre"]])
    V("k2", lambda: nc.vector.tensor_tensor(out=a["kim"][:], in0=a["t1"][:], in1=a["aim"][:], op=ALU.mult), [a["t1"], a["aim"]], [a["kim"]])
    V("k3", lambda: nc.vector.tensor_tensor(out=a["kre"][:], in0=a["kre"][:], in1=a["kim"][:], op=ALU.add), [a["kre"], a["kim"]], [a["kre"]])
    V("k4", lambda: nc.vector.tensor_tensor(out=a["kim"][:], in0=a["t1"][:], in1=a["are"][:], op=ALU.mult), [a["t1"], a["are"]], [a["kim"]])
    V("k5", lambda: nc.vector.tensor_tensor(out=a["t1"][:], in0=a["t0"][:], in1=a["aim"][:], op=ALU.mult), [a["t0"], a["aim"]], [a["t1"]])
    V("k6", lambda: nc.vector.tensor_tensor(out=a["kim"][:], in0=a["kim"][:], in1=a["t1"][:], op=ALU.subtract), [a["kim"], a["t1"]], [a["kim"]])
    V("k7", lambda: nc.vector.tensor_tensor(out=a["kre"][:], in0=a["kre"][:], in1=a["t2"][:], op=ALU.mult), [a["kre"], a["t2"]], [a["kre"]])
    V("k8", lambda: nc.vector.tensor_tensor(out=a["kim"][:], in0=a["kim"][:], in1=a["t2"][:], op=ALU.mult), [a["kim"], a["t2"]], [a["kim"]])
    kre_b = a["kre"][:, :].unsqueeze(2).to_broadcast([128, 32, 16])
    kim_b = a["kim"][:, :].unsqueeze(2).to_broadcast([128, 32, 16])
    V("b1", lambda: nc.vector.tensor_tensor(out=Bb[0][:], in0=Bc[0][:], in1=kre_b, op=ALU.mult), [Bc[0], a["kre"]], [Bb[0]])
    V("b2", lambda: nc.vector.tensor_tensor(out=tB[:], in0=Bc[1][:], in1=kim_b, op=ALU.mult), [Bc[1], a["kim"]], [tB])
    V("b3", lambda: nc.vector.tensor_tensor(out=Bb[0][:], in0=Bb[0][:], in1=tB[:], op=ALU.subtract), [Bb[0], tB], [Bb[0]])
    V("b4", lambda: nc.vector.tensor_tensor(out=Bb[1][:], in0=Bc[1][:], in1=kre_b, op=ALU.mult), [Bc[1], a["kre"]], [Bb[1]])
    V("b5", lambda: nc.vector.tensor_tensor(out=tB[:], in0=Bc[0][:], in1=kim_b, op=ALU.mult), [Bc[0], a["kim"]], [tB])
    V("b6", lambda: nc.vector.tensor_tensor(out=Bb[1][:], in0=Bb[1][:], in1=tB[:], op=ALU.add), [Bb[1], tB], [Bb[1]])
    V("bsm0", lambda: nc.vector.memset(Bsm[:], 0.0), [], [Bsm])
    for g2 in range(2):
        pa = slice(g2 * 64, (g2 + 1) * 64)
        for c in range(2):
            V("bsm", lambda pa=pa, c=c, g2=g2: nc.vector.tensor_copy(out=Bsm[pa, :, c, g2 * 16:(g2 + 1) * 16], in_=Bb[c][pa, :, :]), [Bb[c]], [Bsm])
    for j in range(32):
        for c in range(2):
            ps = psB[(j * 2 + c) % 2][0]
            P.op(pe, lambda ps=ps, j=j, c=c: nc.tensor.transpose(ps[0:32, 0:128], Bsm[:, j, c, :], ident_f[:]), [Bsm, ident_f], [ps])
            P.op(act, lambda ps=ps, j=j, c=c: nc.scalar.copy(out=BT[0:32, j, c, :], in_=ps[0:32, 0:128]), [ps], [BT])
    for c in range(2):
        if c == 1:
            V("cneg", lambda: nc.vector.tensor_scalar_mul(out=Cc[1][:], in0=Cc[1][:], scalar1=-1.0), [Cc[1]], [Cc[1]])
        for g2 in range(2):
            pa = slice(g2 * 64, (g2 + 1) * 64)
            for q in range(4):
                src = Cc[c][pa, :, :].rearrange("p (dg q) o -> p dg q o", q=4)[:, :, q, :]
                dst = Cpad[pa, :, c, q * 32 + g2 * 16:q * 32 + g2 * 16 + 16].rearrange("p (dg q) o -> p dg q o", q=4)[:, :, q, :]
                V("cpad", lambda src=src, dst=dst: nc.vector.tensor_copy(out=dst, in_=src), [Cc[c]], [Cpad])
    P.dma(pool, s5BT[:].rearrange("j p c m -> p j c m"), BT[:], [BT], [s5BT], BT)
    P.dma(pool, s5CP[:].rearrange("j p c m -> p j c m"), Cpad[:], [Cpad], [s5CP], Cpad)
    V("ph", lambda: nc.vector.tensor_scalar_mul(out=a["ph"][:], in0=a["th"][:], scalar1=1.0 / TWO_PI), [a["th"]], [a["ph"]])
    SP.close()
    SM = P.scope()
    CS = [[SM.sb(f"cs{d}{i}", [128, LS]) for i in range(2)] for d in range(2)]
    WD = [[SM.sb(f"wd{d}{i}", [128, LS]) for i in range(2)] for d in range(2)]
    MM = [[SM.sb(f"mm{d}{i}", [128, LS]) for i in range(2)] for d in range(2)]
    PP = [[SM.sb(f"pp{d}{i}", [128, LS], BF16) for i in range(2)] for d in range(2)]
    PQ = [[SM.sb(f"pq{d}{i}", [128, LS], BF16) for i in range(2)] for d in range(2)]
    W = [WD[0][0], WD[0][1], MM[0][0], MM[0][1]]

    for si, (off, L) in enumerate(SEQS):
        nb = max(1, L // 512)
        bw = min(512, L)
        h0r = (a["h0re"] if si == 0 else a["zero"]); h0i = (a["h0im"] if si == 0 else a["zero"])

        def stage_T(cc, gq, d):
            gp = cc * 4 + gq
            j = d * 16 + gp
            cosA, sinA = CS[d]
            ang, kk = WD[d]
            if d == 0:
                P.dma(sp, ust[gq % 2][:, 0:L], su[cc, gq * 32:(gq + 1) * 32, off:off + L], [su], [ust[gq % 2]], ust[gq % 2])
                P.op(act, lambda: nc.scalar.copy(out=ugp[gq % 2][0:32, 0:L], in_=ust[gq % 2][:, 0:L]), [ust[gq % 2]], [ugp[gq % 2]])
            P.dma(sp, BTt[d][:], s5BT[j], [s5BT], [BTt[d]], BTt[d])
            P.dma(sp, CPt[d][:], s5CP[j], [s5CP], [CPt[d]], CPt[d])
            P.op(act, lambda: nc.scalar.activation(out=kk[:, 0:L], in_=tp1[:, 0:L], func=AF.Identity, scale=a["ph"][:, j:j + 1], bias=MAGIC), [tp1, a["ph"]], [kk])
            P.op(act, lambda: nc.scalar.activation(out=kk[:, 0:L], in_=kk[:, 0:L], func=AF.Identity, bias=-MAGIC), [kk], [kk])
            V("fr", lambda: nc.vector.scalar_tensor_tensor(out=ang[:, 0:L], in0=tp1[:, 0:L], scalar=a["ph"][:, j:j + 1], in1=kk[:, 0:L],
                                                           op0=ALU.mult, op1=ALU.subtract), [tp1, a["ph"], kk], [ang])
            P.op(act, lambda: nc.scalar.activation(out=sinA[:, 0:L], in_=ang[:, 0:L], func=AF.Sin, scale=TWO_PI * 0.999998), [ang], [sinA])
            P.op(act, lambda: nc.scalar.activation(out=cosA[:, 0:L], in_=ang[:, 0:L], func=AF.Sin, scale=TWO_PI * 0.5), [ang], [cosA])
            P.op(act, lambda: nc.scalar.activation(out=cosA[:, 0:L], in_=cosA[:, 0:L], func=AF.Square), [cosA], [cosA])
            P.op(act, lambda: nc.scalar.activation(out=cosA[:, 0:L], in_=cosA[:, 0:L], func=AF.Identity, scale=-2.0, bias=1.0), [cosA], [cosA])

        def stage_GS(cc, gq, d):
            gp = cc * 4 + gq
            j = d * 16 + gp
            cosA, sinA = CS[d]
            gre, gim = WD[d]
            mre, mim = MM[d]
            ug = ugp[gq % 2]
            for tb in range(nb):
                t0 = tb * bw
                pb = psB[tb % 2]
                e = bt[0]
                for c in range(2):
                    P.op(pe, lambda c=c, t0=t0, pb=pb: nc.tensor.matmul(pb[c][:, 0:bw], lhsT=BTt[d][:, c, :], rhs=ug[:, t0:t0 + bw],
                                                                        start=True, stop=True), [BTt[d], ug], [pb[c]])
                if d == 0:
                    sl = lambda X, t0=t0: X[:, t0:t0 + bw]
                else:
                    sl = lambda X, t0=t0: X[:, L - t0 - bw:L - t0][:, ::-1]
                P.op(act, lambda pb=pb, e=e: nc.scalar.copy(out=e[0][:, 0:bw], in_=pb[0][:, 0:bw]), [pb[0]], [e[0]])
                P.op(act, lambda pb=pb, e=e: nc.scalar.copy(out=e[1][:, 0:bw], in_=pb[1][:, 0:bw]), [pb[1]], [e[1]])
                V("g1", lambda sl=sl, e=e: nc.vector.tensor_tensor(out=e[2][:, 0:bw], in0=e[0][:, 0:bw], in1=sl(cosA), op=ALU.mult), [e[0], cosA], [e[2]])
                V("g2", lambda sl=sl, e=e: nc.vector.tensor_tensor(out=e[3][:, 0:bw], in0=e[1][:, 0:bw], in1=sl(sinA), op=ALU.mult), [e[1], sinA], [e[3]])
                P.op(pool, lambda sl=sl, e=e: nc.gpsimd.tensor_tensor(out=e[4][:, 0:bw], in0=e[1][:, 0:bw], in1=sl(cosA), op=ALU.mult), [e[1], cosA], [e[4]])
                P.op(pool, lambda sl=sl, e=e: nc.gpsimd.tensor_tensor(out=e[5][:, 0:bw], in0=e[0][:, 0:bw], in1=sl(sinA), op=ALU.mult), [e[0], sinA], [e[5]])
                V("g5", lambda sl=sl, e=e: nc.vector.tensor_tensor(out=sl(gre), in0=e[2][:, 0:bw], in1=e[3][:, 0:bw], op=ALU.add), [e[2], e[3]], [gre])
                P.op(pool, lambda sl=sl, e=e: nc.gpsimd.tensor_tensor(out=sl(gim), in0=e[4][:, 0:bw], in1=e[5][:, 0:bw], op=ALU.subtract), [e[4], e[5]], [gim])
            rb = a["r"][:, j:j + 1].to_broadcast([128, L])
            V("scr", lambda: nc.vector.tensor_tensor_scan(out=mre[:, 0:L], data0=rb, data1=gre[:, 0:L], initial=h0r[:, j:j + 1], op0=ALU.mult, op1=ALU.add),
              [gre, a["r"], h0r], [mre])
            V("sci", lambda: nc.vector.tensor_tensor_scan(out=mim[:, 0:L], data0=rb, data1=gim[:, 0:L], initial=h0i[:, j:j + 1], op0=ALU.mult, op1=ALU.add),
              [gim, a["r"], h0i], [mim])
            if si > 0:
                for ci, X in enumerate((cosA, sinA, mre, mim)):
                    P.op(act, lambda ci=ci, X=X: nc.scalar.copy(out=lastc[:, j, ci:ci + 1], in_=X[:, L - 1:L]), [X], [lastc])

        def stage_Z(cc, gq, d):
            cosA, sinA = CS[d]
            mre, mim = MM[d]
            p1, p2 = PP[d]
            p3, p4 = PQ[d]
            zo = (lambda X: X[:, 0:L]) if d == 0 else (lambda X: X[:, 0:L][:, ::-1])
            V("p1", lambda: nc.vector.tensor_tensor(out=p1[:, 0:L], in0=cosA[:, 0:L], in1=mre[:, 0:L], op=ALU.mult), [cosA, mre], [p1])
            V("p2", lambda: nc.vector.tensor_tensor(out=p2[:, 0:L], in0=sinA[:, 0:L], in1=mim[:, 0:L], op=ALU.mult), [sinA, mim], [p2])
            P.op(pool, lambda: nc.gpsimd.tensor_tensor(out=p3[:, 0:L], in0=sinA[:, 0:L], in1=mre[:, 0:L], op=ALU.mult), [sinA, mre], [p3])
            P.op(pool, lambda: nc.gpsimd.tensor_tensor(out=p4[:, 0:L], in0=cosA[:, 0:L], in1=mim[:, 0:L], op=ALU.mult), [cosA, mim], [p4])
            V("zre", lambda: nc.vector.tensor_tensor(out=zo(zb[d][0]), in0=p1[:, 0:L], in1=p2[:, 0:L], op=ALU.subtract), [p1, p2], [zb[d][0]])
            P.op(pool, lambda: nc.gpsimd.tensor_tensor(out=zo(zb[d][1]), in0=p3[:, 0:L], in1=p4[:, 0:L], op=ALU.add), [p3, p4], [zb[d][1]])
            for tb in range(nb):
                t0 = tb * bw
                for c in range(2):
                    first = (gq == 0 and d == 0 and c == 0)
                    last = (gq == 3 and d == 1 and c == 1)
                    P.op(pe, lambda tb=tb, t0=t0, c=c, first=first, last=last: nc.tensor.matmul(
                        psY[tb][:, 0:bw], lhsT=CPt[d][:, c, :], rhs=zb[d][c][:, t0:t0 + bw], start=first, stop=last), [CPt[d], zb[d][c]], [psY[tb]])

        for cc in range(4):
            its = [(gq, d) for gq in range(4) for d in range(2)]
            stage_T(cc, *its[0])
            for ii, (gq, d) in enumerate(its):
                stage_GS(cc, gq, d)
                if ii + 1 < len(its):
                    stage_T(cc, *its[ii + 1])
                stage_Z(cc, gq, d)
            uT, yv, w1, w2 = W[0], W[1], W[2], W[3]
            P.dma(sp, uT[:, 0:L], su[cc, :, off:off + L], [su], [uT], uT)
            for tb in range(nb):
                t0 = tb * bw
                V("yv", lambda tb=tb, t0=t0: nc.vector.scalar_tensor_tensor(out=yv[:, t0:t0 + bw], in0=uT[:, t0:t0 + bw], scalar=dvec[:, cc:cc + 1],
                                                                          in1=psY[tb][:, 0:bw], op0=ALU.mult, op1=ALU.add), [uT, dvec, psY[tb]], [yv])
            P.op(act, lambda: nc.scalar.activation(out=w1[:, 0:L], in_=yv[:, 0:L], func=AF.Square), [yv], [w1])
            P.op(pool, lambda: nc.gpsimd.tensor_scalar(out=w1[:, 0:L], in0=w1[:, 0:L], scalar1=0.044715, scalar2=1.0, op0=ALU.mult, op1=ALU.add), [w1], [w1])
            P.op(pool, lambda: nc.gpsimd.tensor_tensor(out=w1[:, 0:L], in0=w1[:, 0:L], in1=yv[:, 0:L], op=ALU.mult), [w1, yv], [w1])
            P.op(act, lambda: nc.scalar.activation(out=w2[:, 0:L], in_=w1[:, 0:L], func=AF.Sigmoid, scale=float(2.0 * np.sqrt(2.0 / np.pi))), [w1], [w2])
            V("zz", lambda: nc.vector.tensor_tensor(out=zz[:, cc, 0:L], in0=yv[:, 0:L], in1=w2[:, 0:L], op=ALU.mult), [yv, w2], [zz])
        for oc in range(4):
            for tb in range(nb):
                t0 = tb * bw
                ps = psB[tb % 2][0]
                for cc in range(4):
                    P.op(pe, lambda ps=ps, cc=cc, oc=oc, t0=t0: nc.tensor.matmul(ps[:, 0:bw], lhsT=gw[:, cc, oc * 128:(oc + 1) * 128], rhs=zz[:, cc, t0:t0 + bw],
                                                                               start=(cc == 0), stop=(cc == 3)), [gw, zz], [ps])
                sg = bt[0][tb % 2]
                P.op(act, lambda ps=ps, sg=sg, oc=oc: nc.scalar.activation(out=sg[:, 0:bw], in_=ps[:, 0:bw], func=AF.Sigmoid, bias=gbv[:, oc:oc + 1]), [ps, gbv], [sg])
                ot = bt[0][2 + tb % 2]
                V("glu", lambda sg=sg, ot=ot, oc=oc, t0=t0: nc.vector.tensor_tensor(out=ot[:, 0:bw].bitcast(BF16)[:, 0:bw], in0=zz[:, oc, t0:t0 + bw], in1=sg[:, 0:bw], op=ALU.mult),
                  [zz, sg], [ot])
                P.dma(pool, mixT[8 + oc, :, off + t0:off + t0 + bw], ot[:, 0:bw].bitcast(BF16)[:, 0:bw], [ot], [mixT], ot)
        if si > 0:
            lc = lastc
            h = a
            V("f1", lambda: nc.vector.tensor_tensor(out=h["t0"][:], in0=lc[:, :, 0], in1=lc[:, :, 2], op=ALU.mult), [lc], [h["t0"]])
            V("f2", lambda: nc.vector.tensor_tensor(out=h["t1"][:], in0=lc[:, :, 1], in1=lc[:, :, 3], op=ALU.mult), [lc], [h["t1"]])
            V("f3", lambda: nc.vector.tensor_tensor(out=h["hfre"][:], in0=h["t0"][:], in1=h["t1"][:], op=ALU.subtract), [h["t0"], h["t1"]], [h["hfre"]])
            V("f4", lambda: nc.vector.tensor_tensor(out=h["t0"][:], in0=lc[:, :, 1], in1=lc[:, :, 2], op=ALU.mult), [lc], [h["t0"]])
            V("f5", lambda: nc.vector.tensor_tensor(out=h["t1"][:], in0=lc[:, :, 0], in1=lc[:, :, 3], op=ALU.mult), [lc], [h["t1"]])
            V("f6", lambda: nc.vector.tensor_tensor(out=h["hfim"][:], in0=h["t0"][:], in1=h["t1"][:], op=ALU.add), [h["t0"], h["t1"]], [h["hfim"]])
            with nc.allow_non_contiguous_dma(reason="state layout"):
                for g2 in range(2):
                    pa = slice(g2 * 64, (g2 + 1) * 64)
                    for d in range(2):
                        ds_ = slice(d * 16, (d + 1) * 16)
                        P.dma(sp, E["ns5re"][si - 1, l, d, g2::2].rearrange("gp p -> p gp"), h["hfre"][pa, ds_], [h["hfre"]], [E["ns5re"]], h["hfre"])
                        P.dma(sp, E["ns5im"][si - 1, l, d, g2::2].rearrange("gp p -> p gp"), h["hfim"][pa, ds_], [h["hfim"]], [E["ns5im"]], h["hfim"])
    SM.close()
    S.close()


def mixer_hy(P, nc, l, env):
    pe, dve, act, pool, sp = P.pe, P.dve, P.act, P.pool, P.sp
    E = env
    hy = E["pF"]["hy"]; mixT = E["mixT"]; ident_b = E["ident_b"]; ident_f = E["ident_f"]
    S = P.scope()
    cw = S.sb("cw", [128, 3, 12]); cb = S.sb("cb", [128, 12]); dvv = S.sb("dvv", [128, 2, 4])
    with nc.allow_non_contiguous_dma(reason="small parameter layout transforms"):
        for k in range(3):
            P.dma(sp, cw[:, k, :], E["hy_conv_w"][l, k].rearrange("(c p) -> p c", p=128), [E["hy_conv_w"]], [cw], cw)
        P.dma(sp, cb[:], E["hy_conv_b"][l].rearrange("(c p) -> p c", p=128), [E["hy_conv_b"]], [cb], cb)
        for o in range(2):
            P.dma(sp, dvv[:, o, :], E["hy_d"][l, o].rearrange("(c p) -> p c", p=128), [E["hy_d"]], [dvv], dvv)

    def V(fn, reads, writes):
        P.op(dve, fn, reads, writes)

    def fwd_dft(cfg, src, emit, bufs):
        tC, tS, psC, psS = bufs
        for fk in range(cfg["nfk"]):
            c_, s_ = tC[fk % 2], tS[fk % 2]
            P.dma(sp, c_[:, 0:cfg["ntc"], :], cfg["tFC"][fk], [cfg["tFC"]], [c_], c_)
            P.dma(sp, s_[:, 0:cfg["ntc"], :], cfg["tFS"][fk], [cfg["tFS"]], [s_], s_)
            pc, ps_ = psC[fk % 2], psS[fk % 2]
            for tc in range(cfg["ntc"]):
                P.op(pe, lambda c_=c_, pc=pc, tc=tc: nc.tensor.matmul(pc[:], lhsT=c_[:, tc, :], rhs=src[:, tc, :], start=(tc == 0), stop=(tc == cfg["ntc"] - 1)),
                     [c_, src], [pc])
            for tc in range(cfg["ntc"]):
                P.op(pe, lambda s_=s_, ps_=ps_, tc=tc: nc.tensor.matmul(ps_[:], lhsT=s_[:, tc, :], rhs=src[:, tc, :], start=(tc == 0), stop=(tc == cfg["ntc"] - 1)),
                     [s_, src], [ps_])
            emit(fk, pc, ps_)

    cfgs = {}
    for L in (LS, LP):
        sfx = str(L)
        cfgs[L] = dict(L=L, ntc=L // 128, nfk=L // 128 + 1, nblk=max(1, L // 512), bw=min(512, L),
                       tFC=E["tFC" + sfx], tFS=E["tFS" + sfx], tIC=E["tIC" + sfx], tIS=E["tIS" + sfx],
                       featsT=E["featsT" + sfx], featsTr=E["featsTr" + sfx], negtv=E["negtv" + sfx], negtvr=E["negtvr" + sfx],
                       wk=E["wk" + sfx], sgw=E["sgw" + sfx], Fs=E["Fs" + sfx])

    for L in (LS, LP):
        cfg = cfgs[L]
        ntc, nfk = cfg["ntc"], cfg["nfk"]
        SF = P.scope()
        w1p = SF.sb("w1p", [64, 64]); w2 = SF.sb("w2", [64, 64]); w3 = SF.sb("w3", [64, 2048])
        fq = SF.sb("fq", [64, 1]); fb1 = SF.sb("fb1", [64, 1]); fb2 = SF.sb("fb2", [64, 1])
        ft = SF.sb("ft", [64, LS]); h1 = SF.sb("h1", [64, LS]); h2 = SF.sb("h2", [64, LS]); kk = SF.sb("kkf", [64, LS])
        adec = SF.sb("adec", [128, 512]); ew = SF.sb("ew", [128, 512]); fa = SF.sb("fa", [128, 512])
        fw = SF.sb("fw", [128, 16, 512])
        ff = [SF.sb(f"ff{d}", [128, 16, 512], BF16) for d in range(2)]
        fpm = [SF.sb(f"fpm{d}", [128, 16, 512], BF16) for d in range(2)]
        ntv = SF.sb("ntv", [128, 2, 16]); wkv = SF.sb("wkv", [128, 17]); sgv = SF.sb("sgv", [128, 17])
        ones = SF.sb("ones", [128, 128]); rcp = SF.sb("rcp", [128, 512])
        tC = [SF.sb(f"tC{i}", [128, 16, 128], BF16) for i in range(2)]
        tS = [SF.sb(f"tS{i}", [128, 16, 128], BF16) for i in range(2)]
        fo = [SF.sb(f"fo{i}", [128, 2, 512]) for i in range(2)]
        tmpc = SF.sb("tmpc", [128, 2, 512])
        psM = SF.ps("psM", [128, 512])
        psN = SF.ps("psN", [128, 512])
        psCf = [SF.ps(f"psCf{i}", [128, 512]) for i in range(2)]
        psSf = [SF.ps(f"psSf{i}", [128, 512]) for i in range(2)]
        psCr = SF.ps("psCr", [128, 512]); psSr = SF.ps("psSr", [128, 512])
        V(lambda: nc.vector.memset(w1p[:], 0.0), [], [w1p])
        V(lambda: nc.vector.memset(ones[:], 1.0), [], [ones])
        P.dma(sp, w1p[0:33, :], E["hy_f_w1"][l], [E["hy_f_w1"]], [w1p], w1p)
        P.dma(sp, w2[:], E["hy_f_w2"][l], [E["hy_f_w2"]], [w2], w2)
        P.dma(sp, w3[:], E["hy_f_w3"][l], [E["hy_f_w3"]], [w3], w3)
        with nc.allow_non_contiguous_dma(reason="tiny"):
            P.dma(sp, fq[:], E["hy_f_freq"][l].rearrange("(p o) -> p o", o=1), [E["hy_f_freq"]], [fq], fq)
            P.dma(sp, fb1[:], E["hy_f_b1"][l].rearrange("(p o) -> p o", o=1), [E["hy_f_b1"]], [fb1], fb1)
            P.dma(sp, fb2[:], E["hy_f_b2"][l].rearrange("(p o) -> p o", o=1), [E["hy_f_b2"]], [fb2], fb2)
        P.dma(sp, ntv[:, 0, 0:ntc], cfg["negtv"][:, :], [cfg["negtv"]], [ntv], ntv)
        P.dma(sp, ntv[:, 1, 0:ntc], cfg["negtvr"][:, :], [cfg["negtvr"]], [ntv], ntv)
        P.dma(sp, wkv[:, 0:nfk], cfg["wk"][:, :], [cfg["wk"]], [wkv], wkv)
        P.dma(sp, sgv[:, 0:nfk], cfg["sgw"][:, :], [cfg["sgw"]], [sgv], sgv)
        V(lambda: nc.vector.tensor_tensor(out=fb1[:], in0=fb1[:], in1=fq[:], op=ALU.mult), [fb1, fq], [fb1])
        V(lambda: nc.vector.tensor_tensor(out=fb2[:], in0=fb2[:], in1=fq[:], op=ALU.mult), [fb2, fq], [fb2])

        def rr_sin(y, n):
            V(lambda: nc.vector.tensor_scalar(out=kk[:, 0:n], in0=y[:, 0:n], scalar1=1.0 / TWO_PI, scalar2=MAGIC, op0=ALU.mult, op1=ALU.add), [y], [kk])
            V(lambda: nc.vector.tensor_scalar_add(out=kk[:, 0:n], in0=kk[:, 0:n], scalar1=-MAGIC), [kk], [kk])
            V(lambda: nc.vector.scalar_tensor_tensor(out=y[:, 0:n], in0=kk[:, 0:n], scalar=-CW1, in1=y[:, 0:n], op0=ALU.mult, op1=ALU.add), [kk, y], [y])
            V(lambda: nc.vector.scalar_tensor_tensor(out=y[:, 0:n], in0=kk[:, 0:n], scalar=-CW2, in1=y[:, 0:n], op0=ALU.mult, op1=ALU.add), [kk, y], [y])
            P.op(act, lambda: nc.scalar.activation(out=y[:, 0:n], in_=y[:, 0:n], func=AF.Sin, scale=0.999998), [y], [y])

        for d in range(2):
            P.dma(sp, ft[:, 0:L], (cfg["featsT"] if d == 0 else cfg["featsTr"])[:, :], [cfg["featsT"], cfg["featsTr"]], [ft], ft)
            bw = cfg["bw"]
            for tb in range(cfg["nblk"]):
                t0 = tb * bw
                P.op(pe, lambda t0=t0: nc.tensor.matmul(psM[0:64, 0:bw], lhsT=w1p[:], rhs=ft[:, t0:t0 + bw], start=True, stop=True), [w1p, ft], [psM])
                V(lambda t0=t0: nc.vector.tensor_scalar(out=h1[:, t0:t0 + bw], in0=psM[0:64, 0:bw], scalar1=fq[:, 0:1], scalar2=fb1[:, 0:1],
                                                        op0=ALU.mult, op1=ALU.add), [psM, fq, fb1], [h1])
            rr_sin(h1, L)
            for tb in range(cfg["nblk"]):
                t0 = tb * bw
                P.op(pe, lambda t0=t0: nc.tensor.matmul(psM[0:64, 0:bw], lhsT=w2[:], rhs=h1[:, t0:t0 + bw], start=True, stop=True), [w2, h1], [psM])
                V(lambda t0=t0: nc.vector.tensor_scalar(out=h2[:, t0:t0 + bw], in0=psM[0:64, 0:bw], scalar1=fq[:, 0:1], scalar2=fb2[:, 0:1],
                                                        op0=ALU.mult, op1=ALU.add), [psM, fq, fb2], [h2])
            rr_sin(h2, L)
            for o in range(2):
                col0 = (d * 2 + o) * 512
                P.dma(sp, adec[:], E["hy_decay"][l, col0:col0 + 512].partition_broadcast(128), [E["hy_decay"]], [adec], adec)
                P.op(act, lambda: nc.scalar.activation(out=adec[:], in_=adec[:], func=AF.Abs), [adec], [adec])
                for tc in range(ntc):
                    P.op(pe, lambda tc=tc, col0=col0: nc.tensor.matmul(psM[:], lhsT=h2[:, tc * 128:(tc + 1) * 128], rhs=w3[:, col0:col0 + 512], start=True, stop=True),
                         [h2, w3], [psM])
                    P.op(act, lambda tc=tc, d=d: nc.scalar.activation(out=ew[:], in_=adec[:], func=AF.Exp, scale=ntv[:, d, tc:tc + 1]), [adec, ntv], [ew])
                    V(lambda tc=tc: nc.vector.tensor_tensor(out=fw[:, tc, :], in0=psM[:], in1=ew[:], op=ALU.mult), [psM, ew], [fw])
                    P.op(act, lambda tc=tc: nc.scalar.activation(out=fa[:], in_=fw[:, tc, :], func=AF.Abs), [fw], [fa])
                    P.op(pe, lambda tc=tc: nc.tensor.matmul(psN[:], lhsT=ones[:], rhs=fa[:], start=(tc == 0), stop=(tc == ntc - 1)), [ones, fa], [psN])
                V(lambda: nc.vector.reciprocal(out=rcp[:], in_=psN[:]), [psN], [rcp])
                V(lambda d=d: nc.vector.tensor_tensor(out=ff[d][:, 0:ntc, :], in0=fw[:, 0:ntc, :], in1=rcp[:, :].unsqueeze(1).to_broadcast([128, ntc, 512]),
                                                      op=ALU.mult), [fw, rcp], [ff[d]])
                if d == 1:
                    V(lambda: nc.vector.memset(ff[1][0:1, 0, :], 0.0), [], [ff[1]])
                P.dma(pool, E["fstash"][d, o, :, 0:ntc, :], ff[d][:, 0:ntc, :], [ff[d]], [E["fstash"]], ff[d])
        ne = (L // 2 + 1 + 127) // 128
        for o in range(2):
            for d in range(2):
                P.dma(sp, ff[d][:, 0:ntc, :], E["fstash"][d, o, :, 0:ntc, :], [E["fstash"]], [ff[d]], ff[d])
            V(lambda: nc.vector.tensor_tensor(out=fpm[0][:, 0:ntc, :], in0=ff[0][:, 0:ntc, :], in1=ff[1][:, 0:ntc, :], op=ALU.add), [ff[0], ff[1]], [fpm[0]])
            P.op(pool, lambda: nc.gpsimd.tensor_tensor(out=fpm[1][:, 0:ntc, :], in0=ff[0][:, 0:ntc, :], in1=ff[1][:, 0:ntc, :], op=ALU.subtract), [ff[0], ff[1]], [fpm[1]])
            for fk in range(nfk):
                c_, s_ = tC[fk % 2], tS[fk % 2]
                P.dma(sp, c_[:, 0:ntc, :], cfg["tFC"][fk], [cfg["tFC"]], [c_], c_)
                P.dma(sp, s_[:, 0:ntc, :], cfg["tFS"][fk], [cfg["tFS"]], [s_], s_)
                pcf, psf = psCf[fk % 2], psSf[fk % 2]
                src = fpm[0] if fk < ne else fpm[1]
                for (tab, pp) in ((c_, pcf), (s_, psf)):
                    for tc in range(ntc):
                        P.op(pe, lambda tab=tab, src=src, pp=pp, tc=tc: nc.tensor.matmul(pp[:], lhsT=tab[:, tc, :], rhs=src[:, tc, :],
                                                                                        start=(tc == 0), stop=(tc == ntc - 1)), [tab, src], [pp])
                fo_ = fo[fk % 2]
                P.op(act, lambda pcf=pcf, fk=fk, fo_=fo_: nc.scalar.activation(out=fo_[:, 0, :], in_=pcf[:], func=AF.Copy, scale=wkv[:, fk:fk + 1]), [pcf, wkv], [fo_])
                V(lambda psf=psf, fk=fk, fo_=fo_: nc.vector.tensor_scalar_mul(out=fo_[:, 1, :], in0=psf[:], scalar1=wkv[:, fk:fk + 1]), [psf, wkv], [fo_])
                P.dma(pool, cfg["Fs"][o, fk], fo_[:], [fo_], [cfg["Fs"]], fo_)
        SF.close()

    SC = P.scope()
    xtok = SC.sb("xtok", [128, 16, 512], BF16)
    Z = SC.sb("Z", [128, 17, 2, 512], BF16)
    y1b = SC.sb("y1b", [128, 4, LS], BF16)
    tC = [SC.sb(f"tC{i}", [128, 16, 128], BF16) for i in range(2)]
    tS = [SC.sb(f"tS{i}", [128, 16, 128], BF16) for i in range(2)]
    tIC = SC.sb("tIC", [128, 17, 512], BF16); tIS = SC.sb("tIS", [128, 17, 512], BF16)
    fsb = [SC.sb(f"fsb{i}", [128, 2, 512]) for i in range(2)]
    zin = [SC.sb(f"zin{i}", [128, 514]) for i in range(2)]
    cvo = [SC.sb(f"cvo{i}", [128, 512]) for i in range(2)]
    cvb = SC.sb("cvb", [128, 512], BF16)
    t4 = [SC.sb(f"t4{i}", [128, 512]) for i in range(4)]
    ob = [SC.sb(f"ob{i}", [128, 512], BF16) for i in range(2)]
    psC = [SC.ps(f"psC{i}", [128, 512]) for i in range(2)]
    psS = [SC.ps(f"psS{i}", [128, 512]) for i in range(2)]
    psI = [SC.ps(f"psI{i}", [128, 512]) for i in range(2)]
    psTt = SC.ps("psTt", [128, 4, 128], BF16)
    cjobs = list(E.get("convjobs", []))
    cst = SC.sb("cst", [128, 16, 512]); csb = SC.sb("csb", [128, 16, 512], BF16)

    def conv_step(n=1):
        for _ in range(n):
            if not cjobs:
                return
            (srcT, src, dstT, dst, nk) = cjobs.pop(0)
            P.dma(act, cst[:, 0:nk, :], src.rearrange("(k p) n -> p k n", p=128), [srcT], [cst], cst)
            P.op(act, lambda: nc.scalar.copy(out=csb[:, 0:nk, :], in_=cst[:, 0:nk, :]), [cst], [csb])
            if len(dst.shape) == 4:
                P.dma(pool, dst, csb[:, 0:nk, :].rearrange("p (g k) n -> p g k n", g=2), [csb], [dstT], csb)
            else:
                P.dma(pool, dst, csb[:, 0:nk, :], [csb], [dstT], csb)

    def shortconv(ci, off, t0, bw, rows_n, zi, out):
        P.dma(sp, zi[:, 0:bw], hy[ci, :, off + t0:off + t0 + bw], [hy], [zi], zi)
        nseg = bw // rows_n
        z3 = zi[:, 0:bw].rearrange("p (s n) -> p s n", n=rows_n)
        o3 = out[:, 0:bw].rearrange("p (s n) -> p s n", n=rows_n)
        V(lambda: nc.vector.tensor_scalar(out=out[:, 0:bw], in0=zi[:, 0:bw], scalar1=cw[:, 1, ci:ci + 1], scalar2=cb[:, ci:ci + 1], op0=ALU.mult, op1=ALU.add),
          [zi, cw, cb], [out])
        V(lambda: nc.vector.scalar_tensor_tensor(out=o3[:, :, 1:rows_n], in0=z3[:, :, 0:rows_n - 1], scalar=cw[:, 0, ci:ci + 1], in1=o3[:, :, 1:rows_n],
                                                 op0=ALU.mult, op1=ALU.add), [zi, cw, out], [out])
        V(lambda: nc.vector.scalar_tensor_tensor(out=o3[:, :, 0:rows_n - 1], in0=z3[:, :, 1:rows_n], scalar=cw[:, 2, ci:ci + 1], in1=o3[:, :, 0:rows_n - 1],
                                                 op0=ALU.mult, op1=ALU.add), [zi, cw, out], [out])

    for si, (off, L) in enumerate(SEQS):
        cfg = cfgs[L]
        ntc, nfk, nblk, bw = cfg["ntc"], cfg["nfk"], cfg["nblk"], cfg["bw"]
        rows_n = 64 if si == 0 else L
        for order in range(2):
            for c in range(4):
                for tb in range(nblk):
                    t0 = tb * bw
                    if order == 0:
                        shortconv(c, off, t0, bw, rows_n, zin[0], cvo[0])
                        P.op(act, lambda: nc.scalar.copy(out=cvb[:, 0:bw], in_=cvo[0][:, 0:bw]), [cvo[0]], [cvb])
                        srcb, srcT, so = cvb, cvb, 0
                    else:
                        srcb, srcT, so = y1b, y1b, None
                    for q in range(bw // 128):
                        if order == 0:
                            in_ap = cvb[:, q * 128:(q + 1) * 128]
                        else:
                            in_ap = y1b[:, c, t0 + q * 128:t0 + (q + 1) * 128]
                        P.op(pe, lambda q=q, in_ap=in_ap: nc.tensor.transpose(psTt[:, q, :], in_ap, ident_b[:]), [srcT, ident_b], [psTt])
                    nq = bw // 128
                    P.op(act, lambda c=c, tb=tb, nq=nq: nc.scalar.copy(out=xtok[:, tb * 4:tb * 4 + nq, c * 128:(c + 1) * 128], in_=psTt[:, 0:nq, :]), [psTt], [xtok])

            def emit(fk, pc, ps_, order=order):
                if si == 0:
                    conv_step()
                f_ = fsb[fk % 2]
                P.dma(sp, f_[:], cfg["Fs"][order, fk], [cfg["Fs"]], [f_], f_)
                V(lambda: nc.vector.tensor_tensor(out=t4[0][:], in0=pc[:], in1=f_[:, 0, :], op=ALU.mult), [pc, f_], [t4[0]])
                V(lambda: nc.vector.tensor_tensor(out=t4[1][:], in0=ps_[:], in1=f_[:, 1, :], op=ALU.mult), [ps_, f_], [t4[1]])
                V(lambda: nc.vector.tensor_tensor(out=t4[2][:], in0=pc[:], in1=f_[:, 1, :], op=ALU.mult), [pc, f_], [t4[2]])
                V(lambda: nc.vector.tensor_tensor(out=t4[3][:], in0=ps_[:], in1=f_[:, 0, :], op=ALU.mult), [ps_, f_], [t4[3]])
                P.op(pool, lambda: nc.gpsimd.tensor_tensor(out=Z[:, fk, 0, :], in0=t4[0][:], in1=t4[1][:], op=ALU.subtract), [t4[0], t4[1]], [Z])
                P.op(pool, lambda: nc.gpsimd.tensor_tensor(out=Z[:, fk, 1, :], in0=t4[2][:], in1=t4[3][:], op=ALU.add), [t4[2], t4[3]], [Z])

            fwd_dft(cfg, xtok, emit, (tC, tS, psC, psS))
            for tb in range(nblk):
                t0 = tb * bw
                P.dma(sp, tIC[:, 0:nfk, 0:bw], cfg["tIC"][tb], [cfg["tIC"]], [tIC], tIC)
                P.dma(sp, tIS[:, 0:nfk, 0:bw], cfg["tIS"][tb], [cfg["tIS"]], [tIS], tIS)
                for c in range(4):
                    pi = psI[c % 2]
                    for fk in range(nfk):
                        P.op(pe, lambda fk=fk, c=c, pi=pi: nc.tensor.matmul(pi[:, 0:bw], lhsT=Z[:, fk, 0, c * 128:(c + 1) * 128], rhs=tIC[:, fk, 0:bw],
                                                                           start=(fk == 0), stop=False), [Z, tIC], [pi])
                        P.op(pe, lambda fk=fk, c=c, pi=pi: nc.tensor.matmul(pi[:, 0:bw], lhsT=Z[:, fk, 1, c * 128:(c + 1) * 128], rhs=tIS[:, fk, 0:bw],
                                                                           start=False, stop=(fk == nfk - 1)), [Z, tIS], [pi])
                    if si == 0:
                        conv_step()
                    gate_ci = (4 if order == 0 else 8) + c
                    shortconv(gate_ci, off, t0, bw, rows_n, zin[1], cvo[1])
                    if order == 0:
                        shortconv(c, off, t0, bw, rows_n, zin[0], cvo[0])
                        V(lambda c=c, pi=pi: nc.vector.scalar_tensor_tensor(out=t4[0][:, 0:bw], in0=cvo[0][:, 0:bw], scalar=dvv[:, 0, c:c + 1], in1=pi[:, 0:bw],
                                                                          op0=ALU.mult, op1=ALU.add), [cvo[0], dvv, pi], [t4[0]])
                        V(lambda c=c, t0=t0: nc.vector.tensor_tensor(out=y1b[:, c, t0:t0 + bw], in0=t4[0][:, 0:bw], in1=cvo[1][:, 0:bw], op=ALU.mult),
                          [t4[0], cvo[1]], [y1b])
                    else:
                        o_ = ob[(tb * 4 + c) % 2]
                        V(lambda c=c, pi=pi, t0=t0: nc.vector.scalar_tensor_tensor(out=t4[0][:, 0:bw], in0=y1b[:, c, t0:t0 + bw], scalar=dvv[:, 1, c:c + 1], in1=pi[:, 0:bw],
                                                                                 op0=ALU.mult, op1=ALU.add), [y1b, dvv, pi], [t4[0]])
                        V(lambda o_=o_: nc.vector.tensor_tensor(out=o_[:, 0:bw], in0=t4[0][:, 0:bw], in1=cvo[1][:, 0:bw], op=ALU.mult), [t4[0], cvo[1]], [o_])
                        P.dma(pool, mixT[12 + c, :, off + t0:off + t0 + bw], o_[:, 0:bw], [o_], [mixT], o_)
    conv_step(len(cjobs))
    SC.close()
    S.close()


_NC_CACHE = {}


def _consts():
    ident = np.eye(128, dtype=np.float32)
    s = np.arange(128)[:, None]
    c = np.arange(128)[None, :]
    masks = np.stack([(s <= c), (s >= c)]).astype(np.float32)
    tp1 = np.arange(1, LS + 1, dtype=np.float32)
    out = dict(ident=ident, masks=masks, tp1=tp1)
    bf = ml_dtypes.bfloat16
    for L in (LS, LP):
        sx = str(L); N = 2 * L; ntc = L // 128; nfk = ntc + 1; nblk = max(1, L // 512); bw = min(512, L)
        ne = (L // 2 + 1 + 127) // 128
        kk = np.full(nfk * 128, -1, dtype=np.int64)
        ev = np.arange(0, L + 1, 2); od = np.arange(1, L, 2)
        kk[:len(ev)] = ev; kk[ne * 128:ne * 128 + len(od)] = od
        tt = np.arange(L, dtype=np.int64)
        ang = 2.0 * np.pi * ((np.maximum(kk, 0)[:, None] * tt[None, :]) % N).astype(np.float64) / N
        valid = (kk >= 0).astype(np.float64)[:, None]
        Ckt = np.cos(ang) * valid; Skt = np.sin(ang) * valid
        out["tFC" + sx] = np.ascontiguousarray(Ckt.reshape(nfk, 128, ntc, 128).transpose(0, 3, 2, 1)).astype(bf)
        out["tFS" + sx] = np.ascontiguousarray(Skt.reshape(nfk, 128, ntc, 128).transpose(0, 3, 2, 1)).astype(bf)
        out["tIC" + sx] = np.ascontiguousarray(Ckt.reshape(nfk, 128, nblk, bw).transpose(2, 1, 0, 3)).astype(bf)
        out["tIS" + sx] = np.ascontiguousarray(Skt.reshape(nfk, 128, nblk, bw).transpose(2, 1, 0, 3)).astype(bf)
        t = np.linspace(0.0, 1.0, L, dtype=np.float32)[:, None]
        w = (2.0 * np.float32(np.pi) * np.arange(L, dtype=np.float32)[:, None] / np.float32(L)).astype(np.float32)
        fr = np.linspace(1e-4, 15.0, 16, dtype=np.float32)[None, :]
        feats = np.concatenate([t, np.cos(fr * w), -np.sin(fr * w)], axis=-1).astype(np.float32)
        ridx = (L - np.arange(L)) % L
        fT = np.zeros((64, L), np.float32); fT[:33] = feats.T
        fTr = np.zeros((64, L), np.float32); fTr[:33] = feats[ridx].T
        out["featsT" + sx] = fT; out["featsTr" + sx] = fTr
        out["negtv" + sx] = np.ascontiguousarray(-t[:, 0].reshape(ntc, 128).T)
        out["negtvr" + sx] = np.ascontiguousarray(-t[ridx, 0].reshape(ntc, 128).T)
        wk = np.where((kk == 0) | (kk == L), 1.0 / N, 2.0 / N) * (kk >= 0)
        sg = np.where(kk % 2 == 0, 1.0, -1.0)
        out["wk" + sx] = np.ascontiguousarray(wk.reshape(nfk, 128).T).astype(np.float32)
        out["sgw" + sx] = np.ascontiguousarray((wk * sg).reshape(nfk, 128).T).astype(np.float32)
    return out


def kernel(**inp):
    n = 8
    if "nc" not in _NC_CACHE:
        _NC_CACHE["nc"] = build_program()
    nc = _NC_CACHE["nc"]
    f = lambda a: np.ascontiguousarray(np.asarray(a, dtype=np.float32))
    consts = _consts()
    shared = {k: f(inp[k]) for k in ("ada_w", "ada_b", "w_in", "w_out", "w_up", "w_down", "ln1_g", "ln1_b", "ln2_g", "ln2_b",
                                     "gla_w_gate", "gla_b_gate", "gla_norm_w", "ret_decay_exp",
                                     "s5_a_re", "s5_a_im", "s5_log_step", "s5_b_re", "s5_b_im", "s5_c_re", "s5_c_im", "s5_d", "s5_glu_w", "s5_glu_b",
                                     "hy_conv_w", "hy_conv_b", "hy_f_w1", "hy_f_b1", "hy_f_w2", "hy_f_b2", "hy_f_freq", "hy_f_w3", "hy_decay", "hy_d")}
    shared.update(consts)
    in_maps = []
    for c in range(n):
        m = dict(shared)
        m["x"] = np.ascontiguousarray(np.concatenate([inp["x_sample"][c], inp["x_prompt"][2 * c], inp["x_prompt"][2 * c + 1]], axis=0).astype(np.float32))
        m["cvec"] = np.ascontiguousarray(np.stack([inp["c"][c], inp["c_ctx"]]).astype(np.float32))
        m["sg"] = f(inp["state_gla"][c]); m["sr"] = f(inp["state_ret"][c])
        m["s5re"] = f(inp["state_s5_re"][c]); m["s5im"] = f(inp["state_s5_im"][c])
        in_maps.append(m)
    res = run_bass_kernel_spmd(nc, in_maps, core_ids=list(range(n)))
    R = res.results
    y = np.stack([r["y"] for r in R])
    y_sample = np.ascontiguousarray(y[:, :LS])
    y_prompt = np.ascontiguousarray(y[:, LS:].reshape(n * 2, LP, D))
    cat = lambda k: np.ascontiguousarray(np.concatenate([r[k] for r in R], axis=0))
    return (y_prompt, y_sample, cat("nsg"), cat("nsr"), cat("ns5re"), cat("ns5im"))
```

```python
import numpy as np
import ml_dtypes
from contextlib import ExitStack
import concourse.bass as bass
import concourse.mybir as mybir
from concourse.bass_utils import run_bass_kernel_spmd

F32 = mybir.dt.float32
BF16 = mybir.dt.bfloat16
AF = mybir.ActivationFunctionType
ALU = mybir.AluOpType
AX = mybir.AxisListType

D = 2048
DEPTH = 2
LS = 2048
LP = 256
NTOK = LS + 2 * LP
NTT = NTOK // 128
NTB = NTOK // 512
DFF = 8192
DIN = 5152
ALPHA = (2 * DEPTH) ** 0.25
LN_EPS = 1e-5
NORM_EPS = 1e-6
OFF = dict(gq=0, gk=256, gv=512, gg=1024, glr=1536, rq=1568, rk=1824, rv=2080, rg=2592, su=3104, hy=3616)
SEQS = [(0, LS), (LS, LP), (LS + LP, LP)]


class Buf:
    __slots__ = ("name", "w", "r", "dsem", "dcnt", "sw")

    def __init__(self, name):
        self.name = name
        self.w = None
        self.r = {}
        self.dsem = None
        self.dcnt = 0


class T:
    def __init__(self, t, name, excl=False):
        self.t = t
        self.b = Buf(name)
        self.excl = excl

    def __getitem__(self, k):
        return self.t[k]


class Eng:
    def __init__(self, P, name, h, selfsync=True):
        self.P = P
        self.name = name
        self.h = h
        self.selfsync = selfsync
        self.sem = P.newsem()
        self.cnt = 0
        self.seen = {}
        self.mysems = {id(self.sem)}


class Prog:
    SEM_LIMIT = 12000

    def __init__(self, nc):
        self.nc = nc
        self.es = ExitStack()
        self.nsem = 0
        self.freesems = []
        self.freesems_sw = []
        self.bufs = []
        self.semobj = {}
        self.pe = Eng(self, "pe", nc.tensor, selfsync=False)
        self.dve = Eng(self, "dve", nc.vector)
        self.act = Eng(self, "act", nc.scalar)
        self.pool = Eng(self, "pool", nc.gpsimd)
        self.sp = Eng(self, "sp", nc.sync)
        self.engs = [self.pe, self.dve, self.act, self.pool, self.sp]
        self.dchans = []
        self.nscope = 0

    def newsem(self):
        self.nsem += 1
        s = self.es.enter_context(self.nc.semaphore(f"sem{self.nsem}"))
        self.semobj[id(s)] = s
        return s

    def reg(self, t):
        self.bufs.append(t.b)
        return t

    def dram(self, name, shape, dt, kind="Internal"):
        h = self.nc.dram_tensor(name, list(shape), dt, kind=kind)
        return self.reg(T(h.ap(), name))

    def _need(self, eng, evs):
        for ev in evs:
            if ev is None:
                continue
            sem, val = ev
            k = id(sem)
            if (not eng.selfsync) and k in eng.mysems:
                continue
            if eng.seen.get(k, 0) >= val:
                continue
            eng.h.wait_ge(sem, val)
            eng.seen[k] = val

    def _deps(self, reads, writes, eng=None):
        evs = []
        for t in reads:
            evs.append(t.b.w)
            if t.excl:
                for k, (s, v) in t.b.r.items():
                    if eng is None or k not in eng.mysems:
                        evs.append((s, v))
        for t in writes:
            evs.append(t.b.w)
            for k, (s, v) in t.b.r.items():
                evs.append((s, v))
        return evs

    def _commit(self, ev, reads, writes):
        for t in reads:
            t.b.r[id(ev[0])] = ev
        for t in writes:
            t.b.w = ev
            t.b.r = {}

    def op(self, eng, fn, reads=(), writes=()):
        self._need(eng, self._deps(reads, writes, eng))
        inst = fn()
        eng.cnt += 1
        inst.then_inc(eng.sem, 1)
        ev = (eng.sem, eng.cnt)
        self._commit(ev, reads, writes)
        if eng.cnt >= self.SEM_LIMIT:
            eng.sem = self.newsem()
            eng.mysems.add(id(eng.sem))
            eng.cnt = 0
        return inst

    def dma(self, q, out, in_, reads, writes, chan, **kw):
        self._need(q, self._deps(reads, writes, q))
        sw = (q is self.pool)
        key = "sw" if sw else "hw"
        if not hasattr(chan, "chs"):
            chan.chs = {}
        if key not in chan.chs:
            pool_ = self.freesems_sw if sw else self.freesems
            b = Buf(chan.b.name + "_" + key)
            if pool_:
                b.dsem, b.dcnt = pool_.pop()
            else:
                b.dsem, b.dcnt = self.newsem(), 0
            b.sw = sw
            chan.chs[key] = b
            self.dchans.append(b)
        b = chan.chs[key]
        inst = q.h.dma_start(out=out, in_=in_, **kw)
        b.dcnt += 16
        inst.then_inc(b.dsem, 16)
        ev = (b.dsem, b.dcnt)
        self._commit(ev, reads, writes)
        return inst

    def barrier(self):
        evs = [(e.sem, e.cnt) for e in self.engs if e.cnt > 0]
        evs += [(b.dsem, b.dcnt) for b in self.dchans if b.dcnt > 0]
        for e in self.engs:
            sv = e.selfsync
            e.selfsync = True
            self._need(e, [ev for ev in evs if sv or id(ev[0]) not in e.mysems])
            e.selfsync = sv
        for b in self.bufs:
            b.w = None
            b.r = {}

    def scope(self):
        return Scope(self)


class Scope:
    def __init__(self, P):
        self.P = P
        self.es = ExitStack()
        self.ts = []
        P.nscope += 1
        self.id = P.nscope

    def sb(self, name, shape, dt=F32):
        t = self.es.enter_context(self.P.nc.sbuf_tensor(f"{name}_{self.id}", list(shape), dt))
        tt = self.P.reg(T(t, name))
        self.ts.append(tt)
        return tt

    def ps(self, name, shape, dt=F32):
        t = self.es.enter_context(self.P.nc.psum_tensor(f"{name}_{self.id}", list(shape), dt))
        tt = self.P.reg(T(t, name, excl=True))
        self.ts.append(tt)
        return tt

    def close(self):
        P = self.P
        P.barrier()
        for tt in self.ts:
            for key, cb in getattr(tt, "chs", {}).items():
                (P.freesems_sw if cb.sw else P.freesems).append((cb.dsem, cb.dcnt))
                P.dchans.remove(cb)
            tt.chs = {}
            P.bufs.remove(tt.b)
        self.es.close()


def build_program(debug=False):
    nc = bass.Bass("TRN2", target_bir_lowering=False, dynamic_dma_scratch_size=8192)
    P = Prog(nc)
    pe, dve, act, pool, sp = P.pe, P.dve, P.act, P.pool, P.sp

    def din(name, shape, dt=F32):
        return P.dram(name, shape, dt, kind="ExternalInput")

    def dout(name, shape, dt=F32):
        return P.dram(name, shape, dt, kind="ExternalOutput")

    x_in = din("x", [NTOK, D])
    cvec = din("cvec", [2, D])
    ada_w = din("ada_w", [DEPTH, D, 6 * D])
    ada_b = din("ada_b", [DEPTH, 6 * D])
    w_in = din("w_in", [DEPTH, D, DIN])
    w_out = din("w_out", [DEPTH, D, D])
    w_up = din("w_up", [DEPTH, D, DFF])
    w_down = din("w_down", [DEPTH, DFF, D])
    ln1_g = din("ln1_g", [DEPTH, D]); ln1_b = din("ln1_b", [DEPTH, D])
    ln2_g = din("ln2_g", [DEPTH, D]); ln2_b = din("ln2_b", [DEPTH, D])
    sg_in = din("sg", [DEPTH, 2, 4, 64, 128]); sr_in = din("sr", [DEPTH, 2, 4, 64, 128])
    gla_w_gate = din("gla_w_gate", [DEPTH, 2, 16, 256]); gla_b_gate = din("gla_b_gate", [DEPTH, 2, 256])
    gla_norm_w = din("gla_norm_w", [DEPTH, 128]); ret_decay_exp = din("ret_decay_exp", [DEPTH, 2, 4])
    s5re_in = din("s5re", [DEPTH, 2, 32, 64]); s5im_in = din("s5im", [DEPTH, 2, 32, 64])
    s5_a_re = din("s5_a_re", [DEPTH, 2, 32, 64]); s5_a_im = din("s5_a_im", [DEPTH, 2, 32, 64])
    s5_log_step = din("s5_log_step", [DEPTH, 2, 32])
    s5_b_re = din("s5_b_re", [DEPTH, 2, 32, 64, 16]); s5_b_im = din("s5_b_im", [DEPTH, 2, 32, 64, 16])
    s5_c_re = din("s5_c_re", [DEPTH, 2, 32, 16, 64]); s5_c_im = din("s5_c_im", [DEPTH, 2, 32, 16, 64])
    s5_d = din("s5_d", [DEPTH, 512]); s5_glu_w = din("s5_glu_w", [DEPTH, 512, 512]); s5_glu_b = din("s5_glu_b", [DEPTH, 512])
    tp1_in = din("tp1", [LS])
    hy_conv_w = din("hy_conv_w", [DEPTH, 3, 1536]); hy_conv_b = din("hy_conv_b", [DEPTH, 1536])
    hy_f_w1 = din("hy_f_w1", [DEPTH, 33, 64]); hy_f_b1 = din("hy_f_b1", [DEPTH, 64])
    hy_f_w2 = din("hy_f_w2", [DEPTH, 64, 64]); hy_f_b2 = din("hy_f_b2", [DEPTH, 64])
    hy_f_freq = din("hy_f_freq", [DEPTH, 64]); hy_f_w3 = din("hy_f_w3", [DEPTH, 64, 2048])
    hy_decay = din("hy_decay", [DEPTH, 2048]); hy_d = din("hy_d", [DEPTH, 2, 512])
    fstash = P.dram("fstash", [2, 2, 128, 16, 512], BF16)
    s5BT = P.dram("s5BT", [32, 128, 2, 128], BF16); s5CP = P.dram("s5CP", [32, 128, 2, 128], BF16)
    hyc = {}
    for L_ in (LS, LP):
        sx = str(L_); ntc_ = L_ // 128; nfk_ = ntc_ + 1; nblk_ = max(1, L_ // 512); bw_ = min(512, L_)
        hyc["tFC" + sx] = din("tFC" + sx, [nfk_, 128, ntc_, 128], BF16); hyc["tFS" + sx] = din("tFS" + sx, [nfk_, 128, ntc_, 128], BF16)
        hyc["tIC" + sx] = din("tIC" + sx, [nblk_, 128, nfk_, bw_], BF16); hyc["tIS" + sx] = din("tIS" + sx, [nblk_, 128, nfk_, bw_], BF16)
        hyc["featsT" + sx] = din("featsT" + sx, [64, L_]); hyc["featsTr" + sx] = din("featsTr" + sx, [64, L_])
        hyc["negtv" + sx] = din("negtv" + sx, [128, ntc_]); hyc["negtvr" + sx] = din("negtvr" + sx, [128, ntc_])
        hyc["wk" + sx] = din("wk" + sx, [128, nfk_]); hyc["sgw" + sx] = din("sgw" + sx, [128, nfk_])
        hyc["Fs" + sx] = P.dram("Fs" + sx, [2, nfk_, 128, 2, 512], F32)
    ident_in = din("ident", [128, 128])
    masks_in = din("masks", [2, 128, 128])
    y_out = dout("y", [NTOK, D])
    nsg = dout("nsg", [2, DEPTH, 2, 4, 64, 128]); nsr = dout("nsr", [2, DEPTH, 2, 4, 64, 128])
    ns5re = dout("ns5re", [2, DEPTH, 2, 32, 64]); ns5im = dout("ns5im", [2, DEPTH, 2, 32, 64])
    kind = "ExternalOutput" if debug else "Internal"
    mixonly = isinstance(debug, str) and debug.startswith("mix")
    pkind = "ExternalInput" if mixonly else kind
    xcur = P.dram("xcur", [NTOK, D], F32, kind=kind)
    modvec = P.dram("modvec", [DEPTH, 2, 6 * D], F32)
    wo_bf = P.dram("wo_bf", [DEPTH, 4, 128, 16, 512], BF16)
    wu_bf = P.dram("wu_bf", [DEPTH, 16, 128, 16, 512], BF16)
    wd_bf = P.dram("wd_bf", [DEPTH, 4, 8, 128, 8, 512], BF16)
    pF = {n: P.dram("pF_" + n, [c, 128, NTOK], F32, kind=pkind) for n, c in
          dict(qkg=4, qkr=4, su=4, hy=12).items()}
    pLr = P.dram("pF_lr", [32, NTOK], F32, kind=pkind)
    pT = {n: P.dram("pT_" + n, [NTOK, 512], F32, kind=pkind) for n in ("gv", "gg", "rv", "rg")}
    mixT = P.dram("mixT", [16, 128, NTOK], BF16, kind=kind)

    G = P.scope()
    ident_f = G.sb("ident_f", [128, 128], F32)
    ident_b = G.sb("ident_b", [128, 128], BF16)
    modT = G.sb("modT", [128, DEPTH, 96, 2], F32)
    P.dma(sp, ident_f[:], ident_in[:, :], [ident_in], [ident_f], ident_f)
    P.op(dve, lambda: nc.vector.tensor_copy(out=ident_b[:], in_=ident_f[:]), [ident_f], [ident_b])

    def load_slab(q, dst, src_ap, srcT):
        P.dma(q, dst[:], src_ap.rearrange("(k p) n -> p k n", p=128), [srcT], [dst], dst)

    cast_rr = [0]

    def cast(dst, src, nk):
        engs = [(act, lambda o, i: nc.scalar.copy(out=o, in_=i)),
                (dve, lambda o, i: nc.vector.tensor_copy(out=o, in_=i))]
        h = (nk * 5) // 8
        for (k0, k1) in ((0, h), (h, nk)):
            e, f = engs[cast_rr[0] % 2]
            cast_rr[0] += 1
            P.op(e, lambda f=f, k0=k0, k1=k1: f(dst[:, k0:k1, :], src[:, k0:k1, :]), [src], [dst])

    if not mixonly:
        S0 = P.scope()
        cT = S0.sb("cT", [128, 16, 2])
        abT = S0.sb("abT", [128, DEPTH, 96])
        stg = [S0.sb(f"stg{i}", [128, 16, 512]) for i in range(2)]
        stb = [S0.sb(f"stb{i}", [128, 16, 512], BF16) for i in range(2)]
        modps = [S0.ps(f"modps{l}", [128, 96, 2]) for l in range(DEPTH)]
        with nc.allow_non_contiguous_dma(reason="tiny param transposes"):
            for r in range(2):
                P.dma(sp, cT[:, :, r], cvec[r].rearrange("(k p) -> p k", p=128), [cvec], [cT], cT)
            for l in range(DEPTH):
                P.dma(sp, abT[:, l, :], ada_b[l].rearrange("(j p) -> p j", p=128), [ada_b], [abT], abT)
        P.op(act, lambda: nc.scalar.activation(out=cT[:], in_=cT[:], func=AF.Silu), [cT], [cT])
        cTb = S0.sb("cTb", [128, 16, 2], BF16)
        P.op(dve, lambda: nc.vector.tensor_copy(out=cTb[:], in_=cT[:]), [cT], [cTb])
        it = 0
        for l in range(DEPTH):
            for s in range(24):
                buf0 = stg[it % 2]
                buf = stb[it % 2]
                it += 1
                if s == 0:
                    load_slab(sp, buf0, ada_w[l, :, 0:512], ada_w)
                if s + 1 < 24:
                    load_slab(sp, stg[it % 2], ada_w[l, :, (s + 1) * 512:(s + 2) * 512], ada_w)
                cast(buf, buf0, 16)
                for j in range(4):
                    for k in range(16):
                        P.op(pe, lambda buf=buf, j=j, k=k, s=s, l=l: nc.tensor.matmul(
                            modps[l][:, s * 4 + j, :], lhsT=buf[:, k, j * 128:(j + 1) * 128], rhs=cTb[:, k, :],
                            start=(k == 0), stop=(k == 15)), [buf, cTb], [modps[l]])
            for r in range(2):
                P.op(dve, lambda l=l, r=r: nc.vector.tensor_tensor(out=modT[:, l, :, r], in0=modps[l][:, :, r],
                                                                   in1=abT[:, l, :], op=ALU.add),
                     [modps[l], abT], [modT])
        with nc.allow_non_contiguous_dma(reason="mod vectors to DRAM rows"):
            for l in range(DEPTH):
                for r in range(2):
                    P.dma(sp, modvec[l, r].rearrange("(j p) -> p j", p=128), modT[:, l, :, r], [modT], [modvec], modT)
        for l in range(DEPTH):
            for c0 in (16, 64):
                P.op(dve, lambda l=l, c0=c0: nc.vector.tensor_scalar_add(out=modT[:, l, c0:c0 + 16, :],
                                                                         in0=modT[:, l, c0:c0 + 16, :], scalar1=1.0),
                     [modT], [modT])
        S0.close()

    convjobs = {}
    for l in range(DEPTH):
        jobs = []
        for s_ in range(4):
            jobs.append((w_out, w_out[l, :, s_ * 512:(s_ + 1) * 512], wo_bf, wo_bf[l, s_], 16))
        for s_ in range(16):
            jobs.append((w_up, w_up[l, :, s_ * 512:(s_ + 1) * 512], wu_bf, wu_bf[l, s_], 16))
        for c in range(4):
            for kg in range(0, 8, 2):
                jobs.append((w_down, w_down[l, kg * 1024:(kg + 2) * 1024, c * 512:(c + 1) * 512], wd_bf,
                             wd_bf[l, c, kg:kg + 2].rearrange("g p k n -> p g k n"), 16))
        convjobs[l] = jobs if not mixonly else []

    def ln_stats(S, xt, eps, name):
        st = S.sb(name + "_st", [128, 4, 6])
        mv = S.sb(name + "_mv", [128, 2])
        rs = S.sb(name + "_rs", [128, 1])
        return st, mv, rs

    def do_ln_stats(st, mv, rs, xt, eps):
        for c in range(4):
            P.op(dve, lambda c=c: nc.vector.bn_stats(out=st[:, c, :], in_=xt[:, c * 512:(c + 1) * 512]), [xt], [st])
        P.op(dve, lambda: nc.vector.bn_aggr(out=mv[:], in_=st[:].rearrange("p c s -> p (c s)")), [st], [mv])
        P.op(act, lambda: nc.scalar.activation(out=rs[:], in_=mv[:, 1:2], func=AF.Ln, bias=eps), [mv], [rs])
        P.op(act, lambda: nc.scalar.activation(out=rs[:], in_=rs[:], func=AF.Exp, scale=-0.5), [rs], [rs])

    for l in range(DEPTH):
        xsrc = x_in if l == 0 else xcur
        xdst = xcur if l == 0 else y_out
        if not mixonly:
            SA = P.scope()
            hT = SA.sb("hT", [128, 16, NTOK], BF16)
            SA1 = P.scope()
            xin = [SA1.sb(f"xin{i}", [128, D]) for i in range(2)]
            xn = [SA1.sb(f"xn{i}", [128, D], BF16) for i in range(2)]
            stq = [ln_stats(SA1, None, LN_EPS, f"lnA{i}") for i in range(2)]
            psT = [SA1.ps(f"psT{i}", [128, 16, 128], BF16) for i in range(2)]
            for tt in range(NTT):
                typ = 0 if tt < LS // 128 else 1
                xi, xb_, (st, mv, rs), pt = xin[tt % 2], xn[tt % 2], stq[tt % 2], psT[tt % 2]
                P.dma(sp, xi[:], xsrc[tt * 128:(tt + 1) * 128, :], [xsrc], [xi], xi)
                do_ln_stats(st, mv, rs, xi, LN_EPS)
                P.op(dve, lambda xi=xi, xb_=xb_, mv=mv, rs=rs: nc.vector.tensor_scalar(
                    out=xb_[:], in0=xi[:], scalar1=mv[:, 0:1], scalar2=rs[:, 0:1], op0=ALU.subtract, op1=ALU.mult),
                     [xi, mv, rs], [xb_])
                for j in range(16):
                    P.op(pe, lambda j=j, pt=pt, xb_=xb_: nc.tensor.transpose(pt[:, j, :], xb_[:, j * 128:(j + 1) * 128], ident_b[:]),
                         [xb_, ident_b], [pt])
                for j in range(16):
                    if j % 2 == 0:
                        P.op(dve, lambda j=j, pt=pt, typ=typ, tt=tt: nc.vector.tensor_scalar(
                            out=hT[:, j, tt * 128:(tt + 1) * 128], in0=pt[:, j, :], scalar1=modT[:, l, 16 + j, typ:typ + 1],
                            scalar2=modT[:, l, j, typ:typ + 1], op0=ALU.mult, op1=ALU.add), [pt, modT], [hT])
                    else:
                        P.op(act, lambda j=j, pt=pt, typ=typ, tt=tt: nc.scalar.activation(
                            out=hT[:, j, tt * 128:(tt + 1) * 128], in_=pt[:, j, :], func=AF.Identity,
                            scale=modT[:, l, 16 + j, typ:typ + 1], bias=modT[:, l, j, typ:typ + 1]), [pt, modT], [hT])
            SA1.close()
            SB = P.scope()
            wst = [SB.sb(f"wst{i}", [128, 16, 512]) for i in range(2)]
            wsb = [SB.sb(f"wsb{i}", [128, 16, 512], BF16) for i in range(2)]
            ost = [SB.sb(f"ost{i}", [128, 512]) for i in range(4)]
            psB = [SB.ps(f"psB{i}", [128, 512]) for i in range(4)]
            oi = [0]

            def evac(ps_ap, psT_, dst_dram_ap, dstT, np_, scale=1.0, func=None):
                o = ost[oi[0] % 4]
                e = oi[0] % 2
                oi[0] += 1
                if func is not None or e == 0:
                    P.op(act, lambda: nc.scalar.activation(out=o[0:np_, :], in_=ps_ap, func=(func or AF.Copy), scale=scale),
                         [psT_], [o])
                else:
                    P.op(dve, lambda: nc.vector.tensor_scalar_mul(out=o[0:np_, :], in0=ps_ap, scalar1=scale), [psT_], [o])
                P.dma(pool, dst_dram_ap, o[0:np_, :], [o], [dstT], o)

            slabs = [("F", OFF["gq"], pF["qkg"], (0.125, 0.125, 1.0, 1.0)), ("T", OFF["gv"], pT["gv"], None),
                     ("T", OFF["gg"], pT["gg"], AF.Silu), ("L", OFF["glr"], pLr, None),
                     ("F", OFF["rq"], pF["qkr"], (1.0, 1.0, 0.125, 0.125)), ("T", OFF["rv"], pT["rv"], None),
                     ("T", OFF["rg"], pT["rg"], AF.Silu), ("F", OFF["su"], pF["su"], (1.0,) * 4),
                     ("F3", OFF["hy"], pF["hy"], 0), ("F3", OFF["hy"] + 512, pF["hy"], 4), ("F3", OFF["hy"] + 1024, pF["hy"], 8)]
            for si, (kind_, c0, dstT, extra) in enumerate(slabs):
                a, b = wst[si % 2], wsb[si % 2]
                ncol = 32 if kind_ == "L" else 512
                P.dma(sp, a[:, :, 0:ncol], w_in[l, :, c0:c0 + ncol].rearrange("(k p) n -> p k n", p=128), [w_in], [a], a)
                h8 = 8
                P.op(act, lambda a=a, b=b, ncol=ncol: nc.scalar.copy(out=b[:, 0:8, 0:ncol], in_=a[:, 0:8, 0:ncol]), [a], [b])
                P.op(dve, lambda a=a, b=b, ncol=ncol: nc.vector.tensor_copy(out=b[:, 8:16, 0:ncol], in_=a[:, 8:16, 0:ncol]), [a], [b])
                if kind_ in ("F", "F3"):
                    for tb in range(NTB):
                        for j in range(4):
                            ps = psB[(tb * 4 + j) % 4]
                            for k in range(16):
                                P.op(pe, lambda ps=ps, b=b, j=j, k=k, tb=tb: nc.tensor.matmul(
                                    ps[:], lhsT=b[:, k, j * 128:(j + 1) * 128], rhs=hT[:, k, tb * 512:(tb + 1) * 512],
                                    start=(k == 0), stop=(k == 15)), [b, hT], [ps])
                            if kind_ == "F":
                                evac(ps[:], ps, dstT[j, :, tb * 512:(tb + 1) * 512], dstT, 128, scale=extra[j])
                            else:
                                evac(ps[:], ps, dstT[extra + j, :, tb * 512:(tb + 1) * 512], dstT, 128)
                elif kind_ == "L":
                    for tb in range(NTB):
                        ps = psB[tb % 4]
                        for k in range(16):
                            P.op(pe, lambda ps=ps, b=b, k=k, tb=tb: nc.tensor.matmul(
                                ps[0:32, :], lhsT=b[:, k, 0:32], rhs=hT[:, k, tb * 512:(tb + 1) * 512],
                                start=(k == 0), stop=(k == 15)), [b, hT], [ps])
                        evac(ps[0:32, :], ps, dstT[:, tb * 512:(tb + 1) * 512], dstT, 32)
                else:
                    for tt in range(NTT):
                        ps = psB[tt % 4]
                        for k in range(16):
                            P.op(pe, lambda ps=ps, b=b, k=k, tt=tt: nc.tensor.matmul(
                                ps[:], lhsT=hT[:, k, tt * 128:(tt + 1) * 128], rhs=b[:, k, :],
                                start=(k == 0), stop=(k == 15)), [b, hT], [ps])
                        evac(ps[:], ps, dstT[tt * 128:(tt + 1) * 128, :], dstT, 128, func=extra)
            SB.close()
            SA.close()

        env_ = dict(locals()); env_.update(hyc); env_["convjobs"] = convjobs[l]
        mixers(P, nc, l, env_)
        if debug == 1 or mixonly:
            break

        if not mixonly:
            SD = P.scope()
            mxb = SD.sb("mxb", [128, 16, 512], BF16)
            h2T = mxb
            uT = SD.sb("uT", [128, 64, 512], BF16)
            xa = [[SD.sb(f"xa{s_}{i}", [128, D]) for i in range(4)] for s_ in range(2)]
            bc = {n: SD.sb("bc_" + n, [128, D]) for n in ("g1", "g2", "lg", "lb")}
            wsl = [SD.sb(f"wsl{i}", [128, 8, 512], BF16) for i in range(3)]
            tmp = [SD.sb(f"tmpD{i}", [128, 512]) for i in range(2)]
            xnb = SD.sb("xnb", [128, D], BF16)
            stD = ln_stats(SD, None, LN_EPS, "lnD")
            pb8 = [SD.ps(f"pb8_{i}", [128, 512]) for i in range(8)]
            psAcc = pb8[0:4]
            def load_lnp(gT, bT):
                P.dma(sp, bc["lg"][:], gT[l].partition_broadcast(128), [gT], [bc["lg"]], bc["lg"])
                P.dma(sp, bc["lb"][:], bT[l].partition_broadcast(128), [bT], [bc["lb"]], bc["lb"])
            wi = [0]

            def wload(src_ap, srcT, nk=8):
                w = wsl[wi[0] % 3]
                wi[0] += 1
                P.dma(sp, w[:, 0:nk, :], src_ap, [srcT], [w], w)
                return w

            def resid(xt, c, ps, gname, ti):
                t_ = tmp[ti % 2]
                P.op(dve, lambda: nc.vector.tensor_tensor(out=t_[:], in0=ps[:], in1=bc[gname][:, c * 512:(c + 1) * 512], op=ALU.mult),
                     [ps, bc[gname]], [t_])
                P.op(dve, lambda: nc.vector.scalar_tensor_tensor(out=xt[:, c * 512:(c + 1) * 512], in0=xt[:, c * 512:(c + 1) * 512],
                                                                 scalar=ALPHA, in1=t_[:], op0=ALU.mult, op1=ALU.add),
                     [t_, xt], [xt])

            nmr = SD.sb("nmr", [128, 1])

            def act_norm(dst, src, srcT, dstT):
                st, mv, rs = stD
                P.op(dve, lambda: nc.vector.scalar_tensor_tensor(out=nmr[:], in0=mv[:, 0:1], scalar=-1.0, in1=rs[:], op0=ALU.mult, op1=ALU.mult),
                     [mv, rs], [nmr])
                P.op(act, lambda: nc.scalar.activation(out=dst, in_=src, func=AF.Identity, scale=rs[:, 0:1], bias=nmr[:, 0:1]), [srcT, rs, nmr], [dstT])

            def ln_affine(xt, gn, bn):
                st, mv, rs = stD
                do_ln_stats(st, mv, rs, xt, LN_EPS)
                act_norm(xt[:], xt[:], xt, xt)
                P.op(dve, lambda: nc.vector.tensor_tensor(out=xt[:], in0=xt[:], in1=bc[gn][:], op=ALU.mult), [xt, bc[gn]], [xt])
                P.op(dve, lambda: nc.vector.tensor_tensor(out=xt[:], in0=xt[:], in1=bc[bn][:], op=ALU.add), [xt, bc[bn]], [xt])

            tic = [0]

            def typ_of(tb):
                return 0 if tb < 4 else 1

            def st_load(tb):
                typ = typ_of(tb)
                if tb == 0 or tb == 4:
                    P.dma(sp, bc["g1"][:], modvec[l, typ, 2 * D:3 * D].partition_broadcast(128), [modvec], [bc["g1"]], bc["g1"])
                P.dma(sp, mxb[:], mixT[:, :, tb * 512:(tb + 1) * 512].rearrange("k p n -> p k n"), [mixT], [mxb], mxb)
                for tt in range(4):
                    t0 = tb * 512 + tt * 128
                    P.dma(sp, xa[tb % 2][tt][:], xsrc[t0:t0 + 128, :], [xsrc], [xa[tb % 2][tt]], xa[tb % 2][tt])

            def st_D1(tb):
                X = xa[tb % 2]
                for c in range(4):
                    for hf in range(2):
                        w = wload(wo_bf[l, c, :, hf * 8:(hf + 1) * 8, :], wo_bf)
                        for tt in range(4):
                            for k8 in range(8):
                                k = hf * 8 + k8
                                P.op(pe, lambda w=w, tt=tt, k=k, k8=k8: nc.tensor.matmul(
                                    psAcc[tt][:], lhsT=mxb[:, k, tt * 128:(tt + 1) * 128], rhs=w[:, k8, :],
                                    start=(k == 0), stop=(k == 15)), [w, mxb], [psAcc[tt]])
                    for tt in range(4):
                        resid(X[tt], c, psAcc[tt], "g1", tic[0]); tic[0] += 1

            def st_D2(tb, tt):
                X = xa[tb % 2]
                typ = typ_of(tb)
                if tt == 0:
                    load_lnp(ln1_g, ln1_b)
                ln_affine(X[tt], "lg", "lb")
                st, mv, rs = stD
                do_ln_stats(st, mv, rs, X[tt], LN_EPS)
                act_norm(xnb[:], X[tt][:], X[tt], xnb)
                def tview(j):
                    bank = pb8[6 + j // 8]
                    return bank, bank[:].bitcast(BF16)[:, (j % 8) * 128:(j % 8 + 1) * 128]
                for j in range(16):
                    bank, v = tview(j)
                    P.op(pe, lambda j=j, v=v: nc.tensor.transpose(v, xnb[:, j * 128:(j + 1) * 128], ident_b[:]),
                         [xnb, ident_b], [bank])
                for j in range(16):
                    bank, v = tview(j)
                    if j < 8:
                        P.op(dve, lambda j=j, v=v: nc.vector.tensor_scalar(
                            out=h2T[:, j, tt * 128:(tt + 1) * 128], in0=v, scalar1=modT[:, l, 64 + j, typ:typ + 1],
                            scalar2=modT[:, l, 48 + j, typ:typ + 1], op0=ALU.mult, op1=ALU.add), [bank, modT], [h2T])
                    else:
                        P.op(act, lambda j=j, v=v: nc.scalar.activation(
                            out=h2T[:, j, tt * 128:(tt + 1) * 128], in_=v, func=AF.Identity,
                            scale=modT[:, l, 64 + j, typ:typ + 1], bias=modT[:, l, 48 + j, typ:typ + 1]), [bank, modT], [h2T])

            def st_D3(tb):
                ei = 0
                for s_ in range(16):
                    banks = pb8[(s_ % 2) * 4:(s_ % 2) * 4 + 4]
                    for hf in range(2):
                        w = wload(wu_bf[l, s_, :, hf * 8:(hf + 1) * 8, :], wu_bf)
                        for k8 in range(8):
                            k = hf * 8 + k8
                            for j in range(4):
                                P.op(pe, lambda w=w, j=j, k=k, k8=k8, banks=banks: nc.tensor.matmul(
                                    banks[j][:], lhsT=w[:, k8, j * 128:(j + 1) * 128], rhs=h2T[:, k, :], start=(k == 0), stop=(k == 15)),
                                     [w, h2T], [banks[j]])
                    for j in range(4):
                        t_ = tmp[ei % 2]
                        ei += 1
                        P.op(act, lambda j=j, t_=t_, banks=banks: nc.scalar.activation(out=t_[:], in_=banks[j][:], func=AF.Relu), [banks[j]], [t_])
                        P.op(pool, lambda t_=t_, s_=s_, j=j: nc.gpsimd.tensor_tensor(out=uT[:, s_ * 4 + j, :], in0=t_[:], in1=t_[:], op=ALU.mult),
                             [t_], [uT])

            def st_D4(tb, c):
                X = xa[tb % 2]
                if c == 0 and (tb == 0 or tb == 4):
                    typ = typ_of(tb)
                    P.dma(sp, bc["g2"][:], modvec[l, typ, 5 * D:6 * D].partition_broadcast(128), [modvec], [bc["g2"]], bc["g2"])
                for kg in range(8):
                    w = wload(wd_bf[l, c, kg], wd_bf)
                    for k in range(8):
                        kk = kg * 8 + k
                        for tt in range(4):
                            P.op(pe, lambda w=w, tt=tt, k=k, kk=kk: nc.tensor.matmul(
                                psAcc[tt][:], lhsT=uT[:, kk, tt * 128:(tt + 1) * 128], rhs=w[:, k, :],
                                start=(kk == 0), stop=(kk == 63)), [w, uT], [psAcc[tt]])

            def st_D4r(tb, c):
                X = xa[tb % 2]
                for tt in range(4):
                    resid(X[tt], c, psAcc[tt], "g2", tic[0]); tic[0] += 1

            def st_D5(tb):
                X = xa[tb % 2]
                load_lnp(ln2_g, ln2_b)
                for tt in range(4):
                    ln_affine(X[tt], "lg", "lb")
                    t0 = tb * 512 + tt * 128
                    P.dma(pool, xdst[t0:t0 + 128, :], X[tt][:], [X[tt]], [xdst], X[tt])

            st_load(0)
            st_D1(0)
            for tt in range(4):
                st_D2(0, tt)
            for tb in range(NTB):
                st_D3(tb)
                nxt = tb + 1 < NTB
                if nxt:
                    st_load(tb + 1)
                st_D4(tb, 0)
                st_D4r(tb, 0)
                if nxt:
                    st_D1(tb + 1)
                for c in range(1, 4):
                    st_D4(tb, c)
                    if nxt:
                        st_D2(tb + 1, c - 1)
                        if c == 3:
                            st_D2(tb + 1, 3)
                    st_D4r(tb, c)
                st_D5(tb)
            SD.close()

    G.close()
    P.barrier()
    P.es.close()
    return nc


def mixers(P, nc, l, env):
    sel = env.get("debug")
    sel = sel[4:] if isinstance(sel, str) and sel.startswith("mix:") else "hy,s5,gla,ret"
    if "hy" in sel:
        mixer_hy(P, nc, l, env)
    if "s5" in sel:
        mixer_s5(P, nc, l, env)
    for gi in range(2):
        if ("gla", "ret")[gi] in sel:
            mixer_la(P, nc, l, env, gi)


def mixer_la(P, nc, l, env, gi):
    pe, dve, act, pool, sp = P.pe, P.dve, P.act, P.pool, P.sp
    qk = env["pF"]["qkg" if gi == 0 else "qkr"]
    vT = env["pT"]["gv" if gi == 0 else "rv"]
    gT = env["pT"]["gg" if gi == 0 else "rg"]
    pLr = env["pLr"]
    mixT = env["mixT"]
    ident_b = env["ident_b"]
    st_in = env["sg_in"] if gi == 0 else env["sr_in"]
    st_out = env["nsg"] if gi == 0 else env["nsr"]
    S = P.scope()
    rmask = S.sb("rmask", [128, LS])
    masks = S.sb("masks", [128, 2, 128], F32)
    P.dma(sp, masks[:], env["masks_in"][:].rearrange("m s c -> s m c"), [env["masks_in"]], [masks], masks)
    qdm = [[S.sb(f"qdm{d}{a}", [128, LS], BF16) for a in range(2)] for d in range(2)]
    kd = [S.sb(f"kd{d}", [128, LS], BF16) for d in range(2)]
    dec = S.sb("dec", [128, 2, 32])
    vb = S.sb("vb", [128, 16, 256], BF16)
    vst = S.sb("vst", [128, 4, 256])
    o_all = S.sb("o_all", [128, 16, 256])
    Rf = [S.sb(f"Rf{d}", [128, 256]) for d in range(2)]
    Sbf = [[S.sb(f"Sbf{d}{i}", [128, 256], BF16) for i in range(2)] for d in range(2)]
    kdt = [S.sb(f"kdt{d}", [128, 128], BF16) for d in range(2)]
    att = [S.sb(f"att{d}", [128, 2, 128], BF16) for d in range(2)]
    par = S.sb("par", [128, 2, 2])
    nw = S.sb("nw", [128, 128])
    psA = [S.ps(f"psA{d}", [128, 2, 128]) for d in range(2)]
    psO = [S.ps(f"psO{d}", [128, 256]) for d in range(2)]
    psUp = [S.ps(f"psUp{d}", [128, 256]) for d in range(2)]
    for d in range(2):
        P.op(dve, lambda d=d: nc.vector.memset(kdt[d][:], 0.0), [], [kdt[d]])
        P.op(dve, lambda d=d: nc.vector.memset(att[d][:], 0.0), [], [att[d]])
        for a in range(2):
            P.op(pool, lambda d=d, a=a: nc.gpsimd.memset(qdm[d][a][:], 0.0), [], [qdm[d][a]])
    P.op(pool, lambda: nc.gpsimd.memset(vb[:], 0.0), [], [vb])
    psKt = S.ps("psK", [128, 2, 128], BF16)
    psK = [psKt, psKt]
    P.op(dve, lambda: nc.vector.memset(rmask[:], 1.0), [], [rmask])
    P.op(dve, lambda: nc.vector.memset(rmask[:].rearrange("p (n t) -> p n t", t=128)[:, :, 0:1], 0.0), [], [rmask])
    if gi == 0:
        wg = S.sb("wg", [32, 2, 256])
        lrT = S.sb("lrT", [32, LS])
        P.op(dve, lambda: nc.vector.memset(wg[:], 0.0), [], [wg])
        for d in range(2):
            P.dma(sp, wg[d * 16:(d + 1) * 16, d, :], env["gla_w_gate"][l, d], [env["gla_w_gate"]], [wg], wg)
        with nc.allow_non_contiguous_dma(reason="tiny"):
            for d in range(2):
                P.dma(sp, par[:, d, :], env["gla_b_gate"][l, d].rearrange("(h p) -> p h", p=128), [env["gla_b_gate"]], [par], par)
        P.op(dve, lambda: nc.vector.tensor_scalar_mul(out=par[:], in0=par[:], scalar1=-1.0), [par], [par])
        P.dma(sp, nw[:], env["gla_norm_w"][l].partition_broadcast(128), [env["gla_norm_w"]], [nw], nw)
    else:
        for d in range(2):
            for hp in range(2):
                for a in range(2):
                    P.dma(sp, par[a * 64:(a + 1) * 64, d, hp:hp + 1],
                          env["ret_decay_exp"][l, d, 2 * hp + a:2 * hp + a + 1].partition_broadcast(64),
                          [env["ret_decay_exp"]], [par], par)
        P.op(act, lambda: nc.scalar.activation(out=par[:], in_=par[:], func=AF.Exp, scale=-float(np.log(2.0))), [par], [par])
        P.op(act, lambda: nc.scalar.activation(out=par[:], in_=par[:], func=AF.Ln, scale=-1.0, bias=1.0), [par], [par])
        P.op(dve, lambda: nc.vector.tensor_scalar_mul(out=par[:], in0=par[:], scalar1=-16.0), [par], [par])

    for si, (off, L) in enumerate(SEQS):
        nch = L // 128
        for hp in range(2):
            SP = P.scope()
            qk_f = SP.sb("qk_f", [128, 2, LS])
            lsp = SP.sb("lsp", [128, LS]); cs = SP.sb("cs", [128, LS]); bb = SP.sb("bb", [128, LS])
            Eb = SP.sb("Eb", [128, LS]); Enb = SP.sb("Enb", [128, LS])
            psL = SP.ps("psL", [128, 512])
            P.dma(sp, qk_f[:, 0, 0:L], qk[hp, :, off:off + L], [qk], [qk_f], qk_f)
            P.dma(sp, qk_f[:, 1, 0:L], qk[2 + hp, :, off:off + L], [qk], [qk_f], qk_f)
            if gi == 0:
                P.dma(sp, lrT[:, 0:L], pLr[:, off:off + L], [pLr], [lrT], lrT)
            for d in range(2):
                if gi == 0:
                    for t0 in range(0, L, 512):
                        n_ = min(512, L - t0)
                        P.op(pe, lambda d=d, t0=t0, n_=n_: nc.tensor.matmul(
                            psL[:, 0:n_], lhsT=wg[:, d, hp * 128:(hp + 1) * 128], rhs=lrT[:, t0:t0 + n_], start=True, stop=True),
                             [wg, lrT], [psL])
                        P.op(act, lambda d=d, t0=t0, n_=n_: nc.scalar.activation(
                            out=lsp[:, t0:t0 + n_], in_=psL[:, 0:n_], func=AF.Exp, scale=-1.0, bias=par[:, d, hp:hp + 1]),
                             [psL, par], [lsp])
                    P.op(act, lambda: nc.scalar.activation(out=lsp[:, 0:L], in_=lsp[:, 0:L], func=AF.Ln, bias=1.0), [lsp], [lsp])
                else:
                    P.op(dve, lambda d=d: nc.vector.tensor_scalar_mul(out=lsp[:, 0:L], in0=rmask[:, 0:L], scalar1=0.0), [rmask], [lsp])
                    P.op(dve, lambda d=d: nc.vector.tensor_scalar_add(out=lsp[:, 0:L], in0=lsp[:, 0:L], scalar1=par[:, d, hp:hp + 1]),
                         [lsp, par], [lsp])
                P.op(dve, lambda: nc.vector.tensor_tensor_scan(out=cs[:, 0:L], data0=rmask[:, 0:L], data1=lsp[:, 0:L], initial=0.0,
                                                               op0=ALU.mult, op1=ALU.add), [rmask, lsp], [cs])
                cs3 = cs[:, 0:L].rearrange("p (n t) -> p n t", t=128)
                if d == 0:
                    bsrc = cs
                else:
                    bsrc = bb
                    P.op(dve, lambda: nc.vector.tensor_tensor(out=bb[:, 0:L], in0=lsp[:, 0:L], in1=cs[:, 0:L], op=ALU.subtract), [lsp, cs], [bb])
                    P.op(dve, lambda cs3=cs3: nc.vector.tensor_tensor(
                        out=bb[:, 0:L].rearrange("p (n t) -> p n t", t=128), in0=bb[:, 0:L].rearrange("p (n t) -> p n t", t=128),
                        in1=cs3[:, :, 127:128].to_broadcast([128, nch, 128]), op=ALU.add), [bb, cs], [bb])
                P.op(act, lambda bsrc=bsrc: nc.scalar.activation(out=Eb[:, 0:L], in_=bsrc[:, 0:L], func=AF.Exp, scale=-1.0 / 16.0), [bsrc], [Eb])
                P.op(act, lambda bsrc=bsrc: nc.scalar.activation(out=Enb[:, 0:L], in_=bsrc[:, 0:L], func=AF.Exp, scale=1.0 / 16.0), [bsrc], [Enb])
                for a in range(2):
                    pa = slice(a * 64, (a + 1) * 64)
                    P.op(dve, lambda d=d, a=a, pa=pa: nc.vector.tensor_tensor(out=qdm[d][a][pa, 0:L], in0=qk_f[pa, 0, 0:L], in1=Eb[pa, 0:L], op=ALU.mult),
                         [qk_f, Eb], [qdm[d][a]])
                P.op(pool, lambda d=d: nc.gpsimd.tensor_tensor(out=kd[d][:, 0:L], in0=qk_f[:, 1, 0:L], in1=Enb[:, 0:L], op=ALU.mult), [qk_f, Enb], [kd[d]])
                col = 127 if d == 0 else 0
                P.op(dve, lambda d=d, col=col: nc.vector.tensor_copy(
                    out=dec[:, d, 0:nch], in_=Eb[:, 0:L].rearrange("p (n t) -> p n t", t=128)[:, :, col]), [Eb], [dec])
            SP.close()
            npc = max(1, nch // 4)
            cpp = min(4, nch)
            for pc in range(npc):
                t0 = off + pc * 512
                P.dma(sp, vst[:, 0:cpp, :], vT[t0:t0 + cpp * 128, hp * 256:(hp + 1) * 256].rearrange("(n t) c -> t n c", t=128),
                      [vT], [vst], vst)
                P.op(act, lambda pc=pc: nc.scalar.copy(out=vb[:, pc * 4:pc * 4 + cpp, :], in_=vst[:, 0:cpp, :]), [vst], [vb])
            P.op(dve, lambda: nc.vector.memset(o_all[:, 0:nch, :], 0.0), [], [o_all])
            for d in range(2):
                P.op(dve, lambda d=d: nc.vector.memset(Rf[d][:], 0.0), [], [Rf[d]])
                if si == 0:
                    for a in range(2):
                        P.dma(sp, Rf[d][a * 64:(a + 1) * 64, a * 128:(a + 1) * 128], st_in[l, d, 2 * hp + a], [st_in], [Rf[d]], Rf[d])
                P.op(act, lambda d=d: nc.scalar.copy(out=Sbf[d][0][:], in_=Rf[d][:]), [Rf[d]], [Sbf[d][0]])
            for i in range(nch):
                for d in range(2):
                    n = i if d == 0 else nch - 1 - i
                    npv = (i - 1) if d == 0 else nch - i
                    sl = slice(n * 128, (n + 1) * 128)
                    Scur = Sbf[d][i % 2]
                    Snxt = Sbf[d][(i + 1) % 2]
                    P.op(pe, lambda d=d, sl=sl: nc.tensor.transpose(psK[d][:, d, :], kd[d][:, sl], ident_b[:]), [kd[d], ident_b], [psK[d]])
                    P.op(act, lambda d=d: nc.scalar.copy(out=kdt[d][:], in_=psK[d][:, d, :]), [psK[d]], [kdt[d]])
                    for a in range(2):
                        P.op(pe, lambda d=d, a=a, sl=sl: nc.tensor.matmul(
                            psA[d][:, a, :], lhsT=kd[d][:, sl], rhs=qdm[d][a][:, sl], start=True, stop=True),
                             [kd[d], qdm[d][a]], [psA[d]])
                    P.op(pe, lambda d=d, n=n: nc.tensor.matmul(psUp[d][:], lhsT=kdt[d][:], rhs=vb[:, n, :], start=True, stop=True),
                         [kdt[d], vb], [psUp[d]])
                    P.op(dve, lambda d=d: nc.vector.tensor_tensor(
                        out=att[d][:], in0=psA[d][:], in1=masks[:, d, :].unsqueeze(1).to_broadcast([128, 2, 128]), op=ALU.mult),
                         [psA[d], masks], [att[d]])
                    if i == 0:
                        P.op(dve, lambda d=d: nc.vector.tensor_tensor(out=Rf[d][:], in0=Rf[d][:], in1=psUp[d][:], op=ALU.add),
                             [Rf[d], psUp[d]], [Rf[d]])
                    else:
                        P.op(dve, lambda d=d, npv=npv: nc.vector.scalar_tensor_tensor(
                            out=Rf[d][:], in0=Rf[d][:], scalar=dec[:, d, npv:npv + 1], in1=psUp[d][:], op0=ALU.mult, op1=ALU.add),
                             [Rf[d], psUp[d], dec], [Rf[d]])
                    if i < nch - 1:
                        P.op(act, lambda d=d, n=n, Snxt=Snxt: nc.scalar.activation(out=Snxt[:], in_=Rf[d][:], func=AF.Copy, scale=dec[:, d, n:n + 1]),
                             [Rf[d], dec], [Snxt])
                    for a in range(2):
                        P.op(pe, lambda d=d, a=a, n=n: nc.tensor.matmul(
                            psO[d][:, a * 128:(a + 1) * 128], lhsT=att[d][:, a, :], rhs=vb[:, n, a * 128:(a + 1) * 128], start=True, stop=False),
                             [att[d], vb], [psO[d]])
                        P.op(pe, lambda d=d, a=a, sl=sl, Scur=Scur: nc.tensor.matmul(
                            psO[d][:, a * 128:(a + 1) * 128], lhsT=qdm[d][a][:, sl], rhs=Scur[:, a * 128:(a + 1) * 128],
                            start=False, stop=True), [qdm[d][a], Scur], [psO[d]])
                    P.op(dve, lambda d=d, n=n: nc.vector.tensor_tensor(out=o_all[:, n, :], in0=psO[d][:], in1=o_all[:, n, :], op=ALU.add),
                         [psO[d], o_all], [o_all])
                    if i == nch - 1 and si > 0:
                        P.op(dve, lambda d=d, n=n: nc.vector.tensor_scalar_mul(out=Rf[d][:], in0=Rf[d][:], scalar1=dec[:, d, n:n + 1]),
                             [Rf[d], dec], [Rf[d]])
                        for a in range(2):
                            P.dma(pool, st_out[si - 1, l, d, 2 * hp + a], Rf[d][a * 64:(a + 1) * 64, a * 128:(a + 1) * 128], [Rf[d]], [st_out], Rf[d])
            SQ = P.scope()
            tsq = SQ.sb("tsq", [128, 8, 128]); gst = SQ.sb("gst", [128, 4, 256])
            s1 = SQ.sb("s1", [128, 8]); s2 = SQ.sb("s2", [128, 8]); rs = SQ.sb("rs", [128, 8])
            yb = SQ.sb("yb", [128, 4, 256], BF16); ysb = SQ.sb("ysb", [128, 2, 512], BF16)
            psY = SQ.ps("psY", [128, 2, 512], BF16)
            for pc in range(npc):
                t0 = off + pc * 512
                ntk = cpp * 128
                P.dma(sp, gst[:, 0:cpp, :], gT[t0:t0 + ntk, hp * 256:(hp + 1) * 256].rearrange("(n t) c -> t n c", t=128), [gT], [gst], gst)
                o3 = o_all[:, pc * 4:pc * 4 + cpp, :].rearrange("p n (a v) -> p (n a) v", a=2)
                na = cpp * 2
                P.op(dve, lambda o3=o3: nc.vector.tensor_reduce(out=s1[:, 0:na], in_=o3, axis=AX.X, op=ALU.add), [o_all], [s1])
                P.op(act, lambda o3=o3: nc.scalar.activation(out=tsq[:, 0:na, :], in_=o3, func=AF.Square), [o_all], [tsq])
                P.op(dve, lambda: nc.vector.tensor_reduce(out=s2[:, 0:na], in_=tsq[:, 0:na, :], axis=AX.X, op=ALU.add), [tsq], [s2])
                if gi == 0:
                    P.op(act, lambda: nc.scalar.activation(out=rs[:, 0:na], in_=s2[:, 0:na], func=AF.Ln, scale=1.0 / 128.0, bias=NORM_EPS), [s2], [rs])
                else:
                    P.op(dve, lambda: nc.vector.tensor_scalar_mul(out=s1[:, 0:na], in0=s1[:, 0:na], scalar1=1.0 / 128.0), [s1], [s1])
                    P.op(dve, lambda: nc.vector.tensor_tensor(out=rs[:, 0:na], in0=s1[:, 0:na], in1=s1[:, 0:na], op=ALU.mult), [s1], [rs])
                    P.op(dve, lambda: nc.vector.scalar_tensor_tensor(out=rs[:, 0:na], in0=s2[:, 0:na], scalar=1.0 / 128.0, in1=rs[:, 0:na],
                                                                     op0=ALU.mult, op1=ALU.subtract), [s2, rs], [rs])
                    P.op(act, lambda: nc.scalar.activation(out=rs[:, 0:na], in_=rs[:, 0:na], func=AF.Ln, bias=LN_EPS), [rs], [rs])
                    P.op(dve, lambda o3=o3: nc.vector.tensor_tensor(out=o3, in0=o3, in1=s1[:, 0:na].unsqueeze(2).to_broadcast([128, na, 128]),
                                                                    op=ALU.subtract), [o_all, s1], [o_all])
                P.op(act, lambda: nc.scalar.activation(out=rs[:, 0:na], in_=rs[:, 0:na], func=AF.Exp, scale=-0.5), [rs], [rs])
                P.op(dve, lambda o3=o3: nc.vector.tensor_tensor(out=o3, in0=o3, in1=rs[:, 0:na].unsqueeze(2).to_broadcast([128, na, 128]),
                                                                op=ALU.mult), [o_all, rs], [o_all])
                if gi == 0:
                    P.op(dve, lambda o3=o3: nc.vector.tensor_tensor(out=o3, in0=o3, in1=nw[:, :].unsqueeze(1).to_broadcast([128, na, 128]),
                                                                    op=ALU.mult), [o_all, nw], [o_all])
                P.op(dve, lambda pc=pc: nc.vector.tensor_tensor(out=yb[:, 0:cpp, :], in0=o_all[:, pc * 4:pc * 4 + cpp, :], in1=gst[:, 0:cpp, :],
                                                                op=ALU.mult), [o_all, gst], [yb])
                for n8 in range(cpp):
                    for a in range(2):
                        P.op(pe, lambda n8=n8, a=a: nc.tensor.transpose(psY[:, a, n8 * 128:(n8 + 1) * 128], yb[:, n8, a * 128:(a + 1) * 128],
                                                                        ident_b[:]), [yb, ident_b], [psY])
                P.op(act, lambda: nc.scalar.copy(out=ysb[:, :, 0:ntk], in_=psY[:, :, 0:ntk]), [psY], [ysb])
                for a in range(2):
                    P.dma(pool, mixT[gi * 4 + 2 * hp + a, :, t0:t0 + ntk], ysb[:, a, 0:ntk], [ysb], [mixT], ysb)
            SQ.close()
    S.close()


TWO_PI = float(2.0 * np.pi)
MAGIC = 12582912.0
CW1 = 6.28125
CW2 = float(2.0 * np.pi - 6.28125)


def mixer_s5(P, nc, l, env):
    pe, dve, act, pool, sp = P.pe, P.dve, P.act, P.pool, P.sp
    su = env["pF"]["su"]; mixT = env["mixT"]; ident_f = env["ident_f"]
    E = env
    S = P.scope()
    tp1 = S.sb("tp1", [128, LS])
    bt = [[S.sb(f"bt{i}{j}", [128, 512]) for j in range(6)] for i in range(1)]
    zb = [[S.sb(f"zb{d}{c}", [128, LS], BF16) for c in range(2)] for d in range(2)]
    ust = [S.sb("ust0", [32, LS])] * 2; ugp = [S.sb(f"ugp{i}", [128, LS], BF16) for i in range(2)]
    BTt = [S.sb(f"BTt{d}", [128, 2, 128], BF16) for d in range(2)]
    CPt = [S.sb(f"CPt{d}", [128, 2, 128], BF16) for d in range(2)]
    zz = S.sb("zz", [128, 4, LS], BF16)
    gw = S.sb("gw", [128, 4, 512], BF16)
    pv = {n: S.sb("pv_" + n, [128, 32]) for n in ("are", "aim", "st", "r", "th", "s", "c", "kre", "kim", "t0", "t1", "t2",
                                                   "h0re", "h0im", "hfre", "hfim", "zero", "ph")}
    dvec = S.sb("dvec", [128, 4]); gbv = S.sb("gbv", [128, 4])
    lastc = S.sb("lastc", [128, 32, 4])
    psB = [[S.ps(f"psB{i}{c}", [128, 512]) for c in range(2)] for i in range(2)]
    psY = [S.ps(f"psY{i}", [128, 512]) for i in range(4)]
    s5BT = env["s5BT"]; s5CP = env["s5CP"]

    def V(name, fn, reads, writes):
        P.op(dve, fn, reads, writes)

    P.dma(sp, tp1[:], E["tp1_in"][:].partition_broadcast(128), [E["tp1_in"]], [tp1], tp1)
    for i_ in range(2):
        P.op(pool, lambda i_=i_: nc.gpsimd.memset(ugp[i_][:], 0.0), [], [ugp[i_]])
    P.op(dve, lambda: nc.vector.memset(pv["zero"][:], 0.0), [], [pv["zero"]])
    SP = P.scope()
    BT = SP.sb("BT", [128, 32, 2, 128], BF16)
    Cpad = SP.sb("Cpad", [128, 32, 2, 128], BF16)
    P.op(pool, lambda: nc.gpsimd.memset(BT[:], 0.0), [], [BT])
    P.op(pool, lambda: nc.gpsimd.memset(Cpad[:], 0.0), [], [Cpad])
    Bc = [SP.sb(f"Bc{c}", [128, 32, 16]) for c in range(2)]
    Bb = [SP.sb(f"Bb{c}", [128, 32, 16]) for c in range(2)]
    Bsm = SP.sb("Bsm", [128, 32, 2, 32])
    Cc = [SP.sb(f"Cc{c}", [128, 32, 16]) for c in range(2)]
    tB = SP.sb("tB", [128, 32, 16])
    gwf = SP.sb("gwf", [128, 4, 512])
    with nc.allow_non_contiguous_dma(reason="small parameter layout transforms"):
        for g2 in range(2):
            pa = slice(g2 * 64, (g2 + 1) * 64)
            for d in range(2):
                ds_ = slice(d * 16, (d + 1) * 16)
                P.dma(sp, pv["are"][pa, ds_], E["s5_a_re"][l, d, g2::2].rearrange("gp p -> p gp"), [E["s5_a_re"]], [pv["are"]], pv["are"])
                P.dma(sp, pv["aim"][pa, ds_], E["s5_a_im"][l, d, g2::2].rearrange("gp p -> p gp"), [E["s5_a_im"]], [pv["aim"]], pv["aim"])
                P.dma(sp, pv["st"][pa, ds_], E["s5_log_step"][l, d, g2::2].partition_broadcast(64), [E["s5_log_step"]], [pv["st"]], pv["st"])
                P.dma(sp, pv["h0re"][pa, ds_], E["s5re_in"][l, d, g2::2].rearrange("gp p -> p gp"), [E["s5re_in"]], [pv["h0re"]], pv["h0re"])
                P.dma(sp, pv["h0im"][pa, ds_], E["s5im_in"][l, d, g2::2].rearrange("gp p -> p gp"), [E["s5im_in"]], [pv["h0im"]], pv["h0im"])
                P.dma(sp, Bc[0][pa, ds_, :], E["s5_b_re"][l, d, g2::2].rearrange("gp p i -> p gp i"), [E["s5_b_re"]], [Bc[0]], Bc[0])
                P.dma(sp, Bc[1][pa, ds_, :], E["s5_b_im"][l, d, g2::2].rearrange("gp p i -> p gp i"), [E["s5_b_im"]], [Bc[1]], Bc[1])
                for gp_ in range(16):
                    P.dma(sp, Cc[0][pa, d * 16 + gp_, :], E["s5_c_re"][l, d, 2 * gp_ + g2].rearrange("o p -> p o"), [E["s5_c_re"]], [Cc[0]], Cc[0])
                    P.dma(sp, Cc[1][pa, d * 16 + gp_, :], E["s5_c_im"][l, d, 2 * gp_ + g2].rearrange("o p -> p o"), [E["s5_c_im"]], [Cc[1]], Cc[1])
        P.dma(sp, dvec[:], E["s5_d"][l].rearrange("(c p) -> p c", p=128), [E["s5_d"]], [dvec], dvec)
        P.dma(sp, gbv[:], E["s5_glu_b"][l].rearrange("(c p) -> p c", p=128), [E["s5_glu_b"]], [gbv], gbv)
    P.dma(sp, gwf[:], E["s5_glu_w"][l].rearrange("(c p) n -> p c n", p=128), [E["s5_glu_w"]], [gwf], gwf)
    P.op(act, lambda: nc.scalar.copy(out=gw[:], in_=gwf[:]), [gwf], [gw])
    a = pv
    P.op(act, lambda: nc.scalar.activation(out=a["st"][:], in_=a["st"][:], func=AF.Exp), [a["st"]], [a["st"]])
    V("ar", lambda: nc.vector.tensor_tensor(out=a["r"][:], in0=a["are"][:], in1=a["st"][:], op=ALU.mult), [a["are"], a["st"]], [a["r"]])
    P.op(act, lambda: nc.scalar.activation(out=a["r"][:], in_=a["r"][:], func=AF.Exp), [a["r"]], [a["r"]])
    V("th", lambda: nc.vector.tensor_tensor(out=a["th"][:], in0=a["aim"][:], in1=a["st"][:], op=ALU.mult), [a["aim"], a["st"]], [a["th"]])

    def range_reduce(y, k, n, Y, K):
        V("rr1", lambda: nc.vector.tensor_scalar(out=k[:, 0:n], in0=y[:, 0:n], scalar1=1.0 / TWO_PI, scalar2=MAGIC, op0=ALU.mult, op1=ALU.add), [Y], [K])
        V("rr2", lambda: nc.vector.tensor_scalar_add(out=k[:, 0:n], in0=k[:, 0:n], scalar1=-MAGIC), [K], [K])
        V("rr3", lambda: nc.vector.scalar_tensor_tensor(out=y[:, 0:n], in0=k[:, 0:n], scalar=-CW1, in1=y[:, 0:n], op0=ALU.mult, op1=ALU.add), [K, Y], [Y])
        V("rr4", lambda: nc.vector.scalar_tensor_tensor(out=y[:, 0:n], in0=k[:, 0:n], scalar=-CW2, in1=y[:, 0:n], op0=ALU.mult, op1=ALU.add), [K, Y], [Y])

    def sincos(y, n, sn, cs, Y, SN, CS):
        P.op(act, lambda: nc.scalar.activation(out=sn[:, 0:n], in_=y[:, 0:n], func=AF.Sin, scale=0.999998), [Y], [SN])
        P.op(act, lambda: nc.scalar.activation(out=cs[:, 0:n], in_=y[:, 0:n], func=AF.Sin, scale=0.5), [Y], [CS])
        P.op(act, lambda: nc.scalar.activation(out=cs[:, 0:n], in_=cs[:, 0:n], func=AF.Square), [CS], [CS])
        P.op(pool, lambda: nc.gpsimd.tensor_scalar(out=cs[:, 0:n], in0=cs[:, 0:n], scalar1=-2.0, scalar2=1.0, op0=ALU.mult, op1=ALU.add), [CS], [CS])

    range_reduce(a["th"], a["t0"], 32, a["th"], a["t0"])
    sincos(a["th"], 32, a["s"], a["c"], a["th"], a["s"], a["c"])
    V("lre", lambda: nc.vector.tensor_tensor(out=a["t0"][:], in0=a["r"][:], in1=a["c"][:], op=ALU.mult), [a["r"], a["c"]], [a["t0"]])
    V("lre1", lambda: nc.vector.tensor_scalar_add(out=a["t0"][:], in0=a["t0"][:], scalar1=-1.0), [a["t0"]], [a["t0"]])
    V("lim", lambda: nc.vector.tensor_tensor(out=a["t1"][:], in0=a["r"][:], in1=a["s"][:], op=ALU.mult), [a["r"], a["s"]], [a["t1"]])
    V("den", lambda: nc.vector.tensor_tensor(out=a["t2"][:], in0=a["are"][:], in1=a["are"][:], op=ALU.mult), [a["are"]], [a["t2"]])
    V("den2", lambda: nc.vector.tensor_tensor(out=a["kre"][:], in0=a["aim"][:], in1=a["aim"][:], op=ALU.mult), [a["aim"]], [a["kre"]])
    V("den3", lambda: nc.vector.tensor_tensor(out=a["t2"][:], in0=a["t2"][:], in1=a["kre"][:], op=ALU.add), [a["t2"], a["kre"]], [a["t2"]])
    V("rden", lambda: nc.vector.reciprocal(out=a["t2"][:], in_=a["t2"][:]), [a["t2"]], [a["t2"]])
    V("k1", lambda: nc.vector.tensor_tensor(out=a["kre"][:], in0=a["t0"][:], in1=a["are"][:], op=ALU.mult), [a["t0"], a["are"]], [a["kre"]])
    V("k2", lambda: nc.vector.tensor_tensor(out=a["kim"][:], in0=a["t1"][:], in1=a["aim"][:], op=ALU.mult), [a["t1"], a["aim"]], [a["kim"]])
    V("k3", lambda: nc.vector.tensor_tensor(out=a["kre"][:], in0=a["kre"][:], in1=a["kim"][:], op=ALU.add), [a["kre"], a["kim"]], [a["kre"]])
    V("k4", lambda: nc.vector.tensor_tensor(out=a["kim"][:], in0=a["t1"][:], in1=a["are"][:], op=ALU.mult), [a["t1"], a["are"]], [a["kim"]])
    V("k5", lambda: nc.vector.tensor_tensor(out=a["t1"][:], in0=a["t0"][:], in1=a["aim"][:], op=ALU.mult), [a["t0"], a["aim"]], [a["t1"]])
    V("k6", lambda: nc.vector.tensor_tensor(out=a["kim"][:], in0=a["kim"][:], in1=a["t1"][:], op=ALU.subtract), [a["kim"], a["t1"]], [a["kim"]])
    V("k7", lambda: nc.vector.tensor_tensor(out=a["kre"][:], in0=a["kre"][:], in1=a["t2"][:], op=ALU.mult), [a["kre"], a["t2"]], [a["kre"]])
    V("k8", lambda: nc.vector.tensor_tensor(out=a["kim"][:], in0=a["kim"][:], in1=a["t2"][:], op=ALU.mult), [a["kim"], a["t2"]], [a["kim"]])
    kre_b = a["kre"][:, :].unsqueeze(2).to_broadcast([128, 32, 16])
    kim_b = a["kim"][:, :].unsqueeze(2).to_broadcast([128, 32, 16])
    V("b1", lambda: nc.vector.tensor_tensor(out=Bb[0][:], in0=Bc[0][:], in1=kre_b, op=ALU.mult), [Bc[0], a["kre"]], [Bb[0]])
    V("b2", lambda: nc.vector.tensor_tensor(out=tB[:], in0=Bc[1][:], in1=kim_b, op=ALU.mult), [Bc[1], a["kim"]], [tB])
    V("b3", lambda: nc.vector.tensor_tensor(out=Bb[0][:], in0=Bb[0][:], in1=tB[:], op=ALU.subtract), [Bb[0], tB], [Bb[0]])
    V("b4", lambda: nc.vector.tensor_tensor(out=Bb[1][:], in0=Bc[1][:], in1=kre_b, op=ALU.mult), [Bc[1], a["kre"]], [Bb[1]])
    V("b5", lambda: nc.vector.tensor_tensor(out=tB[:], in0=Bc[0][:], in1=kim_b, op=ALU.mult), [Bc[0], a["kim"]], [tB])
    V("b6", lambda: nc.vector.tensor_tensor(out=Bb[1][:], in0=Bb[1][:], in1=tB[:], op=ALU.add), [Bb[1], tB], [Bb[1]])
    V("bsm0", lambda: nc.vector.memset(Bsm[:], 0.0), [], [Bsm])
    for g2 in range(2):
        pa = slice(g2 * 64, (g2 + 1) * 64)
        for c in range(2):
            V("bsm", lambda pa=pa, c=c, g2=g2: nc.vector.tensor_copy(out=Bsm[pa, :, c, g2 * 16:(g2 + 1) * 16], in_=Bb[c][pa, :, :]), [Bb[c]], [Bsm])
    for j in range(32):
        for c in range(2):
            ps = psB[(j * 2 + c) % 2][0]
            P.op(pe, lambda ps=ps, j=j, c=c: nc.tensor.transpose(ps[0:32, 0:128], Bsm[:, j, c, :], ident_f[:]), [Bsm, ident_f], [ps])
            P.op(act, lambda ps=ps, j=j, c=c: nc.scalar.copy(out=BT[0:32, j, c, :], in_=ps[0:32, 0:128]), [ps], [BT])
    for c in range(2):
        if c == 1:
            V("cneg", lambda: nc.vector.tensor_scalar_mul(out=Cc[1][:], in0=Cc[1][:], scalar1=-1.0), [Cc[1]], [Cc[1]])
        for g2 in range(2):
            pa = slice(g2 * 64, (g2 + 1) * 64)
            for q in range(4):
                src = Cc[c][pa, :, :].rearrange("p (dg q) o -> p dg q o", q=4)[:, :, q, :]
                dst = Cpad[pa, :, c, q * 32 + g2 * 16:q * 32 + g2 * 16 + 16].rearrange("p (dg q) o -> p dg q o", q=4)[:, :, q, :]
                V("cpad", lambda src=src, dst=dst: nc.vector.tensor_copy(out=dst, in_=src), [Cc[c]], [Cpad])
    P.dma(pool, s5BT[:].rearrange("j p c m -> p j c m"), BT[:], [BT], [s5BT], BT)
    P.dma(pool, s5CP[:].rearrange("j p c m -> p j c m"), Cpad[:], [Cpad], [s5CP], Cpad)
    V("ph", lambda: nc.vector.tensor_scalar_mul(out=a["ph"][:], in0=a["th"][:], scalar1=1.0 / TWO_PI), [a["th"]], [a["ph"]])
    SP.close()
    SM = P.scope()
    CS = [[SM.sb(f"cs{d}{i}", [128, LS]) for i in range(2)] for d in range(2)]
    WD = [[SM.sb(f"wd{d}{i}", [128, LS]) for i in range(2)] for d in range(2)]
    MM = [[SM.sb(f"mm{d}{i}", [128, LS]) for i in range(2)] for d in range(2)]
    PP = [[SM.sb(f"pp{d}{i}", [128, LS], BF16) for i in range(2)] for d in range(2)]
    PQ = [[SM.sb(f"pq{d}{i}", [128, LS], BF16) for i in range(2)] for d in range(2)]
    W = [WD[0][0], WD[0][1], MM[0][0], MM[0][1]]

    for si, (off, L) in enumerate(SEQS):
        nb = max(1, L // 512)
        bw = min(512, L)
        h0r = (a["h0re"] if si == 0 else a["zero"]); h0i = (a["h0im"] if si == 0 else a["zero"])

        def stage_T(cc, gq, d):
            gp = cc * 4 + gq
            j = d * 16 + gp
            cosA, sinA = CS[d]
            ang, kk = WD[d]
            if d == 0:
                P.dma(sp, ust[gq % 2][:, 0:L], su[cc, gq * 32:(gq + 1) * 32, off:off + L], [su], [ust[gq % 2]], ust[gq % 2])
                P.op(act, lambda: nc.scalar.copy(out=ugp[gq % 2][0:32, 0:L], in_=ust[gq % 2][:, 0:L]), [ust[gq % 2]], [ugp[gq % 2]])
            P.dma(sp, BTt[d][:], s5BT[j], [s5BT], [BTt[d]], BTt[d])
            P.dma(sp, CPt[d][:], s5CP[j], [s5CP], [CPt[d]], CPt[d])
            P.op(act, lambda: nc.scalar.activation(out=kk[:, 0:L], in_=tp1[:, 0:L], func=AF.Identity, scale=a["ph"][:, j:j + 1], bias=MAGIC), [tp1, a["ph"]], [kk])
            P.op(act, lambda: nc.scalar.activation(out=kk[:, 0:L], in_=kk[:, 0:L], func=AF.Identity, bias=-MAGIC), [kk], [kk])
            V("fr", lambda: nc.vector.scalar_tensor_tensor(out=ang[:, 0:L], in0=tp1[:, 0:L], scalar=a["ph"][:, j:j + 1], in1=kk[:, 0:L],
                                                           op0=ALU.mult, op1=ALU.subtract), [tp1, a["ph"], kk], [ang])
            P.op(act, lambda: nc.scalar.activation(out=sinA[:, 0:L], in_=ang[:, 0:L], func=AF.Sin, scale=TWO_PI * 0.999998), [ang], [sinA])
            P.op(act, lambda: nc.scalar.activation(out=cosA[:, 0:L], in_=ang[:, 0:L], func=AF.Sin, scale=TWO_PI * 0.5), [ang], [cosA])
            P.op(act, lambda: nc.scalar.activation(out=cosA[:, 0:L], in_=cosA[:, 0:L], func=AF.Square), [cosA], [cosA])
            P.op(act, lambda: nc.scalar.activation(out=cosA[:, 0:L], in_=cosA[:, 0:L], func=AF.Identity, scale=-2.0, bias=1.0), [cosA], [cosA])

        def stage_GS(cc, gq, d):
            gp = cc * 4 + gq
            j = d * 16 + gp
            cosA, sinA = CS[d]
            gre, gim = WD[d]
            mre, mim = MM[d]
            ug = ugp[gq % 2]
            for tb in range(nb):
                t0 = tb * bw
                pb = psB[tb % 2]
                e = bt[0]
                for c in range(2):
                    P.op(pe, lambda c=c, t0=t0, pb=pb: nc.tensor.matmul(pb[c][:, 0:bw], lhsT=BTt[d][:, c, :], rhs=ug[:, t0:t0 + bw],
                                                                        start=True, stop=True), [BTt[d], ug], [pb[c]])
                if d == 0:
                    sl = lambda X, t0=t0: X[:, t0:t0 + bw]
                else:
                    sl = lambda X, t0=t0: X[:, L - t0 - bw:L - t0][:, ::-1]
                P.op(act, lambda pb=pb, e=e: nc.scalar.copy(out=e[0][:, 0:bw], in_=pb[0][:, 0:bw]), [pb[0]], [e[0]])
                P.op(act, lambda pb=pb, e=e: nc.scalar.copy(out=e[1][:, 0:bw], in_=pb[1][:, 0:bw]), [pb[1]], [e[1]])
                V("g1", lambda sl=sl, e=e: nc.vector.tensor_tensor(out=e[2][:, 0:bw], in0=e[0][:, 0:bw], in1=sl(cosA), op=ALU.mult), [e[0], cosA], [e[2]])
                V("g2", lambda sl=sl, e=e: nc.vector.tensor_tensor(out=e[3][:, 0:bw], in0=e[1][:, 0:bw], in1=sl(sinA), op=ALU.mult), [e[1], sinA], [e[3]])
                P.op(pool, lambda sl=sl, e=e: nc.gpsimd.tensor_tensor(out=e[4][:, 0:bw], in0=e[1][:, 0:bw], in1=sl(cosA), op=ALU.mult), [e[1], cosA], [e[4]])
                P.op(pool, lambda sl=sl, e=e: nc.gpsimd.tensor_tensor(out=e[5][:, 0:bw], in0=e[0][:, 0:bw], in1=sl(sinA), op=ALU.mult), [e[0], sinA], [e[5]])
                V("g5", lambda sl=sl, e=e: nc.vector.tensor_tensor(out=sl(gre), in0=e[2][:, 0:bw], in1=e[3][:, 0:bw], op=ALU.add), [e[2], e[3]], [gre])
                P.op(pool, lambda sl=sl, e=e: nc.gpsimd.tensor_tensor(out=sl(gim), in0=e[4][:, 0:bw], in1=e[5][:, 0:bw], op=ALU.subtract), [e[4], e[5]], [gim])
            rb = a["r"][:, j:j + 1].to_broadcast([128, L])
            V("scr", lambda: nc.vector.tensor_tensor_scan(out=mre[:, 0:L], data0=rb, data1=gre[:, 0:L], initial=h0r[:, j:j + 1], op0=ALU.mult, op1=ALU.add),
              [gre, a["r"], h0r], [mre])
            V("sci", lambda: nc.vector.tensor_tensor_scan(out=mim[:, 0:L], data0=rb, data1=gim[:, 0:L], initial=h0i[:, j:j + 1], op0=ALU.mult, op1=ALU.add),
              [gim, a["r"], h0i], [mim])
            if si > 0:
                for ci, X in enumerate((cosA, sinA, mre, mim)):
                    P.op(act, lambda ci=ci, X=X: nc.scalar.copy(out=lastc[:, j, ci:ci + 1], in_=X[:, L - 1:L]), [X], [lastc])

        def stage_Z(cc, gq, d):
            cosA, sinA = CS[d]
            mre, mim = MM[d]
            p1, p2 = PP[d]
            p3, p4 = PQ[d]
            zo = (lambda X: X[:, 0:L]) if d == 0 else (lambda X: X[:, 0:L][:, ::-1])
            V("p1", lambda: nc.vector.tensor_tensor(out=p1[:, 0:L], in0=cosA[:, 0:L], in1=mre[:, 0:L], op=ALU.mult), [cosA, mre], [p1])
            V("p2", lambda: nc.vector.tensor_tensor(out=p2[:, 0:L], in0=sinA[:, 0:L], in1=mim[:, 0:L], op=ALU.mult), [sinA, mim], [p2])
            P.op(pool, lambda: nc.gpsimd.tensor_tensor(out=p3[:, 0:L], in0=sinA[:, 0:L], in1=mre[:, 0:L], op=ALU.mult), [sinA, mre], [p3])
            P.op(pool, lambda: nc.gpsimd.tensor_tensor(out=p4[:, 0:L], in0=cosA[:, 0:L], in1=mim[:, 0:L], op=ALU.mult), [cosA, mim], [p4])
            V("zre", lambda: nc.vector.tensor_tensor(out=zo(zb[d][0]), in0=p1[:, 0:L], in1=p2[:, 0:L], op=ALU.subtract), [p1, p2], [zb[d][0]])
            P.op(pool, lambda: nc.gpsimd.tensor_tensor(out=zo(zb[d][1]), in0=p3[:, 0:L], in1=p4[:, 0:L], op=ALU.add), [p3, p4], [zb[d][1]])
            for tb in range(nb):
                t0 = tb * bw
                for c in range(2):
                    first = (gq == 0 and d == 0 and c == 0)
                    last = (gq == 3 and d == 1 and c == 1)
                    P.op(pe, lambda tb=tb, t0=t0, c=c, first=first, last=last: nc.tensor.matmul(
                        psY[tb][:, 0:bw], lhsT=CPt[d][:, c, :], rhs=zb[d][c][:, t0:t0 + bw], start=first, stop=last), [CPt[d], zb[d][c]], [psY[tb]])

        for cc in range(4):
            its = [(gq, d) for gq in range(4) for d in range(2)]
            stage_T(cc, *its[0])
            for ii, (gq, d) in enumerate(its):
                stage_GS(cc, gq, d)
                if ii + 1 < len(its):
                    stage_T(cc, *its[ii + 1])
                stage_Z(cc, gq, d)
            uT, yv, w1, w2 = W[0], W[1], W[2], W[3]
            P.dma(sp, uT[:, 0:L], su[cc, :, off:off + L], [su], [uT], uT)
            for tb in range(nb):
                t0 = tb * bw
                V("yv", lambda tb=tb, t0=t0: nc.vector.scalar_tensor_tensor(out=yv[:, t0:t0 + bw], in0=uT[:, t0:t0 + bw], scalar=dvec[:, cc:cc + 1],
                                                                          in1=psY[tb][:, 0:bw], op0=ALU.mult, op1=ALU.add), [uT, dvec, psY[tb]], [yv])
            P.op(act, lambda: nc.scalar.activation(out=w1[:, 0:L], in_=yv[:, 0:L], func=AF.Square), [yv], [w1])
            P.op(pool, lambda: nc.gpsimd.tensor_scalar(out=w1[:, 0:L], in0=w1[:, 0:L], scalar1=0.044715, scalar2=1.0, op0=ALU.mult, op1=ALU.add), [w1], [w1])
            P.op(pool, lambda: nc.gpsimd.tensor_tensor(out=w1[:, 0:L], in0=w1[:, 0:L], in1=yv[:, 0:L], op=ALU.mult), [w1, yv], [w1])
            P.op(act, lambda: nc.scalar.activation(out=w2[:, 0:L], in_=w1[:, 0:L], func=AF.Sigmoid, scale=float(2.0 * np.sqrt(2.0 / np.pi))), [w1], [w2])
            V("zz", lambda: nc.vector.tensor_tensor(out=zz[:, cc, 0:L], in0=yv[:, 0:L], in1=w2[:, 0:L], op=ALU.mult), [yv, w2], [zz])
        for oc in range(4):
            for tb in range(nb):
                t0 = tb * bw
                ps = psB[tb % 2][0]
                for cc in range(4):
                    P.op(pe, lambda ps=ps, cc=cc, oc=oc, t0=t0: nc.tensor.matmul(ps[:, 0:bw], lhsT=gw[:, cc, oc * 128:(oc + 1) * 128], rhs=zz[:, cc, t0:t0 + bw],
                                                                               start=(cc == 0), stop=(cc == 3)), [gw, zz], [ps])
                sg = bt[0][tb % 2]
                P.op(act, lambda ps=ps, sg=sg, oc=oc: nc.scalar.activation(out=sg[:, 0:bw], in_=ps[:, 0:bw], func=AF.Sigmoid, bias=gbv[:, oc:oc + 1]), [ps, gbv], [sg])
                ot = bt[0][2 + tb % 2]
                V("glu", lambda sg=sg, ot=ot, oc=oc, t0=t0: nc.vector.tensor_tensor(out=ot[:, 0:bw].bitcast(BF16)[:, 0:bw], in0=zz[:, oc, t0:t0 + bw], in1=sg[:, 0:bw], op=ALU.mult),
                  [zz, sg], [ot])
                P.dma(pool, mixT[8 + oc, :, off + t0:off + t0 + bw], ot[:, 0:bw].bitcast(BF16)[:, 0:bw], [ot], [mixT], ot)
        if si > 0:
            lc = lastc
            h = a
            V("f1", lambda: nc.vector.tensor_tensor(out=h["t0"][:], in0=lc[:, :, 0], in1=lc[:, :, 2], op=ALU.mult), [lc], [h["t0"]])
            V("f2", lambda: nc.vector.tensor_tensor(out=h["t1"][:], in0=lc[:, :, 1], in1=lc[:, :, 3], op=ALU.mult), [lc], [h["t1"]])
            V("f3", lambda: nc.vector.tensor_tensor(out=h["hfre"][:], in0=h["t0"][:], in1=h["t1"][:], op=ALU.subtract), [h["t0"], h["t1"]], [h["hfre"]])
            V("f4", lambda: nc.vector.tensor_tensor(out=h["t0"][:], in0=lc[:, :, 1], in1=lc[:, :, 2], op=ALU.mult), [lc], [h["t0"]])
            V("f5", lambda: nc.vector.tensor_tensor(out=h["t1"][:], in0=lc[:, :, 0], in1=lc[:, :, 3], op=ALU.mult), [lc], [h["t1"]])
            V("f6", lambda: nc.vector.tensor_tensor(out=h["hfim"][:], in0=h["t0"][:], in1=h["t1"][:], op=ALU.add), [h["t0"], h["t1"]], [h["hfim"]])
            with nc.allow_non_contiguous_dma(reason="state layout"):
                for g2 in range(2):
                    pa = slice(g2 * 64, (g2 + 1) * 64)
                    for d in range(2):
                        ds_ = slice(d * 16, (d + 1) * 16)
                        P.dma(sp, E["ns5re"][si - 1, l, d, g2::2].rearrange("gp p -> p gp"), h["hfre"][pa, ds_], [h["hfre"]], [E["ns5re"]], h["hfre"])
                        P.dma(sp, E["ns5im"][si - 1, l, d, g2::2].rearrange("gp p -> p gp"), h["hfim"][pa, ds_], [h["hfim"]], [E["ns5im"]], h["hfim"])
    SM.close()
    S.close()


def mixer_hy(P, nc, l, env):
    pe, dve, act, pool, sp = P.pe, P.dve, P.act, P.pool, P.sp
    E = env
    hy = E["pF"]["hy"]; mixT = E["mixT"]; ident_b = E["ident_b"]; ident_f = E["ident_f"]
    S = P.scope()
    cw = S.sb("cw", [128, 3, 12]); cb = S.sb("cb", [128, 12]); dvv = S.sb("dvv", [128, 2, 4])
    with nc.allow_non_contiguous_dma(reason="small parameter layout transforms"):
        for k in range(3):
            P.dma(sp, cw[:, k, :], E["hy_conv_w"][l, k].rearrange("(c p) -> p c", p=128), [E["hy_conv_w"]], [cw], cw)
        P.dma(sp, cb[:], E["hy_conv_b"][l].rearrange("(c p) -> p c", p=128), [E["hy_conv_b"]], [cb], cb)
        for o in range(2):
            P.dma(sp, dvv[:, o, :], E["hy_d"][l, o].rearrange("(c p) -> p c", p=128), [E["hy_d"]], [dvv], dvv)

    def V(fn, reads, writes):
        P.op(dve, fn, reads, writes)

    def fwd_dft(cfg, src, emit, bufs):
        tC, tS, psC, psS = bufs
        for fk in range(cfg["nfk"]):
            c_, s_ = tC[fk % 2], tS[fk % 2]
            P.dma(sp, c_[:, 0:cfg["ntc"], :], cfg["tFC"][fk], [cfg["tFC"]], [c_], c_)
            P.dma(sp, s_[:, 0:cfg["ntc"], :], cfg["tFS"][fk], [cfg["tFS"]], [s_], s_)
            pc, ps_ = psC[fk % 2], psS[fk % 2]
            for tc in range(cfg["ntc"]):
                P.op(pe, lambda c_=c_, pc=pc, tc=tc: nc.tensor.matmul(pc[:], lhsT=c_[:, tc, :], rhs=src[:, tc, :], start=(tc == 0), stop=(tc == cfg["ntc"] - 1)),
                     [c_, src], [pc])
            for tc in range(cfg["ntc"]):
                P.op(pe, lambda s_=s_, ps_=ps_, tc=tc: nc.tensor.matmul(ps_[:], lhsT=s_[:, tc, :], rhs=src[:, tc, :], start=(tc == 0), stop=(tc == cfg["ntc"] - 1)),
                     [s_, src], [ps_])
            emit(fk, pc, ps_)

    cfgs = {}
    for L in (LS, LP):
        sfx = str(L)
        cfgs[L] = dict(L=L, ntc=L // 128, nfk=L // 128 + 1, nblk=max(1, L // 512), bw=min(512, L),
                       tFC=E["tFC" + sfx], tFS=E["tFS" + sfx], tIC=E["tIC" + sfx], tIS=E["tIS" + sfx],
                       featsT=E["featsT" + sfx], featsTr=E["featsTr" + sfx], negtv=E["negtv" + sfx], negtvr=E["negtvr" + sfx],
                       wk=E["wk" + sfx], sgw=E["sgw" + sfx], Fs=E["Fs" + sfx])

    for L in (LS, LP):
        cfg = cfgs[L]
        ntc, nfk = cfg["ntc"], cfg["nfk"]
        SF = P.scope()
        w1p = SF.sb("w1p", [64, 64]); w2 = SF.sb("w2", [64, 64]); w3 = SF.sb("w3", [64, 2048])
        fq = SF.sb("fq", [64, 1]); fb1 = SF.sb("fb1", [64, 1]); fb2 = SF.sb("fb2", [64, 1])
        ft = SF.sb("ft", [64, LS]); h1 = SF.sb("h1", [64, LS]); h2 = SF.sb("h2", [64, LS]); kk = SF.sb("kkf", [64, LS])
        adec = SF.sb("adec", [128, 512]); ew = SF.sb("ew", [128, 512]); fa = SF.sb("fa", [128, 512])
        fw = SF.sb("fw", [128, 16, 512])
        ff = [SF.sb(f"ff{d}", [128, 16, 512], BF16) for d in range(2)]
        fpm = [SF.sb(f"fpm{d}", [128, 16, 512], BF16) for d in range(2)]
        ntv = SF.sb("ntv", [128, 2, 16]); wkv = SF.sb("wkv", [128, 17]); sgv = SF.sb("sgv", [128, 17])
        ones = SF.sb("ones", [128, 128]); rcp = SF.sb("rcp", [128, 512])
        tC = [SF.sb(f"tC{i}", [128, 16, 128], BF16) for i in range(2)]
        tS = [SF.sb(f"tS{i}", [128, 16, 128], BF16) for i in range(2)]
        fo = [SF.sb(f"fo{i}", [128, 2, 512]) for i in range(2)]
        tmpc = SF.sb("tmpc", [128, 2, 512])
        psM = SF.ps("psM", [128, 512])
        psN = SF.ps("psN", [128, 512])
        psCf = [SF.ps(f"psCf{i}", [128, 512]) for i in range(2)]
        psSf = [SF.ps(f"psSf{i}", [128, 512]) for i in range(2)]
        psCr = SF.ps("psCr", [128, 512]); psSr = SF.ps("psSr", [128, 512])
        V(lambda: nc.vector.memset(w1p[:], 0.0), [], [w1p])
        V(lambda: nc.vector.memset(ones[:], 1.0), [], [ones])
        P.dma(sp, w1p[0:33, :], E["hy_f_w1"][l], [E["hy_f_w1"]], [w1p], w1p)
        P.dma(sp, w2[:], E["hy_f_w2"][l], [E["hy_f_w2"]], [w2], w2)
        P.dma(sp, w3[:], E["hy_f_w3"][l], [E["hy_f_w3"]], [w3], w3)
        with nc.allow_non_contiguous_dma(reason="tiny"):
            P.dma(sp, fq[:], E["hy_f_freq"][l].rearrange("(p o) -> p o", o=1), [E["hy_f_freq"]], [fq], fq)
            P.dma(sp, fb1[:], E["hy_f_b1"][l].rearrange("(p o) -> p o", o=1), [E["hy_f_b1"]], [fb1], fb1)
            P.dma(sp, fb2[:], E["hy_f_b2"][l].rearrange("(p o) -> p o", o=1), [E["hy_f_b2"]], [fb2], fb2)
        P.dma(sp, ntv[:, 0, 0:ntc], cfg["negtv"][:, :], [cfg["negtv"]], [ntv], ntv)
        P.dma(sp, ntv[:, 1, 0:ntc], cfg["negtvr"][:, :], [cfg["negtvr"]], [ntv], ntv)
        P.dma(sp, wkv[:, 0:nfk], cfg["wk"][:, :], [cfg["wk"]], [wkv], wkv)
        P.dma(sp, sgv[:, 0:nfk], cfg["sgw"][:, :], [cfg["sgw"]], [sgv], sgv)
        V(lambda: nc.vector.tensor_tensor(out=fb1[:], in0=fb1[:], in1=fq[:], op=ALU.mult), [fb1, fq], [fb1])
        V(lambda: nc.vector.tensor_tensor(out=fb2[:], in0=fb2[:], in1=fq[:], op=ALU.mult), [fb2, fq], [fb2])

        def rr_sin(y, n):
            V(lambda: nc.vector.tensor_scalar(out=kk[:, 0:n], in0=y[:, 0:n], scalar1=1.0 / TWO_PI, scalar2=MAGIC, op0=ALU.mult, op1=ALU.add), [y], [kk])
            V(lambda: nc.vector.tensor_scalar_add(out=kk[:, 0:n], in0=kk[:, 0:n], scalar1=-MAGIC), [kk], [kk])
            V(lambda: nc.vector.scalar_tensor_tensor(out=y[:, 0:n], in0=kk[:, 0:n], scalar=-CW1, in1=y[:, 0:n], op0=ALU.mult, op1=ALU.add), [kk, y], [y])
            V(lambda: nc.vector.scalar_tensor_tensor(out=y[:, 0:n], in0=kk[:, 0:n], scalar=-CW2, in1=y[:, 0:n], op0=ALU.mult, op1=ALU.add), [kk, y], [y])
            P.op(act, lambda: nc.scalar.activation(out=y[:, 0:n], in_=y[:, 0:n], func=AF.Sin, scale=0.999998), [y], [y])

        for d in range(2):
            P.dma(sp, ft[:, 0:L], (cfg["featsT"] if d == 0 else cfg["featsTr"])[:, :], [cfg["featsT"], cfg["featsTr"]], [ft], ft)
            bw = cfg["bw"]
            for tb in range(cfg["nblk"]):
                t0 = tb * bw
                P.op(pe, lambda t0=t0: nc.tensor.matmul(psM[0:64, 0:bw], lhsT=w1p[:], rhs=ft[:, t0:t0 + bw], start=True, stop=True), [w1p, ft], [psM])
                V(lambda t0=t0: nc.vector.tensor_scalar(out=h1[:, t0:t0 + bw], in0=psM[0:64, 0:bw], scalar1=fq[:, 0:1], scalar2=fb1[:, 0:1],
                                                        op0=ALU.mult, op1=ALU.add), [psM, fq, fb1], [h1])
            rr_sin(h1, L)
            for tb in range(cfg["nblk"]):
                t0 = tb * bw
                P.op(pe, lambda t0=t0: nc.tensor.matmul(psM[0:64, 0:bw], lhsT=w2[:], rhs=h1[:, t0:t0 + bw], start=True, stop=True), [w2, h1], [psM])
                V(lambda t0=t0: nc.vector.tensor_scalar(out=h2[:, t0:t0 + bw], in0=psM[0:64, 0:bw], scalar1=fq[:, 0:1], scalar2=fb2[:, 0:1],
                                                        op0=ALU.mult, op1=ALU.add), [psM, fq, fb2], [h2])
            rr_sin(h2, L)
            for o in range(2):
                col0 = (d * 2 + o) * 512
                P.dma(sp, adec[:], E["hy_decay"][l, col0:col0 + 512].partition_broadcast(128), [E["hy_decay"]], [adec], adec)
                P.op(act, lambda: nc.scalar.activation(out=adec[:], in_=adec[:], func=AF.Abs), [adec], [adec])
                for tc in range(ntc):
                    P.op(pe, lambda tc=tc, col0=col0: nc.tensor.matmul(psM[:], lhsT=h2[:, tc * 128:(tc + 1) * 128], rhs=w3[:, col0:col0 + 512], start=True, stop=True),
                         [h2, w3], [psM])
                    P.op(act, lambda tc=tc, d=d: nc.scalar.activation(out=ew[:], in_=adec[:], func=AF.Exp, scale=ntv[:, d, tc:tc + 1]), [adec, ntv], [ew])
                    V(lambda tc=tc: nc.vector.tensor_tensor(out=fw[:, tc, :], in0=psM[:], in1=ew[:], op=ALU.mult), [psM, ew], [fw])
                    P.op(act, lambda tc=tc: nc.scalar.activation(out=fa[:], in_=fw[:, tc, :], func=AF.Abs), [fw], [fa])
                    P.op(pe, lambda tc=tc: nc.tensor.matmul(psN[:], lhsT=ones[:], rhs=fa[:], start=(tc == 0), stop=(tc == ntc - 1)), [ones, fa], [psN])
                V(lambda: nc.vector.reciprocal(out=rcp[:], in_=psN[:]), [psN], [rcp])
                V(lambda d=d: nc.vector.tensor_tensor(out=ff[d][:, 0:ntc, :], in0=fw[:, 0:ntc, :], in1=rcp[:, :].unsqueeze(1).to_broadcast([128, ntc, 512]),
                                                      op=ALU.mult), [fw, rcp], [ff[d]])
                if d == 1:
                    V(lambda: nc.vector.memset(ff[1][0:1, 0, :], 0.0), [], [ff[1]])
                P.dma(pool, E["fstash"][d, o, :, 0:ntc, :], ff[d][:, 0:ntc, :], [ff[d]], [E["fstash"]], ff[d])
        ne = (L // 2 + 1 + 127) // 128
        for o in range(2):
            for d in range(2):
                P.dma(sp, ff[d][:, 0:ntc, :], E["fstash"][d, o, :, 0:ntc, :], [E["fstash"]], [ff[d]], ff[d])
            V(lambda: nc.vector.tensor_tensor(out=fpm[0][:, 0:ntc, :], in0=ff[0][:, 0:ntc, :], in1=ff[1][:, 0:ntc, :], op=ALU.add), [ff[0], ff[1]], [fpm[0]])
            P.op(pool, lambda: nc.gpsimd.tensor_tensor(out=fpm[1][:, 0:ntc, :], in0=ff[0][:, 0:ntc, :], in1=ff[1][:, 0:ntc, :], op=ALU.subtract), [ff[0], ff[1]], [fpm[1]])
            for fk in range(nfk):
                c_, s_ = tC[fk % 2], tS[fk % 2]
                P.dma(sp, c_[:, 0:ntc, :], cfg["tFC"][fk], [cfg["tFC"]], [c_], c_)
                P.dma(sp, s_[:, 0:ntc, :], cfg["tFS"][fk], [cfg["tFS"]], [s_], s_)
                pcf, psf = psCf[fk % 2], psSf[fk % 2]
                src = fpm[0] if fk < ne else fpm[1]
                for (tab, pp) in ((c_, pcf), (s_, psf)):
                    for tc in range(ntc):
                        P.op(pe, lambda tab=tab, src=src, pp=pp, tc=tc: nc.tensor.matmul(pp[:], lhsT=tab[:, tc, :], rhs=src[:, tc, :],
                                                                                        start=(tc == 0), stop=(tc == ntc - 1)), [tab, src], [pp])
                fo_ = fo[fk % 2]
                P.op(act, lambda pcf=pcf, fk=fk, fo_=fo_: nc.scalar.activation(out=fo_[:, 0, :], in_=pcf[:], func=AF.Copy, scale=wkv[:, fk:fk + 1]), [pcf, wkv], [fo_])
                V(lambda psf=psf, fk=fk, fo_=fo_: nc.vector.tensor_scalar_mul(out=fo_[:, 1, :], in0=psf[:], scalar1=wkv[:, fk:fk + 1]), [psf, wkv], [fo_])
                P.dma(pool, cfg["Fs"][o, fk], fo_[:], [fo_], [cfg["Fs"]], fo_)
        SF.close()

    SC = P.scope()
    xtok = SC.sb("xtok", [128, 16, 512], BF16)
    Z = SC.sb("Z", [128, 17, 2, 512], BF16)
    y1b = SC.sb("y1b", [128, 4, LS], BF16)
    tC = [SC.sb(f"tC{i}", [128, 16, 128], BF16) for i in range(2)]
    tS = [SC.sb(f"tS{i}", [128, 16, 128], BF16) for i in range(2)]
    tIC = SC.sb("tIC", [128, 17, 512], BF16); tIS = SC.sb("tIS", [128, 17, 512], BF16)
    fsb = [SC.sb(f"fsb{i}", [128, 2, 512]) for i in range(2)]
    zin = [SC.sb(f"zin{i}", [128, 514]) for i in range(2)]
    cvo = [SC.sb(f"cvo{i}", [128, 512]) for i in range(2)]
    cvb = SC.sb("cvb", [128, 512], BF16)
    t4 = [SC.sb(f"t4{i}", [128, 512]) for i in range(4)]
    ob = [SC.sb(f"ob{i}", [128, 512], BF16) for i in range(2)]
    psC = [SC.ps(f"psC{i}", [128, 512]) for i in range(2)]
    psS = [SC.ps(f"psS{i}", [128, 512]) for i in range(2)]
    psI = [SC.ps(f"psI{i}", [128, 512]) for i in range(2)]
    psTt = SC.ps("psTt", [128, 4, 128], BF16)
    cjobs = list(E.get("convjobs", []))
    cst = SC.sb("cst", [128, 16, 512]); csb = SC.sb("csb", [128, 16, 512], BF16)

    def conv_step(n=1):
        for _ in range(n):
            if not cjobs:
                return
            (srcT, src, dstT, dst, nk) = cjobs.pop(0)
            P.dma(act, cst[:, 0:nk, :], src.rearrange("(k p) n -> p k n", p=128), [srcT], [cst], cst)
            P.op(act, lambda: nc.scalar.copy(out=csb[:, 0:nk, :], in_=cst[:, 0:nk, :]), [cst], [csb])
            if len(dst.shape) == 4:
                P.dma(pool, dst, csb[:, 0:nk, :].rearrange("p (g k) n -> p g k n", g=2), [csb], [dstT], csb)
            else:
                P.dma(pool, dst, csb[:, 0:nk, :], [csb], [dstT], csb)

    def shortconv(ci, off, t0, bw, rows_n, zi, out):
        P.dma(sp, zi[:, 0:bw], hy[ci, :, off + t0:off + t0 + bw], [hy], [zi], zi)
        nseg = bw // rows_n
        z3 = zi[:, 0:bw].rearrange("p (s n) -> p s n", n=rows_n)
        o3 = out[:, 0:bw].rearrange("p (s n) -> p s n", n=rows_n)
        V(lambda: nc.vector.tensor_scalar(out=out[:, 0:bw], in0=zi[:, 0:bw], scalar1=cw[:, 1, ci:ci + 1], scalar2=cb[:, ci:ci + 1], op0=ALU.mult, op1=ALU.add),
          [zi, cw, cb], [out])
        V(lambda: nc.vector.scalar_tensor_tensor(out=o3[:, :, 1:rows_n], in0=z3[:, :, 0:rows_n - 1], scalar=cw[:, 0, ci:ci + 1], in1=o3[:, :, 1:rows_n],
                                                 op0=ALU.mult, op1=ALU.add), [zi, cw, out], [out])
        V(lambda: nc.vector.scalar_tensor_tensor(out=o3[:, :, 0:rows_n - 1], in0=z3[:, :, 1:rows_n], scalar=cw[:, 2, ci:ci + 1], in1=o3[:, :, 0:rows_n - 1],
                                                 op0=ALU.mult, op1=ALU.add), [zi, cw, out], [out])

    for si, (off, L) in enumerate(SEQS):
        cfg = cfgs[L]
        ntc, nfk, nblk, bw = cfg["ntc"], cfg["nfk"], cfg["nblk"], cfg["bw"]
        rows_n = 64 if si == 0 else L
        for order in range(2):
            for c in range(4):
                for tb in range(nblk):
                    t0 = tb * bw
                    if order == 0:
                        shortconv(c, off, t0, bw, rows_n, zin[0], cvo[0])
                        P.op(act, lambda: nc.scalar.copy(out=cvb[:, 0:bw], in_=cvo[0][:, 0:bw]), [cvo[0]], [cvb])
                        srcb, srcT, so = cvb, cvb, 0
                    else:
                        srcb, srcT, so = y1b, y1b, None
                    for q in range(bw // 128):
                        if order == 0:
                            in_ap = cvb[:, q * 128:(q + 1) * 128]
                        else:
                            in_ap = y1b[:, c, t0 + q * 128:t0 + (q + 1) * 128]
                        P.op(pe, lambda q=q, in_ap=in_ap: nc.tensor.transpose(psTt[:, q, :], in_ap, ident_b[:]), [srcT, ident_b], [psTt])
                    nq = bw // 128
                    P.op(act, lambda c=c, tb=tb, nq=nq: nc.scalar.copy(out=xtok[:, tb * 4:tb * 4 + nq, c * 128:(c + 1) * 128], in_=psTt[:, 0:nq, :]), [psTt], [xtok])

            def emit(fk, pc, ps_, order=order):
                if si == 0:
                    conv_step()
                f_ = fsb[fk % 2]
                P.dma(sp, f_[:], cfg["Fs"][order, fk], [cfg["Fs"]], [f_], f_)
                V(lambda: nc.vector.tensor_tensor(out=t4[0][:], in0=pc[:], in1=f_[:, 0, :], op=ALU.mult), [pc, f_], [t4[0]])
                V(lambda: nc.vector.tensor_tensor(out=t4[1][:], in0=ps_[:], in1=f_[:, 1, :], op=ALU.mult), [ps_, f_], [t4[1]])
                V(lambda: nc.vector.tensor_tensor(out=t4[2][:], in0=pc[:], in1=f_[:, 1, :], op=ALU.mult), [pc, f_], [t4[2]])
                V(lambda: nc.vector.tensor_tensor(out=t4[3][:], in0=ps_[:], in1=f_[:, 0, :], op=ALU.mult), [ps_, f_], [t4[3]])
                V(lambda: nc.vector.tensor_tensor(out=Z[:, fk, 0, :], in0=t4[0][:], in1=t4[1][:], op=ALU.subtract), [t4[0], t4[1]], [Z])
                V(lambda: nc.vector.tensor_tensor(out=Z[:, fk, 1, :], in0=t4[2][:], in1=t4[3][:], op=ALU.add), [t4[2], t4[3]], [Z])

            fwd_dft(cfg, xtok, emit, (tC, tS, psC, psS))
            for tb in range(nblk):
                t0 = tb * bw
                P.dma(sp, tIC[:, 0:nfk, 0:bw], cfg["tIC"][tb], [cfg["tIC"]], [tIC], tIC)
                P.dma(sp, tIS[:, 0:nfk, 0:bw], cfg["tIS"][tb], [cfg["tIS"]], [tIS], tIS)
                for c in range(4):
                    pi = psI[c % 2]
                    for fk in range(nfk):
                        P.op(pe, lambda fk=fk, c=c, pi=pi: nc.tensor.matmul(pi[:, 0:bw], lhsT=Z[:, fk, 0, c * 128:(c + 1) * 128], rhs=tIC[:, fk, 0:bw],
                                                                           start=(fk == 0), stop=False), [Z, tIC], [pi])
                        P.op(pe, lambda fk=fk, c=c, pi=pi: nc.tensor.matmul(pi[:, 0:bw], lhsT=Z[:, fk, 1, c * 128:(c + 1) * 128], rhs=tIS[:, fk, 0:bw],
                                                                           start=False, stop=(fk == nfk - 1)), [Z, tIS], [pi])
                    if si == 0:
                        conv_step()
                    gate_ci = (4 if order == 0 else 8) + c
                    shortconv(gate_ci, off, t0, bw, rows_n, zin[1], cvo[1])
                    if order == 0:
                        shortconv(c, off, t0, bw, rows_n, zin[0], cvo[0])
                        V(lambda c=c, pi=pi: nc.vector.scalar_tensor_tensor(out=t4[0][:, 0:bw], in0=cvo[0][:, 0:bw], scalar=dvv[:, 0, c:c + 1], in1=pi[:, 0:bw],
                                                                          op0=ALU.mult, op1=ALU.add), [cvo[0], dvv, pi], [t4[0]])
                        V(lambda c=c, t0=t0: nc.vector.tensor_tensor(out=y1b[:, c, t0:t0 + bw], in0=t4[0][:, 0:bw], in1=cvo[1][:, 0:bw], op=ALU.mult),
                          [t4[0], cvo[1]], [y1b])
                    else:
                        o_ = ob[(tb * 4 + c) % 2]
                        V(lambda c=c, pi=pi, t0=t0: nc.vector.scalar_tensor_tensor(out=t4[0][:, 0:bw], in0=y1b[:, c, t0:t0 + bw], scalar=dvv[:, 1, c:c + 1], in1=pi[:, 0:bw],
                                                                                 op0=ALU.mult, op1=ALU.add), [y1b, dvv, pi], [t4[0]])
                        V(lambda o_=o_: nc.vector.tensor_tensor(out=o_[:, 0:bw], in0=t4[0][:, 0:bw], in1=cvo[1][:, 0:bw], op=ALU.mult), [t4[0], cvo[1]], [o_])
                        P.dma(pool, mixT[12 + c, :, off + t0:off + t0 + bw], o_[:, 0:bw], [o_], [mixT], o_)
    conv_step(len(cjobs))
    SC.close()
    S.close()


_NC_CACHE = {}


def _consts():
    ident = np.eye(128, dtype=np.float32)
    s = np.arange(128)[:, None]
    c = np.arange(128)[None, :]
    masks = np.stack([(s <= c), (s >= c)]).astype(np.float32)
    tp1 = np.arange(1, LS + 1, dtype=np.float32)
    out = dict(ident=ident, masks=masks, tp1=tp1)
    bf = ml_dtypes.bfloat16
    for L in (LS, LP):
        sx = str(L); N = 2 * L; ntc = L // 128; nfk = ntc + 1; nblk = max(1, L // 512); bw = min(512, L)
        ne = (L // 2 + 1 + 127) // 128
        kk = np.full(nfk * 128, -1, dtype=np.int64)
        ev = np.arange(0, L + 1, 2); od = np.arange(1, L, 2)
        kk[:len(ev)] = ev; kk[ne * 128:ne * 128 + len(od)] = od
        tt = np.arange(L, dtype=np.int64)
        ang = 2.0 * np.pi * ((np.maximum(kk, 0)[:, None] * tt[None, :]) % N).astype(np.float64) / N
        valid = (kk >= 0).astype(np.float64)[:, None]
        Ckt = np.cos(ang) * valid; Skt = np.sin(ang) * valid
        out["tFC" + sx] = np.ascontiguousarray(Ckt.reshape(nfk, 128, ntc, 128).transpose(0, 3, 2, 1)).astype(bf)
        out["tFS" + sx] = np.ascontiguousarray(Skt.reshape(nfk, 128, ntc, 128).transpose(0, 3, 2, 1)).astype(bf)
        out["tIC" + sx] = np.ascontiguousarray(Ckt.reshape(nfk, 128, nblk, bw).transpose(2, 1, 0, 3)).astype(bf)
        out["tIS" + sx] = np.ascontiguousarray(Skt.reshape(nfk, 128, nblk, bw).transpose(2, 1, 0, 3)).astype(bf)
        t = np.linspace(0.0, 1.0, L, dtype=np.float32)[:, None]
        w = (2.0 * np.float32(np.pi) * np.arange(L, dtype=np.float32)[:, None] / np.float32(L)).astype(np.float32)
        fr = np.linspace(1e-4, 15.0, 16, dtype=np.float32)[None, :]
        feats = np.concatenate([t, np.cos(fr * w), -np.sin(fr * w)], axis=-1).astype(np.float32)
        ridx = (L - np.arange(L)) % L
        fT = np.zeros((64, L), np.float32); fT[:33] = feats.T
        fTr = np.zeros((64, L), np.float32); fTr[:33] = feats[ridx].T
        out["featsT" + sx] = fT; out["featsTr" + sx] = fTr
        out["negtv" + sx] = np.ascontiguousarray(-t[:, 0].reshape(ntc, 128).T)
        out["negtvr" + sx] = np.ascontiguousarray(-t[ridx, 0].reshape(ntc, 128).T)
        wk = np.where((kk == 0) | (kk == L), 1.0 / N, 2.0 / N) * (kk >= 0)
        sg = np.where(kk % 2 == 0, 1.0, -1.0)
        out["wk" + sx] = np.ascontiguousarray(wk.reshape(nfk, 128).T).astype(np.float32)
        out["sgw" + sx] = np.ascontiguousarray((wk * sg).reshape(nfk, 128).T).astype(np.float32)
    return out


def kernel(**inp):
    n = 8
    if "nc" not in _NC_CACHE:
        _NC_CACHE["nc"] = build_program()
    nc = _NC_CACHE["nc"]
    f = lambda a: np.ascontiguousarray(np.asarray(a, dtype=np.float32))
    consts = _consts()
    shared = {k: f(inp[k]) for k in ("ada_w", "ada_b", "w_in", "w_out", "w_up", "w_down", "ln1_g", "ln1_b", "ln2_g", "ln2_b",
                                     "gla_w_gate", "gla_b_gate", "gla_norm_w", "ret_decay_exp",
                                     "s5_a_re", "s5_a_im", "s5_log_step", "s5_b_re", "s5_b_im", "s5_c_re", "s5_c_im", "s5_d", "s5_glu_w", "s5_glu_b",
                                     "hy_conv_w", "hy_conv_b", "hy_f_w1", "hy_f_b1", "hy_f_w2", "hy_f_b2", "hy_f_freq", "hy_f_w3", "hy_decay", "hy_d")}
    shared.update(consts)
    in_maps = []
    for c in range(n):
        m = dict(shared)
        m["x"] = np.ascontiguousarray(np.concatenate([inp["x_sample"][c], inp["x_prompt"][2 * c], inp["x_prompt"][2 * c + 1]], axis=0).astype(np.float32))
        m["cvec"] = np.ascontiguousarray(np.stack([inp["c"][c], inp["c_ctx"]]).astype(np.float32))
        m["sg"] = f(inp["state_gla"][c]); m["sr"] = f(inp["state_ret"][c])
        m["s5re"] = f(inp["state_s5_re"][c]); m["s5im"] = f(inp["state_s5_im"][c])
        in_maps.append(m)
    res = run_bass_kernel_spmd(nc, in_maps, core_ids=list(range(n)))
    R = res.results
    y = np.stack([r["y"] for r in R])
    y_sample = np.ascontiguousarray(y[:, :LS])
    y_prompt = np.ascontiguousarray(y[:, LS:].reshape(n * 2, LP, D))
    cat = lambda k: np.ascontiguousarray(np.concatenate([r[k] for r in R], axis=0))
    return (y_prompt, y_sample, cat("nsg"), cat("nsr"), cat("ns5re"), cat("ns5im"))
```

```python
import numpy as np
import ml_dtypes
from contextlib import ExitStack
import concourse.bass as bass
import concourse.mybir as mybir
from concourse.bass_utils import run_bass_kernel_spmd

F32 = mybir.dt.float32
BF16 = mybir.dt.bfloat16
AF = mybir.ActivationFunctionType
ALU = mybir.AluOpType
AX = mybir.AxisListType

D = 2048
DEPTH = 2
LS = 2048
LP = 256
NTOK = LS + 2 * LP
NTT = NTOK // 128
NTB = NTOK // 512
DFF = 8192
DIN = 5152
ALPHA = (2 * DEPTH) ** 0.25
LN_EPS = 1e-5
NORM_EPS = 1e-6
OFF = dict(gq=0, gk=256, gv=512, gg=1024, glr=1536, rq=1568, rk=1824, rv=2080, rg=2592, su=3104, hy=3616)
SEQS = [(0, LS), (LS, LP), (LS + LP, LP)]


class Buf:
    __slots__ = ("name", "w", "r", "dsem", "dcnt", "sw")

    def __init__(self, name):
        self.name = name
        self.w = None
        self.r = {}
        self.dsem = None
        self.dcnt = 0


class T:
    def __init__(self, t, name, excl=False):
        self.t = t
        self.b = Buf(name)
        self.excl = excl

    def __getitem__(self, k):
        return self.t[k]


class Eng:
    def __init__(self, P, name, h, selfsync=True):
        self.P = P
        self.name = name
        self.h = h
        self.selfsync = selfsync
        self.sem = P.newsem()
        self.cnt = 0
        self.seen = {}
        self.mysems = {id(self.sem)}


class Prog:
    SEM_LIMIT = 12000

    def __init__(self, nc):
        self.nc = nc
        self.es = ExitStack()
        self.nsem = 0
        self.freesems = []
        self.freesems_sw = []
        self.bufs = []
        self.semobj = {}
        self.pe = Eng(self, "pe", nc.tensor, selfsync=False)
        self.dve = Eng(self, "dve", nc.vector)
        self.act = Eng(self, "act", nc.scalar)
        self.pool = Eng(self, "pool", nc.gpsimd)
        self.sp = Eng(self, "sp", nc.sync)
        self.engs = [self.pe, self.dve, self.act, self.pool, self.sp]
        self.dchans = []
        self.nscope = 0

    def newsem(self):
        self.nsem += 1
        s = self.es.enter_context(self.nc.semaphore(f"sem{self.nsem}"))
        self.semobj[id(s)] = s
        return s

    def reg(self, t):
        self.bufs.append(t.b)
        return t

    def dram(self, name, shape, dt, kind="Internal"):
        h = self.nc.dram_tensor(name, list(shape), dt, kind=kind)
        return self.reg(T(h.ap(), name))

    def _need(self, eng, evs):
        for ev in evs:
            if ev is None:
                continue
            sem, val = ev
            k = id(sem)
            if (not eng.selfsync) and k in eng.mysems:
                continue
            if eng.seen.get(k, 0) >= val:
                continue
            eng.h.wait_ge(sem, val)
            eng.seen[k] = val

    def _deps(self, reads, writes, eng=None):
        evs = []
        for t in reads:
            evs.append(t.b.w)
            if t.excl:
                for k, (s, v) in t.b.r.items():
                    if eng is None or k not in eng.mysems:
                        evs.append((s, v))
        for t in writes:
            evs.append(t.b.w)
            for k, (s, v) in t.b.r.items():
                evs.append((s, v))
        return evs

    def _commit(self, ev, reads, writes):
        for t in reads:
            t.b.r[id(ev[0])] = ev
        for t in writes:
            t.b.w = ev
            t.b.r = {}

    def op(self, eng, fn, reads=(), writes=()):
        self._need(eng, self._deps(reads, writes, eng))
        inst = fn()
        eng.cnt += 1
        inst.then_inc(eng.sem, 1)
        ev = (eng.sem, eng.cnt)
        self._commit(ev, reads, writes)
        if eng.cnt >= self.SEM_LIMIT:
            eng.sem = self.newsem()
            eng.mysems.add(id(eng.sem))
            eng.cnt = 0
        return inst

    def dma(self, q, out, in_, reads, writes, chan, **kw):
        self._need(q, self._deps(reads, writes, q))
        sw = (q is self.pool)
        key = "sw" if sw else "hw"
        if not hasattr(chan, "chs"):
            chan.chs = {}
        if key not in chan.chs:
            pool_ = self.freesems_sw if sw else self.freesems
            b = Buf(chan.b.name + "_" + key)
            if pool_:
                b.dsem, b.dcnt = pool_.pop()
            else:
                b.dsem, b.dcnt = self.newsem(), 0
            b.sw = sw
            chan.chs[key] = b
            self.dchans.append(b)
        b = chan.chs[key]
        inst = q.h.dma_start(out=out, in_=in_, **kw)
        b.dcnt += 16
        inst.then_inc(b.dsem, 16)
        ev = (b.dsem, b.dcnt)
        self._commit(ev, reads, writes)
        return inst

    def barrier(self):
        evs = [(e.sem, e.cnt) for e in self.engs if e.cnt > 0]
        evs += [(b.dsem, b.dcnt) for b in self.dchans if b.dcnt > 0]
        for e in self.engs:
            sv = e.selfsync
            e.selfsync = True
            self._need(e, [ev for ev in evs if sv or id(ev[0]) not in e.mysems])
            e.selfsync = sv
        for b in self.bufs:
            b.w = None
            b.r = {}

    def scope(self):
        return Scope(self)


class Scope:
    def __init__(self, P):
        self.P = P
        self.es = ExitStack()
        self.ts = []
        P.nscope += 1
        self.id = P.nscope

    def sb(self, name, shape, dt=F32):
        t = self.es.enter_context(self.P.nc.sbuf_tensor(f"{name}_{self.id}", list(shape), dt))
        tt = self.P.reg(T(t, name))
        self.ts.append(tt)
        return tt

    def ps(self, name, shape, dt=F32):
        t = self.es.enter_context(self.P.nc.psum_tensor(f"{name}_{self.id}", list(shape), dt))
        tt = self.P.reg(T(t, name, excl=True))
        self.ts.append(tt)
        return tt

    def close(self):
        P = self.P
        P.barrier()
        for tt in self.ts:
            for key, cb in getattr(tt, "chs", {}).items():
                (P.freesems_sw if cb.sw else P.freesems).append((cb.dsem, cb.dcnt))
                P.dchans.remove(cb)
            tt.chs = {}
            P.bufs.remove(tt.b)
        self.es.close()


def build_program(debug=False):
    nc = bass.Bass("TRN2", target_bir_lowering=False, dynamic_dma_scratch_size=8192)
    P = Prog(nc)
    pe, dve, act, pool, sp = P.pe, P.dve, P.act, P.pool, P.sp

    def din(name, shape, dt=F32):
        return P.dram(name, shape, dt, kind="ExternalInput")

    def dout(name, shape, dt=F32):
        return P.dram(name, shape, dt, kind="ExternalOutput")

    x_in = din("x", [NTOK, D])
    cvec = din("cvec", [2, D])
    ada_w = din("ada_w", [DEPTH, D, 6 * D])
    ada_b = din("ada_b", [DEPTH, 6 * D])
    w_in = din("w_in", [DEPTH, D, DIN])
    w_out = din("w_out", [DEPTH, D, D])
    w_up = din("w_up", [DEPTH, D, DFF])
    w_down = din("w_down", [DEPTH, DFF, D])
    ln1_g = din("ln1_g", [DEPTH, D]); ln1_b = din("ln1_b", [DEPTH, D])
    ln2_g = din("ln2_g", [DEPTH, D]); ln2_b = din("ln2_b", [DEPTH, D])
    sg_in = din("sg", [DEPTH, 2, 4, 64, 128]); sr_in = din("sr", [DEPTH, 2, 4, 64, 128])
    gla_w_gate = din("gla_w_gate", [DEPTH, 2, 16, 256]); gla_b_gate = din("gla_b_gate", [DEPTH, 2, 256])
    gla_norm_w = din("gla_norm_w", [DEPTH, 128]); ret_decay_exp = din("ret_decay_exp", [DEPTH, 2, 4])
    s5re_in = din("s5re", [DEPTH, 2, 32, 64]); s5im_in = din("s5im", [DEPTH, 2, 32, 64])
    s5_a_re = din("s5_a_re", [DEPTH, 2, 32, 64]); s5_a_im = din("s5_a_im", [DEPTH, 2, 32, 64])
    s5_log_step = din("s5_log_step", [DEPTH, 2, 32])
    s5_b_re = din("s5_b_re", [DEPTH, 2, 32, 64, 16]); s5_b_im = din("s5_b_im", [DEPTH, 2, 32, 64, 16])
    s5_c_re = din("s5_c_re", [DEPTH, 2, 32, 16, 64]); s5_c_im = din("s5_c_im", [DEPTH, 2, 32, 16, 64])
    s5_d = din("s5_d", [DEPTH, 512]); s5_glu_w = din("s5_glu_w", [DEPTH, 512, 512]); s5_glu_b = din("s5_glu_b", [DEPTH, 512])
    tp1_in = din("tp1", [LS])
    hy_conv_w = din("hy_conv_w", [DEPTH, 3, 1536]); hy_conv_b = din("hy_conv_b", [DEPTH, 1536])
    hy_f_w1 = din("hy_f_w1", [DEPTH, 33, 64]); hy_f_b1 = din("hy_f_b1", [DEPTH, 64])
    hy_f_w2 = din("hy_f_w2", [DEPTH, 64, 64]); hy_f_b2 = din("hy_f_b2", [DEPTH, 64])
    hy_f_freq = din("hy_f_freq", [DEPTH, 64]); hy_f_w3 = din("hy_f_w3", [DEPTH, 64, 2048])
    hy_decay = din("hy_decay", [DEPTH, 2048]); hy_d = din("hy_d", [DEPTH, 2, 512])
    fstash = P.dram("fstash", [2, 2, 128, 16, 512], BF16)
    s5BT = P.dram("s5BT", [32, 128, 2, 128], BF16); s5CP = P.dram("s5CP", [32, 128, 2, 128], BF16)
    hyc = {}
    for L_ in (LS, LP):
        sx = str(L_); ntc_ = L_ // 128; nfk_ = ntc_ + 1; nblk_ = max(1, L_ // 512); bw_ = min(512, L_)
        hyc["tFC" + sx] = din("tFC" + sx, [nfk_, 128, ntc_, 128], BF16); hyc["tFS" + sx] = din("tFS" + sx, [nfk_, 128, ntc_, 128], BF16)
        hyc["tIC" + sx] = din("tIC" + sx, [nblk_, 128, nfk_, bw_], BF16); hyc["tIS" + sx] = din("tIS" + sx, [nblk_, 128, nfk_, bw_], BF16)
        hyc["featsT" + sx] = din("featsT" + sx, [64, L_]); hyc["featsTr" + sx] = din("featsTr" + sx, [64, L_])
        hyc["negtv" + sx] = din("negtv" + sx, [128, ntc_]); hyc["negtvr" + sx] = din("negtvr" + sx, [128, ntc_])
        hyc["wk" + sx] = din("wk" + sx, [128, nfk_]); hyc["sgw" + sx] = din("sgw" + sx, [128, nfk_])
        hyc["Fs" + sx] = P.dram("Fs" + sx, [2, nfk_, 128, 2, 512], F32)
    ident_in = din("ident", [128, 128])
    masks_in = din("masks", [2, 128, 128])
    y_out = dout("y", [NTOK, D])
    nsg = dout("nsg", [2, DEPTH, 2, 4, 64, 128]); nsr = dout("nsr", [2, DEPTH, 2, 4, 64, 128])
    ns5re = dout("ns5re", [2, DEPTH, 2, 32, 64]); ns5im = dout("ns5im", [2, DEPTH, 2, 32, 64])
    kind = "ExternalOutput" if debug else "Internal"
    mixonly = isinstance(debug, str) and debug.startswith("mix")
    pkind = "ExternalInput" if mixonly else kind
    xcur = P.dram("xcur", [NTOK, D], F32, kind=kind)
    modvec = P.dram("modvec", [DEPTH, 2, 6 * D], F32)
    wo_bf = P.dram("wo_bf", [DEPTH, 4, 128, 16, 512], BF16)
    wu_bf = P.dram("wu_bf", [DEPTH, 16, 128, 16, 512], BF16)
    wd_bf = P.dram("wd_bf", [DEPTH, 4, 8, 128, 8, 512], BF16)
    pF = {n: P.dram("pF_" + n, [c, 128, NTOK], F32, kind=pkind) for n, c in
          dict(qkg=4, qkr=4, su=4, hy=12).items()}
    pLr = P.dram("pF_lr", [32, NTOK], F32, kind=pkind)
    pT = {n: P.dram("pT_" + n, [NTOK, 512], F32, kind=pkind) for n in ("gv", "gg", "rv", "rg")}
    mixT = P.dram("mixT", [16, 128, NTOK], BF16, kind=kind)

    G = P.scope()
    ident_f = G.sb("ident_f", [128, 128], F32)
    ident_b = G.sb("ident_b", [128, 128], BF16)
    modT = G.sb("modT", [128, DEPTH, 96, 2], F32)
    P.dma(sp, ident_f[:], ident_in[:, :], [ident_in], [ident_f], ident_f)
    P.op(dve, lambda: nc.vector.tensor_copy(out=ident_b[:], in_=ident_f[:]), [ident_f], [ident_b])

    def load_slab(q, dst, src_ap, srcT):
        P.dma(q, dst[:], src_ap.rearrange("(k p) n -> p k n", p=128), [srcT], [dst], dst)

    cast_rr = [0]

    def cast(dst, src, nk):
        engs = [(act, lambda o, i: nc.scalar.copy(out=o, in_=i)),
                (dve, lambda o, i: nc.vector.tensor_copy(out=o, in_=i))]
        h = (nk * 5) // 8
        for (k0, k1) in ((0, h), (h, nk)):
            e, f = engs[cast_rr[0] % 2]
            cast_rr[0] += 1
            P.op(e, lambda f=f, k0=k0, k1=k1: f(dst[:, k0:k1, :], src[:, k0:k1, :]), [src], [dst])

    if not mixonly:
        S0 = P.scope()
        cT = S0.sb("cT", [128, 16, 2])
        abT = S0.sb("abT", [128, DEPTH, 96])
        stg = [S0.sb(f"stg{i}", [128, 16, 512]) for i in range(2)]
        stb = [S0.sb(f"stb{i}", [128, 16, 512], BF16) for i in range(2)]
        modps = [S0.ps(f"modps{l}", [128, 96, 2]) for l in range(DEPTH)]
        with nc.allow_non_contiguous_dma(reason="tiny param transposes"):
            for r in range(2):
                P.dma(sp, cT[:, :, r], cvec[r].rearrange("(k p) -> p k", p=128), [cvec], [cT], cT)
            for l in range(DEPTH):
                P.dma(sp, abT[:, l, :], ada_b[l].rearrange("(j p) -> p j", p=128), [ada_b], [abT], abT)
        P.op(act, lambda: nc.scalar.activation(out=cT[:], in_=cT[:], func=AF.Silu), [cT], [cT])
        cTb = S0.sb("cTb", [128, 16, 2], BF16)
        P.op(dve, lambda: nc.vector.tensor_copy(out=cTb[:], in_=cT[:]), [cT], [cTb])
        it = 0
        for l in range(DEPTH):
            for s in range(24):
                buf0 = stg[it % 2]
                buf = stb[it % 2]
                it += 1
                if s == 0:
                    load_slab(sp, buf0, ada_w[l, :, 0:512], ada_w)
                if s + 1 < 24:
                    load_slab(sp, stg[it % 2], ada_w[l, :, (s + 1) * 512:(s + 2) * 512], ada_w)
                cast(buf, buf0, 16)
                for j in range(4):
                    for k in range(16):
                        P.op(pe, lambda buf=buf, j=j, k=k, s=s, l=l: nc.tensor.matmul(
                            modps[l][:, s * 4 + j, :], lhsT=buf[:, k, j * 128:(j + 1) * 128], rhs=cTb[:, k, :],
                            start=(k == 0), stop=(k == 15)), [buf, cTb], [modps[l]])
            for r in range(2):
                P.op(dve, lambda l=l, r=r: nc.vector.tensor_tensor(out=modT[:, l, :, r], in0=modps[l][:, :, r],
                                                                   in1=abT[:, l, :], op=ALU.add),
                     [modps[l], abT], [modT])
        with nc.allow_non_contiguous_dma(reason="mod vectors to DRAM rows"):
            for l in range(DEPTH):
                for r in range(2):
                    P.dma(sp, modvec[l, r].rearrange("(j p) -> p j", p=128), modT[:, l, :, r], [modT], [modvec], modT)
        for l in range(DEPTH):
            for c0 in (16, 64):
                P.op(dve, lambda l=l, c0=c0: nc.vector.tensor_scalar_add(out=modT[:, l, c0:c0 + 16, :],
                                                                         in0=modT[:, l, c0:c0 + 16, :], scalar1=1.0),
                     [modT], [modT])
        S0.close()

    convjobs = {}
    for l in range(DEPTH):
        jobs = []
        for s_ in range(4):
            jobs.append((w_out, w_out[l, :, s_ * 512:(s_ + 1) * 512], wo_bf, wo_bf[l, s_], 16))
        for s_ in range(16):
            jobs.append((w_up, w_up[l, :, s_ * 512:(s_ + 1) * 512], wu_bf, wu_bf[l, s_], 16))
        for c in range(4):
            for kg in range(0, 8, 2):
                jobs.append((w_down, w_down[l, kg * 1024:(kg + 2) * 1024, c * 512:(c + 1) * 512], wd_bf,
                             wd_bf[l, c, kg:kg + 2].rearrange("g p k n -> p g k n"), 16))
        convjobs[l] = jobs if not mixonly else []

    def ln_stats(S, xt, eps, name):
        st = S.sb(name + "_st", [128, 4, 6])
        mv = S.sb(name + "_mv", [128, 2])
        rs = S.sb(name + "_rs", [128, 1])
        return st, mv, rs

    def do_ln_stats(st, mv, rs, xt, eps):
        for c in range(4):
            P.op(dve, lambda c=c: nc.vector.bn_stats(out=st[:, c, :], in_=xt[:, c * 512:(c + 1) * 512]), [xt], [st])
        P.op(dve, lambda: nc.vector.bn_aggr(out=mv[:], in_=st[:].rearrange("p c s -> p (c s)")), [st], [mv])
        P.op(act, lambda: nc.scalar.activation(out=rs[:], in_=mv[:, 1:2], func=AF.Ln, bias=eps), [mv], [rs])
        P.op(act, lambda: nc.scalar.activation(out=rs[:], in_=rs[:], func=AF.Exp, scale=-0.5), [rs], [rs])

    for l in range(DEPTH):
        xsrc = x_in if l == 0 else xcur
        xdst = xcur if l == 0 else y_out
        if not mixonly:
            SA = P.scope()
            hT = SA.sb("hT", [128, 16, NTOK], BF16)
            SA1 = P.scope()
            xin = [SA1.sb(f"xin{i}", [128, D]) for i in range(2)]
            xn = [SA1.sb(f"xn{i}", [128, D], BF16) for i in range(2)]
            stq = [ln_stats(SA1, None, LN_EPS, f"lnA{i}") for i in range(2)]
            psT = [SA1.ps(f"psT{i}", [128, 16, 128], BF16) for i in range(2)]
            for tt in range(NTT):
                typ = 0 if tt < LS // 128 else 1
                xi, xb_, (st, mv, rs), pt = xin[tt % 2], xn[tt % 2], stq[tt % 2], psT[tt % 2]
                P.dma(sp, xi[:], xsrc[tt * 128:(tt + 1) * 128, :], [xsrc], [xi], xi)
                do_ln_stats(st, mv, rs, xi, LN_EPS)
                P.op(dve, lambda xi=xi, xb_=xb_, mv=mv, rs=rs: nc.vector.tensor_scalar(
                    out=xb_[:], in0=xi[:], scalar1=mv[:, 0:1], scalar2=rs[:, 0:1], op0=ALU.subtract, op1=ALU.mult),
                     [xi, mv, rs], [xb_])
                for j in range(16):
                    P.op(pe, lambda j=j, pt=pt, xb_=xb_: nc.tensor.transpose(pt[:, j, :], xb_[:, j * 128:(j + 1) * 128], ident_b[:]),
                         [xb_, ident_b], [pt])
                for j in range(16):
                    if j % 2 == 0:
                        P.op(dve, lambda j=j, pt=pt, typ=typ, tt=tt: nc.vector.tensor_scalar(
                            out=hT[:, j, tt * 128:(tt + 1) * 128], in0=pt[:, j, :], scalar1=modT[:, l, 16 + j, typ:typ + 1],
                            scalar2=modT[:, l, j, typ:typ + 1], op0=ALU.mult, op1=ALU.add), [pt, modT], [hT])
                    else:
                        P.op(act, lambda j=j, pt=pt, typ=typ, tt=tt: nc.scalar.activation(
                            out=hT[:, j, tt * 128:(tt + 1) * 128], in_=pt[:, j, :], func=AF.Identity,
                            scale=modT[:, l, 16 + j, typ:typ + 1], bias=modT[:, l, j, typ:typ + 1]), [pt, modT], [hT])
            SA1.close()
            SB = P.scope()
            wst = [SB.sb(f"wst{i}", [128, 16, 512]) for i in range(2)]
            wsb = [SB.sb(f"wsb{i}", [128, 16, 512], BF16) for i in range(2)]
            ost = [SB.sb(f"ost{i}", [128, 512]) for i in range(4)]
            psB = [SB.ps(f"psB{i}", [128, 512]) for i in range(4)]
            oi = [0]

            def evac(ps_ap, psT_, dst_dram_ap, dstT, np_, scale=1.0, func=None):
                o = ost[oi[0] % 4]
                e = oi[0] % 2
                oi[0] += 1
                if func is not None or e == 0:
                    P.op(act, lambda: nc.scalar.activation(out=o[0:np_, :], in_=ps_ap, func=(func or AF.Copy), scale=scale),
                         [psT_], [o])
                else:
                    P.op(dve, lambda: nc.vector.tensor_scalar_mul(out=o[0:np_, :], in0=ps_ap, scalar1=scale), [psT_], [o])
                P.dma(pool, dst_dram_ap, o[0:np_, :], [o], [dstT], o)

            slabs = [("F", OFF["gq"], pF["qkg"], (0.125, 0.125, 1.0, 1.0)), ("T", OFF["gv"], pT["gv"], None),
                     ("T", OFF["gg"], pT["gg"], AF.Silu), ("L", OFF["glr"], pLr, None),
                     ("F", OFF["rq"], pF["qkr"], (1.0, 1.0, 0.125, 0.125)), ("T", OFF["rv"], pT["rv"], None),
                     ("T", OFF["rg"], pT["rg"], AF.Silu), ("F", OFF["su"], pF["su"], (1.0,) * 4),
                     ("F3", OFF["hy"], pF["hy"], 0), ("F3", OFF["hy"] + 512, pF["hy"], 4), ("F3", OFF["hy"] + 1024, pF["hy"], 8)]
            for si, (kind_, c0, dstT, extra) in enumerate(slabs):
                a, b = wst[si % 2], wsb[si % 2]
                ncol = 32 if kind_ == "L" else 512
                P.dma(sp, a[:, :, 0:ncol], w_in[l, :, c0:c0 + ncol].rearrange("(k p) n -> p k n", p=128), [w_in], [a], a)
                h8 = 8
                P.op(act, lambda a=a, b=b, ncol=ncol: nc.scalar.copy(out=b[:, 0:8, 0:ncol], in_=a[:, 0:8, 0:ncol]), [a], [b])
                P.op(dve, lambda a=a, b=b, ncol=ncol: nc.vector.tensor_copy(out=b[:, 8:16, 0:ncol], in_=a[:, 8:16, 0:ncol]), [a], [b])
                if kind_ in ("F", "F3"):
                    for tb in range(NTB):
                        for j in range(4):
                            ps = psB[(tb * 4 + j) % 4]
                            for k in range(16):
                                P.op(pe, lambda ps=ps, b=b, j=j, k=k, tb=tb: nc.tensor.matmul(
                                    ps[:], lhsT=b[:, k, j * 128:(j + 1) * 128], rhs=hT[:, k, tb * 512:(tb + 1) * 512],
                                    start=(k == 0), stop=(k == 15)), [b, hT], [ps])
                            if kind_ == "F":
                                evac(ps[:], ps, dstT[j, :, tb * 512:(tb + 1) * 512], dstT, 128, scale=extra[j])
                            else:
                                evac(ps[:], ps, dstT[extra + j, :, tb * 512:(tb + 1) * 512], dstT, 128)
                elif kind_ == "L":
                    for tb in range(NTB):
                        ps = psB[tb % 4]
                        for k in range(16):
                            P.op(pe, lambda ps=ps, b=b, k=k, tb=tb: nc.tensor.matmul(
                                ps[0:32, :], lhsT=b[:, k, 0:32], rhs=hT[:, k, tb * 512:(tb + 1) * 512],
                                start=(k == 0), stop=(k == 15)), [b, hT], [ps])
                        evac(ps[0:32, :], ps, dstT[:, tb * 512:(tb + 1) * 512], dstT, 32)
                else:
                    for tt in range(NTT):
                        ps = psB[tt % 4]
                        for k in range(16):
                            P.op(pe, lambda ps=ps, b=b, k=k, tt=tt: nc.tensor.matmul(
                                ps[:], lhsT=hT[:, k, tt * 128:(tt + 1) * 128], rhs=b[:, k, :],
                                start=(k == 0), stop=(k == 15)), [b, hT], [ps])
                        evac(ps[:], ps, dstT[tt * 128:(tt + 1) * 128, :], dstT, 128, func=extra)
            SB.close()
            SA.close()

        env_ = dict(locals()); env_.update(hyc); env_["convjobs"] = convjobs[l]
        mixers(P, nc, l, env_)
        if debug == 1 or mixonly:
            break

        if not mixonly:
            SD = P.scope()
            mxb = SD.sb("mxb", [128, 16, 512], BF16)
            h2T = mxb
            uT = SD.sb("uT", [128, 64, 512], BF16)
            xa = [[SD.sb(f"xa{s_}{i}", [128, D]) for i in range(4)] for s_ in range(2)]
            bc = {n: SD.sb("bc_" + n, [128, D]) for n in ("g1", "g2", "lg", "lb")}
            wsl = [SD.sb(f"wsl{i}", [128, 8, 512], BF16) for i in range(3)]
            tmp = [SD.sb(f"tmpD{i}", [128, 512]) for i in range(2)]
            xnb = SD.sb("xnb", [128, D], BF16)
            stD = ln_stats(SD, None, LN_EPS, "lnD")
            pb8 = [SD.ps(f"pb8_{i}", [128, 512]) for i in range(8)]
            psAcc = pb8[0:4]
            def load_lnp(gT, bT):
                P.dma(sp, bc["lg"][:], gT[l].partition_broadcast(128), [gT], [bc["lg"]], bc["lg"])
                P.dma(sp, bc["lb"][:], bT[l].partition_broadcast(128), [bT], [bc["lb"]], bc["lb"])
            wi = [0]

            def wload(src_ap, srcT, nk=8):
                w = wsl[wi[0] % 3]
                wi[0] += 1
                P.dma(sp, w[:, 0:nk, :], src_ap, [srcT], [w], w)
                return w

            def resid(xt, c, ps, gname, ti):
                t_ = tmp[ti % 2]
                P.op(dve, lambda: nc.vector.tensor_tensor(out=t_[:], in0=ps[:], in1=bc[gname][:, c * 512:(c + 1) * 512], op=ALU.mult),
                     [ps, bc[gname]], [t_])
                P.op(dve, lambda: nc.vector.scalar_tensor_tensor(out=xt[:, c * 512:(c + 1) * 512], in0=xt[:, c * 512:(c + 1) * 512],
                                                                 scalar=ALPHA, in1=t_[:], op0=ALU.mult, op1=ALU.add),
                     [t_, xt], [xt])

            nmr = SD.sb("nmr", [128, 1])

            def act_norm(dst, src, srcT, dstT):
                st, mv, rs = stD
                P.op(dve, lambda: nc.vector.scalar_tensor_tensor(out=nmr[:], in0=mv[:, 0:1], scalar=-1.0, in1=rs[:], op0=ALU.mult, op1=ALU.mult),
                     [mv, rs], [nmr])
                P.op(act, lambda: nc.scalar.activation(out=dst, in_=src, func=AF.Identity, scale=rs[:, 0:1], bias=nmr[:, 0:1]), [srcT, rs, nmr], [dstT])

            def ln_affine(xt, gn, bn):
                st, mv, rs = stD
                do_ln_stats(st, mv, rs, xt, LN_EPS)
                act_norm(xt[:], xt[:], xt, xt)
                P.op(dve, lambda: nc.vector.tensor_tensor(out=xt[:], in0=xt[:], in1=bc[gn][:], op=ALU.mult), [xt, bc[gn]], [xt])
                P.op(dve, lambda: nc.vector.tensor_tensor(out=xt[:], in0=xt[:], in1=bc[bn][:], op=ALU.add), [xt, bc[bn]], [xt])

            tic = [0]

            def typ_of(tb):
                return 0 if tb < 4 else 1

            def st_load(tb):
                typ = typ_of(tb)
                if tb == 0 or tb == 4:
                    P.dma(sp, bc["g1"][:], modvec[l, typ, 2 * D:3 * D].partition_broadcast(128), [modvec], [bc["g1"]], bc["g1"])
                P.dma(sp, mxb[:], mixT[:, :, tb * 512:(tb + 1) * 512].rearrange("k p n -> p k n"), [mixT], [mxb], mxb)
                for tt in range(4):
                    t0 = tb * 512 + tt * 128
                    P.dma(sp, xa[tb % 2][tt][:], xsrc[t0:t0 + 128, :], [xsrc], [xa[tb % 2][tt]], xa[tb % 2][tt])

            def st_D1(tb):
                X = xa[tb % 2]
                for c in range(4):
                    for hf in range(2):
                        w = wload(wo_bf[l, c, :, hf * 8:(hf + 1) * 8, :], wo_bf)
                        for tt in range(4):
                            for k8 in range(8):
                                k = hf * 8 + k8
                                P.op(pe, lambda w=w, tt=tt, k=k, k8=k8: nc.tensor.matmul(
                                    psAcc[tt][:], lhsT=mxb[:, k, tt * 128:(tt + 1) * 128], rhs=w[:, k8, :],
                                    start=(k == 0), stop=(k == 15)), [w, mxb], [psAcc[tt]])
                    for tt in range(4):
                        resid(X[tt], c, psAcc[tt], "g1", tic[0]); tic[0] += 1

            def st_D2(tb, tt):
                X = xa[tb % 2]
                typ = typ_of(tb)
                if tt == 0:
                    load_lnp(ln1_g, ln1_b)
                ln_affine(X[tt], "lg", "lb")
                st, mv, rs = stD
                do_ln_stats(st, mv, rs, X[tt], LN_EPS)
                act_norm(xnb[:], X[tt][:], X[tt], xnb)
                def tview(j):
                    bank = pb8[6 + j // 8]
                    return bank, bank[:].bitcast(BF16)[:, (j % 8) * 128:(j % 8 + 1) * 128]
                for j in range(16):
                    bank, v = tview(j)
                    P.op(pe, lambda j=j, v=v: nc.tensor.transpose(v, xnb[:, j * 128:(j + 1) * 128], ident_b[:]),
                         [xnb, ident_b], [bank])
                for j in range(16):
                    bank, v = tview(j)
                    if j < 8:
                        P.op(dve, lambda j=j, v=v: nc.vector.tensor_scalar(
                            out=h2T[:, j, tt * 128:(tt + 1) * 128], in0=v, scalar1=modT[:, l, 64 + j, typ:typ + 1],
                            scalar2=modT[:, l, 48 + j, typ:typ + 1], op0=ALU.mult, op1=ALU.add), [bank, modT], [h2T])
                    else:
                        P.op(act, lambda j=j, v=v: nc.scalar.activation(
                            out=h2T[:, j, tt * 128:(tt + 1) * 128], in_=v, func=AF.Identity,
                            scale=modT[:, l, 64 + j, typ:typ + 1], bias=modT[:, l, 48 + j, typ:typ + 1]), [bank, modT], [h2T])

            def st_D3(tb):
                ei = 0
                for s_ in range(16):
                    banks = pb8[(s_ % 2) * 4:(s_ % 2) * 4 + 4]
                    for hf in range(2):
                        w = wload(wu_bf[l, s_, :, hf * 8:(hf + 1) * 8, :], wu_bf)
                        for k8 in range(8):
                            k = hf * 8 + k8
                            for j in range(4):
                                P.op(pe, lambda w=w, j=j, k=k, k8=k8, banks=banks: nc.tensor.matmul(
                                    banks[j][:], lhsT=w[:, k8, j * 128:(j + 1) * 128], rhs=h2T[:, k, :], start=(k == 0), stop=(k == 15)),
                                     [w, h2T], [banks[j]])
                    for j in range(4):
                        t_ = tmp[ei % 2]
                        ei += 1
                        P.op(act, lambda j=j, t_=t_, banks=banks: nc.scalar.activation(out=t_[:], in_=banks[j][:], func=AF.Relu), [banks[j]], [t_])
                        P.op(pool, lambda t_=t_, s_=s_, j=j: nc.gpsimd.tensor_tensor(out=uT[:, s_ * 4 + j, :], in0=t_[:], in1=t_[:], op=ALU.mult),
                             [t_], [uT])

            def st_D4(tb, c):
                X = xa[tb % 2]
                if c == 0 and (tb == 0 or tb == 4):
                    typ = typ_of(tb)
                    P.dma(sp, bc["g2"][:], modvec[l, typ, 5 * D:6 * D].partition_broadcast(128), [modvec], [bc["g2"]], bc["g2"])
                for kg in range(8):
                    w = wload(wd_bf[l, c, kg], wd_bf)
                    for k in range(8):
                        kk = kg * 8 + k
                        for tt in range(4):
                            P.op(pe, lambda w=w, tt=tt, k=k, kk=kk: nc.tensor.matmul(
                                psAcc[tt][:], lhsT=uT[:, kk, tt * 128:(tt + 1) * 128], rhs=w[:, k, :],
                                start=(kk == 0), stop=(kk == 63)), [w, uT], [psAcc[tt]])

            def st_D4r(tb, c):
                X = xa[tb % 2]
                for tt in range(4):
                    resid(X[tt], c, psAcc[tt], "g2", tic[0]); tic[0] += 1

            def st_D5(tb):
                X = xa[tb % 2]
                load_lnp(ln2_g, ln2_b)
                for tt in range(4):
                    ln_affine(X[tt], "lg", "lb")
                    t0 = tb * 512 + tt * 128
                    P.dma(pool, xdst[t0:t0 + 128, :], X[tt][:], [X[tt]], [xdst], X[tt])

            st_load(0)
            st_D1(0)
            for tt in range(4):
                st_D2(0, tt)
            for tb in range(NTB):
                st_D3(tb)
                nxt = tb + 1 < NTB
                if nxt:
                    st_load(tb + 1)
                st_D4(tb, 0)
                st_D4r(tb, 0)
                if nxt:
                    st_D1(tb + 1)
                for c in range(1, 4):
                    st_D4(tb, c)
                    if nxt:
                        st_D2(tb + 1, c - 1)
                        if c == 3:
                            st_D2(tb + 1, 3)
                    st_D4r(tb, c)
                st_D5(tb)
            SD.close()

    G.close()
    P.barrier()
    P.es.close()
    return nc


def mixers(P, nc, l, env):
    sel = env.get("debug")
    sel = sel[4:] if isinstance(sel, str) and sel.startswith("mix:") else "hy,s5,gla,ret"
    if "hy" in sel:
        mixer_hy(P, nc, l, env)
    if "s5" in sel:
        mixer_s5(P, nc, l, env)
    for gi in range(2):
        if ("gla", "ret")[gi] in sel:
            mixer_la(P, nc, l, env, gi)


def mixer_la(P, nc, l, env, gi):
    pe, dve, act, pool, sp = P.pe, P.dve, P.act, P.pool, P.sp
    qk = env["pF"]["qkg" if gi == 0 else "qkr"]
    vT = env["pT"]["gv" if gi == 0 else "rv"]
    gT = env["pT"]["gg" if gi == 0 else "rg"]
    pLr = env["pLr"]
    mixT = env["mixT"]
    ident_b = env["ident_b"]
    st_in = env["sg_in"] if gi == 0 else env["sr_in"]
    st_out = env["nsg"] if gi == 0 else env["nsr"]
    S = P.scope()
    rmask = S.sb("rmask", [128, LS])
    masks = S.sb("masks", [128, 2, 128], F32)
    P.dma(sp, masks[:], env["masks_in"][:].rearrange("m s c -> s m c"), [env["masks_in"]], [masks], masks)
    qdm = [[S.sb(f"qdm{d}{a}", [128, LS], BF16) for a in range(2)] for d in range(2)]
    kd = [S.sb(f"kd{d}", [128, LS], BF16) for d in range(2)]
    dec = S.sb("dec", [128, 2, 32])
    vb = S.sb("vb", [128, 16, 256], BF16)
    vst = S.sb("vst", [128, 4, 256])
    o_all = S.sb("o_all", [128, 16, 256])
    Rf = [S.sb(f"Rf{d}", [128, 256]) for d in range(2)]
    Sbf = [[S.sb(f"Sbf{d}{i}", [128, 256], BF16) for i in range(2)] for d in range(2)]
    kdt = [S.sb(f"kdt{d}", [128, 128], BF16) for d in range(2)]
    att = [S.sb(f"att{d}", [128, 2, 128], BF16) for d in range(2)]
    par = S.sb("par", [128, 2, 2])
    nw = S.sb("nw", [128, 128])
    psA = [S.ps(f"psA{d}", [128, 2, 128]) for d in range(2)]
    psO = [S.ps(f"psO{d}", [128, 256]) for d in range(2)]
    psUp = [S.ps(f"psUp{d}", [128, 256]) for d in range(2)]
    for d in range(2):
        P.op(dve, lambda d=d: nc.vector.memset(kdt[d][:], 0.0), [], [kdt[d]])
        P.op(dve, lambda d=d: nc.vector.memset(att[d][:], 0.0), [], [att[d]])
        for a in range(2):
            P.op(pool, lambda d=d, a=a: nc.gpsimd.memset(qdm[d][a][:], 0.0), [], [qdm[d][a]])
    P.op(pool, lambda: nc.gpsimd.memset(vb[:], 0.0), [], [vb])
    psKt = S.ps("psK", [128, 2, 128], BF16)
    psK = [psKt, psKt]
    P.op(dve, lambda: nc.vector.memset(rmask[:], 1.0), [], [rmask])
    P.op(dve, lambda: nc.vector.memset(rmask[:].rearrange("p (n t) -> p n t", t=128)[:, :, 0:1], 0.0), [], [rmask])
    if gi == 0:
        wg = S.sb("wg", [32, 2, 256])
        lrT = S.sb("lrT", [32, LS])
        P.op(dve, lambda: nc.vector.memset(wg[:], 0.0), [], [wg])
        for d in range(2):
            P.dma(sp, wg[d * 16:(d + 1) * 16, d, :], env["gla_w_gate"][l, d], [env["gla_w_gate"]], [wg], wg)
        with nc.allow_non_contiguous_dma(reason="tiny"):
            for d in range(2):
                P.dma(sp, par[:, d, :], env["gla_b_gate"][l, d].rearrange("(h p) -> p h", p=128), [env["gla_b_gate"]], [par], par)
        P.op(dve, lambda: nc.vector.tensor_scalar_mul(out=par[:], in0=par[:], scalar1=-1.0), [par], [par])
        P.dma(sp, nw[:], env["gla_norm_w"][l].partition_broadcast(128), [env["gla_norm_w"]], [nw], nw)
    else:
        for d in range(2):
            for hp in range(2):
                for a in range(2):
                    P.dma(sp, par[a * 64:(a + 1) * 64, d, hp:hp + 1],
                          env["ret_decay_exp"][l, d, 2 * hp + a:2 * hp + a + 1].partition_broadcast(64),
                          [env["ret_decay_exp"]], [par], par)
        P.op(act, lambda: nc.scalar.activation(out=par[:], in_=par[:], func=AF.Exp, scale=-float(np.log(2.0))), [par], [par])
        P.op(act, lambda: nc.scalar.activation(out=par[:], in_=par[:], func=AF.Ln, scale=-1.0, bias=1.0), [par], [par])
        P.op(dve, lambda: nc.vector.tensor_scalar_mul(out=par[:], in0=par[:], scalar1=-16.0), [par], [par])

    for si, (off, L) in enumerate(SEQS):
        nch = L // 128
        for hp in range(2):
            SP = P.scope()
            qk_f = SP.sb("qk_f", [128, 2, LS])
            lsp = SP.sb("lsp", [128, LS]); cs = SP.sb("cs", [128, LS]); bb = SP.sb("bb", [128, LS])
            Eb = SP.sb("Eb", [128, LS]); Enb = SP.sb("Enb", [128, LS])
            psL = SP.ps("psL", [128, 512])
            P.dma(sp, qk_f[:, 0, 0:L], qk[hp, :, off:off + L], [qk], [qk_f], qk_f)
            P.dma(sp, qk_f[:, 1, 0:L], qk[2 + hp, :, off:off + L], [qk], [qk_f], qk_f)
            if gi == 0:
                P.dma(sp, lrT[:, 0:L], pLr[:, off:off + L], [pLr], [lrT], lrT)
            for d in range(2):
                if gi == 0:
                    for t0 in range(0, L, 512):
                        n_ = min(512, L - t0)
                        P.op(pe, lambda d=d, t0=t0, n_=n_: nc.tensor.matmul(
                            psL[:, 0:n_], lhsT=wg[:, d, hp * 128:(hp + 1) * 128], rhs=lrT[:, t0:t0 + n_], start=True, stop=True),
                             [wg, lrT], [psL])
                        P.op(act, lambda d=d, t0=t0, n_=n_: nc.scalar.activation(
                            out=lsp[:, t0:t0 + n_], in_=psL[:, 0:n_], func=AF.Exp, scale=-1.0, bias=par[:, d, hp:hp + 1]),
                             [psL, par], [lsp])
                    P.op(act, lambda: nc.scalar.activation(out=lsp[:, 0:L], in_=lsp[:, 0:L], func=AF.Ln, bias=1.0), [lsp], [lsp])
                else:
                    P.op(dve, lambda d=d: nc.vector.tensor_scalar_mul(out=lsp[:, 0:L], in0=rmask[:, 0:L], scalar1=0.0), [rmask], [lsp])
                    P.op(dve, lambda d=d: nc.vector.tensor_scalar_add(out=lsp[:, 0:L], in0=lsp[:, 0:L], scalar1=par[:, d, hp:hp + 1]),
                         [lsp, par], [lsp])
                P.op(dve, lambda: nc.vector.tensor_tensor_scan(out=cs[:, 0:L], data0=rmask[:, 0:L], data1=lsp[:, 0:L], initial=0.0,
                                                               op0=ALU.mult, op1=ALU.add), [rmask, lsp], [cs])
                cs3 = cs[:, 0:L].rearrange("p (n t) -> p n t", t=128)
                if d == 0:
                    bsrc = cs
                else:
                    bsrc = bb
                    P.op(dve, lambda: nc.vector.tensor_tensor(out=bb[:, 0:L], in0=lsp[:, 0:L], in1=cs[:, 0:L], op=ALU.subtract), [lsp, cs], [bb])
                    P.op(dve, lambda cs3=cs3: nc.vector.tensor_tensor(
                        out=bb[:, 0:L].rearrange("p (n t) -> p n t", t=128), in0=bb[:, 0:L].rearrange("p (n t) -> p n t", t=128),
                        in1=cs3[:, :, 127:128].to_broadcast([128, nch, 128]), op=ALU.add), [bb, cs], [bb])
                P.op(act, lambda bsrc=bsrc: nc.scalar.activation(out=Eb[:, 0:L], in_=bsrc[:, 0:L], func=AF.Exp, scale=-1.0 / 16.0), [bsrc], [Eb])
                P.op(act, lambda bsrc=bsrc: nc.scalar.activation(out=Enb[:, 0:L], in_=bsrc[:, 0:L], func=AF.Exp, scale=1.0 / 16.0), [bsrc], [Enb])
                for a in range(2):
                    pa = slice(a * 64, (a + 1) * 64)
                    P.op(dve, lambda d=d, a=a, pa=pa: nc.vector.tensor_tensor(out=qdm[d][a][pa, 0:L], in0=qk_f[pa, 0, 0:L], in1=Eb[pa, 0:L], op=ALU.mult),
                         [qk_f, Eb], [qdm[d][a]])
                P.op(pool, lambda d=d: nc.gpsimd.tensor_tensor(out=kd[d][:, 0:L], in0=qk_f[:, 1, 0:L], in1=Enb[:, 0:L], op=ALU.mult), [qk_f, Enb], [kd[d]])
                col = 127 if d == 0 else 0
                P.op(dve, lambda d=d, col=col: nc.vector.tensor_copy(
                    out=dec[:, d, 0:nch], in_=Eb[:, 0:L].rearrange("p (n t) -> p n t", t=128)[:, :, col]), [Eb], [dec])
            SP.close()
            npc = max(1, nch // 4)
            cpp = min(4, nch)
            for pc in range(npc):
                t0 = off + pc * 512
                P.dma(sp, vst[:, 0:cpp, :], vT[t0:t0 + cpp * 128, hp * 256:(hp + 1) * 256].rearrange("(n t) c -> t n c", t=128),
                      [vT], [vst], vst)
                P.op(act, lambda pc=pc: nc.scalar.copy(out=vb[:, pc * 4:pc * 4 + cpp, :], in_=vst[:, 0:cpp, :]), [vst], [vb])
            P.op(dve, lambda: nc.vector.memset(o_all[:, 0:nch, :], 0.0), [], [o_all])
            for d in range(2):
                P.op(dve, lambda d=d: nc.vector.memset(Rf[d][:], 0.0), [], [Rf[d]])
                if si == 0:
                    for a in range(2):
                        P.dma(sp, Rf[d][a * 64:(a + 1) * 64, a * 128:(a + 1) * 128], st_in[l, d, 2 * hp + a], [st_in], [Rf[d]], Rf[d])
                P.op(act, lambda d=d: nc.scalar.copy(out=Sbf[d][0][:], in_=Rf[d][:]), [Rf[d]], [Sbf[d][0]])
            for i in range(nch):
                for d in range(2):
                    n = i if d == 0 else nch - 1 - i
                    npv = (i - 1) if d == 0 else nch - i
                    sl = slice(n * 128, (n + 1) * 128)
                    Scur = Sbf[d][i % 2]
                    Snxt = Sbf[d][(i + 1) % 2]
                    P.op(pe, lambda d=d, sl=sl: nc.tensor.transpose(psK[d][:, d, :], kd[d][:, sl], ident_b[:]), [kd[d], ident_b], [psK[d]])
                    P.op(act, lambda d=d: nc.scalar.copy(out=kdt[d][:], in_=psK[d][:, d, :]), [psK[d]], [kdt[d]])
                    for a in range(2):
                        P.op(pe, lambda d=d, a=a, sl=sl: nc.tensor.matmul(
                            psA[d][:, a, :], lhsT=kd[d][:, sl], rhs=qdm[d][a][:, sl], start=True, stop=True),
                             [kd[d], qdm[d][a]], [psA[d]])
                    P.op(pe, lambda d=d, n=n: nc.tensor.matmul(psUp[d][:], lhsT=kdt[d][:], rhs=vb[:, n, :], start=True, stop=True),
                         [kdt[d], vb], [psUp[d]])
                    P.op(dve, lambda d=d: nc.vector.tensor_tensor(
                        out=att[d][:], in0=psA[d][:], in1=masks[:, d, :].unsqueeze(1).to_broadcast([128, 2, 128]), op=ALU.mult),
                         [psA[d], masks], [att[d]])
                    if i == 0:
                        P.op(dve, lambda d=d: nc.vector.tensor_tensor(out=Rf[d][:], in0=Rf[d][:], in1=psUp[d][:], op=ALU.add),
                             [Rf[d], psUp[d]], [Rf[d]])
                    else:
                        P.op(dve, lambda d=d, npv=npv: nc.vector.scalar_tensor_tensor(
                            out=Rf[d][:], in0=Rf[d][:], scalar=dec[:, d, npv:npv + 1], in1=psUp[d][:], op0=ALU.mult, op1=ALU.add),
                             [Rf[d], psUp[d], dec], [Rf[d]])
                    if i < nch - 1:
                        P.op(act, lambda d=d, n=n, Snxt=Snxt: nc.scalar.activation(out=Snxt[:], in_=Rf[d][:], func=AF.Copy, scale=dec[:, d, n:n + 1]),
                             [Rf[d], dec], [Snxt])
                    for a in range(2):
                        P.op(pe, lambda d=d, a=a, n=n: nc.tensor.matmul(
                            psO[d][:, a * 128:(a + 1) * 128], lhsT=att[d][:, a, :], rhs=vb[:, n, a * 128:(a + 1) * 128], start=True, stop=False),
                             [att[d], vb], [psO[d]])
                        P.op(pe, lambda d=d, a=a, sl=sl, Scur=Scur: nc.tensor.matmul(
                            psO[d][:, a * 128:(a + 1) * 128], lhsT=qdm[d][a][:, sl], rhs=Scur[:, a * 128:(a + 1) * 128],
                            start=False, stop=True), [qdm[d][a], Scur], [psO[d]])
                    P.op(dve, lambda d=d, n=n: nc.vector.tensor_tensor(out=o_all[:, n, :], in0=psO[d][:], in1=o_all[:, n, :], op=ALU.add),
                         [psO[d], o_all], [o_all])
                    if i == nch - 1 and si > 0:
                        P.op(dve, lambda d=d, n=n: nc.vector.tensor_scalar_mul(out=Rf[d][:], in0=Rf[d][:], scalar1=dec[:, d, n:n + 1]),
                             [Rf[d], dec], [Rf[d]])
                        for a in range(2):
                            P.dma(pool, st_out[si - 1, l, d, 2 * hp + a], Rf[d][a * 64:(a + 1) * 64, a * 128:(a + 1) * 128], [Rf[d]], [st_out], Rf[d])
            SQ = P.scope()
            tsq = SQ.sb("tsq", [128, 8, 128]); gst = SQ.sb("gst", [128, 4, 256])
            s1 = SQ.sb("s1", [128, 8]); s2 = SQ.sb("s2", [128, 8]); rs = SQ.sb("rs", [128, 8])
            yb = SQ.sb("yb", [128, 4, 256], BF16); ysb = SQ.sb("ysb", [128, 2, 512], BF16)
            psY = SQ.ps("psY", [128, 2, 512], BF16)
            for pc in range(npc):
                t0 = off + pc * 512
                ntk = cpp * 128
                P.dma(sp, gst[:, 0:cpp, :], gT[t0:t0 + ntk, hp * 256:(hp + 1) * 256].rearrange("(n t) c -> t n c", t=128), [gT], [gst], gst)
                o3 = o_all[:, pc * 4:pc * 4 + cpp, :].rearrange("p n (a v) -> p (n a) v", a=2)
                na = cpp * 2
                P.op(dve, lambda o3=o3: nc.vector.tensor_reduce(out=s1[:, 0:na], in_=o3, axis=AX.X, op=ALU.add), [o_all], [s1])
                P.op(act, lambda o3=o3: nc.scalar.activation(out=tsq[:, 0:na, :], in_=o3, func=AF.Square), [o_all], [tsq])
                P.op(dve, lambda: nc.vector.tensor_reduce(out=s2[:, 0:na], in_=tsq[:, 0:na, :], axis=AX.X, op=ALU.add), [tsq], [s2])
                if gi == 0:
                    P.op(act, lambda: nc.scalar.activation(out=rs[:, 0:na], in_=s2[:, 0:na], func=AF.Ln, scale=1.0 / 128.0, bias=NORM_EPS), [s2], [rs])
                else:
                    P.op(dve, lambda: nc.vector.tensor_scalar_mul(out=s1[:, 0:na], in0=s1[:, 0:na], scalar1=1.0 / 128.0), [s1], [s1])
                    P.op(dve, lambda: nc.vector.tensor_tensor(out=rs[:, 0:na], in0=s1[:, 0:na], in1=s1[:, 0:na], op=ALU.mult), [s1], [rs])
                    P.op(dve, lambda: nc.vector.scalar_tensor_tensor(out=rs[:, 0:na], in0=s2[:, 0:na], scalar=1.0 / 128.0, in1=rs[:, 0:na],
                                                                     op0=ALU.mult, op1=ALU.subtract), [s2, rs], [rs])
                    P.op(act, lambda: nc.scalar.activation(out=rs[:, 0:na], in_=rs[:, 0:na], func=AF.Ln, bias=LN_EPS), [rs], [rs])
                    P.op(dve, lambda o3=o3: nc.vector.tensor_tensor(out=o3, in0=o3, in1=s1[:, 0:na].unsqueeze(2).to_broadcast([128, na, 128]),
                                                                    op=ALU.subtract), [o_all, s1], [o_all])
                P.op(act, lambda: nc.scalar.activation(out=rs[:, 0:na], in_=rs[:, 0:na], func=AF.Exp, scale=-0.5), [rs], [rs])
                P.op(dve, lambda o3=o3: nc.vector.tensor_tensor(out=o3, in0=o3, in1=rs[:, 0:na].unsqueeze(2).to_broadcast([128, na, 128]),
                                                                op=ALU.mult), [o_all, rs], [o_all])
                if gi == 0:
                    P.op(dve, lambda o3=o3: nc.vector.tensor_tensor(out=o3, in0=o3, in1=nw[:, :].unsqueeze(1).to_broadcast([128, na, 128]),
                                                                    op=ALU.mult), [o_all, nw], [o_all])
                P.op(dve, lambda pc=pc: nc.vector.tensor_tensor(out=yb[:, 0:cpp, :], in0=o_all[:, pc * 4:pc * 4 + cpp, :], in1=gst[:, 0:cpp, :],
                                                                op=ALU.mult), [o_all, gst], [yb])
                for n8 in range(cpp):
                    for a in range(2):
                        P.op(pe, lambda n8=n8, a=a: nc.tensor.transpose(psY[:, a, n8 * 128:(n8 + 1) * 128], yb[:, n8, a * 128:(a + 1) * 128],
                                                                        ident_b[:]), [yb, ident_b], [psY])
                P.op(act, lambda: nc.scalar.copy(out=ysb[:, :, 0:ntk], in_=psY[:, :, 0:ntk]), [psY], [ysb])
                for a in range(2):
                    P.dma(pool, mixT[gi * 4 + 2 * hp + a, :, t0:t0 + ntk], ysb[:, a, 0:ntk], [ysb], [mixT], ysb)
            SQ.close()
    S.close()


TWO_PI = float(2.0 * np.pi)
MAGIC = 12582912.0
CW1 = 6.28125
CW2 = float(2.0 * np.pi - 6.28125)


def mixer_s5(P, nc, l, env):
    pe, dve, act, pool, sp = P.pe, P.dve, P.act, P.pool, P.sp
    su = env["pF"]["su"]; mixT = env["mixT"]; ident_f = env["ident_f"]
    E = env
    S = P.scope()
    tp1 = S.sb("tp1", [128, LS])
    bt = [[S.sb(f"bt{i}{j}", [128, 512]) for j in range(6)] for i in range(1)]
    zb = [[S.sb(f"zb{d}{c}", [128, LS], BF16) for c in range(2)] for d in range(2)]
    ust = [S.sb("ust0", [32, LS])] * 2; ugp = [S.sb(f"ugp{i}", [128, LS], BF16) for i in range(2)]
    BTt = [S.sb(f"BTt{d}", [128, 2, 128], BF16) for d in range(2)]
    CPt = [S.sb(f"CPt{d}", [128, 2, 128], BF16) for d in range(2)]
    zz = S.sb("zz", [128, 4, LS], BF16)
    gw = S.sb("gw", [128, 4, 512], BF16)
    pv = {n: S.sb("pv_" + n, [128, 32]) for n in ("are", "aim", "st", "r", "th", "s", "c", "kre", "kim", "t0", "t1", "t2",
                                                   "h0re", "h0im", "hfre", "hfim", "zero", "ph")}
    dvec = S.sb("dvec", [128, 4]); gbv = S.sb("gbv", [128, 4])
    lastc = S.sb("lastc", [128, 32, 4])
    psB = [[S.ps(f"psB{i}{c}", [128, 512]) for c in range(2)] for i in range(2)]
    psY = [S.ps(f"psY{i}", [128, 512]) for i in range(4)]
    s5BT = env["s5BT"]; s5CP = env["s5CP"]

    def V(name, fn, reads, writes):
        P.op(dve, fn, reads, writes)

    P.dma(sp, tp1[:], E["tp1_in"][:].partition_broadcast(128), [E["tp1_in"]], [tp1], tp1)
    for i_ in range(2):
        P.op(pool, lambda i_=i_: nc.gpsimd.memset(ugp[i_][:], 0.0), [], [ugp[i_]])
    P.op(dve, lambda: nc.vector.memset(pv["zero"][:], 0.0), [], [pv["zero"]])
    SP = P.scope()
    BT = SP.sb("BT", [128, 32, 2, 128], BF16)
    Cpad = SP.sb("Cpad", [128, 32, 2, 128], BF16)
    P.op(pool, lambda: nc.gpsimd.memset(BT[:], 0.0), [], [BT])
    P.op(pool, lambda: nc.gpsimd.memset(Cpad[:], 0.0), [], [Cpad])
    Bc = [SP.sb(f"Bc{c}", [128, 32, 16]) for c in range(2)]
    Bb = [SP.sb(f"Bb{c}", [128, 32, 16]) for c in range(2)]
    Bsm = SP.sb("Bsm", [128, 32, 2, 32])
    Cc = [SP.sb(f"Cc{c}", [128, 32, 16]) for c in range(2)]
    tB = SP.sb("tB", [128, 32, 16])
    gwf = SP.sb("gwf", [128, 4, 512])
    with nc.allow_non_contiguous_dma(reason="small parameter layout transforms"):
        for g2 in range(2):
            pa = slice(g2 * 64, (g2 + 1) * 64)
            for d in range(2):
                ds_ = slice(d * 16, (d + 1) * 16)
                P.dma(sp, pv["are"][pa, ds_], E["s5_a_re"][l, d, g2::2].rearrange("gp p -> p gp"), [E["s5_a_re"]], [pv["are"]], pv["are"])
                P.dma(sp, pv["aim"][pa, ds_], E["s5_a_im"][l, d, g2::2].rearrange("gp p -> p gp"), [E["s5_a_im"]], [pv["aim"]], pv["aim"])
                P.dma(sp, pv["st"][pa, ds_], E["s5_log_step"][l, d, g2::2].partition_broadcast(64), [E["s5_log_step"]], [pv["st"]], pv["st"])
                P.dma(sp, pv["h0re"][pa, ds_], E["s5re_in"][l, d, g2::2].rearrange("gp p -> p gp"), [E["s5re_in"]], [pv["h0re"]], pv["h0re"])
                P.dma(sp, pv["h0im"][pa, ds_], E["s5im_in"][l, d, g2::2].rearrange("gp p -> p gp"), [E["s5im_in"]], [pv["h0im"]], pv["h0im"])
                P.dma(sp, Bc[0][pa, ds_, :], E["s5_b_re"][l, d, g2::2].rearrange("gp p i -> p gp i"), [E["s5_b_re"]], [Bc[0]], Bc[0])
                P.dma(sp, Bc[1][pa, ds_, :], E["s5_b_im"][l, d, g2::2].rearrange("gp p i -> p gp i"), [E["s5_b_im"]], [Bc[1]], Bc[1])
                for gp_ in range(16):
                    P.dma(sp, Cc[0][pa, d * 16 + gp_, :], E["s5_c_re"][l, d, 2 * gp_ + g2].rearrange("o p -> p o"), [E["s5_c_re"]], [Cc[0]], Cc[0])
                    P.dma(sp, Cc[1][pa, d * 16 + gp_, :], E["s5_c_im"][l, d, 2 * gp_ + g2].rearrange("o p -> p o"), [E["s5_c_im"]], [Cc[1]], Cc[1])
        P.dma(sp, dvec[:], E["s5_d"][l].rearrange("(c p) -> p c", p=128), [E["s5_d"]], [dvec], dvec)
        P.dma(sp, gbv[:], E["s5_glu_b"][l].rearrange("(c p) -> p c", p=128), [E["s5_glu_b"]], [gbv], gbv)
    P.dma(sp, gwf[:], E["s5_glu_w"][l].rearrange("(c p) n -> p c n", p=128), [E["s5_glu_w"]], [gwf], gwf)
    P.op(act, lambda: nc.scalar.copy(out=gw[:], in_=gwf[:]), [gwf], [gw])
    a = pv
    P.op(act, lambda: nc.scalar.activation(out=a["st"][:], in_=a["st"][:], func=AF.Exp), [a["st"]], [a["st"]])
    V("ar", lambda: nc.vector.tensor_tensor(out=a["r"][:], in0=a["are"][:], in1=a["st"][:], op=ALU.mult), [a["are"], a["st"]], [a["r"]])
    P.op(act, lambda: nc.scalar.activation(out=a["r"][:], in_=a["r"][:], func=AF.Exp), [a["r"]], [a["r"]])
    V("th", lambda: nc.vector.tensor_tensor(out=a["th"][:], in0=a["aim"][:], in1=a["st"][:], op=ALU.mult), [a["aim"], a["st"]], [a["th"]])

    def range_reduce(y, k, n, Y, K):
        V("rr1", lambda: nc.vector.tensor_scalar(out=k[:, 0:n], in0=y[:, 0:n], scalar1=1.0 / TWO_PI, scalar2=MAGIC, op0=ALU.mult, op1=ALU.add), [Y], [K])
        V("rr2", lambda: nc.vector.tensor_scalar_add(out=k[:, 0:n], in0=k[:, 0:n], scalar1=-MAGIC), [K], [K])
        V("rr3", lambda: nc.vector.scalar_tensor_tensor(out=y[:, 0:n], in0=k[:, 0:n], scalar=-CW1, in1=y[:, 0:n], op0=ALU.mult, op1=ALU.add), [K, Y], [Y])
        V("rr4", lambda: nc.vector.scalar_tensor_tensor(out=y[:, 0:n], in0=k[:, 0:n], scalar=-CW2, in1=y[:, 0:n], op0=ALU.mult, op1=ALU.add), [K, Y], [Y])

    def sincos(y, n, sn, cs, Y, SN, CS):
        P.op(act, lambda: nc.scalar.activation(out=sn[:, 0:n], in_=y[:, 0:n], func=AF.Sin, scale=0.999998), [Y], [SN])
        P.op(act, lambda: nc.scalar.activation(out=cs[:, 0:n], in_=y[:, 0:n], func=AF.Sin, scale=0.5), [Y], [CS])
        P.op(act, lambda: nc.scalar.activation(out=cs[:, 0:n], in_=cs[:, 0:n], func=AF.Square), [CS], [CS])
        P.op(pool, lambda: nc.gpsimd.tensor_scalar(out=cs[:, 0:n], in0=cs[:, 0:n], scalar1=-2.0, scalar2=1.0, op0=ALU.mult, op1=ALU.add), [CS], [CS])

    range_reduce(a["th"], a["t0"], 32, a["th"], a["t0"])
    sincos(a["th"], 32, a["s"], a["c"], a["th"], a["s"], a["c"])
    V("lre", lambda: nc.vector.tensor_tensor(out=a["t0"][:], in0=a["r"][:], in1=a["c"][:], op=ALU.mult), [a["r"], a["c"]], [a["t0"]])
    V("lre1", lambda: nc.vector.tensor_scalar_add(out=a["t0"][:], in0=a["t0"][:], scalar1=-1.0), [a["t0"]], [a["t0"]])
    V("lim", lambda: nc.vector.tensor_tensor(out=a["t1"][:], in0=a["r"][:], in1=a["s"][:], op=ALU.mult), [a["r"], a["s"]], [a["t1"]])
    V("den", lambda: nc.vector.tensor_tensor(out=a["t2"][:], in0=a["are"][:], in1=a["are"][:], op=ALU.mult), [a["are"]], [a["t2"]])
    V("den2", lambda: nc.vector.tensor_tensor(out=a["kre"][:], in0=a["aim"][:], in1=a["aim"][:], op=ALU.mult), [a["aim"]], [a["kre"]])
    V("den3", lambda: nc.vector.tensor_tensor(out=a["t2"][:], in0=a["t2"][:], in1=a["kre"][:], op=ALU.add), [a["t2"], a["kre"]], [a["t2"]])
    V("rden", lambda: nc.vector.reciprocal(out=a["t2"][:], in_=a["t2"][:]), [a["t2"]], [a["t2"]])
    V("k1", lambda: nc.vector.tensor_tensor(out=a["kre"][:], in0=a["t0"][:], in1=a["are"][:], op=ALU.mult), [a["t0"], a["are"]], [a["kre"]])
    V("k2", lambda: nc.vector.tensor_tensor(out=a["kim"][:], in0=a["t1"][:], in1=a["aim"][:], op=ALU.mult), [a["t1"], a["aim"]], [a["kim"]])
    V("k3", lambda: nc.vector.tensor_tensor(out=a["kre"][:], in0=a["kre"][:], in1=a["kim"][:], op=ALU.add), [a["kre"], a["kim"]], [a["kre"]])
    V("k4", lambda: nc.vector.tensor_tensor(out=a["kim"][:], in0=a["t1"][:], in1=a["are"][:], op=ALU.mult), [a["t1"], a["are"]], [a["kim"]])
    V("k5", lambda: nc.vector.tensor_tensor(out=a["t1"][:], in0=a["t0"][:], in1=a["aim"][:], op=ALU.mult), [a["t0"], a["aim"]], [a["t1"]])
    V("k6", lambda: nc.vector.tensor_tensor(out=a["kim"][:], in0=a["kim"][:], in1=a["t1"][:], op=ALU.subtract), [a["kim"], a["t1"]], [a["kim"]])
    V("k7", lambda: nc.vector.tensor_tensor(out=a["kre"][:], in0=a["kre"][:], in1=a["t2"][:], op=ALU.mult), [a["kre"], a["t2"]], [a["kre"]])
    V("k8", lambda: nc.vector.tensor_tensor(out=a["kim"][:], in0=a["kim"][:], in1=a["t2"][:], op=ALU.mult), [a["kim"], a["t2"]], [a["kim"]])
    kre_b = a["kre"][:, :].unsqueeze(2).to_broadcast([128, 32, 16])
    kim_b = a["kim"][:, :].unsqueeze(2).to_broadcast([128, 32, 16])
    V("b1", lambda: nc.vector.tensor_tensor(out=Bb[0][:], in0=Bc[0][:], in1=kre_b, op=ALU.mult), [Bc[0], a["kre"]], [Bb[0]])
    V("b2", lambda: nc.vector.tensor_tensor(out=tB[:], in0=Bc[1][:], in1=kim_b, op=ALU.mult), [Bc[1], a["kim"]], [tB])
    V("b3", lambda: nc.vector.tensor_tensor(out=Bb[0][:], in0=Bb[0][:], in1=tB[:], op=ALU.subtract), [Bb[0], tB], [Bb[0]])
    V("b4", lambda: nc.vector.tensor_tensor(out=Bb[1][:], in0=Bc[1][:], in1=kre_b, op=ALU.mult), [Bc[1], a["kre"]], [Bb[1]])
    V("b5", lambda: nc.vector.tensor_tensor(out=tB[:], in0=Bc[0][:], in1=kim_b, op=ALU.mult), [Bc[0], a["kim"]], [tB])
    V("b6", lambda: nc.vector.tensor_tensor(out=Bb[1][:], in0=Bb[1][:], in1=tB[:], op=ALU.add), [Bb[1], tB], [Bb[1]])
    V("bsm0", lambda: nc.vector.memset(Bsm[:], 0.0), [], [Bsm])
    for g2 in range(2):
        pa = slice(g2 * 64, (g2 + 1) * 64)
        for c in range(2):
            V("bsm", lambda pa=pa, c=c, g2=g2: nc.vector.tensor_copy(out=Bsm[pa, :, c, g2 * 16:(g2 + 1) * 16], in_=Bb[c][pa, :, :]), [Bb[c]], [Bsm])
    for j in range(32):
        for c in range(2):
            ps = psB[(j * 2 + c) % 2][0]
            P.op(pe, lambda ps=ps, j=j, c=c: nc.tensor.transpose(ps[0:32, 0:128], Bsm[:, j, c, :], ident_f[:]), [Bsm, ident_f], [ps])
            P.op(act, lambda ps=ps, j=j, c=c: nc.scalar.copy(out=BT[0:32, j, c, :], in_=ps[0:32, 0:128]), [ps], [BT])
    for c in range(2):
        if c == 1:
            V("cneg", lambda: nc.vector.tensor_scalar_mul(out=Cc[1][:], in0=Cc[1][:], scalar1=-1.0), [Cc[1]], [Cc[1]])
        for g2 in range(2):
            pa = slice(g2 * 64, (g2 + 1) * 64)
            for q in range(4):
                src = Cc[c][pa, :, :].rearrange("p (dg q) o -> p dg q o", q=4)[:, :, q, :]
                dst = Cpad[pa, :, c, q * 32 + g2 * 16:q * 32 + g2 * 16 + 16].rearrange("p (dg q) o -> p dg q o", q=4)[:, :, q, :]
                V("cpad", lambda src=src, dst=dst: nc.vector.tensor_copy(out=dst, in_=src), [Cc[c]], [Cpad])
    P.dma(pool, s5BT[:].rearrange("j p c m -> p j c m"), BT[:], [BT], [s5BT], BT)
    P.dma(pool, s5CP[:].rearrange("j p c m -> p j c m"), Cpad[:], [Cpad], [s5CP], Cpad)
    V("ph", lambda: nc.vector.tensor_scalar_mul(out=a["ph"][:], in0=a["th"][:], scalar1=1.0 / TWO_PI), [a["th"]], [a["ph"]])
    SP.close()
    SM = P.scope()
    CS = [[SM.sb(f"cs{d}{i}", [128, LS]) for i in range(2)] for d in range(2)]
    WD = [[SM.sb(f"wd{d}{i}", [128, LS]) for i in range(2)] for d in range(2)]
    MM = [[SM.sb(f"mm{d}{i}", [128, LS]) for i in range(2)] for d in range(2)]
    PP = [[SM.sb(f"pp{d}{i}", [128, LS], BF16) for i in range(2)] for d in range(2)]
    PQ = [[SM.sb(f"pq{d}{i}", [128, LS], BF16) for i in range(2)] for d in range(2)]
    W = [WD[0][0], WD[0][1], MM[0][0], MM[0][1]]

    for si, (off, L) in enumerate(SEQS):
        nb = max(1, L // 512)
        bw = min(512, L)
        h0r = (a["h0re"] if si == 0 else a["zero"]); h0i = (a["h0im"] if si == 0 else a["zero"])

        def stage_T(cc, gq, d):
            gp = cc * 4 + gq
            j = d * 16 + gp
            cosA, sinA = CS[d]
            ang, kk = WD[d]
            if d == 0:
                P.dma(sp, ust[gq % 2][:, 0:L], su[cc, gq * 32:(gq + 1) * 32, off:off + L], [su], [ust[gq % 2]], ust[gq % 2])
                P.op(act, lambda: nc.scalar.copy(out=ugp[gq % 2][0:32, 0:L], in_=ust[gq % 2][:, 0:L]), [ust[gq % 2]], [ugp[gq % 2]])
            P.dma(sp, BTt[d][:], s5BT[j], [s5BT], [BTt[d]], BTt[d])
            P.dma(sp, CPt[d][:], s5CP[j], [s5CP], [CPt[d]], CPt[d])
            P.op(act, lambda: nc.scalar.activation(out=kk[:, 0:L], in_=tp1[:, 0:L], func=AF.Identity, scale=a["ph"][:, j:j + 1], bias=MAGIC), [tp1, a["ph"]], [kk])
            P.op(act, lambda: nc.scalar.activation(out=kk[:, 0:L], in_=kk[:, 0:L], func=AF.Identity, bias=-MAGIC), [kk], [kk])
            V("fr", lambda: nc.vector.scalar_tensor_tensor(out=ang[:, 0:L], in0=tp1[:, 0:L], scalar=a["ph"][:, j:j + 1], in1=kk[:, 0:L],
                                                           op0=ALU.mult, op1=ALU.subtract), [tp1, a["ph"], kk], [ang])
            P.op(act, lambda: nc.scalar.activation(out=sinA[:, 0:L], in_=ang[:, 0:L], func=AF.Sin, scale=TWO_PI * 0.999998), [ang], [sinA])
            P.op(act, lambda: nc.scalar.activation(out=cosA[:, 0:L], in_=ang[:, 0:L], func=AF.Sin, scale=TWO_PI * 0.5), [ang], [cosA])
            P.op(act, lambda: nc.scalar.activation(out=cosA[:, 0:L], in_=cosA[:, 0:L], func=AF.Square), [cosA], [cosA])
            P.op(act, lambda: nc.scalar.activation(out=cosA[:, 0:L], in_=cosA[:, 0:L], func=AF.Identity, scale=-2.0, bias=1.0), [cosA], [cosA])

        def stage_GS(cc, gq, d):
            gp = cc * 4 + gq
            j = d * 16 + gp
            cosA, sinA = CS[d]
            gre, gim = WD[d]
            mre, mim = MM[d]
            ug = ugp[gq % 2]
            for tb in range(nb):
                t0 = tb * bw
                pb = psB[tb % 2]
                e = bt[0]
                for c in range(2):
                    P.op(pe, lambda c=c, t0=t0, pb=pb: nc.tensor.matmul(pb[c][:, 0:bw], lhsT=BTt[d][:, c, :], rhs=ug[:, t0:t0 + bw],
                                                                        start=True, stop=True), [BTt[d], ug], [pb[c]])
                if d == 0:
                    sl = lambda X, t0=t0: X[:, t0:t0 + bw]
                else:
                    sl = lambda X, t0=t0: X[:, L - t0 - bw:L - t0][:, ::-1]
                P.op(act, lambda pb=pb, e=e: nc.scalar.copy(out=e[0][:, 0:bw], in_=pb[0][:, 0:bw]), [pb[0]], [e[0]])
                P.op(act, lambda pb=pb, e=e: nc.scalar.copy(out=e[1][:, 0:bw], in_=pb[1][:, 0:bw]), [pb[1]], [e[1]])
                V("g1", lambda sl=sl, e=e: nc.vector.tensor_tensor(out=e[2][:, 0:bw], in0=e[0][:, 0:bw], in1=sl(cosA), op=ALU.mult), [e[0], cosA], [e[2]])
                V("g2", lambda sl=sl, e=e: nc.vector.tensor_tensor(out=e[3][:, 0:bw], in0=e[1][:, 0:bw], in1=sl(sinA), op=ALU.mult), [e[1], sinA], [e[3]])
                P.op(pool, lambda sl=sl, e=e: nc.gpsimd.tensor_tensor(out=e[4][:, 0:bw], in0=e[1][:, 0:bw], in1=sl(cosA), op=ALU.mult), [e[1], cosA], [e[4]])
                P.op(pool, lambda sl=sl, e=e: nc.gpsimd.tensor_tensor(out=e[5][:, 0:bw], in0=e[0][:, 0:bw], in1=sl(sinA), op=ALU.mult), [e[0], sinA], [e[5]])
                V("g5", lambda sl=sl, e=e: nc.vector.tensor_tensor(out=sl(gre), in0=e[2][:, 0:bw], in1=e[3][:, 0:bw], op=ALU.add), [e[2], e[3]], [gre])
                P.op(pool, lambda sl=sl, e=e: nc.gpsimd.tensor_tensor(out=sl(gim), in0=e[4][:, 0:bw], in1=e[5][:, 0:bw], op=ALU.subtract), [e[4], e[5]], [gim])
            rb = a["r"][:, j:j + 1].to_broadcast([128, L])
            V("scr", lambda: nc.vector.tensor_tensor_scan(out=mre[:, 0:L], data0=rb, data1=gre[:, 0:L], initial=h0r[:, j:j + 1], op0=ALU.mult, op1=ALU.add),
              [gre, a["r"], h0r], [mre])
            V("sci", lambda: nc.vector.tensor_tensor_scan(out=mim[:, 0:L], data0=rb, data1=gim[:, 0:L], initial=h0i[:, j:j + 1], op0=ALU.mult, op1=ALU.add),
              [gim, a["r"], h0i], [mim])
            if si > 0:
                for ci, X in enumerate((cosA, sinA, mre, mim)):
                    P.op(act, lambda ci=ci, X=X: nc.scalar.copy(out=lastc[:, j, ci:ci + 1], in_=X[:, L - 1:L]), [X], [lastc])

        def stage_Z(cc, gq, d):
            cosA, sinA = CS[d]
            mre, mim = MM[d]
            p1, p2 = PP[d]
            p3, p4 = PQ[d]
            zo = (lambda X: X[:, 0:L]) if d == 0 else (lambda X: X[:, 0:L][:, ::-1])
            V("p1", lambda: nc.vector.tensor_tensor(out=p1[:, 0:L], in0=cosA[:, 0:L], in1=mre[:, 0:L], op=ALU.mult), [cosA, mre], [p1])
            V("p2", lambda: nc.vector.tensor_tensor(out=p2[:, 0:L], in0=sinA[:, 0:L], in1=mim[:, 0:L], op=ALU.mult), [sinA, mim], [p2])
            P.op(pool, lambda: nc.gpsimd.tensor_tensor(out=p3[:, 0:L], in0=sinA[:, 0:L], in1=mre[:, 0:L], op=ALU.mult), [sinA, mre], [p3])
            P.op(pool, lambda: nc.gpsimd.tensor_tensor(out=p4[:, 0:L], in0=cosA[:, 0:L], in1=mim[:, 0:L], op=ALU.mult), [cosA, mim], [p4])
            V("zre", lambda: nc.vector.tensor_tensor(out=zo(zb[d][0]), in0=p1[:, 0:L], in1=p2[:, 0:L], op=ALU.subtract), [p1, p2], [zb[d][0]])
            P.op(pool, lambda: nc.gpsimd.tensor_tensor(out=zo(zb[d][1]), in0=p3[:, 0:L], in1=p4[:, 0:L], op=ALU.add), [p3, p4], [zb[d][1]])
            for tb in range(nb):
                t0 = tb * bw
                for c in range(2):
                    first = (gq == 0 and d == 0 and c == 0)
                    last = (gq == 3 and d == 1 and c == 1)
                    P.op(pe, lambda tb=tb, t0=t0, c=c, first=first, last=last: nc.tensor.matmul(
                        psY[tb][:, 0:bw], lhsT=CPt[d][:, c, :], rhs=zb[d][c][:, t0:t0 + bw], start=first, stop=last), [CPt[d], zb[d][c]], [psY[tb]])

        for cc in range(4):
            its = [(gq, d) for gq in range(4) for d in range(2)]
            stage_T(cc, *its[0])
            for ii, (gq, d) in enumerate(its):
                stage_GS(cc, gq, d)
                if ii + 1 < len(its):
                    stage_T(cc, *its[ii + 1])
                stage_Z(cc, gq, d)
            uT, yv, w1, w2 = W[0], W[1], W[2], W[3]
            P.dma(sp, uT[:, 0:L], su[cc, :, off:off + L], [su], [uT], uT)
            for tb in range(nb):
                t0 = tb * bw
                V("yv", lambda tb=tb, t0=t0: nc.vector.scalar_tensor_tensor(out=yv[:, t0:t0 + bw], in0=uT[:, t0:t0 + bw], scalar=dvec[:, cc:cc + 1],
                                                                          in1=psY[tb][:, 0:bw], op0=ALU.mult, op1=ALU.add), [uT, dvec, psY[tb]], [yv])
            P.op(act, lambda: nc.scalar.activation(out=w1[:, 0:L], in_=yv[:, 0:L], func=AF.Square), [yv], [w1])
            P.op(pool, lambda: nc.gpsimd.tensor_scalar(out=w1[:, 0:L], in0=w1[:, 0:L], scalar1=0.044715, scalar2=1.0, op0=ALU.mult, op1=ALU.add), [w1], [w1])
            P.op(pool, lambda: nc.gpsimd.tensor_tensor(out=w1[:, 0:L], in0=w1[:, 0:L], in1=yv[:, 0:L], op=ALU.mult), [w1, yv], [w1])
            P.op(act, lambda: nc.scalar.activation(out=w2[:, 0:L], in_=w1[:, 0:L], func=AF.Sigmoid, scale=float(2.0 * np.sqrt(2.0 / np.pi))), [w1], [w2])
            V("zz", lambda: nc.vector.tensor_tensor(out=zz[:, cc, 0:L], in0=yv[:, 0:L], in1=w2[:, 0:L], op=ALU.mult), [yv, w2], [zz])
        for oc in range(4):
            for tb in range(nb):
                t0 = tb * bw
                ps = psB[tb % 2][0]
                for cc in range(4):
                    P.op(pe, lambda ps=ps, cc=cc, oc=oc, t0=t0: nc.tensor.matmul(ps[:, 0:bw], lhsT=gw[:, cc, oc * 128:(oc + 1) * 128], rhs=zz[:, cc, t0:t0 + bw],
                                                                               start=(cc == 0), stop=(cc == 3)), [gw, zz], [ps])
                sg = bt[0][tb % 2]
                P.op(act, lambda ps=ps, sg=sg, oc=oc: nc.scalar.activation(out=sg[:, 0:bw], in_=ps[:, 0:bw], func=AF.Sigmoid, bias=gbv[:, oc:oc + 1]), [ps, gbv], [sg])
                ot = bt[0][2 + tb % 2]
                V("glu", lambda sg=sg, ot=ot, oc=oc, t0=t0: nc.vector.tensor_tensor(out=ot[:, 0:bw].bitcast(BF16)[:, 0:bw], in0=zz[:, oc, t0:t0 + bw], in1=sg[:, 0:bw], op=ALU.mult),
                  [zz, sg], [ot])
                P.dma(pool, mixT[8 + oc, :, off + t0:off + t0 + bw], ot[:, 0:bw].bitcast(BF16)[:, 0:bw], [ot], [mixT], ot)
        if si > 0:
            lc = lastc
            h = a
            V("f1", lambda: nc.vector.tensor_tensor(out=h["t0"][:], in0=lc[:, :, 0], in1=lc[:, :, 2], op=ALU.mult), [lc], [h["t0"]])
            V("f2", lambda: nc.vector.tensor_tensor(out=h["t1"][:], in0=lc[:, :, 1], in1=lc[:, :, 3], op=ALU.mult), [lc], [h["t1"]])
            V("f3", lambda: nc.vector.tensor_tensor(out=h["hfre"][:], in0=h["t0"][:], in1=h["t1"][:], op=ALU.subtract), [h["t0"], h["t1"]], [h["hfre"]])
            V("f4", lambda: nc.vector.tensor_tensor(out=h["t0"][:], in0=lc[:, :, 1], in1=lc[:, :, 2], op=ALU.mult), [lc], [h["t0"]])
            V("f5", lambda: nc.vector.tensor_tensor(out=h["t1"][:], in0=lc[:, :, 0], in1=lc[:, :, 3], op=ALU.mult), [lc], [h["t1"]])
            V("f6", lambda: nc.vector.tensor_tensor(out=h["hfim"][:], in0=h["t0"][:], in1=h["t1"][:], op=ALU.add), [h["t0"], h["t1"]], [h["hfim"]])
            with nc.allow_non_contiguous_dma(reason="state layout"):
                for g2 in range(2):
                    pa = slice(g2 * 64, (g2 + 1) * 64)
                    for d in range(2):
                        ds_ = slice(d * 16, (d + 1) * 16)
                        P.dma(sp, E["ns5re"][si - 1, l, d, g2::2].rearrange("gp p -> p gp"), h["hfre"][pa, ds_], [h["hfre"]], [E["ns5re"]], h["hfre"])
                        P.dma(sp, E["ns5im"][si - 1, l, d, g2::2].rearrange("gp p -> p gp"), h["hfim"][pa, ds_], [h["hfim"]], [E["ns5im"]], h["hfim"])
    SM.close()
    S.close()


def mixer_hy(P, nc, l, env):
    pe, dve, act, pool, sp = P.pe, P.dve, P.act, P.pool, P.sp
    E = env
    hy = E["pF"]["hy"]; mixT = E["mixT"]; ident_b = E["ident_b"]; ident_f = E["ident_f"]
    S = P.scope()
    cw = S.sb("cw", [128, 3, 12]); cb = S.sb("cb", [128, 12]); dvv = S.sb("dvv", [128, 2, 4])
    with nc.allow_non_contiguous_dma(reason="small parameter layout transforms"):
        for k in range(3):
            P.dma(sp, cw[:, k, :], E["hy_conv_w"][l, k].rearrange("(c p) -> p c", p=128), [E["hy_conv_w"]], [cw], cw)
        P.dma(sp, cb[:], E["hy_conv_b"][l].rearrange("(c p) -> p c", p=128), [E["hy_conv_b"]], [cb], cb)
        for o in range(2):
            P.dma(sp, dvv[:, o, :], E["hy_d"][l, o].rearrange("(c p) -> p c", p=128), [E["hy_d"]], [dvv], dvv)

    def V(fn, reads, writes):
        P.op(dve, fn, reads, writes)

    def fwd_dft(cfg, src, emit, bufs):
        tC, tS, psC, psS = bufs
        for fk in range(cfg["nfk"]):
            c_, s_ = tC[fk % 2], tS[fk % 2]
            P.dma(sp, c_[:, 0:cfg["ntc"], :], cfg["tFC"][fk], [cfg["tFC"]], [c_], c_)
            P.dma(sp, s_[:, 0:cfg["ntc"], :], cfg["tFS"][fk], [cfg["tFS"]], [s_], s_)
            pc, ps_ = psC[fk % 2], psS[fk % 2]
            for tc in range(cfg["ntc"]):
                P.op(pe, lambda c_=c_, pc=pc, tc=tc: nc.tensor.matmul(pc[:], lhsT=c_[:, tc, :], rhs=src[:, tc, :], start=(tc == 0), stop=(tc == cfg["ntc"] - 1)),
                     [c_, src], [pc])
            for tc in range(cfg["ntc"]):
                P.op(pe, lambda s_=s_, ps_=ps_, tc=tc: nc.tensor.matmul(ps_[:], lhsT=s_[:, tc, :], rhs=src[:, tc, :], start=(tc == 0), stop=(tc == cfg["ntc"] - 1)),
                     [s_, src], [ps_])
            emit(fk, pc, ps_)

    cfgs = {}
    for L in (LS, LP):
        sfx = str(L)
        cfgs[L] = dict(L=L, ntc=L // 128, nfk=L // 128 + 1, nblk=max(1, L // 512), bw=min(512, L),
                       tFC=E["tFC" + sfx], tFS=E["tFS" + sfx], tIC=E["tIC" + sfx], tIS=E["tIS" + sfx],
                       featsT=E["featsT" + sfx], featsTr=E["featsTr" + sfx], negtv=E["negtv" + sfx], negtvr=E["negtvr" + sfx],
                       wk=E["wk" + sfx], sgw=E["sgw" + sfx], Fs=E["Fs" + sfx])

    for L in (LS, LP):
        cfg = cfgs[L]
        ntc, nfk = cfg["ntc"], cfg["nfk"]
        SF = P.scope()
        w1p = SF.sb("w1p", [64, 64]); w2 = SF.sb("w2", [64, 64]); w3 = SF.sb("w3", [64, 2048])
        fq = SF.sb("fq", [64, 1]); fb1 = SF.sb("fb1", [64, 1]); fb2 = SF.sb("fb2", [64, 1])
        ft = SF.sb("ft", [64, LS]); h1 = SF.sb("h1", [64, LS]); h2 = SF.sb("h2", [64, LS]); kk = SF.sb("kkf", [64, LS])
        adec = SF.sb("adec", [128, 512]); ew = SF.sb("ew", [128, 512]); fa = SF.sb("fa", [128, 512])
        fw = SF.sb("fw", [128, 16, 512])
        ff = [SF.sb(f"ff{d}", [128, 16, 512], BF16) for d in range(2)]
        fpm = [SF.sb(f"fpm{d}", [128, 16, 512], BF16) for d in range(2)]
        ntv = SF.sb("ntv", [128, 2, 16]); wkv = SF.sb("wkv", [128, 17]); sgv = SF.sb("sgv", [128, 17])
        ones = SF.sb("ones", [128, 128]); rcp = SF.sb("rcp", [128, 512])
        tC = [SF.sb(f"tC{i}", [128, 16, 128], BF16) for i in range(2)]
        tS = [SF.sb(f"tS{i}", [128, 16, 128], BF16) for i in range(2)]
        fo = [SF.sb(f"fo{i}", [128, 2, 512]) for i in range(2)]
        tmpc = SF.sb("tmpc", [128, 2, 512])
        psM = SF.ps("psM", [128, 512])
        psN = SF.ps("psN", [128, 512])
        psCf = [SF.ps(f"psCf{i}", [128, 512]) for i in range(2)]
        psSf = [SF.ps(f"psSf{i}", [128, 512]) for i in range(2)]
        psCr = SF.ps("psCr", [128, 512]); psSr = SF.ps("psSr", [128, 512])
        V(lambda: nc.vector.memset(w1p[:], 0.0), [], [w1p])
        V(lambda: nc.vector.memset(ones[:], 1.0), [], [ones])
        P.dma(sp, w1p[0:33, :], E["hy_f_w1"][l], [E["hy_f_w1"]], [w1p], w1p)
        P.dma(sp, w2[:], E["hy_f_w2"][l], [E["hy_f_w2"]], [w2], w2)
        P.dma(sp, w3[:], E["hy_f_w3"][l], [E["hy_f_w3"]], [w3], w3)
        with nc.allow_non_contiguous_dma(reason="tiny"):
            P.dma(sp, fq[:], E["hy_f_freq"][l].rearrange("(p o) -> p o", o=1), [E["hy_f_freq"]], [fq], fq)
            P.dma(sp, fb1[:], E["hy_f_b1"][l].rearrange("(p o) -> p o", o=1), [E["hy_f_b1"]], [fb1], fb1)
            P.dma(sp, fb2[:], E["hy_f_b2"][l].rearrange("(p o) -> p o", o=1), [E["hy_f_b2"]], [fb2], fb2)
        P.dma(sp, ntv[:, 0, 0:ntc], cfg["negtv"][:, :], [cfg["negtv"]], [ntv], ntv)
        P.dma(sp, ntv[:, 1, 0:ntc], cfg["negtvr"][:, :], [cfg["negtvr"]], [ntv], ntv)
        P.dma(sp, wkv[:, 0:nfk], cfg["wk"][:, :], [cfg["wk"]], [wkv], wkv)
        P.dma(sp, sgv[:, 0:nfk], cfg["sgw"][:, :], [cfg["sgw"]], [sgv], sgv)
        V(lambda: nc.vector.tensor_tensor(out=fb1[:], in0=fb1[:], in1=fq[:], op=ALU.mult), [fb1, fq], [fb1])
        V(lambda: nc.vector.tensor_tensor(out=fb2[:], in0=fb2[:], in1=fq[:], op=ALU.mult), [fb2, fq], [fb2])

        def rr_sin(y, n):
            V(lambda: nc.vector.tensor_scalar(out=kk[:, 0:n], in0=y[:, 0:n], scalar1=1.0 / TWO_PI, scalar2=MAGIC, op0=ALU.mult, op1=ALU.add), [y], [kk])
            V(lambda: nc.vector.tensor_scalar_add(out=kk[:, 0:n], in0=kk[:, 0:n], scalar1=-MAGIC), [kk], [kk])
            V(lambda: nc.vector.scalar_tensor_tensor(out=y[:, 0:n], in0=kk[:, 0:n], scalar=-CW1, in1=y[:, 0:n], op0=ALU.mult, op1=ALU.add), [kk, y], [y])
            V(lambda: nc.vector.scalar_tensor_tensor(out=y[:, 0:n], in0=kk[:, 0:n], scalar=-CW2, in1=y[:, 0:n], op0=ALU.mult, op1=ALU.add), [kk, y], [y])
            P.op(act, lambda: nc.scalar.activation(out=y[:, 0:n], in_=y[:, 0:n], func=AF.Sin, scale=0.999998), [y], [y])

        for d in range(2):
            P.dma(sp, ft[:, 0:L], (cfg["featsT"] if d == 0 else cfg["featsTr"])[:, :], [cfg["featsT"], cfg["featsTr"]], [ft], ft)
            bw = cfg["bw"]
            for tb in range(cfg["nblk"]):
                t0 = tb * bw
                P.op(pe, lambda t0=t0: nc.tensor.matmul(psM[0:64, 0:bw], lhsT=w1p[:], rhs=ft[:, t0:t0 + bw], start=True, stop=True), [w1p, ft], [psM])
                V(lambda t0=t0: nc.vector.tensor_scalar(out=h1[:, t0:t0 + bw], in0=psM[0:64, 0:bw], scalar1=fq[:, 0:1], scalar2=fb1[:, 0:1],
                                                        op0=ALU.mult, op1=ALU.add), [psM, fq, fb1], [h1])
            rr_sin(h1, L)
            for tb in range(cfg["nblk"]):
                t0 = tb * bw
                P.op(pe, lambda t0=t0: nc.tensor.matmul(psM[0:64, 0:bw], lhsT=w2[:], rhs=h1[:, t0:t0 + bw], start=True, stop=True), [w2, h1], [psM])
                V(lambda t0=t0: nc.vector.tensor_scalar(out=h2[:, t0:t0 + bw], in0=psM[0:64, 0:bw], scalar1=fq[:, 0:1], scalar2=fb2[:, 0:1],
                                                        op0=ALU.mult, op1=ALU.add), [psM, fq, fb2], [h2])
            rr_sin(h2, L)
            for o in range(2):
                col0 = (d * 2 + o) * 512
                P.dma(sp, adec[:], E["hy_decay"][l, col0:col0 + 512].partition_broadcast(128), [E["hy_decay"]], [adec], adec)
                P.op(act, lambda: nc.scalar.activation(out=adec[:], in_=adec[:], func=AF.Abs), [adec], [adec])
                for tc in range(ntc):
                    P.op(pe, lambda tc=tc, col0=col0: nc.tensor.matmul(psM[:], lhsT=h2[:, tc * 128:(tc + 1) * 128], rhs=w3[:, col0:col0 + 512], start=True, stop=True),
                         [h2, w3], [psM])
                    P.op(act, lambda tc=tc, d=d: nc.scalar.activation(out=ew[:], in_=adec[:], func=AF.Exp, scale=ntv[:, d, tc:tc + 1]), [adec, ntv], [ew])
                    V(lambda tc=tc: nc.vector.tensor_tensor(out=fw[:, tc, :], in0=psM[:], in1=ew[:], op=ALU.mult), [psM, ew], [fw])
                    P.op(act, lambda tc=tc: nc.scalar.activation(out=fa[:], in_=fw[:, tc, :], func=AF.Abs), [fw], [fa])
                    P.op(pe, lambda tc=tc: nc.tensor.matmul(psN[:], lhsT=ones[:], rhs=fa[:], start=(tc == 0), stop=(tc == ntc - 1)), [ones, fa], [psN])
                V(lambda: nc.vector.reciprocal(out=rcp[:], in_=psN[:]), [psN], [rcp])
                V(lambda d=d: nc.vector.tensor_tensor(out=ff[d][:, 0:ntc, :], in0=fw[:, 0:ntc, :], in1=rcp[:, :].unsqueeze(1).to_broadcast([128, ntc, 512]),
                                                      op=ALU.mult), [fw, rcp], [ff[d]])
                if d == 1:
                    V(lambda: nc.vector.memset(ff[1][0:1, 0, :], 0.0), [], [ff[1]])
                P.dma(pool, E["fstash"][d, o, :, 0:ntc, :], ff[d][:, 0:ntc, :], [ff[d]], [E["fstash"]], ff[d])
        ne = (L // 2 + 1 + 127) // 128
        for o in range(2):
            for d in range(2):
                P.dma(sp, ff[d][:, 0:ntc, :], E["fstash"][d, o, :, 0:ntc, :], [E["fstash"]], [ff[d]], ff[d])
            V(lambda: nc.vector.tensor_tensor(out=fpm[0][:, 0:ntc, :], in0=ff[0][:, 0:ntc, :], in1=ff[1][:, 0:ntc, :], op=ALU.add), [ff[0], ff[1]], [fpm[0]])
            P.op(pool, lambda: nc.gpsimd.tensor_tensor(out=fpm[1][:, 0:ntc, :], in0=ff[0][:, 0:ntc, :], in1=ff[1][:, 0:ntc, :], op=ALU.subtract), [ff[0], ff[1]], [fpm[1]])
            for fk in range(nfk):
                c_, s_ = tC[fk % 2], tS[fk % 2]
                P.dma(sp, c_[:, 0:ntc, :], cfg["tFC"][fk], [cfg["tFC"]], [c_], c_)
                P.dma(sp, s_[:, 0:ntc, :], cfg["tFS"][fk], [cfg["tFS"]], [s_], s_)
                pcf, psf = psCf[fk % 2], psSf[fk % 2]
                src = fpm[0] if fk < ne else fpm[1]
                for (tab, pp) in ((c_, pcf), (s_, psf)):
                    for tc in range(ntc):
                        P.op(pe, lambda tab=tab, src=src, pp=pp, tc=tc: nc.tensor.matmul(pp[:], lhsT=tab[:, tc, :], rhs=src[:, tc, :],
                                                                                        start=(tc == 0), stop=(tc == ntc - 1)), [tab, src], [pp])
                fo_ = fo[fk % 2]
                P.op(act, lambda pcf=pcf, fk=fk, fo_=fo_: nc.scalar.activation(out=fo_[:, 0, :], in_=pcf[:], func=AF.Copy, scale=wkv[:, fk:fk + 1]), [pcf, wkv], [fo_])
                V(lambda psf=psf, fk=fk, fo_=fo_: nc.vector.tensor_scalar_mul(out=fo_[:, 1, :], in0=psf[:], scalar1=wkv[:, fk:fk + 1]), [psf, wkv], [fo_])
                P.dma(pool, cfg["Fs"][o, fk], fo_[:], [fo_], [cfg["Fs"]], fo_)
        SF.close()

    SC = P.scope()
    xtok = SC.sb("xtok", [128, 16, 512], BF16)
    Z = SC.sb("Z", [128, 17, 2, 512], BF16)
    y1b = SC.sb("y1b", [128, 4, LS], BF16)
    tC = [SC.sb(f"tC{i}", [128, 16, 128], BF16) for i in range(2)]
    tS = [SC.sb(f"tS{i}", [128, 16, 128], BF16) for i in range(2)]
    tIC = SC.sb("tIC", [128, 17, 512], BF16); tIS = SC.sb("tIS", [128, 17, 512], BF16)
    fsb = [SC.sb(f"fsb{i}", [128, 2, 512]) for i in range(2)]
    zin = [SC.sb(f"zin{i}", [128, 514]) for i in range(2)]
    cvo = [SC.sb(f"cvo{i}", [128, 512]) for i in range(2)]
    cvb2 = [SC.sb(f"cvb{i}", [128, 512], BF16) for i in range(2)]
    t4 = [SC.sb(f"t4{i}", [128, 512]) for i in range(4)]
    ob = [SC.sb(f"ob{i}", [128, 512], BF16) for i in range(2)]
    psC = [SC.ps(f"psC{i}", [128, 512]) for i in range(2)]
    psS = [SC.ps(f"psS{i}", [128, 512]) for i in range(2)]
    psI = [SC.ps(f"psI{i}", [128, 512]) for i in range(2)]
    psTt2 = [SC.ps(f"psTt{i}", [128, 4, 128], BF16) for i in range(2)]
    cjobs = list(E.get("convjobs", []))
    cst = SC.sb("cst", [128, 16, 512]); csb = SC.sb("csb", [128, 16, 512], BF16)

    def conv_step(n=1):
        for _ in range(n):
            if not cjobs:
                return
            (srcT, src, dstT, dst, nk) = cjobs.pop(0)
            P.dma(act, cst[:, 0:nk, :], src.rearrange("(k p) n -> p k n", p=128), [srcT], [cst], cst)
            P.op(act, lambda: nc.scalar.copy(out=csb[:, 0:nk, :], in_=cst[:, 0:nk, :]), [cst], [csb])
            if len(dst.shape) == 4:
                P.dma(pool, dst, csb[:, 0:nk, :].rearrange("p (g k) n -> p g k n", g=2), [csb], [dstT], csb)
            else:
                P.dma(pool, dst, csb[:, 0:nk, :], [csb], [dstT], csb)

    def shortconv(ci, off, t0, bw, rows_n, zi, out):
        P.dma(sp, zi[:, 0:bw], hy[ci, :, off + t0:off + t0 + bw], [hy], [zi], zi)
        nseg = bw // rows_n
        z3 = zi[:, 0:bw].rearrange("p (s n) -> p s n", n=rows_n)
        o3 = out[:, 0:bw].rearrange("p (s n) -> p s n", n=rows_n)
        V(lambda: nc.vector.tensor_scalar(out=out[:, 0:bw], in0=zi[:, 0:bw], scalar1=cw[:, 1, ci:ci + 1], scalar2=cb[:, ci:ci + 1], op0=ALU.mult, op1=ALU.add),
          [zi, cw, cb], [out])
        V(lambda: nc.vector.scalar_tensor_tensor(out=o3[:, :, 1:rows_n], in0=z3[:, :, 0:rows_n - 1], scalar=cw[:, 0, ci:ci + 1], in1=o3[:, :, 1:rows_n],
                                                 op0=ALU.mult, op1=ALU.add), [zi, cw, out], [out])
        V(lambda: nc.vector.scalar_tensor_tensor(out=o3[:, :, 0:rows_n - 1], in0=z3[:, :, 1:rows_n], scalar=cw[:, 2, ci:ci + 1], in1=o3[:, :, 0:rows_n - 1],
                                                 op0=ALU.mult, op1=ALU.add), [zi, cw, out], [out])

    for si, (off, L) in enumerate(SEQS):
        cfg = cfgs[L]
        ntc, nfk, nblk, bw = cfg["ntc"], cfg["nfk"], cfg["nblk"], cfg["bw"]
        rows_n = 64 if si == 0 else L
        for order in range(2):
            bi = 0
            for c in range(4):
                for tb in range(nblk):
                    t0 = tb * bw
                    cvb = cvb2[bi % 2]; psTt = psTt2[bi % 2]
                    if order == 0:
                        shortconv(c, off, t0, bw, rows_n, zin[bi % 2], cvo[bi % 2])
                        P.op(act, lambda cvb=cvb, bi=bi: nc.scalar.copy(out=cvb[:, 0:bw], in_=cvo[bi % 2][:, 0:bw]), [cvo[bi % 2]], [cvb])
                        srcT = cvb
                    else:
                        srcT = y1b
                    for q in range(bw // 128):
                        if order == 0:
                            in_ap = cvb[:, q * 128:(q + 1) * 128]
                        else:
                            in_ap = y1b[:, c, t0 + q * 128:t0 + (q + 1) * 128]
                        P.op(pe, lambda q=q, in_ap=in_ap, psTt=psTt: nc.tensor.transpose(psTt[:, q, :], in_ap, ident_b[:]), [srcT, ident_b], [psTt])
                    nq = bw // 128
                    if bi % 2 == 0:
                        P.op(act, lambda c=c, tb=tb, nq=nq, psTt=psTt: nc.scalar.copy(out=xtok[:, tb * 4:tb * 4 + nq, c * 128:(c + 1) * 128], in_=psTt[:, 0:nq, :]),
                             [psTt], [xtok])
                    else:
                        V(lambda c=c, tb=tb, nq=nq, psTt=psTt: nc.vector.tensor_copy(out=xtok[:, tb * 4:tb * 4 + nq, c * 128:(c + 1) * 128], in_=psTt[:, 0:nq, :]),
                          [psTt], [xtok])
                    bi += 1

            def emit(fk, pc, ps_, order=order):
                if si == 0:
                    conv_step()
                f_ = fsb[fk % 2]
                P.dma(sp, f_[:], cfg["Fs"][order, fk], [cfg["Fs"]], [f_], f_)
                V(lambda: nc.vector.tensor_tensor(out=t4[0][:], in0=pc[:], in1=f_[:, 0, :], op=ALU.mult), [pc, f_], [t4[0]])
                V(lambda: nc.vector.tensor_tensor(out=t4[1][:], in0=ps_[:], in1=f_[:, 1, :], op=ALU.mult), [ps_, f_], [t4[1]])
                V(lambda: nc.vector.tensor_tensor(out=t4[2][:], in0=pc[:], in1=f_[:, 1, :], op=ALU.mult), [pc, f_], [t4[2]])
                V(lambda: nc.vector.tensor_tensor(out=t4[3][:], in0=ps_[:], in1=f_[:, 0, :], op=ALU.mult), [ps_, f_], [t4[3]])
                V(lambda: nc.vector.tensor_tensor(out=Z[:, fk, 0, :], in0=t4[0][:], in1=t4[1][:], op=ALU.subtract), [t4[0], t4[1]], [Z])
                V(lambda: nc.vector.tensor_tensor(out=Z[:, fk, 1, :], in0=t4[2][:], in1=t4[3][:], op=ALU.add), [t4[2], t4[3]], [Z])

            fwd_dft(cfg, xtok, emit, (tC, tS, psC, psS))
            for tb in range(nblk):
                t0 = tb * bw
                P.dma(sp, tIC[:, 0:nfk, 0:bw], cfg["tIC"][tb], [cfg["tIC"]], [tIC], tIC)
                P.dma(sp, tIS[:, 0:nfk, 0:bw], cfg["tIS"][tb], [cfg["tIS"]], [tIS], tIS)
                for c in range(4):
                    pi = psI[c % 2]
                    for fk in range(nfk):
                        P.op(pe, lambda fk=fk, c=c, pi=pi: nc.tensor.matmul(pi[:, 0:bw], lhsT=Z[:, fk, 0, c * 128:(c + 1) * 128], rhs=tIC[:, fk, 0:bw],
                                                                           start=(fk == 0), stop=False), [Z, tIC], [pi])
                        P.op(pe, lambda fk=fk, c=c, pi=pi: nc.tensor.matmul(pi[:, 0:bw], lhsT=Z[:, fk, 1, c * 128:(c + 1) * 128], rhs=tIS[:, fk, 0:bw],
                                                                           start=False, stop=(fk == nfk - 1)), [Z, tIS], [pi])
                    if si == 0:
                        conv_step()
                    gate_ci = (4 if order == 0 else 8) + c
                    shortconv(gate_ci, off, t0, bw, rows_n, zin[1], cvo[1])
                    if order == 0:
                        shortconv(c, off, t0, bw, rows_n, zin[0], cvo[0])
                        V(lambda c=c, pi=pi: nc.vector.scalar_tensor_tensor(out=t4[0][:, 0:bw], in0=cvo[0][:, 0:bw], scalar=dvv[:, 0, c:c + 1], in1=pi[:, 0:bw],
                                                                          op0=ALU.mult, op1=ALU.add), [cvo[0], dvv, pi], [t4[0]])
                        V(lambda c=c, t0=t0: nc.vector.tensor_tensor(out=y1b[:, c, t0:t0 + bw], in0=t4[0][:, 0:bw], in1=cvo[1][:, 0:bw], op=ALU.mult),
                          [t4[0], cvo[1]], [y1b])
                    else:
                        o_ = ob[(tb * 4 + c) % 2]
                        V(lambda c=c, pi=pi, t0=t0: nc.vector.scalar_tensor_tensor(out=t4[0][:, 0:bw], in0=y1b[:, c, t0:t0 + bw], scalar=dvv[:, 1, c:c + 1], in1=pi[:, 0:bw],
                                                                                 op0=ALU.mult, op1=ALU.add), [y1b, dvv, pi], [t4[0]])
                        V(lambda o_=o_: nc.vector.tensor_tensor(out=o_[:, 0:bw], in0=t4[0][:, 0:bw], in1=cvo[1][:, 0:bw], op=ALU.mult), [t4[0], cvo[1]], [o_])
                        P.dma(pool, mixT[12 + c, :, off + t0:off + t0 + bw], o_[:, 0:bw], [o_], [mixT], o_)
    conv_step(len(cjobs))
    SC.close()
    S.close()


_NC_CACHE = {}


def _consts():
    ident = np.eye(128, dtype=np.float32)
    s = np.arange(128)[:, None]
    c = np.arange(128)[None, :]
    masks = np.stack([(s <= c), (s >= c)]).astype(np.float32)
    tp1 = np.arange(1, LS + 1, dtype=np.float32)
    out = dict(ident=ident, masks=masks, tp1=tp1)
    bf = ml_dtypes.bfloat16
    for L in (LS, LP):
        sx = str(L); N = 2 * L; ntc = L // 128; nfk = ntc + 1; nblk = max(1, L // 512); bw = min(512, L)
        ne = (L // 2 + 1 + 127) // 128
        kk = np.full(nfk * 128, -1, dtype=np.int64)
        ev = np.arange(0, L + 1, 2); od = np.arange(1, L, 2)
        kk[:len(ev)] = ev; kk[ne * 128:ne * 128 + len(od)] = od
        tt = np.arange(L, dtype=np.int64)
        ang = 2.0 * np.pi * ((np.maximum(kk, 0)[:, None] * tt[None, :]) % N).astype(np.float64) / N
        valid = (kk >= 0).astype(np.float64)[:, None]
        Ckt = np.cos(ang) * valid; Skt = np.sin(ang) * valid
        out["tFC" + sx] = np.ascontiguousarray(Ckt.reshape(nfk, 128, ntc, 128).transpose(0, 3, 2, 1)).astype(bf)
        out["tFS" + sx] = np.ascontiguousarray(Skt.reshape(nfk, 128, ntc, 128).transpose(0, 3, 2, 1)).astype(bf)
        out["tIC" + sx] = np.ascontiguousarray(Ckt.reshape(nfk, 128, nblk, bw).transpose(2, 1, 0, 3)).astype(bf)
        out["tIS" + sx] = np.ascontiguousarray(Skt.reshape(nfk, 128, nblk, bw).transpose(2, 1, 0, 3)).astype(bf)
        t = np.linspace(0.0, 1.0, L, dtype=np.float32)[:, None]
        w = (2.0 * np.float32(np.pi) * np.arange(L, dtype=np.float32)[:, None] / np.float32(L)).astype(np.float32)
        fr = np.linspace(1e-4, 15.0, 16, dtype=np.float32)[None, :]
        feats = np.concatenate([t, np.cos(fr * w), -np.sin(fr * w)], axis=-1).astype(np.float32)
        ridx = (L - np.arange(L)) % L
        fT = np.zeros((64, L), np.float32); fT[:33] = feats.T
        fTr = np.zeros((64, L), np.float32); fTr[:33] = feats[ridx].T
        out["featsT" + sx] = fT; out["featsTr" + sx] = fTr
        out["negtv" + sx] = np.ascontiguousarray(-t[:, 0].reshape(ntc, 128).T)
        out["negtvr" + sx] = np.ascontiguousarray(-t[ridx, 0].reshape(ntc, 128).T)
        wk = np.where((kk == 0) | (kk == L), 1.0 / N, 2.0 / N) * (kk >= 0)
        sg = np.where(kk % 2 == 0, 1.0, -1.0)
        out["wk" + sx] = np.ascontiguousarray(wk.reshape(nfk, 128).T).astype(np.float32)
        out["sgw" + sx] = np.ascontiguousarray((wk * sg).reshape(nfk, 128).T).astype(np.float32)
    return out


def kernel(**inp):
    n = 8
    if "nc" not in _NC_CACHE:
        _NC_CACHE["nc"] = build_program()
    nc = _NC_CACHE["nc"]
    f = lambda a: np.ascontiguousarray(np.asarray(a, dtype=np.float32))
    consts = _consts()
    shared = {k: f(inp[k]) for k in ("ada_w", "ada_b", "w_in", "w_out", "w_up", "w_down", "ln1_g", "ln1_b", "ln2_g", "ln2_b",
                                     "gla_w_gate", "gla_b_gate", "gla_norm_w", "ret_decay_exp",
                                     "s5_a_re", "s5_a_im", "s5_log_step", "s5_b_re", "s5_b_im", "s5_c_re", "s5_c_im", "s5_d", "s5_glu_w", "s5_glu_b",
                                     "hy_conv_w", "hy_conv_b", "hy_f_w1", "hy_f_b1", "hy_f_w2", "hy_f_b2", "hy_f_freq", "hy_f_w3", "hy_decay", "hy_d")}
    shared.update(consts)
    in_maps = []
    for c in range(n):
        m = dict(shared)
        m["x"] = np.ascontiguousarray(np.concatenate([inp["x_sample"][c], inp["x_prompt"][2 * c], inp["x_prompt"][2 * c + 1]], axis=0).astype(np.float32))
        m["cvec"] = np.ascontiguousarray(np.stack([inp["c"][c], inp["c_ctx"]]).astype(np.float32))
        m["sg"] = f(inp["state_gla"][c]); m["sr"] = f(inp["state_ret"][c])
        m["s5re"] = f(inp["state_s5_re"][c]); m["s5im"] = f(inp["state_s5_im"][c])
        in_maps.append(m)
    res = run_bass_kernel_spmd(nc, in_maps, core_ids=list(range(n)))
    R = res.results
    y = np.stack([r["y"] for r in R])
    y_sample = np.ascontiguousarray(y[:, :LS])
    y_prompt = np.ascontiguousarray(y[:, LS:].reshape(n * 2, LP, D))
    cat = lambda k: np.ascontiguousarray(np.concatenate([r[k] for r in R], axis=0))
    return (y_prompt, y_sample, cat("nsg"), cat("nsr"), cat("ns5re"), cat("ns5im"))
```

```python
import numpy as np
import ml_dtypes
from contextlib import ExitStack
import concourse.bass as bass
import concourse.mybir as mybir
from concourse.bass_utils import run_bass_kernel_spmd

F32 = mybir.dt.float32
BF16 = mybir.dt.bfloat16
AF = mybir.ActivationFunctionType
ALU = mybir.AluOpType
AX = mybir.AxisListType

D = 2048
DEPTH = 2
LS = 2048
LP = 256
NTOK = LS + 2 * LP
NTT = NTOK // 128
NTB = NTOK // 512
DFF = 8192
DIN = 5152
ALPHA = (2 * DEPTH) ** 0.25
LN_EPS = 1e-5
NORM_EPS = 1e-6
OFF = dict(gq=0, gk=256, gv=512, gg=1024, glr=1536, rq=1568, rk=1824, rv=2080, rg=2592, su=3104, hy=3616)
SEQS = [(0, LS), (LS, LP), (LS + LP, LP)]


class Buf:
    __slots__ = ("name", "w", "r", "dsem", "dcnt", "sw")

    def __init__(self, name):
        self.name = name
        self.w = None
        self.r = {}
        self.dsem = None
        self.dcnt = 0


class T:
    def __init__(self, t, name, excl=False):
        self.t = t
        self.b = Buf(name)
        self.excl = excl

    def __getitem__(self, k):
        return self.t[k]


class Eng:
    def __init__(self, P, name, h, selfsync=True):
        self.P = P
        self.name = name
        self.h = h
        self.selfsync = selfsync
        self.sem = P.newsem()
        self.cnt = 0
        self.seen = {}
        self.mysems = {id(self.sem)}


class Prog:
    SEM_LIMIT = 12000

    def __init__(self, nc):
        self.nc = nc
        self.es = ExitStack()
        self.nsem = 0
        self.freesems = []
        self.freesems_sw = []
        self.bufs = []
        self.semobj = {}
        self.pe = Eng(self, "pe", nc.tensor, selfsync=False)
        self.dve = Eng(self, "dve", nc.vector)
        self.act = Eng(self, "act", nc.scalar)
        self.pool = Eng(self, "pool", nc.gpsimd)
        self.sp = Eng(self, "sp", nc.sync)
        self.engs = [self.pe, self.dve, self.act, self.pool, self.sp]
        self.dchans = []
        self.nscope = 0

    def newsem(self):
        self.nsem += 1
        s = self.es.enter_context(self.nc.semaphore(f"sem{self.nsem}"))
        self.semobj[id(s)] = s
        return s

    def reg(self, t):
        self.bufs.append(t.b)
        return t

    def dram(self, name, shape, dt, kind="Internal"):
        h = self.nc.dram_tensor(name, list(shape), dt, kind=kind)
        return self.reg(T(h.ap(), name))

    def _need(self, eng, evs):
        for ev in evs:
            if ev is None:
                continue
            sem, val = ev
            k = id(sem)
            if (not eng.selfsync) and k in eng.mysems:
                continue
            if eng.seen.get(k, 0) >= val:
                continue
            eng.h.wait_ge(sem, val)
            eng.seen[k] = val

    def _deps(self, reads, writes, eng=None):
        evs = []
        for t in reads:
            evs.append(t.b.w)
            if t.excl:
                for k, (s, v) in t.b.r.items():
                    if eng is None or k not in eng.mysems:
                        evs.append((s, v))
        for t in writes:
            evs.append(t.b.w)
            for k, (s, v) in t.b.r.items():
                evs.append((s, v))
        return evs

    def _commit(self, ev, reads, writes):
        for t in reads:
            t.b.r[id(ev[0])] = ev
        for t in writes:
            t.b.w = ev
            t.b.r = {}

    def op(self, eng, fn, reads=(), writes=()):
        self._need(eng, self._deps(reads, writes, eng))
        inst = fn()
        eng.cnt += 1
        inst.then_inc(eng.sem, 1)
        ev = (eng.sem, eng.cnt)
        self._commit(ev, reads, writes)
        if eng.cnt >= self.SEM_LIMIT:
            eng.sem = self.newsem()
            eng.mysems.add(id(eng.sem))
            eng.cnt = 0
        return inst

    def dma(self, q, out, in_, reads, writes, chan, **kw):
        self._need(q, self._deps(reads, writes, q))
        sw = (q is self.pool)
        key = "sw" if sw else "hw"
        if not hasattr(chan, "chs"):
            chan.chs = {}
        if key not in chan.chs:
            pool_ = self.freesems_sw if sw else self.freesems
            b = Buf(chan.b.name + "_" + key)
            if pool_:
                b.dsem, b.dcnt = pool_.pop()
            else:
                b.dsem, b.dcnt = self.newsem(), 0
            b.sw = sw
            chan.chs[key] = b
            self.dchans.append(b)
        b = chan.chs[key]
        inst = q.h.dma_start(out=out, in_=in_, **kw)
        b.dcnt += 16
        inst.then_inc(b.dsem, 16)
        ev = (b.dsem, b.dcnt)
        self._commit(ev, reads, writes)
        return inst

    def barrier(self):
        evs = [(e.sem, e.cnt) for e in self.engs if e.cnt > 0]
        evs += [(b.dsem, b.dcnt) for b in self.dchans if b.dcnt > 0]
        for e in self.engs:
            sv = e.selfsync
            e.selfsync = True
            self._need(e, [ev for ev in evs if sv or id(ev[0]) not in e.mysems])
            e.selfsync = sv
        for b in self.bufs:
            b.w = None
            b.r = {}

    def scope(self):
        return Scope(self)


class Scope:
    def __init__(self, P):
        self.P = P
        self.es = ExitStack()
        self.ts = []
        P.nscope += 1
        self.id = P.nscope

    def sb(self, name, shape, dt=F32):
        t = self.es.enter_context(self.P.nc.sbuf_tensor(f"{name}_{self.id}", list(shape), dt))
        tt = self.P.reg(T(t, name))
        self.ts.append(tt)
        return tt

    def ps(self, name, shape, dt=F32):
        t = self.es.enter_context(self.P.nc.psum_tensor(f"{name}_{self.id}", list(shape), dt))
        tt = self.P.reg(T(t, name, excl=True))
        self.ts.append(tt)
        return tt

    def close(self):
        P = self.P
        P.barrier()
        for tt in self.ts:
            for key, cb in getattr(tt, "chs", {}).items():
                (P.freesems_sw if cb.sw else P.freesems).append((cb.dsem, cb.dcnt))
                P.dchans.remove(cb)
            tt.chs = {}
            P.bufs.remove(tt.b)
        self.es.close()


def build_program(debug=False):
    nc = bass.Bass("TRN2", target_bir_lowering=False, dynamic_dma_scratch_size=8192)
    P = Prog(nc)
    pe, dve, act, pool, sp = P.pe, P.dve, P.act, P.pool, P.sp

    def din(name, shape, dt=F32):
        return P.dram(name, shape, dt, kind="ExternalInput")

    def dout(name, shape, dt=F32):
        return P.dram(name, shape, dt, kind="ExternalOutput")

    x_in = din("x", [NTOK, D])
    cvec = din("cvec", [2, D])
    ada_w = din("ada_w", [DEPTH, D, 6 * D])
    ada_b = din("ada_b", [DEPTH, 6 * D])
    w_in = din("w_in", [DEPTH, D, DIN])
    w_out = din("w_out", [DEPTH, D, D])
    w_up = din("w_up", [DEPTH, D, DFF])
    w_down = din("w_down", [DEPTH, DFF, D])
    ln1_g = din("ln1_g", [DEPTH, D]); ln1_b = din("ln1_b", [DEPTH, D])
    ln2_g = din("ln2_g", [DEPTH, D]); ln2_b = din("ln2_b", [DEPTH, D])
    sg_in = din("sg", [DEPTH, 2, 4, 64, 128]); sr_in = din("sr", [DEPTH, 2, 4, 64, 128])
    gla_w_gate = din("gla_w_gate", [DEPTH, 2, 16, 256]); gla_b_gate = din("gla_b_gate", [DEPTH, 2, 256])
    gla_norm_w = din("gla_norm_w", [DEPTH, 128]); ret_decay_exp = din("ret_decay_exp", [DEPTH, 2, 4])
    s5re_in = din("s5re", [DEPTH, 2, 32, 64]); s5im_in = din("s5im", [DEPTH, 2, 32, 64])
    s5_a_re = din("s5_a_re", [DEPTH, 2, 32, 64]); s5_a_im = din("s5_a_im", [DEPTH, 2, 32, 64])
    s5_log_step = din("s5_log_step", [DEPTH, 2, 32])
    s5_b_re = din("s5_b_re", [DEPTH, 2, 32, 64, 16]); s5_b_im = din("s5_b_im", [DEPTH, 2, 32, 64, 16])
    s5_c_re = din("s5_c_re", [DEPTH, 2, 32, 16, 64]); s5_c_im = din("s5_c_im", [DEPTH, 2, 32, 16, 64])
    s5_d = din("s5_d", [DEPTH, 512]); s5_glu_w = din("s5_glu_w", [DEPTH, 512, 512]); s5_glu_b = din("s5_glu_b", [DEPTH, 512])
    tp1_in = din("tp1", [LS])
    hy_conv_w = din("hy_conv_w", [DEPTH, 3, 1536]); hy_conv_b = din("hy_conv_b", [DEPTH, 1536])
    hy_f_w1 = din("hy_f_w1", [DEPTH, 33, 64]); hy_f_b1 = din("hy_f_b1", [DEPTH, 64])
    hy_f_w2 = din("hy_f_w2", [DEPTH, 64, 64]); hy_f_b2 = din("hy_f_b2", [DEPTH, 64])
    hy_f_freq = din("hy_f_freq", [DEPTH, 64]); hy_f_w3 = din("hy_f_w3", [DEPTH, 64, 2048])
    hy_decay = din("hy_decay", [DEPTH, 2048]); hy_d = din("hy_d", [DEPTH, 2, 512])
    fstash = P.dram("fstash", [2, 2, 128, 16, 512], BF16)
    s5BT = P.dram("s5BT", [32, 128, 2, 128], BF16); s5CP = P.dram("s5CP", [32, 128, 2, 128], BF16)
    hyc = {}
    for L_ in (LS, LP):
        sx = str(L_); ntc_ = L_ // 128; nfk_ = ntc_ + 1; nblk_ = max(1, L_ // 512); bw_ = min(512, L_)
        hyc["tFC" + sx] = din("tFC" + sx, [nfk_, 128, ntc_, 128], BF16); hyc["tFS" + sx] = din("tFS" + sx, [nfk_, 128, ntc_, 128], BF16)
        hyc["tIC" + sx] = din("tIC" + sx, [nblk_, 128, nfk_, bw_], BF16); hyc["tIS" + sx] = din("tIS" + sx, [nblk_, 128, nfk_, bw_], BF16)
        hyc["featsT" + sx] = din("featsT" + sx, [64, L_]); hyc["featsTr" + sx] = din("featsTr" + sx, [64, L_])
        hyc["negtv" + sx] = din("negtv" + sx, [128, ntc_]); hyc["negtvr" + sx] = din("negtvr" + sx, [128, ntc_])
        hyc["wk" + sx] = din("wk" + sx, [128, nfk_]); hyc["sgw" + sx] = din("sgw" + sx, [128, nfk_])
        hyc["Fs" + sx] = P.dram("Fs" + sx, [2, nfk_, 128, 2, 512], F32)
    ident_in = din("ident", [128, 128])
    masks_in = din("masks", [2, 128, 128])
    y_out = dout("y", [NTOK, D])
    nsg = dout("nsg", [2, DEPTH, 2, 4, 64, 128]); nsr = dout("nsr", [2, DEPTH, 2, 4, 64, 128])
    ns5re = dout("ns5re", [2, DEPTH, 2, 32, 64]); ns5im = dout("ns5im", [2, DEPTH, 2, 32, 64])
    kind = "ExternalOutput" if debug else "Internal"
    mixonly = isinstance(debug, str) and debug.startswith("mix")
    pkind = "ExternalInput" if mixonly else kind
    xcur = P.dram("xcur", [NTOK, D], F32, kind=kind)
    modvec = P.dram("modvec", [DEPTH, 2, 6 * D], F32)
    wo_bf = P.dram("wo_bf", [DEPTH, 4, 128, 16, 512], BF16)
    wu_bf = P.dram("wu_bf", [DEPTH, 16, 128, 16, 512], BF16)
    wd_bf = P.dram("wd_bf", [DEPTH, 4, 8, 128, 8, 512], BF16)
    pF = {n: P.dram("pF_" + n, [c, 128, NTOK], F32, kind=pkind) for n, c in
          dict(qkg=4, qkr=4, su=4, hy=12).items()}
    pLr = P.dram("pF_lr", [32, NTOK], F32, kind=pkind)
    pT = {n: P.dram("pT_" + n, [NTOK, 512], F32, kind=pkind) for n in ("gv", "gg", "rv", "rg")}
    mixT = P.dram("mixT", [16, 128, NTOK], BF16, kind=kind)

    G = P.scope()
    ident_f = G.sb("ident_f", [128, 128], F32)
    ident_b = G.sb("ident_b", [128, 128], BF16)
    modT = G.sb("modT", [128, DEPTH, 96, 2], F32)
    P.dma(sp, ident_f[:], ident_in[:, :], [ident_in], [ident_f], ident_f)
    P.op(dve, lambda: nc.vector.tensor_copy(out=ident_b[:], in_=ident_f[:]), [ident_f], [ident_b])

    def load_slab(q, dst, src_ap, srcT):
        P.dma(q, dst[:], src_ap.rearrange("(k p) n -> p k n", p=128), [srcT], [dst], dst)

    cast_rr = [0]

    def cast(dst, src, nk):
        engs = [(act, lambda o, i: nc.scalar.copy(out=o, in_=i)),
                (dve, lambda o, i: nc.vector.tensor_copy(out=o, in_=i))]
        h = (nk * 5) // 8
        for (k0, k1) in ((0, h), (h, nk)):
            e, f = engs[cast_rr[0] % 2]
            cast_rr[0] += 1
            P.op(e, lambda f=f, k0=k0, k1=k1: f(dst[:, k0:k1, :], src[:, k0:k1, :]), [src], [dst])

    if not mixonly:
        S0 = P.scope()
        cT = S0.sb("cT", [128, 16, 2])
        abT = S0.sb("abT", [128, DEPTH, 96])
        stg = [S0.sb(f"stg{i}", [128, 16, 512]) for i in range(2)]
        stb = [S0.sb(f"stb{i}", [128, 16, 512], BF16) for i in range(2)]
        modps = [S0.ps(f"modps{l}", [128, 96, 2]) for l in range(DEPTH)]
        with nc.allow_non_contiguous_dma(reason="tiny param transposes"):
            for r in range(2):
                P.dma(sp, cT[:, :, r], cvec[r].rearrange("(k p) -> p k", p=128), [cvec], [cT], cT)
            for l in range(DEPTH):
                P.dma(sp, abT[:, l, :], ada_b[l].rearrange("(j p) -> p j", p=128), [ada_b], [abT], abT)
        P.op(act, lambda: nc.scalar.activation(out=cT[:], in_=cT[:], func=AF.Silu), [cT], [cT])
        cTb = S0.sb("cTb", [128, 16, 2], BF16)
        P.op(dve, lambda: nc.vector.tensor_copy(out=cTb[:], in_=cT[:]), [cT], [cTb])
        it = 0
        for l in range(DEPTH):
            for s in range(24):
                buf0 = stg[it % 2]
                buf = stb[it % 2]
                it += 1
                if s == 0:
                    load_slab(sp, buf0, ada_w[l, :, 0:512], ada_w)
                if s + 1 < 24:
                    load_slab(sp, stg[it % 2], ada_w[l, :, (s + 1) * 512:(s + 2) * 512], ada_w)
                cast(buf, buf0, 16)
                for j in range(4):
                    for k in range(16):
                        P.op(pe, lambda buf=buf, j=j, k=k, s=s, l=l: nc.tensor.matmul(
                            modps[l][:, s * 4 + j, :], lhsT=buf[:, k, j * 128:(j + 1) * 128], rhs=cTb[:, k, :],
                            start=(k == 0), stop=(k == 15)), [buf, cTb], [modps[l]])
            for r in range(2):
                P.op(dve, lambda l=l, r=r: nc.vector.tensor_tensor(out=modT[:, l, :, r], in0=modps[l][:, :, r],
                                                                   in1=abT[:, l, :], op=ALU.add),
                     [modps[l], abT], [modT])
        with nc.allow_non_contiguous_dma(reason="mod vectors to DRAM rows"):
            for l in range(DEPTH):
                for r in range(2):
                    P.dma(sp, modvec[l, r].rearrange("(j p) -> p j", p=128), modT[:, l, :, r], [modT], [modvec], modT)
        for l in range(DEPTH):
            for c0 in (16, 64):
                P.op(dve, lambda l=l, c0=c0: nc.vector.tensor_scalar_add(out=modT[:, l, c0:c0 + 16, :],
                                                                         in0=modT[:, l, c0:c0 + 16, :], scalar1=1.0),
                     [modT], [modT])
        S0.close()

    convjobs = {}
    for l in range(DEPTH):
        jobs = []
        for s_ in range(4):
            jobs.append((w_out, w_out[l, :, s_ * 512:(s_ + 1) * 512], wo_bf, wo_bf[l, s_], 16))
        for s_ in range(16):
            jobs.append((w_up, w_up[l, :, s_ * 512:(s_ + 1) * 512], wu_bf, wu_bf[l, s_], 16))
        for c in range(4):
            for kg in range(0, 8, 2):
                jobs.append((w_down, w_down[l, kg * 1024:(kg + 2) * 1024, c * 512:(c + 1) * 512], wd_bf,
                             wd_bf[l, c, kg:kg + 2].rearrange("g p k n -> p g k n"), 16))
        convjobs[l] = jobs if not mixonly else []

    def ln_stats(S, xt, eps, name):
        st = S.sb(name + "_st", [128, 4, 6])
        mv = S.sb(name + "_mv", [128, 2])
        rs = S.sb(name + "_rs", [128, 1])
        return st, mv, rs

    def do_ln_stats(st, mv, rs, xt, eps):
        for c in range(4):
            P.op(dve, lambda c=c: nc.vector.bn_stats(out=st[:, c, :], in_=xt[:, c * 512:(c + 1) * 512]), [xt], [st])
        P.op(dve, lambda: nc.vector.bn_aggr(out=mv[:], in_=st[:].rearrange("p c s -> p (c s)")), [st], [mv])
        P.op(act, lambda: nc.scalar.activation(out=rs[:], in_=mv[:, 1:2], func=AF.Ln, bias=eps), [mv], [rs])
        P.op(act, lambda: nc.scalar.activation(out=rs[:], in_=rs[:], func=AF.Exp, scale=-0.5), [rs], [rs])

    for l in range(DEPTH):
        xsrc = x_in if l == 0 else xcur
        xdst = xcur if l == 0 else y_out
        if not mixonly:
            SA = P.scope()
            hT = SA.sb("hT", [128, 16, NTOK], BF16)
            SA1 = P.scope()
            xin = [SA1.sb(f"xin{i}", [128, D]) for i in range(2)]
            xn = [SA1.sb(f"xn{i}", [128, D], BF16) for i in range(2)]
            stq = [ln_stats(SA1, None, LN_EPS, f"lnA{i}") for i in range(2)]
            psT = [SA1.ps(f"psT{i}", [128, 16, 128], BF16) for i in range(2)]
            for tt in range(NTT):
                typ = 0 if tt < LS // 128 else 1
                xi, xb_, (st, mv, rs), pt = xin[tt % 2], xn[tt % 2], stq[tt % 2], psT[tt % 2]
                P.dma(sp, xi[:], xsrc[tt * 128:(tt + 1) * 128, :], [xsrc], [xi], xi)
                do_ln_stats(st, mv, rs, xi, LN_EPS)
                P.op(dve, lambda xi=xi, xb_=xb_, mv=mv, rs=rs: nc.vector.tensor_scalar(
                    out=xb_[:], in0=xi[:], scalar1=mv[:, 0:1], scalar2=rs[:, 0:1], op0=ALU.subtract, op1=ALU.mult),
                     [xi, mv, rs], [xb_])
                for j in range(16):
                    P.op(pe, lambda j=j, pt=pt, xb_=xb_: nc.tensor.transpose(pt[:, j, :], xb_[:, j * 128:(j + 1) * 128], ident_b[:]),
                         [xb_, ident_b], [pt])
                for j in range(16):
                    if j % 2 == 0:
                        P.op(dve, lambda j=j, pt=pt, typ=typ, tt=tt: nc.vector.tensor_scalar(
                            out=hT[:, j, tt * 128:(tt + 1) * 128], in0=pt[:, j, :], scalar1=modT[:, l, 16 + j, typ:typ + 1],
                            scalar2=modT[:, l, j, typ:typ + 1], op0=ALU.mult, op1=ALU.add), [pt, modT], [hT])
                    else:
                        P.op(act, lambda j=j, pt=pt, typ=typ, tt=tt: nc.scalar.activation(
                            out=hT[:, j, tt * 128:(tt + 1) * 128], in_=pt[:, j, :], func=AF.Identity,
                            scale=modT[:, l, 16 + j, typ:typ + 1], bias=modT[:, l, j, typ:typ + 1]), [pt, modT], [hT])
            SA1.close()
            SB = P.scope()
            wst = [SB.sb(f"wst{i}", [128, 16, 512]) for i in range(2)]
            wsb = [SB.sb(f"wsb{i}", [128, 16, 512], BF16) for i in range(2)]
            ost = [SB.sb(f"ost{i}", [128, 512]) for i in range(4)]
            psB = [SB.ps(f"psB{i}", [128, 512]) for i in range(4)]
            oi = [0]

            def evac(ps_ap, psT_, dst_dram_ap, dstT, np_, scale=1.0, func=None):
                o = ost[oi[0] % 4]
                e = oi[0] % 2
                oi[0] += 1
                if func is not None or e == 0:
                    P.op(act, lambda: nc.scalar.activation(out=o[0:np_, :], in_=ps_ap, func=(func or AF.Copy), scale=scale),
                         [psT_], [o])
                else:
                    P.op(dve, lambda: nc.vector.tensor_scalar_mul(out=o[0:np_, :], in0=ps_ap, scalar1=scale), [psT_], [o])
                P.dma(pool, dst_dram_ap, o[0:np_, :], [o], [dstT], o)

            slabs = [("F", OFF["gq"], pF["qkg"], (0.125, 0.125, 1.0, 1.0)), ("T", OFF["gv"], pT["gv"], None),
                     ("T", OFF["gg"], pT["gg"], AF.Silu), ("L", OFF["glr"], pLr, None),
                     ("F", OFF["rq"], pF["qkr"], (1.0, 1.0, 0.125, 0.125)), ("T", OFF["rv"], pT["rv"], None),
                     ("T", OFF["rg"], pT["rg"], AF.Silu), ("F", OFF["su"], pF["su"], (1.0,) * 4),
                     ("F3", OFF["hy"], pF["hy"], 0), ("F3", OFF["hy"] + 512, pF["hy"], 4), ("F3", OFF["hy"] + 1024, pF["hy"], 8)]
            for si, (kind_, c0, dstT, extra) in enumerate(slabs):
                a, b = wst[si % 2], wsb[si % 2]
                ncol = 32 if kind_ == "L" else 512
                P.dma(sp, a[:, :, 0:ncol], w_in[l, :, c0:c0 + ncol].rearrange("(k p) n -> p k n", p=128), [w_in], [a], a)
                h8 = 8
                P.op(act, lambda a=a, b=b, ncol=ncol: nc.scalar.copy(out=b[:, 0:8, 0:ncol], in_=a[:, 0:8, 0:ncol]), [a], [b])
                P.op(dve, lambda a=a, b=b, ncol=ncol: nc.vector.tensor_copy(out=b[:, 8:16, 0:ncol], in_=a[:, 8:16, 0:ncol]), [a], [b])
                if kind_ in ("F", "F3"):
                    for tb in range(NTB):
                        for j in range(4):
                            ps = psB[(tb * 4 + j) % 4]
                            for k in range(16):
                                P.op(pe, lambda ps=ps, b=b, j=j, k=k, tb=tb: nc.tensor.matmul(
                                    ps[:], lhsT=b[:, k, j * 128:(j + 1) * 128], rhs=hT[:, k, tb * 512:(tb + 1) * 512],
                                    start=(k == 0), stop=(k == 15)), [b, hT], [ps])
                            if kind_ == "F":
                                evac(ps[:], ps, dstT[j, :, tb * 512:(tb + 1) * 512], dstT, 128, scale=extra[j])
                            else:
                                evac(ps[:], ps, dstT[extra + j, :, tb * 512:(tb + 1) * 512], dstT, 128)
                elif kind_ == "L":
                    for tb in range(NTB):
                        ps = psB[tb % 4]
                        for k in range(16):
                            P.op(pe, lambda ps=ps, b=b, k=k, tb=tb: nc.tensor.matmul(
                                ps[0:32, :], lhsT=b[:, k, 0:32], rhs=hT[:, k, tb * 512:(tb + 1) * 512],
                                start=(k == 0), stop=(k == 15)), [b, hT], [ps])
                        evac(ps[0:32, :], ps, dstT[:, tb * 512:(tb + 1) * 512], dstT, 32)
                else:
                    for tt in range(NTT):
                        ps = psB[tt % 4]
                        for k in range(16):
                            P.op(pe, lambda ps=ps, b=b, k=k, tt=tt: nc.tensor.matmul(
                                ps[:], lhsT=hT[:, k, tt * 128:(tt + 1) * 128], rhs=b[:, k, :],
                                start=(k == 0), stop=(k == 15)), [b, hT], [ps])
                        evac(ps[:], ps, dstT[tt * 128:(tt + 1) * 128, :], dstT, 128, func=extra)
            SB.close()
            SA.close()

        env_ = dict(locals()); env_.update(hyc); env_["convjobs"] = convjobs[l]
        mixers(P, nc, l, env_)
        if debug == 1 or mixonly:
            break

        if not mixonly:
            SD = P.scope()
            mxb = SD.sb("mxb", [128, 16, 512], BF16)
            h2T = mxb
            uT = SD.sb("uT", [128, 64, 512], BF16)
            xa = [[SD.sb(f"xa{s_}{i}", [128, D]) for i in range(4)] for s_ in range(2)]
            bc = {n: SD.sb("bc_" + n, [128, D]) for n in ("g1", "g2", "lg", "lb")}
            wsl = [SD.sb(f"wsl{i}", [128, 8, 512], BF16) for i in range(3)]
            tmp = [SD.sb(f"tmpD{i}", [128, 512]) for i in range(2)]
            xnb = SD.sb("xnb", [128, D], BF16)
            stD = ln_stats(SD, None, LN_EPS, "lnD")
            pb8 = [SD.ps(f"pb8_{i}", [128, 512]) for i in range(8)]
            psAcc = pb8[0:4]
            def load_lnp(gT, bT):
                P.dma(sp, bc["lg"][:], gT[l].partition_broadcast(128), [gT], [bc["lg"]], bc["lg"])
                P.dma(sp, bc["lb"][:], bT[l].partition_broadcast(128), [bT], [bc["lb"]], bc["lb"])
            wi = [0]

            def wload(src_ap, srcT, nk=8):
                w = wsl[wi[0] % 3]
                wi[0] += 1
                P.dma(sp, w[:, 0:nk, :], src_ap, [srcT], [w], w)
                return w

            def resid(xt, c, ps, gname, ti):
                t_ = tmp[ti % 2]
                P.op(dve, lambda: nc.vector.tensor_tensor(out=t_[:], in0=ps[:], in1=bc[gname][:, c * 512:(c + 1) * 512], op=ALU.mult),
                     [ps, bc[gname]], [t_])
                P.op(dve, lambda: nc.vector.scalar_tensor_tensor(out=xt[:, c * 512:(c + 1) * 512], in0=xt[:, c * 512:(c + 1) * 512],
                                                                 scalar=ALPHA, in1=t_[:], op0=ALU.mult, op1=ALU.add),
                     [t_, xt], [xt])

            nmr = SD.sb("nmr", [128, 1])

            def act_norm(dst, src, srcT, dstT):
                st, mv, rs = stD
                P.op(dve, lambda: nc.vector.scalar_tensor_tensor(out=nmr[:], in0=mv[:, 0:1], scalar=-1.0, in1=rs[:], op0=ALU.mult, op1=ALU.mult),
                     [mv, rs], [nmr])
                P.op(act, lambda: nc.scalar.activation(out=dst, in_=src, func=AF.Identity, scale=rs[:, 0:1], bias=nmr[:, 0:1]), [srcT, rs, nmr], [dstT])

            def ln_affine(xt, gn, bn):
                st, mv, rs = stD
                do_ln_stats(st, mv, rs, xt, LN_EPS)
                act_norm(xt[:], xt[:], xt, xt)
                P.op(dve, lambda: nc.vector.tensor_tensor(out=xt[:], in0=xt[:], in1=bc[gn][:], op=ALU.mult), [xt, bc[gn]], [xt])
                P.op(dve, lambda: nc.vector.tensor_tensor(out=xt[:], in0=xt[:], in1=bc[bn][:], op=ALU.add), [xt, bc[bn]], [xt])

            tic = [0]

            def typ_of(tb):
                return 0 if tb < 4 else 1

            def st_load(tb):
                typ = typ_of(tb)
                if tb == 0 or tb == 4:
                    P.dma(sp, bc["g1"][:], modvec[l, typ, 2 * D:3 * D].partition_broadcast(128), [modvec], [bc["g1"]], bc["g1"])
                P.dma(sp, mxb[:], mixT[:, :, tb * 512:(tb + 1) * 512].rearrange("k p n -> p k n"), [mixT], [mxb], mxb)
                for tt in range(4):
                    t0 = tb * 512 + tt * 128
                    P.dma(sp, xa[tb % 2][tt][:], xsrc[t0:t0 + 128, :], [xsrc], [xa[tb % 2][tt]], xa[tb % 2][tt])

            def st_D1(tb):
                X = xa[tb % 2]
                for c in range(4):
                    for hf in range(2):
                        w = wload(wo_bf[l, c, :, hf * 8:(hf + 1) * 8, :], wo_bf)
                        for tt in range(4):
                            for k8 in range(8):
                                k = hf * 8 + k8
                                P.op(pe, lambda w=w, tt=tt, k=k, k8=k8: nc.tensor.matmul(
                                    psAcc[tt][:], lhsT=mxb[:, k, tt * 128:(tt + 1) * 128], rhs=w[:, k8, :],
                                    start=(k == 0), stop=(k == 15)), [w, mxb], [psAcc[tt]])
                    for tt in range(4):
                        resid(X[tt], c, psAcc[tt], "g1", tic[0]); tic[0] += 1

            def st_D2(tb, tt):
                X = xa[tb % 2]
                typ = typ_of(tb)
                if tt == 0:
                    load_lnp(ln1_g, ln1_b)
                ln_affine(X[tt], "lg", "lb")
                st, mv, rs = stD
                do_ln_stats(st, mv, rs, X[tt], LN_EPS)
                act_norm(xnb[:], X[tt][:], X[tt], xnb)
                def tview(j):
                    bank = pb8[6 + j // 8]
                    return bank, bank[:].bitcast(BF16)[:, (j % 8) * 128:(j % 8 + 1) * 128]
                for j in range(16):
                    bank, v = tview(j)
                    P.op(pe, lambda j=j, v=v: nc.tensor.transpose(v, xnb[:, j * 128:(j + 1) * 128], ident_b[:]),
                         [xnb, ident_b], [bank])
                for j in range(16):
                    bank, v = tview(j)
                    if j < 8:
                        P.op(dve, lambda j=j, v=v: nc.vector.tensor_scalar(
                            out=h2T[:, j, tt * 128:(tt + 1) * 128], in0=v, scalar1=modT[:, l, 64 + j, typ:typ + 1],
                            scalar2=modT[:, l, 48 + j, typ:typ + 1], op0=ALU.mult, op1=ALU.add), [bank, modT], [h2T])
                    else:
                        P.op(act, lambda j=j, v=v: nc.scalar.activation(
                            out=h2T[:, j, tt * 128:(tt + 1) * 128], in_=v, func=AF.Identity,
                            scale=modT[:, l, 64 + j, typ:typ + 1], bias=modT[:, l, 48 + j, typ:typ + 1]), [bank, modT], [h2T])

            def st_D3(tb):
                ei = 0
                for s_ in range(16):
                    banks = pb8[(s_ % 2) * 4:(s_ % 2) * 4 + 4]
                    for hf in range(2):
                        w = wload(wu_bf[l, s_, :, hf * 8:(hf + 1) * 8, :], wu_bf)
                        for k8 in range(8):
                            k = hf * 8 + k8
                            for j in range(4):
                                P.op(pe, lambda w=w, j=j, k=k, k8=k8, banks=banks: nc.tensor.matmul(
                                    banks[j][:], lhsT=w[:, k8, j * 128:(j + 1) * 128], rhs=h2T[:, k, :], start=(k == 0), stop=(k == 15)),
                                     [w, h2T], [banks[j]])
                    for j in range(4):
                        t_ = tmp[ei % 2]
                        ei += 1
                        P.op(act, lambda j=j, t_=t_, banks=banks: nc.scalar.activation(out=t_[:], in_=banks[j][:], func=AF.Relu), [banks[j]], [t_])
                        P.op(pool, lambda t_=t_, s_=s_, j=j: nc.gpsimd.tensor_tensor(out=uT[:, s_ * 4 + j, :], in0=t_[:], in1=t_[:], op=ALU.mult),
                             [t_], [uT])

            def st_D4(tb, c):
                X = xa[tb % 2]
                if c == 0 and (tb == 0 or tb == 4):
                    typ = typ_of(tb)
                    P.dma(sp, bc["g2"][:], modvec[l, typ, 5 * D:6 * D].partition_broadcast(128), [modvec], [bc["g2"]], bc["g2"])
                for kg in range(8):
                    w = wload(wd_bf[l, c, kg], wd_bf)
                    for k in range(8):
                        kk = kg * 8 + k
                        for tt in range(4):
                            P.op(pe, lambda w=w, tt=tt, k=k, kk=kk: nc.tensor.matmul(
                                psAcc[tt][:], lhsT=uT[:, kk, tt * 128:(tt + 1) * 128], rhs=w[:, k, :],
                                start=(kk == 0), stop=(kk == 63)), [w, uT], [psAcc[tt]])

            def st_D4r(tb, c):
                X = xa[tb % 2]
                for tt in range(4):
                    resid(X[tt], c, psAcc[tt], "g2", tic[0]); tic[0] += 1

            def st_D5(tb):
                X = xa[tb % 2]
                load_lnp(ln2_g, ln2_b)
                for tt in range(4):
                    ln_affine(X[tt], "lg", "lb")
                    t0 = tb * 512 + tt * 128
                    P.dma(pool, xdst[t0:t0 + 128, :], X[tt][:], [X[tt]], [xdst], X[tt])

            st_load(0)
            st_D1(0)
            for tt in range(4):
                st_D2(0, tt)
            for tb in range(NTB):
                st_D3(tb)
                nxt = tb + 1 < NTB
                if nxt:
                    st_load(tb + 1)
                st_D4(tb, 0)
                st_D4r(tb, 0)
                if nxt:
                    st_D1(tb + 1)
                for c in range(1, 4):
                    st_D4(tb, c)
                    if nxt:
                        st_D2(tb + 1, c - 1)
                        if c == 3:
                            st_D2(tb + 1, 3)
                    st_D4r(tb, c)
                st_D5(tb)
            SD.close()

    G.close()
    P.barrier()
    P.es.close()
    return nc


def mixers(P, nc, l, env):
    sel = env.get("debug")
    sel = sel[4:] if isinstance(sel, str) and sel.startswith("mix:") else "hy,s5,gla,ret"
    if "hy" in sel:
        mixer_hy(P, nc, l, env)
    if "s5" in sel:
        mixer_s5(P, nc, l, env)
    for gi in range(2):
        if ("gla", "ret")[gi] in sel:
            mixer_la(P, nc, l, env, gi)


def mixer_la(P, nc, l, env, gi):
    pe, dve, act, pool, sp = P.pe, P.dve, P.act, P.pool, P.sp
    qk = env["pF"]["qkg" if gi == 0 else "qkr"]
    vT = env["pT"]["gv" if gi == 0 else "rv"]
    gT = env["pT"]["gg" if gi == 0 else "rg"]
    pLr = env["pLr"]
    mixT = env["mixT"]
    ident_b = env["ident_b"]
    st_in = env["sg_in"] if gi == 0 else env["sr_in"]
    st_out = env["nsg"] if gi == 0 else env["nsr"]
    S = P.scope()
    rmask = S.sb("rmask", [128, LS])
    masks = S.sb("masks", [128, 2, 128], F32)
    P.dma(sp, masks[:], env["masks_in"][:].rearrange("m s c -> s m c"), [env["masks_in"]], [masks], masks)
    qdm = [[S.sb(f"qdm{d}{a}", [128, LS], BF16) for a in range(2)] for d in range(2)]
    kd = [S.sb(f"kd{d}", [128, LS], BF16) for d in range(2)]
    dec = S.sb("dec", [128, 2, 32])
    vb = S.sb("vb", [128, 16, 256], BF16)
    vst = S.sb("vst", [128, 4, 256])
    o_all = S.sb("o_all", [128, 16, 256])
    Rf = [S.sb(f"Rf{d}", [128, 256]) for d in range(2)]
    Sbf = [[S.sb(f"Sbf{d}{i}", [128, 256], BF16) for i in range(2)] for d in range(2)]
    kdt = [S.sb(f"kdt{d}", [128, 128], BF16) for d in range(2)]
    att = [S.sb(f"att{d}", [128, 2, 128], BF16) for d in range(2)]
    par = S.sb("par", [128, 2, 2])
    nw = S.sb("nw", [128, 128])
    psA = [S.ps(f"psA{d}", [128, 2, 128]) for d in range(2)]
    psO = [S.ps(f"psO{d}", [128, 256]) for d in range(2)]
    psUp = [S.ps(f"psUp{d}", [128, 256]) for d in range(2)]
    for d in range(2):
        P.op(dve, lambda d=d: nc.vector.memset(kdt[d][:], 0.0), [], [kdt[d]])
        P.op(dve, lambda d=d: nc.vector.memset(att[d][:], 0.0), [], [att[d]])
        for a in range(2):
            P.op(pool, lambda d=d, a=a: nc.gpsimd.memset(qdm[d][a][:], 0.0), [], [qdm[d][a]])
    P.op(pool, lambda: nc.gpsimd.memset(vb[:], 0.0), [], [vb])
    psKt = S.ps("psK", [128, 2, 128], BF16)
    psK = [psKt, psKt]
    P.op(dve, lambda: nc.vector.memset(rmask[:], 1.0), [], [rmask])
    P.op(dve, lambda: nc.vector.memset(rmask[:].rearrange("p (n t) -> p n t", t=128)[:, :, 0:1], 0.0), [], [rmask])
    if gi == 0:
        wg = S.sb("wg", [32, 2, 256])
        lrT = S.sb("lrT", [32, LS])
        P.op(dve, lambda: nc.vector.memset(wg[:], 0.0), [], [wg])
        for d in range(2):
            P.dma(sp, wg[d * 16:(d + 1) * 16, d, :], env["gla_w_gate"][l, d], [env["gla_w_gate"]], [wg], wg)
        with nc.allow_non_contiguous_dma(reason="tiny"):
            for d in range(2):
                P.dma(sp, par[:, d, :], env["gla_b_gate"][l, d].rearrange("(h p) -> p h", p=128), [env["gla_b_gate"]], [par], par)
        P.op(dve, lambda: nc.vector.tensor_scalar_mul(out=par[:], in0=par[:], scalar1=-1.0), [par], [par])
        P.dma(sp, nw[:], env["gla_norm_w"][l].partition_broadcast(128), [env["gla_norm_w"]], [nw], nw)
    else:
        for d in range(2):
            for hp in range(2):
                for a in range(2):
                    P.dma(sp, par[a * 64:(a + 1) * 64, d, hp:hp + 1],
                          env["ret_decay_exp"][l, d, 2 * hp + a:2 * hp + a + 1].partition_broadcast(64),
                          [env["ret_decay_exp"]], [par], par)
        P.op(act, lambda: nc.scalar.activation(out=par[:], in_=par[:], func=AF.Exp, scale=-float(np.log(2.0))), [par], [par])
        P.op(act, lambda: nc.scalar.activation(out=par[:], in_=par[:], func=AF.Ln, scale=-1.0, bias=1.0), [par], [par])
        P.op(dve, lambda: nc.vector.tensor_scalar_mul(out=par[:], in0=par[:], scalar1=-16.0), [par], [par])

    for si, (off, L) in enumerate(SEQS):
        nch = L // 128
        for hp in range(2):
            SP = P.scope()
            qk_f = SP.sb("qk_f", [128, 2, LS])
            lsp = SP.sb("lsp", [128, LS]); cs = SP.sb("cs", [128, LS]); bb = SP.sb("bb", [128, LS])
            Eb = SP.sb("Eb", [128, LS]); Enb = SP.sb("Enb", [128, LS])
            psL = SP.ps("psL", [128, 512])
            P.dma(sp, qk_f[:, 0, 0:L], qk[hp, :, off:off + L], [qk], [qk_f], qk_f)
            P.dma(sp, qk_f[:, 1, 0:L], qk[2 + hp, :, off:off + L], [qk], [qk_f], qk_f)
            if gi == 0:
                P.dma(sp, lrT[:, 0:L], pLr[:, off:off + L], [pLr], [lrT], lrT)
            for d in range(2):
                if gi == 0:
                    for t0 in range(0, L, 512):
                        n_ = min(512, L - t0)
                        P.op(pe, lambda d=d, t0=t0, n_=n_: nc.tensor.matmul(
                            psL[:, 0:n_], lhsT=wg[:, d, hp * 128:(hp + 1) * 128], rhs=lrT[:, t0:t0 + n_], start=True, stop=True),
                             [wg, lrT], [psL])
                        P.op(act, lambda d=d, t0=t0, n_=n_: nc.scalar.activation(
                            out=lsp[:, t0:t0 + n_], in_=psL[:, 0:n_], func=AF.Exp, scale=-1.0, bias=par[:, d, hp:hp + 1]),
                             [psL, par], [lsp])
                    P.op(act, lambda: nc.scalar.activation(out=lsp[:, 0:L], in_=lsp[:, 0:L], func=AF.Ln, bias=1.0), [lsp], [lsp])
                else:
                    P.op(dve, lambda d=d: nc.vector.tensor_scalar_mul(out=lsp[:, 0:L], in0=rmask[:, 0:L], scalar1=0.0), [rmask], [lsp])
                    P.op(dve, lambda d=d: nc.vector.tensor_scalar_add(out=lsp[:, 0:L], in0=lsp[:, 0:L], scalar1=par[:, d, hp:hp + 1]),
                         [lsp, par], [lsp])
                P.op(dve, lambda: nc.vector.tensor_tensor_scan(out=cs[:, 0:L], data0=rmask[:, 0:L], data1=lsp[:, 0:L], initial=0.0,
                                                               op0=ALU.mult, op1=ALU.add), [rmask, lsp], [cs])
                cs3 = cs[:, 0:L].rearrange("p (n t) -> p n t", t=128)
                if d == 0:
                    bsrc = cs
                else:
                    bsrc = bb
                    P.op(dve, lambda: nc.vector.tensor_tensor(out=bb[:, 0:L], in0=lsp[:, 0:L], in1=cs[:, 0:L], op=ALU.subtract), [lsp, cs], [bb])
                    P.op(dve, lambda cs3=cs3: nc.vector.tensor_tensor(
                        out=bb[:, 0:L].rearrange("p (n t) -> p n t", t=128), in0=bb[:, 0:L].rearrange("p (n t) -> p n t", t=128),
                        in1=cs3[:, :, 127:128].to_broadcast([128, nch, 128]), op=ALU.add), [bb, cs], [bb])
                P.op(act, lambda bsrc=bsrc: nc.scalar.activation(out=Eb[:, 0:L], in_=bsrc[:, 0:L], func=AF.Exp, scale=-1.0 / 16.0), [bsrc], [Eb])
                P.op(act, lambda bsrc=bsrc: nc.scalar.activation(out=Enb[:, 0:L], in_=bsrc[:, 0:L], func=AF.Exp, scale=1.0 / 16.0), [bsrc], [Enb])
                for a in range(2):
                    pa = slice(a * 64, (a + 1) * 64)
                    P.op(dve, lambda d=d, a=a, pa=pa: nc.vector.tensor_tensor(out=qdm[d][a][pa, 0:L], in0=qk_f[pa, 0, 0:L], in1=Eb[pa, 0:L], op=ALU.mult),
                         [qk_f, Eb], [qdm[d][a]])
                P.op(pool, lambda d=d: nc.gpsimd.tensor_tensor(out=kd[d][:, 0:L], in0=qk_f[:, 1, 0:L], in1=Enb[:, 0:L], op=ALU.mult), [qk_f, Enb], [kd[d]])
                col = 127 if d == 0 else 0
                P.op(dve, lambda d=d, col=col: nc.vector.tensor_copy(
                    out=dec[:, d, 0:nch], in_=Eb[:, 0:L].rearrange("p (n t) -> p n t", t=128)[:, :, col]), [Eb], [dec])
            SP.close()
            npc = max(1, nch // 4)
            cpp = min(4, nch)
            for pc in range(npc):
                t0 = off + pc * 512
                P.dma(sp, vst[:, 0:cpp, :], vT[t0:t0 + cpp * 128, hp * 256:(hp + 1) * 256].rearrange("(n t) c -> t n c", t=128),
                      [vT], [vst], vst)
                P.op(act, lambda pc=pc: nc.scalar.copy(out=vb[:, pc * 4:pc * 4 + cpp, :], in_=vst[:, 0:cpp, :]), [vst], [vb])
            P.op(dve, lambda: nc.vector.memset(o_all[:, 0:nch, :], 0.0), [], [o_all])
            for d in range(2):
                P.op(dve, lambda d=d: nc.vector.memset(Rf[d][:], 0.0), [], [Rf[d]])
                if si == 0:
                    for a in range(2):
                        P.dma(sp, Rf[d][a * 64:(a + 1) * 64, a * 128:(a + 1) * 128], st_in[l, d, 2 * hp + a], [st_in], [Rf[d]], Rf[d])
                P.op(act, lambda d=d: nc.scalar.copy(out=Sbf[d][0][:], in_=Rf[d][:]), [Rf[d]], [Sbf[d][0]])
            for i in range(nch):
                for d in range(2):
                    n = i if d == 0 else nch - 1 - i
                    npv = (i - 1) if d == 0 else nch - i
                    sl = slice(n * 128, (n + 1) * 128)
                    Scur = Sbf[d][i % 2]
                    Snxt = Sbf[d][(i + 1) % 2]
                    P.op(pe, lambda d=d, sl=sl: nc.tensor.transpose(psK[d][:, d, :], kd[d][:, sl], ident_b[:]), [kd[d], ident_b], [psK[d]])
                    P.op(act, lambda d=d: nc.scalar.copy(out=kdt[d][:], in_=psK[d][:, d, :]), [psK[d]], [kdt[d]])
                    for a in range(2):
                        P.op(pe, lambda d=d, a=a, sl=sl: nc.tensor.matmul(
                            psA[d][:, a, :], lhsT=kd[d][:, sl], rhs=qdm[d][a][:, sl], start=True, stop=True),
                             [kd[d], qdm[d][a]], [psA[d]])
                    P.op(pe, lambda d=d, n=n: nc.tensor.matmul(psUp[d][:], lhsT=kdt[d][:], rhs=vb[:, n, :], start=True, stop=True),
                         [kdt[d], vb], [psUp[d]])
                    P.op(dve, lambda d=d: nc.vector.tensor_tensor(
                        out=att[d][:], in0=psA[d][:], in1=masks[:, d, :].unsqueeze(1).to_broadcast([128, 2, 128]), op=ALU.mult),
                         [psA[d], masks], [att[d]])
                    if i == 0:
                        P.op(dve, lambda d=d: nc.vector.tensor_tensor(out=Rf[d][:], in0=Rf[d][:], in1=psUp[d][:], op=ALU.add),
                             [Rf[d], psUp[d]], [Rf[d]])
                    else:
                        P.op(dve, lambda d=d, npv=npv: nc.vector.scalar_tensor_tensor(
                            out=Rf[d][:], in0=Rf[d][:], scalar=dec[:, d, npv:npv + 1], in1=psUp[d][:], op0=ALU.mult, op1=ALU.add),
                             [Rf[d], psUp[d], dec], [Rf[d]])
                    if i < nch - 1:
                        P.op(act, lambda d=d, n=n, Snxt=Snxt: nc.scalar.activation(out=Snxt[:], in_=Rf[d][:], func=AF.Copy, scale=dec[:, d, n:n + 1]),
                             [Rf[d], dec], [Snxt])
                    for a in range(2):
                        P.op(pe, lambda d=d, a=a, n=n: nc.tensor.matmul(
                            psO[d][:, a * 128:(a + 1) * 128], lhsT=att[d][:, a, :], rhs=vb[:, n, a * 128:(a + 1) * 128], start=True, stop=False),
                             [att[d], vb], [psO[d]])
                        P.op(pe, lambda d=d, a=a, sl=sl, Scur=Scur: nc.tensor.matmul(
                            psO[d][:, a * 128:(a + 1) * 128], lhsT=qdm[d][a][:, sl], rhs=Scur[:, a * 128:(a + 1) * 128],
                            start=False, stop=True), [qdm[d][a], Scur], [psO[d]])
                    P.op(dve, lambda d=d, n=n: nc.vector.tensor_tensor(out=o_all[:, n, :], in0=psO[d][:], in1=o_all[:, n, :], op=ALU.add),
                         [psO[d], o_all], [o_all])
                    if i == nch - 1 and si > 0:
                        P.op(dve, lambda d=d, n=n: nc.vector.tensor_scalar_mul(out=Rf[d][:], in0=Rf[d][:], scalar1=dec[:, d, n:n + 1]),
                             [Rf[d], dec], [Rf[d]])
                        for a in range(2):
                            P.dma(pool, st_out[si - 1, l, d, 2 * hp + a], Rf[d][a * 64:(a + 1) * 64, a * 128:(a + 1) * 128], [Rf[d]], [st_out], Rf[d])
            SQ = P.scope()
            tsq = SQ.sb("tsq", [128, 8, 128]); gst = SQ.sb("gst", [128, 4, 256])
            s1 = SQ.sb("s1", [128, 8]); s2 = SQ.sb("s2", [128, 8]); rs = SQ.sb("rs", [128, 8])
            yb = SQ.sb("yb", [128, 4, 256], BF16); ysb = SQ.sb("ysb", [128, 2, 512], BF16)
            psY = SQ.ps("psY", [128, 2, 512], BF16)
            for pc in range(npc):
                t0 = off + pc * 512
                ntk = cpp * 128
                P.dma(sp, gst[:, 0:cpp, :], gT[t0:t0 + ntk, hp * 256:(hp + 1) * 256].rearrange("(n t) c -> t n c", t=128), [gT], [gst], gst)
                o3 = o_all[:, pc * 4:pc * 4 + cpp, :].rearrange("p n (a v) -> p (n a) v", a=2)
                na = cpp * 2
                P.op(dve, lambda o3=o3: nc.vector.tensor_reduce(out=s1[:, 0:na], in_=o3, axis=AX.X, op=ALU.add), [o_all], [s1])
                P.op(act, lambda o3=o3: nc.scalar.activation(out=tsq[:, 0:na, :], in_=o3, func=AF.Square), [o_all], [tsq])
                P.op(dve, lambda: nc.vector.tensor_reduce(out=s2[:, 0:na], in_=tsq[:, 0:na, :], axis=AX.X, op=ALU.add), [tsq], [s2])
                if gi == 0:
                    P.op(act, lambda: nc.scalar.activation(out=rs[:, 0:na], in_=s2[:, 0:na], func=AF.Ln, scale=1.0 / 128.0, bias=NORM_EPS), [s2], [rs])
                else:
                    P.op(dve, lambda: nc.vector.tensor_scalar_mul(out=s1[:, 0:na], in0=s1[:, 0:na], scalar1=1.0 / 128.0), [s1], [s1])
                    P.op(dve, lambda: nc.vector.tensor_tensor(out=rs[:, 0:na], in0=s1[:, 0:na], in1=s1[:, 0:na], op=ALU.mult), [s1], [rs])
                    P.op(dve, lambda: nc.vector.scalar_tensor_tensor(out=rs[:, 0:na], in0=s2[:, 0:na], scalar=1.0 / 128.0, in1=rs[:, 0:na],
                                                                     op0=ALU.mult, op1=ALU.subtract), [s2, rs], [rs])
                    P.op(act, lambda: nc.scalar.activation(out=rs[:, 0:na], in_=rs[:, 0:na], func=AF.Ln, bias=LN_EPS), [rs], [rs])
                    P.op(dve, lambda o3=o3: nc.vector.tensor_tensor(out=o3, in0=o3, in1=s1[:, 0:na].unsqueeze(2).to_broadcast([128, na, 128]),
                                                                    op=ALU.subtract), [o_all, s1], [o_all])
                P.op(act, lambda: nc.scalar.activation(out=rs[:, 0:na], in_=rs[:, 0:na], func=AF.Exp, scale=-0.5), [rs], [rs])
                P.op(dve, lambda o3=o3: nc.vector.tensor_tensor(out=o3, in0=o3, in1=rs[:, 0:na].unsqueeze(2).to_broadcast([128, na, 128]),
                                                                op=ALU.mult), [o_all, rs], [o_all])
                if gi == 0:
                    P.op(dve, lambda o3=o3: nc.vector.tensor_tensor(out=o3, in0=o3, in1=nw[:, :].unsqueeze(1).to_broadcast([128, na, 128]),
                                                                    op=ALU.mult), [o_all, nw], [o_all])
                P.op(dve, lambda pc=pc: nc.vector.tensor_tensor(out=yb[:, 0:cpp, :], in0=o_all[:, pc * 4:pc * 4 + cpp, :], in1=gst[:, 0:cpp, :],
                                                                op=ALU.mult), [o_all, gst], [yb])
                for n8 in range(cpp):
                    for a in range(2):
                        P.op(pe, lambda n8=n8, a=a: nc.tensor.transpose(psY[:, a, n8 * 128:(n8 + 1) * 128], yb[:, n8, a * 128:(a + 1) * 128],
                                                                        ident_b[:]), [yb, ident_b], [psY])
                P.op(act, lambda: nc.scalar.copy(out=ysb[:, :, 0:ntk], in_=psY[:, :, 0:ntk]), [psY], [ysb])
                for a in range(2):
                    P.dma(pool, mixT[gi * 4 + 2 * hp + a, :, t0:t0 + ntk], ysb[:, a, 0:ntk], [ysb], [mixT], ysb)
            SQ.close()
    S.close()


TWO_PI = float(2.0 * np.pi)
MAGIC = 12582912.0
CW1 = 6.28125
CW2 = float(2.0 * np.pi - 6.28125)


def mixer_s5(P, nc, l, env):
    pe, dve, act, pool, sp = P.pe, P.dve, P.act, P.pool, P.sp
    su = env["pF"]["su"]; mixT = env["mixT"]; ident_f = env["ident_f"]
    E = env
    S = P.scope()
    tp1 = S.sb("tp1", [128, LS])
    bt = [[S.sb(f"bt{i}{j}", [128, 512]) for j in range(6)] for i in range(1)]
    zb = [[S.sb(f"zb{d}{c}", [128, LS], BF16) for c in range(2)] for d in range(2)]
    ust = [S.sb("ust0", [32, LS])] * 2; ugp = [S.sb(f"ugp{i}", [128, LS], BF16) for i in range(2)]
    BTt = [S.sb(f"BTt{d}", [128, 2, 128], BF16) for d in range(2)]
    CPt = [S.sb(f"CPt{d}", [128, 2, 128], BF16) for d in range(2)]
    zz = S.sb("zz", [128, 4, LS], BF16)
    gw = S.sb("gw", [128, 4, 512], BF16)
    pv = {n: S.sb("pv_" + n, [128, 32]) for n in ("are", "aim", "st", "r", "th", "s", "c", "kre", "kim", "t0", "t1", "t2",
                                                   "h0re", "h0im", "hfre", "hfim", "zero", "ph")}
    dvec = S.sb("dvec", [128, 4]); gbv = S.sb("gbv", [128, 4])
    lastc = S.sb("lastc", [128, 32, 4])
    psB = [[S.ps(f"psB{i}{c}", [128, 512]) for c in range(2)] for i in range(2)]
    psY = [S.ps(f"psY{i}", [128, 512]) for i in range(4)]
    s5BT = env["s5BT"]; s5CP = env["s5CP"]

    def V(name, fn, reads, writes):
        P.op(dve, fn, reads, writes)

    P.dma(sp, tp1[:], E["tp1_in"][:].partition_broadcast(128), [E["tp1_in"]], [tp1], tp1)
    for i_ in range(2):
        P.op(pool, lambda i_=i_: nc.gpsimd.memset(ugp[i_][:], 0.0), [], [ugp[i_]])
    P.op(dve, lambda: nc.vector.memset(pv["zero"][:], 0.0), [], [pv["zero"]])
    SP = P.scope()
    BT = SP.sb("BT", [128, 32, 2, 128], BF16)
    Cpad = SP.sb("Cpad", [128, 32, 2, 128], BF16)
    P.op(pool, lambda: nc.gpsimd.memset(BT[:], 0.0), [], [BT])
    P.op(pool, lambda: nc.gpsimd.memset(Cpad[:], 0.0), [], [Cpad])
    Bc = [SP.sb(f"Bc{c}", [128, 32, 16]) for c in range(2)]
    Bb = [SP.sb(f"Bb{c}", [128, 32, 16]) for c in range(2)]
    Bsm = SP.sb("Bsm", [128, 32, 2, 32])
    Cc = [SP.sb(f"Cc{c}", [128, 32, 16]) for c in range(2)]
    tB = SP.sb("tB", [128, 32, 16])
    gwf = SP.sb("gwf", [128, 4, 512])
    with nc.allow_non_contiguous_dma(reason="small parameter layout transforms"):
        for g2 in range(2):
            pa = slice(g2 * 64, (g2 + 1) * 64)
            for d in range(2):
                ds_ = slice(d * 16, (d + 1) * 16)
                P.dma(sp, pv["are"][pa, ds_], E["s5_a_re"][l, d, g2::2].rearrange("gp p -> p gp"), [E["s5_a_re"]], [pv["are"]], pv["are"])
                P.dma(sp, pv["aim"][pa, ds_], E["s5_a_im"][l, d, g2::2].rearrange("gp p -> p gp"), [E["s5_a_im"]], [pv["aim"]], pv["aim"])
                P.dma(sp, pv["st"][pa, ds_], E["s5_log_step"][l, d, g2::2].partition_broadcast(64), [E["s5_log_step"]], [pv["st"]], pv["st"])
                P.dma(sp, pv["h0re"][pa, ds_], E["s5re_in"][l, d, g2::2].rearrange("gp p -> p gp"), [E["s5re_in"]], [pv["h0re"]], pv["h0re"])
                P.dma(sp, pv["h0im"][pa, ds_], E["s5im_in"][l, d, g2::2].rearrange("gp p -> p gp"), [E["s5im_in"]], [pv["h0im"]], pv["h0im"])
                P.dma(sp, Bc[0][pa, ds_, :], E["s5_b_re"][l, d, g2::2].rearrange("gp p i -> p gp i"), [E["s5_b_re"]], [Bc[0]], Bc[0])
                P.dma(sp, Bc[1][pa, ds_, :], E["s5_b_im"][l, d, g2::2].rearrange("gp p i -> p gp i"), [E["s5_b_im"]], [Bc[1]], Bc[1])
                for gp_ in range(16):
                    P.dma(sp, Cc[0][pa, d * 16 + gp_, :], E["s5_c_re"][l, d, 2 * gp_ + g2].rearrange("o p -> p o"), [E["s5_c_re"]], [Cc[0]], Cc[0])
                    P.dma(sp, Cc[1][pa, d * 16 + gp_, :], E["s5_c_im"][l, d, 2 * gp_ + g2].rearrange("o p -> p o"), [E["s5_c_im"]], [Cc[1]], Cc[1])
        P.dma(sp, dvec[:], E["s5_d"][l].rearrange("(c p) -> p c", p=128), [E["s5_d"]], [dvec], dvec)
        P.dma(sp, gbv[:], E["s5_glu_b"][l].rearrange("(c p) -> p c", p=128), [E["s5_glu_b"]], [gbv], gbv)
    P.dma(sp, gwf[:], E["s5_glu_w"][l].rearrange("(c p) n -> p c n", p=128), [E["s5_glu_w"]], [gwf], gwf)
    P.op(act, lambda: nc.scalar.copy(out=gw[:], in_=gwf[:]), [gwf], [gw])
    a = pv
    P.op(act, lambda: nc.scalar.activation(out=a["st"][:], in_=a["st"][:], func=AF.Exp), [a["st"]], [a["st"]])
    V("ar", lambda: nc.vector.tensor_tensor(out=a["r"][:], in0=a["are"][:], in1=a["st"][:], op=ALU.mult), [a["are"], a["st"]], [a["r"]])
    P.op(act, lambda: nc.scalar.activation(out=a["r"][:], in_=a["r"][:], func=AF.Exp), [a["r"]], [a["r"]])
    V("th", lambda: nc.vector.tensor_tensor(out=a["th"][:], in0=a["aim"][:], in1=a["st"][:], op=ALU.mult), [a["aim"], a["st"]], [a["th"]])

    def range_reduce(y, k, n, Y, K):
        V("rr1", lambda: nc.vector.tensor_scalar(out=k[:, 0:n], in0=y[:, 0:n], scalar1=1.0 / TWO_PI, scalar2=MAGIC, op0=ALU.mult, op1=ALU.add), [Y], [K])
        V("rr2", lambda: nc.vector.tensor_scalar_add(out=k[:, 0:n], in0=k[:, 0:n], scalar1=-MAGIC), [K], [K])
        V("rr3", lambda: nc.vector.scalar_tensor_tensor(out=y[:, 0:n], in0=k[:, 0:n], scalar=-CW1, in1=y[:, 0:n], op0=ALU.mult, op1=ALU.add), [K, Y], [Y])
        V("rr4", lambda: nc.vector.scalar_tensor_tensor(out=y[:, 0:n], in0=k[:, 0:n], scalar=-CW2, in1=y[:, 0:n], op0=ALU.mult, op1=ALU.add), [K, Y], [Y])

    def sincos(y, n, sn, cs, Y, SN, CS):
        P.op(act, lambda: nc.scalar.activation(out=sn[:, 0:n], in_=y[:, 0:n], func=AF.Sin, scale=0.999998), [Y], [SN])
        P.op(act, lambda: nc.scalar.activation(out=cs[:, 0:n], in_=y[:, 0:n], func=AF.Sin, scale=0.5), [Y], [CS])
        P.op(act, lambda: nc.scalar.activation(out=cs[:, 0:n], in_=cs[:, 0:n], func=AF.Square), [CS], [CS])
        P.op(pool, lambda: nc.gpsimd.tensor_scalar(out=cs[:, 0:n], in0=cs[:, 0:n], scalar1=-2.0, scalar2=1.0, op0=ALU.mult, op1=ALU.add), [CS], [CS])

    range_reduce(a["th"], a["t0"], 32, a["th"], a["t0"])
    sincos(a["th"], 32, a["s"], a["c"], a["th"], a["s"], a["c"])
    V("lre", lambda: nc.vector.tensor_tensor(out=a["t0"][:], in0=a["r"][:], in1=a["c"][:], op=ALU.mult), [a["r"], a["c"]], [a["t0"]])
    V("lre1", lambda: nc.vector.tensor_scalar_add(out=a["t0"][:], in0=a["t0"][:], scalar1=-1.0), [a["t0"]], [a["t0"]])
    V("lim", lambda: nc.vector.tensor_tensor(out=a["t1"][:], in0=a["r"][:], in1=a["s"][:], op=ALU.mult), [a["r"], a["s"]], [a["t1"]])
    V("den", lambda: nc.vector.tensor_tensor(out=a["t2"][:], in0=a["are"][:], in1=a["are"][:], op=ALU.mult), [a["are"]], [a["t2"]])
    V("den2", lambda: nc.vector.tensor_tensor(out=a["kre"][:], in0=a["aim"][:], in1=a["aim"][:], op=ALU.mult), [a["aim"]], [a["kre"]])
    V("den3", lambda: nc.vector.tensor_tensor(out=a["t2"][:], in0=a["t2"][:], in1=a["kre"][:], op=ALU.add), [a["t2"], a["kre"]], [a["t2"]])
    V("rden", lambda: nc.vector.reciprocal(out=a["t2"][:], in_=a["t2"][:]), [a["t2"]], [a["t2"]])
    V("k1", lambda: nc.vector.tensor_tensor(out=a["kre"][:], in0=a["t0"][:], in1=a["are"][:], op=ALU.mult), [a["t0"], a["are"]], [a["kre"]])
    V("k2", lambda: nc.vector.tensor_tensor(out=a["kim"][:], in0=a["t1"][:], in1=a["aim"][:], op=ALU.mult), [a["t1"], a["aim"]], [a["kim"]])
    V("k3", lambda: nc.vector.tensor_tensor(out=a["kre"][:], in0=a["kre"][:], in1=a["kim"][:], op=ALU.add), [a["kre"], a["kim"]], [a["kre"]])
    V("k4", lambda: nc.vector.tensor_tensor(out=a["kim"][:], in0=a["t1"][:], in1=a["are"][:], op=ALU.mult), [a["t1"], a["are"]], [a["kim"]])
    V("k5", lambda: nc.vector.tensor_tensor(out=a["t1"][:], in0=a["t0"][:], in1=a["aim"][:], op=ALU.mult), [a["t0"], a["aim"]], [a["t1"]])
    V("k6", lambda: nc.vector.tensor_tensor(out=a["kim"][:], in0=a["kim"][:], in1=a["t1"][:], op=ALU.subtract), [a["kim"], a["t1"]], [a["kim"]])
    V("k7", lambda: nc.vector.tensor_tensor(out=a["kre"][:], in0=a["kre"][:], in1=a["t2"][:], op=ALU.mult), [a["kre"], a["t2"]], [a["kre"]])
    V("k8", lambda: nc.vector.tensor_tensor(out=a["kim"][:], in0=a["kim"][:], in1=a["t2"][:], op=ALU.mult), [a["kim"], a["t2"]], [a["kim"]])
    kre_b = a["kre"][:, :].unsqueeze(2).to_broadcast([128, 32, 16])
    kim_b = a["kim"][:, :].unsqueeze(2).to_broadcast([128, 32, 16])
    V("b1", lambda: nc.vector.tensor_tensor(out=Bb[0][:], in0=Bc[0][:], in1=kre_b, op=ALU.mult), [Bc[0], a["kre"]], [Bb[0]])
    V("b2", lambda: nc.vector.tensor_tensor(out=tB[:], in0=Bc[1][:], in1=kim_b, op=ALU.mult), [Bc[1], a["kim"]], [tB])
    V("b3", lambda: nc.vector.tensor_tensor(out=Bb[0][:], in0=Bb[0][:], in1=tB[:], op=ALU.subtract), [Bb[0], tB], [Bb[0]])
    V("b4", lambda: nc.vector.tensor_tensor(out=Bb[1][:], in0=Bc[1][:], in1=kre_b, op=ALU.mult), [Bc[1], a["kre"]], [Bb[1]])
    V("b5", lambda: nc.vector.tensor_tensor(out=tB[:], in0=Bc[0][:], in1=kim_b, op=ALU.mult), [Bc[0], a["kim"]], [tB])
    V("b6", lambda: nc.vector.tensor_tensor(out=Bb[1][:], in0=Bb[1][:], in1=tB[:], op=ALU.add), [Bb[1], tB], [Bb[1]])
    V("bsm0", lambda: nc.vector.memset(Bsm[:], 0.0), [], [Bsm])
    for g2 in range(2):
        pa = slice(g2 * 64, (g2 + 1) * 64)
        for c in range(2):
            V("bsm", lambda pa=pa, c=c, g2=g2: nc.vector.tensor_copy(out=Bsm[pa, :, c, g2 * 16:(g2 + 1) * 16], in_=Bb[c][pa, :, :]), [Bb[c]], [Bsm])
    for j in range(32):
        for c in range(2):
            ps = psB[(j * 2 + c) % 2][0]
            P.op(pe, lambda ps=ps, j=j, c=c: nc.tensor.transpose(ps[0:32, 0:128], Bsm[:, j, c, :], ident_f[:]), [Bsm, ident_f], [ps])
            P.op(act, lambda ps=ps, j=j, c=c: nc.scalar.copy(out=BT[0:32, j, c, :], in_=ps[0:32, 0:128]), [ps], [BT])
    for c in range(2):
        if c == 1:
            V("cneg", lambda: nc.vector.tensor_scalar_mul(out=Cc[1][:], in0=Cc[1][:], scalar1=-1.0), [Cc[1]], [Cc[1]])
        for g2 in range(2):
            pa = slice(g2 * 64, (g2 + 1) * 64)
            for q in range(4):
                src = Cc[c][pa, :, :].rearrange("p (dg q) o -> p dg q o", q=4)[:, :, q, :]
                dst = Cpad[pa, :, c, q * 32 + g2 * 16:q * 32 + g2 * 16 + 16].rearrange("p (dg q) o -> p dg q o", q=4)[:, :, q, :]
                V("cpad", lambda src=src, dst=dst: nc.vector.tensor_copy(out=dst, in_=src), [Cc[c]], [Cpad])
    P.dma(pool, s5BT[:].rearrange("j p c m -> p j c m"), BT[:], [BT], [s5BT], BT)
    P.dma(pool, s5CP[:].rearrange("j p c m -> p j c m"), Cpad[:], [Cpad], [s5CP], Cpad)
    V("ph", lambda: nc.vector.tensor_scalar_mul(out=a["ph"][:], in0=a["th"][:], scalar1=1.0 / TWO_PI), [a["th"]], [a["ph"]])
    SP.close()
    SM = P.scope()
    CS = [[SM.sb(f"cs{d}{i}", [128, LS]) for i in range(2)] for d in range(2)]
    WD = [[SM.sb(f"wd{d}{i}", [128, LS]) for i in range(2)] for d in range(2)]
    MM = [[SM.sb(f"mm{d}{i}", [128, LS]) for i in range(2)] for d in range(2)]
    PP = [[SM.sb(f"pp{d}{i}", [128, LS], BF16) for i in range(2)] for d in range(2)]
    PQ = [[SM.sb(f"pq{d}{i}", [128, LS], BF16) for i in range(2)] for d in range(2)]
    W = [WD[0][0], WD[0][1], MM[0][0], MM[0][1]]

    for si, (off, L) in enumerate(SEQS):
        nb = max(1, L // 512)
        bw = min(512, L)
        h0r = (a["h0re"] if si == 0 else a["zero"]); h0i = (a["h0im"] if si == 0 else a["zero"])

        def stage_T(cc, gq, d):
            gp = cc * 4 + gq
            j = d * 16 + gp
            cosA, sinA = CS[d]
            ang, kk = WD[d]
            if d == 0:
                P.dma(sp, ust[gq % 2][:, 0:L], su[cc, gq * 32:(gq + 1) * 32, off:off + L], [su], [ust[gq % 2]], ust[gq % 2])
                P.op(act, lambda: nc.scalar.copy(out=ugp[gq % 2][0:32, 0:L], in_=ust[gq % 2][:, 0:L]), [ust[gq % 2]], [ugp[gq % 2]])
            P.dma(sp, BTt[d][:], s5BT[j], [s5BT], [BTt[d]], BTt[d])
            P.dma(sp, CPt[d][:], s5CP[j], [s5CP], [CPt[d]], CPt[d])
            P.op(act, lambda: nc.scalar.activation(out=kk[:, 0:L], in_=tp1[:, 0:L], func=AF.Identity, scale=a["ph"][:, j:j + 1], bias=MAGIC), [tp1, a["ph"]], [kk])
            P.op(act, lambda: nc.scalar.activation(out=kk[:, 0:L], in_=kk[:, 0:L], func=AF.Identity, bias=-MAGIC), [kk], [kk])
            V("fr", lambda: nc.vector.scalar_tensor_tensor(out=ang[:, 0:L], in0=tp1[:, 0:L], scalar=a["ph"][:, j:j + 1], in1=kk[:, 0:L],
                                                           op0=ALU.mult, op1=ALU.subtract), [tp1, a["ph"], kk], [ang])
            P.op(act, lambda: nc.scalar.activation(out=sinA[:, 0:L], in_=ang[:, 0:L], func=AF.Sin, scale=TWO_PI * 0.999998), [ang], [sinA])
            P.op(act, lambda: nc.scalar.activation(out=cosA[:, 0:L], in_=ang[:, 0:L], func=AF.Sin, scale=TWO_PI * 0.5), [ang], [cosA])
            P.op(act, lambda: nc.scalar.activation(out=cosA[:, 0:L], in_=cosA[:, 0:L], func=AF.Square), [cosA], [cosA])
            P.op(act, lambda: nc.scalar.activation(out=cosA[:, 0:L], in_=cosA[:, 0:L], func=AF.Identity, scale=-2.0, bias=1.0), [cosA], [cosA])

        def stage_GS(cc, gq, d):
            gp = cc * 4 + gq
            j = d * 16 + gp
            cosA, sinA = CS[d]
            gre, gim = WD[d]
            mre, mim = MM[d]
            ug = ugp[gq % 2]
            for tb in range(nb):
                t0 = tb * bw
                pb = psB[tb % 2]
                e = bt[0]
                for c in range(2):
                    P.op(pe, lambda c=c, t0=t0, pb=pb: nc.tensor.matmul(pb[c][:, 0:bw], lhsT=BTt[d][:, c, :], rhs=ug[:, t0:t0 + bw],
                                                                        start=True, stop=True), [BTt[d], ug], [pb[c]])
                if d == 0:
                    sl = lambda X, t0=t0: X[:, t0:t0 + bw]
                else:
                    sl = lambda X, t0=t0: X[:, L - t0 - bw:L - t0][:, ::-1]
                P.op(act, lambda pb=pb, e=e: nc.scalar.copy(out=e[0][:, 0:bw], in_=pb[0][:, 0:bw]), [pb[0]], [e[0]])
                P.op(act, lambda pb=pb, e=e: nc.scalar.copy(out=e[1][:, 0:bw], in_=pb[1][:, 0:bw]), [pb[1]], [e[1]])
                V("g1", lambda sl=sl, e=e: nc.vector.tensor_tensor(out=e[2][:, 0:bw], in0=e[0][:, 0:bw], in1=sl(cosA), op=ALU.mult), [e[0], cosA], [e[2]])
                V("g2", lambda sl=sl, e=e: nc.vector.tensor_tensor(out=e[3][:, 0:bw], in0=e[1][:, 0:bw], in1=sl(sinA), op=ALU.mult), [e[1], sinA], [e[3]])
                P.op(pool, lambda sl=sl, e=e: nc.gpsimd.tensor_tensor(out=e[4][:, 0:bw], in0=e[1][:, 0:bw], in1=sl(cosA), op=ALU.mult), [e[1], cosA], [e[4]])
                P.op(pool, lambda sl=sl, e=e: nc.gpsimd.tensor_tensor(out=e[5][:, 0:bw], in0=e[0][:, 0:bw], in1=sl(sinA), op=ALU.mult), [e[0], sinA], [e[5]])
                V("g5", lambda sl=sl, e=e: nc.vector.tensor_tensor(out=sl(gre), in0=e[2][:, 0:bw], in1=e[3][:, 0:bw], op=ALU.add), [e[2], e[3]], [gre])
                P.op(pool, lambda sl=sl, e=e: nc.gpsimd.tensor_tensor(out=sl(gim), in0=e[4][:, 0:bw], in1=e[5][:, 0:bw], op=ALU.subtract), [e[4], e[5]], [gim])
            rb = a["r"][:, j:j + 1].to_broadcast([128, L])
            V("scr", lambda: nc.vector.tensor_tensor_scan(out=mre[:, 0:L], data0=rb, data1=gre[:, 0:L], initial=h0r[:, j:j + 1], op0=ALU.mult, op1=ALU.add),
              [gre, a["r"], h0r], [mre])
            V("sci", lambda: nc.vector.tensor_tensor_scan(out=mim[:, 0:L], data0=rb, data1=gim[:, 0:L], initial=h0i[:, j:j + 1], op0=ALU.mult, op1=ALU.add),
              [gim, a["r"], h0i], [mim])
            if si > 0:
                for ci, X in enumerate((cosA, sinA, mre, mim)):
                    P.op(act, lambda ci=ci, X=X: nc.scalar.copy(out=lastc[:, j, ci:ci + 1], in_=X[:, L - 1:L]), [X], [lastc])

        def stage_Z(cc, gq, d):
            cosA, sinA = CS[d]
            mre, mim = MM[d]
            p1, p2 = PP[d]
            p3, p4 = PQ[d]
            zo = (lambda X: X[:, 0:L]) if d == 0 else (lambda X: X[:, 0:L][:, ::-1])
            V("p1", lambda: nc.vector.tensor_tensor(out=p1[:, 0:L], in0=cosA[:, 0:L], in1=mre[:, 0:L], op=ALU.mult), [cosA, mre], [p1])
            V("p2", lambda: nc.vector.tensor_tensor(out=p2[:, 0:L], in0=sinA[:, 0:L], in1=mim[:, 0:L], op=ALU.mult), [sinA, mim], [p2])
            P.op(pool, lambda: nc.gpsimd.tensor_tensor(out=p3[:, 0:L], in0=sinA[:, 0:L], in1=mre[:, 0:L], op=ALU.mult), [sinA, mre], [p3])
            P.op(pool, lambda: nc.gpsimd.tensor_tensor(out=p4[:, 0:L], in0=cosA[:, 0:L], in1=mim[:, 0:L], op=ALU.mult), [cosA, mim], [p4])
            V("zre", lambda: nc.vector.tensor_tensor(out=zo(zb[d][0]), in0=p1[:, 0:L], in1=p2[:, 0:L], op=ALU.subtract), [p1, p2], [zb[d][0]])
            P.op(pool, lambda: nc.gpsimd.tensor_tensor(out=zo(zb[d][1]), in0=p3[:, 0:L], in1=p4[:, 0:L], op=ALU.add), [p3, p4], [zb[d][1]])
            for tb in range(nb):
                t0 = tb * bw
                for c in range(2):
                    first = (gq == 0 and d == 0 and c == 0)
                    last = (gq == 3 and d == 1 and c == 1)
                    P.op(pe, lambda tb=tb, t0=t0, c=c, first=first, last=last: nc.tensor.matmul(
                        psY[tb][:, 0:bw], lhsT=CPt[d][:, c, :], rhs=zb[d][c][:, t0:t0 + bw], start=first, stop=last), [CPt[d], zb[d][c]], [psY[tb]])

        for cc in range(4):
            its = [(gq, d) for gq in range(4) for d in range(2)]
            stage_T(cc, *its[0])
            for ii, (gq, d) in enumerate(its):
                stage_GS(cc, gq, d)
                if ii + 1 < len(its):
                    stage_T(cc, *its[ii + 1])
                stage_Z(cc, gq, d)
            uT, yv, w1, w2 = W[0], W[1], W[2], W[3]
            P.dma(sp, uT[:, 0:L], su[cc, :, off:off + L], [su], [uT], uT)
            for tb in range(nb):
                t0 = tb * bw
                V("yv", lambda tb=tb, t0=t0: nc.vector.scalar_tensor_tensor(out=yv[:, t0:t0 + bw], in0=uT[:, t0:t0 + bw], scalar=dvec[:, cc:cc + 1],
                                                                          in1=psY[tb][:, 0:bw], op0=ALU.mult, op1=ALU.add), [uT, dvec, psY[tb]], [yv])
            P.op(act, lambda: nc.scalar.activation(out=w1[:, 0:L], in_=yv[:, 0:L], func=AF.Square), [yv], [w1])
            P.op(pool, lambda: nc.gpsimd.tensor_scalar(out=w1[:, 0:L], in0=w1[:, 0:L], scalar1=0.044715, scalar2=1.0, op0=ALU.mult, op1=ALU.add), [w1], [w1])
            P.op(pool, lambda: nc.gpsimd.tensor_tensor(out=w1[:, 0:L], in0=w1[:, 0:L], in1=yv[:, 0:L], op=ALU.mult), [w1, yv], [w1])
            P.op(act, lambda: nc.scalar.activation(out=w2[:, 0:L], in_=w1[:, 0:L], func=AF.Sigmoid, scale=float(2.0 * np.sqrt(2.0 / np.pi))), [w1], [w2])
            V("zz", lambda: nc.vector.tensor_tensor(out=zz[:, cc, 0:L], in0=yv[:, 0:L], in1=w2[:, 0:L], op=ALU.mult), [yv, w2], [zz])
        for oc in range(4):
            for tb in range(nb):
                t0 = tb * bw
                ps = psB[tb % 2][0]
                for cc in range(4):
                    P.op(pe, lambda ps=ps, cc=cc, oc=oc, t0=t0: nc.tensor.matmul(ps[:, 0:bw], lhsT=gw[:, cc, oc * 128:(oc + 1) * 128], rhs=zz[:, cc, t0:t0 + bw],
                                                                               start=(cc == 0), stop=(cc == 3)), [gw, zz], [ps])
                sg = bt[0][tb % 2]
                P.op(act, lambda ps=ps, sg=sg, oc=oc: nc.scalar.activation(out=sg[:, 0:bw], in_=ps[:, 0:bw], func=AF.Sigmoid, bias=gbv[:, oc:oc + 1]), [ps, gbv], [sg])
                ot = bt[0][2 + tb % 2]
                V("glu", lambda sg=sg, ot=ot, oc=oc, t0=t0: nc.vector.tensor_tensor(out=ot[:, 0:bw].bitcast(BF16)[:, 0:bw], in0=zz[:, oc, t0:t0 + bw], in1=sg[:, 0:bw], op=ALU.mult),
                  [zz, sg], [ot])
                P.dma(pool, mixT[8 + oc, :, off + t0:off + t0 + bw], ot[:, 0:bw].bitcast(BF16)[:, 0:bw], [ot], [mixT], ot)
        if si > 0:
            lc = lastc
            h = a
            V("f1", lambda: nc.vector.tensor_tensor(out=h["t0"][:], in0=lc[:, :, 0], in1=lc[:, :, 2], op=ALU.mult), [lc], [h["t0"]])
            V("f2", lambda: nc.vector.tensor_tensor(out=h["t1"][:], in0=lc[:, :, 1], in1=lc[:, :, 3], op=ALU.mult), [lc], [h["t1"]])
            V("f3", lambda: nc.vector.tensor_tensor(out=h["hfre"][:], in0=h["t0"][:], in1=h["t1"][:], op=ALU.subtract), [h["t0"], h["t1"]], [h["hfre"]])
            V("f4", lambda: nc.vector.tensor_tensor(out=h["t0"][:], in0=lc[:, :, 1], in1=lc[:, :, 2], op=ALU.mult), [lc], [h["t0"]])
            V("f5", lambda: nc.vector.tensor_tensor(out=h["t1"][:], in0=lc[:, :, 0], in1=lc[:, :, 3], op=ALU.mult), [lc], [h["t1"]])
            V("f6", lambda: nc.vector.tensor_tensor(out=h["hfim"][:], in0=h["t0"][:], in1=h["t1"][:], op=ALU.add), [h["t0"], h["t1"]], [h["hfim"]])
            with nc.allow_non_contiguous_dma(reason="state layout"):
                for g2 in range(2):
                    pa = slice(g2 * 64, (g2 + 1) * 64)
                    for d in range(2):
                        ds_ = slice(d * 16, (d + 1) * 16)
                        P.dma(sp, E["ns5re"][si - 1, l, d, g2::2].rearrange("gp p -> p gp"), h["hfre"][pa, ds_], [h["hfre"]], [E["ns5re"]], h["hfre"])
                        P.dma(sp, E["ns5im"][si - 1, l, d, g2::2].rearrange("gp p -> p gp"), h["hfim"][pa, ds_], [h["hfim"]], [E["ns5im"]], h["hfim"])
    SM.close()
    S.close()


def mixer_hy(P, nc, l, env):
    pe, dve, act, pool, sp = P.pe, P.dve, P.act, P.pool, P.sp
    E = env
    hy = E["pF"]["hy"]; mixT = E["mixT"]; ident_b = E["ident_b"]; ident_f = E["ident_f"]
    S = P.scope()
    cw = S.sb("cw", [128, 3, 12]); cb = S.sb("cb", [128, 12]); dvv = S.sb("dvv", [128, 2, 4])
    with nc.allow_non_contiguous_dma(reason="small parameter layout transforms"):
        for k in range(3):
            P.dma(sp, cw[:, k, :], E["hy_conv_w"][l, k].rearrange("(c p) -> p c", p=128), [E["hy_conv_w"]], [cw], cw)
        P.dma(sp, cb[:], E["hy_conv_b"][l].rearrange("(c p) -> p c", p=128), [E["hy_conv_b"]], [cb], cb)
        for o in range(2):
            P.dma(sp, dvv[:, o, :], E["hy_d"][l, o].rearrange("(c p) -> p c", p=128), [E["hy_d"]], [dvv], dvv)

    def V(fn, reads, writes):
        P.op(dve, fn, reads, writes)

    def fwd_dft(cfg, src, emit, bufs):
        tC, tS, psC, psS = bufs
        for fk in range(cfg["nfk"]):
            c_, s_ = tC[fk % 2], tS[fk % 2]
            P.dma(sp, c_[:, 0:cfg["ntc"], :], cfg["tFC"][fk], [cfg["tFC"]], [c_], c_)
            P.dma(sp, s_[:, 0:cfg["ntc"], :], cfg["tFS"][fk], [cfg["tFS"]], [s_], s_)
            pc, ps_ = psC[fk % 2], psS[fk % 2]
            for tc in range(cfg["ntc"]):
                P.op(pe, lambda c_=c_, pc=pc, tc=tc: nc.tensor.matmul(pc[:], lhsT=c_[:, tc, :], rhs=src[:, tc, :], start=(tc == 0), stop=(tc == cfg["ntc"] - 1)),
                     [c_, src], [pc])
            for tc in range(cfg["ntc"]):
                P.op(pe, lambda s_=s_, ps_=ps_, tc=tc: nc.tensor.matmul(ps_[:], lhsT=s_[:, tc, :], rhs=src[:, tc, :], start=(tc == 0), stop=(tc == cfg["ntc"] - 1)),
                     [s_, src], [ps_])
            emit(fk, pc, ps_)

    cfgs = {}
    for L in (LS, LP):
        sfx = str(L)
        cfgs[L] = dict(L=L, ntc=L // 128, nfk=L // 128 + 1, nblk=max(1, L // 512), bw=min(512, L),
                       tFC=E["tFC" + sfx], tFS=E["tFS" + sfx], tIC=E["tIC" + sfx], tIS=E["tIS" + sfx],
                       featsT=E["featsT" + sfx], featsTr=E["featsTr" + sfx], negtv=E["negtv" + sfx], negtvr=E["negtvr" + sfx],
                       wk=E["wk" + sfx], sgw=E["sgw" + sfx], Fs=E["Fs" + sfx])

    for L in (LS, LP):
        cfg = cfgs[L]
        ntc, nfk = cfg["ntc"], cfg["nfk"]
        SF = P.scope()
        w1p = SF.sb("w1p", [64, 64]); w2 = SF.sb("w2", [64, 64]); w3 = SF.sb("w3", [64, 2048])
        fq = SF.sb("fq", [64, 1]); fb1 = SF.sb("fb1", [64, 1]); fb2 = SF.sb("fb2", [64, 1])
        ft = SF.sb("ft", [64, LS]); h1 = SF.sb("h1", [64, LS]); h2 = SF.sb("h2", [64, LS]); kk = SF.sb("kkf", [64, LS])
        adec = SF.sb("adec", [128, 512]); ew = SF.sb("ew", [128, 512]); fa = SF.sb("fa", [128, 512])
        fw = SF.sb("fw", [128, 16, 512])
        ff = [SF.sb(f"ff{d}", [128, 16, 512], BF16) for d in range(2)]
        fpm = [SF.sb(f"fpm{d}", [128, 16, 512], BF16) for d in range(2)]
        ntv = SF.sb("ntv", [128, 2, 16]); wkv = SF.sb("wkv", [128, 17]); sgv = SF.sb("sgv", [128, 17])
        ones = SF.sb("ones", [128, 128]); rcp = SF.sb("rcp", [128, 512])
        tC = [SF.sb(f"tC{i}", [128, 16, 128], BF16) for i in range(2)]
        tS = [SF.sb(f"tS{i}", [128, 16, 128], BF16) for i in range(2)]
        fo = [SF.sb(f"fo{i}", [128, 2, 512]) for i in range(2)]
        tmpc = SF.sb("tmpc", [128, 2, 512])
        psM = SF.ps("psM", [128, 512])
        psN = SF.ps("psN", [128, 512])
        psCf = [SF.ps(f"psCf{i}", [128, 512]) for i in range(2)]
        psSf = [SF.ps(f"psSf{i}", [128, 512]) for i in range(2)]
        psCr = SF.ps("psCr", [128, 512]); psSr = SF.ps("psSr", [128, 512])
        V(lambda: nc.vector.memset(w1p[:], 0.0), [], [w1p])
        V(lambda: nc.vector.memset(ones[:], 1.0), [], [ones])
        P.dma(sp, w1p[0:33, :], E["hy_f_w1"][l], [E["hy_f_w1"]], [w1p], w1p)
        P.dma(sp, w2[:], E["hy_f_w2"][l], [E["hy_f_w2"]], [w2], w2)
        P.dma(sp, w3[:], E["hy_f_w3"][l], [E["hy_f_w3"]], [w3], w3)
        with nc.allow_non_contiguous_dma(reason="tiny"):
            P.dma(sp, fq[:], E["hy_f_freq"][l].rearrange("(p o) -> p o", o=1), [E["hy_f_freq"]], [fq], fq)
            P.dma(sp, fb1[:], E["hy_f_b1"][l].rearrange("(p o) -> p o", o=1), [E["hy_f_b1"]], [fb1], fb1)
            P.dma(sp, fb2[:], E["hy_f_b2"][l].rearrange("(p o) -> p o", o=1), [E["hy_f_b2"]], [fb2], fb2)
        P.dma(sp, ntv[:, 0, 0:ntc], cfg["negtv"][:, :], [cfg["negtv"]], [ntv], ntv)
        P.dma(sp, ntv[:, 1, 0:ntc], cfg["negtvr"][:, :], [cfg["negtvr"]], [ntv], ntv)
        P.dma(sp, wkv[:, 0:nfk], cfg["wk"][:, :], [cfg["wk"]], [wkv], wkv)
        P.dma(sp, sgv[:, 0:nfk], cfg["sgw"][:, :], [cfg["sgw"]], [sgv], sgv)
        V(lambda: nc.vector.tensor_tensor(out=fb1[:], in0=fb1[:], in1=fq[:], op=ALU.mult), [fb1, fq], [fb1])
        V(lambda: nc.vector.tensor_tensor(out=fb2[:], in0=fb2[:], in1=fq[:], op=ALU.mult), [fb2, fq], [fb2])

        def rr_sin(y, n):
            V(lambda: nc.vector.tensor_scalar(out=kk[:, 0:n], in0=y[:, 0:n], scalar1=1.0 / TWO_PI, scalar2=MAGIC, op0=ALU.mult, op1=ALU.add), [y], [kk])
            V(lambda: nc.vector.tensor_scalar_add(out=kk[:, 0:n], in0=kk[:, 0:n], scalar1=-MAGIC), [kk], [kk])
            V(lambda: nc.vector.scalar_tensor_tensor(out=y[:, 0:n], in0=kk[:, 0:n], scalar=-CW1, in1=y[:, 0:n], op0=ALU.mult, op1=ALU.add), [kk, y], [y])
            V(lambda: nc.vector.scalar_tensor_tensor(out=y[:, 0:n], in0=kk[:, 0:n], scalar=-CW2, in1=y[:, 0:n], op0=ALU.mult, op1=ALU.add), [kk, y], [y])
            P.op(act, lambda: nc.scalar.activation(out=y[:, 0:n], in_=y[:, 0:n], func=AF.Sin, scale=0.999998), [y], [y])

        for d in range(2):
            P.dma(sp, ft[:, 0:L], (cfg["featsT"] if d == 0 else cfg["featsTr"])[:, :], [cfg["featsT"], cfg["featsTr"]], [ft], ft)
            bw = cfg["bw"]
            for tb in range(cfg["nblk"]):
                t0 = tb * bw
                P.op(pe, lambda t0=t0: nc.tensor.matmul(psM[0:64, 0:bw], lhsT=w1p[:], rhs=ft[:, t0:t0 + bw], start=True, stop=True), [w1p, ft], [psM])
                V(lambda t0=t0: nc.vector.tensor_scalar(out=h1[:, t0:t0 + bw], in0=psM[0:64, 0:bw], scalar1=fq[:, 0:1], scalar2=fb1[:, 0:1],
                                                        op0=ALU.mult, op1=ALU.add), [psM, fq, fb1], [h1])
            rr_sin(h1, L)
            for tb in range(cfg["nblk"]):
                t0 = tb * bw
                P.op(pe, lambda t0=t0: nc.tensor.matmul(psM[0:64, 0:bw], lhsT=w2[:], rhs=h1[:, t0:t0 + bw], start=True, stop=True), [w2, h1], [psM])
                V(lambda t0=t0: nc.vector.tensor_scalar(out=h2[:, t0:t0 + bw], in0=psM[0:64, 0:bw], scalar1=fq[:, 0:1], scalar2=fb2[:, 0:1],
                                                        op0=ALU.mult, op1=ALU.add), [psM, fq, fb2], [h2])
            rr_sin(h2, L)
            for o in range(2):
                col0 = (d * 2 + o) * 512
                P.dma(sp, adec[:], E["hy_decay"][l, col0:col0 + 512].partition_broadcast(128), [E["hy_decay"]], [adec], adec)
                P.op(act, lambda: nc.scalar.activation(out=adec[:], in_=adec[:], func=AF.Abs), [adec], [adec])
                for tc in range(ntc):
                    P.op(pe, lambda tc=tc, col0=col0: nc.tensor.matmul(psM[:], lhsT=h2[:, tc * 128:(tc + 1) * 128], rhs=w3[:, col0:col0 + 512], start=True, stop=True),
                         [h2, w3], [psM])
                    P.op(act, lambda tc=tc, d=d: nc.scalar.activation(out=ew[:], in_=adec[:], func=AF.Exp, scale=ntv[:, d, tc:tc + 1]), [adec, ntv], [ew])
                    V(lambda tc=tc: nc.vector.tensor_tensor(out=fw[:, tc, :], in0=psM[:], in1=ew[:], op=ALU.mult), [psM, ew], [fw])
                    P.op(act, lambda tc=tc: nc.scalar.activation(out=fa[:], in_=fw[:, tc, :], func=AF.Abs), [fw], [fa])
                    P.op(pe, lambda tc=tc: nc.tensor.matmul(psN[:], lhsT=ones[:], rhs=fa[:], start=(tc == 0), stop=(tc == ntc - 1)), [ones, fa], [psN])
                V(lambda: nc.vector.reciprocal(out=rcp[:], in_=psN[:]), [psN], [rcp])
                V(lambda d=d: nc.vector.tensor_tensor(out=ff[d][:, 0:ntc, :], in0=fw[:, 0:ntc, :], in1=rcp[:, :].unsqueeze(1).to_broadcast([128, ntc, 512]),
                                                      op=ALU.mult), [fw, rcp], [ff[d]])
                if d == 1:
                    V(lambda: nc.vector.memset(ff[1][0:1, 0, :], 0.0), [], [ff[1]])
                P.dma(pool, E["fstash"][d, o, :, 0:ntc, :], ff[d][:, 0:ntc, :], [ff[d]], [E["fstash"]], ff[d])
        ne = (L // 2 + 1 + 127) // 128
        for o in range(2):
            for d in range(2):
                P.dma(sp, ff[d][:, 0:ntc, :], E["fstash"][d, o, :, 0:ntc, :], [E["fstash"]], [ff[d]], ff[d])
            V(lambda: nc.vector.tensor_tensor(out=fpm[0][:, 0:ntc, :], in0=ff[0][:, 0:ntc, :], in1=ff[1][:, 0:ntc, :], op=ALU.add), [ff[0], ff[1]], [fpm[0]])
            P.op(pool, lambda: nc.gpsimd.tensor_tensor(out=fpm[1][:, 0:ntc, :], in0=ff[0][:, 0:ntc, :], in1=ff[1][:, 0:ntc, :], op=ALU.subtract), [ff[0], ff[1]], [fpm[1]])
            for fk in range(nfk):
                c_, s_ = tC[fk % 2], tS[fk % 2]
                P.dma(sp, c_[:, 0:ntc, :], cfg["tFC"][fk], [cfg["tFC"]], [c_], c_)
                P.dma(sp, s_[:, 0:ntc, :], cfg["tFS"][fk], [cfg["tFS"]], [s_], s_)
                pcf, psf = psCf[fk % 2], psSf[fk % 2]
                src = fpm[0] if fk < ne else fpm[1]
                for (tab, pp) in ((c_, pcf), (s_, psf)):
                    for tc in range(ntc):
                        P.op(pe, lambda tab=tab, src=src, pp=pp, tc=tc: nc.tensor.matmul(pp[:], lhsT=tab[:, tc, :], rhs=src[:, tc, :],
                                                                                        start=(tc == 0), stop=(tc == ntc - 1)), [tab, src], [pp])
                fo_ = fo[fk % 2]
                P.op(act, lambda pcf=pcf, fk=fk, fo_=fo_: nc.scalar.activation(out=fo_[:, 0, :], in_=pcf[:], func=AF.Copy, scale=wkv[:, fk:fk + 1]), [pcf, wkv], [fo_])
                V(lambda psf=psf, fk=fk, fo_=fo_: nc.vector.tensor_scalar_mul(out=fo_[:, 1, :], in0=psf[:], scalar1=wkv[:, fk:fk + 1]), [psf, wkv], [fo_])
                P.dma(pool, cfg["Fs"][o, fk], fo_[:], [fo_], [cfg["Fs"]], fo_)
        SF.close()

    SC = P.scope()
    xtok = SC.sb("xtok", [128, 16, 512], BF16)
    Z = SC.sb("Z", [128, 17, 2, 512], BF16)
    y1b = SC.sb("y1b", [128, 4, LS], BF16)
    tC = [SC.sb(f"tC{i}", [128, 16, 128], BF16) for i in range(2)]
    tS = [SC.sb(f"tS{i}", [128, 16, 128], BF16) for i in range(2)]
    tIC = SC.sb("tIC", [128, 17, 512], BF16); tIS = SC.sb("tIS", [128, 17, 512], BF16)
    fsb = [SC.sb(f"fsb{i}", [128, 2, 512]) for i in range(2)]
    zin = [SC.sb(f"zin{i}", [128, 514]) for i in range(2)]
    cvo = [SC.sb(f"cvo{i}", [128, 512]) for i in range(2)]
    cvb2 = [SC.sb(f"cvb{i}", [128, 512], BF16) for i in range(2)]
    t4 = [SC.sb(f"t4{i}", [128, 512]) for i in range(4)]
    ob = [SC.sb(f"ob{i}", [128, 512], BF16) for i in range(2)]
    psC = [SC.ps(f"psC{i}", [128, 512]) for i in range(2)]
    psS = [SC.ps(f"psS{i}", [128, 512]) for i in range(2)]
    psI = [SC.ps(f"psI{i}", [128, 512]) for i in range(2)]
    psTt2 = [SC.ps(f"psTt{i}", [128, 4, 128], BF16) for i in range(2)]
    cjobs = list(E.get("convjobs", []))
    cst = SC.sb("cst", [128, 16, 512]); csb = SC.sb("csb", [128, 16, 512], BF16)

    def conv_step(n=1):
        for _ in range(n):
            if not cjobs:
                return
            (srcT, src, dstT, dst, nk) = cjobs.pop(0)
            P.dma(act, cst[:, 0:nk, :], src.rearrange("(k p) n -> p k n", p=128), [srcT], [cst], cst)
            P.op(act, lambda: nc.scalar.copy(out=csb[:, 0:nk, :], in_=cst[:, 0:nk, :]), [cst], [csb])
            if len(dst.shape) == 4:
                P.dma(pool, dst, csb[:, 0:nk, :].rearrange("p (g k) n -> p g k n", g=2), [csb], [dstT], csb)
            else:
                P.dma(pool, dst, csb[:, 0:nk, :], [csb], [dstT], csb)

    def shortconv(ci, off, t0, bw, rows_n, zi, out):
        P.dma(sp, zi[:, 0:bw], hy[ci, :, off + t0:off + t0 + bw], [hy], [zi], zi)
        nseg = bw // rows_n
        z3 = zi[:, 0:bw].rearrange("p (s n) -> p s n", n=rows_n)
        o3 = out[:, 0:bw].rearrange("p (s n) -> p s n", n=rows_n)
        V(lambda: nc.vector.tensor_scalar(out=out[:, 0:bw], in0=zi[:, 0:bw], scalar1=cw[:, 1, ci:ci + 1], scalar2=cb[:, ci:ci + 1], op0=ALU.mult, op1=ALU.add),
          [zi, cw, cb], [out])
        V(lambda: nc.vector.scalar_tensor_tensor(out=o3[:, :, 1:rows_n], in0=z3[:, :, 0:rows_n - 1], scalar=cw[:, 0, ci:ci + 1], in1=o3[:, :, 1:rows_n],
                                                 op0=ALU.mult, op1=ALU.add), [zi, cw, out], [out])
        V(lambda: nc.vector.scalar_tensor_tensor(out=o3[:, :, 0:rows_n - 1], in0=z3[:, :, 1:rows_n], scalar=cw[:, 2, ci:ci + 1], in1=o3[:, :, 0:rows_n - 1],
                                                 op0=ALU.mult, op1=ALU.add), [zi, cw, out], [out])

    for si, (off, L) in enumerate(SEQS):
        cfg = cfgs[L]
        ntc, nfk, nblk, bw = cfg["ntc"], cfg["nfk"], cfg["nblk"], cfg["bw"]
        rows_n = 64 if si == 0 else L
        for order in range(2):
            bi = 0
            for c in range(4):
                for tb in range(nblk):
                    t0 = tb * bw
                    cvb = cvb2[bi % 2]; psTt = psTt2[bi % 2]
                    if order == 0:
                        shortconv(c, off, t0, bw, rows_n, zin[bi % 2], cvo[bi % 2])
                        P.op(act, lambda cvb=cvb, bi=bi: nc.scalar.copy(out=cvb[:, 0:bw], in_=cvo[bi % 2][:, 0:bw]), [cvo[bi % 2]], [cvb])
                        srcT = cvb
                    else:
                        srcT = y1b
                    for q in range(bw // 128):
                        if order == 0:
                            in_ap = cvb[:, q * 128:(q + 1) * 128]
                        else:
                            in_ap = y1b[:, c, t0 + q * 128:t0 + (q + 1) * 128]
                        P.op(pe, lambda q=q, in_ap=in_ap, psTt=psTt: nc.tensor.transpose(psTt[:, q, :], in_ap, ident_b[:]), [srcT, ident_b], [psTt])
                    nq = bw // 128
                    if bi % 2 == 0:
                        P.op(act, lambda c=c, tb=tb, nq=nq, psTt=psTt: nc.scalar.copy(out=xtok[:, tb * 4:tb * 4 + nq, c * 128:(c + 1) * 128], in_=psTt[:, 0:nq, :]),
                             [psTt], [xtok])
                    else:
                        V(lambda c=c, tb=tb, nq=nq, psTt=psTt: nc.vector.tensor_copy(out=xtok[:, tb * 4:tb * 4 + nq, c * 128:(c + 1) * 128], in_=psTt[:, 0:nq, :]),
                          [psTt], [xtok])
                    bi += 1

            def emit(fk, pc, ps_, order=order):
                if si == 0 and fk % 2 == 0:
                    conv_step()
                f_ = fsb[fk % 2]
                P.dma(sp, f_[:], cfg["Fs"][order, fk], [cfg["Fs"]], [f_], f_)
                V(lambda: nc.vector.tensor_tensor(out=t4[0][:], in0=pc[:], in1=f_[:, 0, :], op=ALU.mult), [pc, f_], [t4[0]])
                V(lambda: nc.vector.tensor_tensor(out=t4[1][:], in0=ps_[:], in1=f_[:, 1, :], op=ALU.mult), [ps_, f_], [t4[1]])
                V(lambda: nc.vector.tensor_tensor(out=t4[2][:], in0=pc[:], in1=f_[:, 1, :], op=ALU.mult), [pc, f_], [t4[2]])
                V(lambda: nc.vector.tensor_tensor(out=t4[3][:], in0=ps_[:], in1=f_[:, 0, :], op=ALU.mult), [ps_, f_], [t4[3]])
                V(lambda: nc.vector.tensor_tensor(out=Z[:, fk, 0, :], in0=t4[0][:], in1=t4[1][:], op=ALU.subtract), [t4[0], t4[1]], [Z])
                V(lambda: nc.vector.tensor_tensor(out=Z[:, fk, 1, :], in0=t4[2][:], in1=t4[3][:], op=ALU.add), [t4[2], t4[3]], [Z])

            fwd_dft(cfg, xtok, emit, (tC, tS, psC, psS))
            for tb in range(nblk):
                t0 = tb * bw
                P.dma(sp, tIC[:, 0:nfk, 0:bw], cfg["tIC"][tb], [cfg["tIC"]], [tIC], tIC)
                P.dma(sp, tIS[:, 0:nfk, 0:bw], cfg["tIS"][tb], [cfg["tIS"]], [tIS], tIS)
                for c in range(4):
                    pi = psI[c % 2]
                    for fk in range(nfk):
                        P.op(pe, lambda fk=fk, c=c, pi=pi: nc.tensor.matmul(pi[:, 0:bw], lhsT=Z[:, fk, 0, c * 128:(c + 1) * 128], rhs=tIC[:, fk, 0:bw],
                                                                           start=(fk == 0), stop=False), [Z, tIC], [pi])
                        P.op(pe, lambda fk=fk, c=c, pi=pi: nc.tensor.matmul(pi[:, 0:bw], lhsT=Z[:, fk, 1, c * 128:(c + 1) * 128], rhs=tIS[:, fk, 0:bw],
                                                                           start=False, stop=(fk == nfk - 1)), [Z, tIS], [pi])
                    if si == 0 and c % 2 == 0:
                        conv_step()
                    gate_ci = (4 if order == 0 else 8) + c
                    shortconv(gate_ci, off, t0, bw, rows_n, zin[1], cvo[1])
                    if order == 0:
                        shortconv(c, off, t0, bw, rows_n, zin[0], cvo[0])
                        V(lambda c=c, pi=pi: nc.vector.scalar_tensor_tensor(out=t4[0][:, 0:bw], in0=cvo[0][:, 0:bw], scalar=dvv[:, 0, c:c + 1], in1=pi[:, 0:bw],
                                                                          op0=ALU.mult, op1=ALU.add), [cvo[0], dvv, pi], [t4[0]])
                        V(lambda c=c, t0=t0: nc.vector.tensor_tensor(out=y1b[:, c, t0:t0 + bw], in0=t4[0][:, 0:bw], in1=cvo[1][:, 0:bw], op=ALU.mult),
                          [t4[0], cvo[1]], [y1b])
                    else:
                        o_ = ob[(tb * 4 + c) % 2]
                        V(lambda c=c, pi=pi, t0=t0: nc.vector.scalar_tensor_tensor(out=t4[0][:, 0:bw], in0=y1b[:, c, t0:t0 + bw], scalar=dvv[:, 1, c:c + 1], in1=pi[:, 0:bw],
                                                                                 op0=ALU.mult, op1=ALU.add), [y1b, dvv, pi], [t4[0]])
                        V(lambda o_=o_: nc.vector.tensor_tensor(out=o_[:, 0:bw], in0=t4[0][:, 0:bw], in1=cvo[1][:, 0:bw], op=ALU.mult), [t4[0], cvo[1]], [o_])
                        P.dma(pool, mixT[12 + c, :, off + t0:off + t0 + bw], o_[:, 0:bw], [o_], [mixT], o_)
    conv_step(len(cjobs))
    SC.close()
    S.close()


_NC_CACHE = {}


def _consts():
    ident = np.eye(128, dtype=np.float32)
    s = np.arange(128)[:, None]
    c = np.arange(128)[None, :]
    masks = np.stack([(s <= c), (s >= c)]).astype(np.float32)
    tp1 = np.arange(1, LS + 1, dtype=np.float32)
    out = dict(ident=ident, masks=masks, tp1=tp1)
    bf = ml_dtypes.bfloat16
    for L in (LS, LP):
        sx = str(L); N = 2 * L; ntc = L // 128; nfk = ntc + 1; nblk = max(1, L // 512); bw = min(512, L)
        ne = (L // 2 + 1 + 127) // 128
        kk = np.full(nfk * 128, -1, dtype=np.int64)
        ev = np.arange(0, L + 1, 2); od = np.arange(1, L, 2)
        kk[:len(ev)] = ev; kk[ne * 128:ne * 128 + len(od)] = od
        tt = np.arange(L, dtype=np.int64)
        ang = 2.0 * np.pi * ((np.maximum(kk, 0)[:, None] * tt[None, :]) % N).astype(np.float64) / N
        valid = (kk >= 0).astype(np.float64)[:, None]
        Ckt = np.cos(ang) * valid; Skt = np.sin(ang) * valid
        out["tFC" + sx] = np.ascontiguousarray(Ckt.reshape(nfk, 128, ntc, 128).transpose(0, 3, 2, 1)).astype(bf)
        out["tFS" + sx] = np.ascontiguousarray(Skt.reshape(nfk, 128, ntc, 128).transpose(0, 3, 2, 1)).astype(bf)
        out["tIC" + sx] = np.ascontiguousarray(Ckt.reshape(nfk, 128, nblk, bw).transpose(2, 1, 0, 3)).astype(bf)
        out["tIS" + sx] = np.ascontiguousarray(Skt.reshape(nfk, 128, nblk, bw).transpose(2, 1, 0, 3)).astype(bf)
        t = np.linspace(0.0, 1.0, L, dtype=np.float32)[:, None]
        w = (2.0 * np.float32(np.pi) * np.arange(L, dtype=np.float32)[:, None] / np.float32(L)).astype(np.float32)
        fr = np.linspace(1e-4, 15.0, 16, dtype=np.float32)[None, :]
        feats = np.concatenate([t, np.cos(fr * w), -np.sin(fr * w)], axis=-1).astype(np.float32)
        ridx = (L - np.arange(L)) % L
        fT = np.zeros((64, L), np.float32); fT[:33] = feats.T
        fTr = np.zeros((64, L), np.float32); fTr[:33] = feats[ridx].T
        out["featsT" + sx] = fT; out["featsTr" + sx] = fTr
        out["negtv" + sx] = np.ascontiguousarray(-t[:, 0].reshape(ntc, 128).T)
        out["negtvr" + sx] = np.ascontiguousarray(-t[ridx, 0].reshape(ntc, 128).T)
        wk = np.where((kk == 0) | (kk == L), 1.0 / N, 2.0 / N) * (kk >= 0)
        sg = np.where(kk % 2 == 0, 1.0, -1.0)
        out["wk" + sx] = np.ascontiguousarray(wk.reshape(nfk, 128).T).astype(np.float32)
        out["sgw" + sx] = np.ascontiguousarray((wk * sg).reshape(nfk, 128).T).astype(np.float32)
    return out


def kernel(**inp):
    n = 8
    if "nc" not in _NC_CACHE:
        _NC_CACHE["nc"] = build_program()
    nc = _NC_CACHE["nc"]
    f = lambda a: np.ascontiguousarray(np.asarray(a, dtype=np.float32))
    consts = _consts()
    shared = {k: f(inp[k]) for k in ("ada_w", "ada_b", "w_in", "w_out", "w_up", "w_down", "ln1_g", "ln1_b", "ln2_g", "ln2_b",
                                     "gla_w_gate", "gla_b_gate", "gla_norm_w", "ret_decay_exp",
                                     "s5_a_re", "s5_a_im", "s5_log_step", "s5_b_re", "s5_b_im", "s5_c_re", "s5_c_im", "s5_d", "s5_glu_w", "s5_glu_b",
                                     "hy_conv_w", "hy_conv_b", "hy_f_w1", "hy_f_b1", "hy_f_w2", "hy_f_b2", "hy_f_freq", "hy_f_w3", "hy_decay", "hy_d")}
    shared.update(consts)
    in_maps = []
    for c in range(n):
        m = dict(shared)
        m["x"] = np.ascontiguousarray(np.concatenate([inp["x_sample"][c], inp["x_prompt"][2 * c], inp["x_prompt"][2 * c + 1]], axis=0).astype(np.float32))
        m["cvec"] = np.ascontiguousarray(np.stack([inp["c"][c], inp["c_ctx"]]).astype(np.float32))
        m["sg"] = f(inp["state_gla"][c]); m["sr"] = f(inp["state_ret"][c])
        m["s5re"] = f(inp["state_s5_re"][c]); m["s5im"] = f(inp["state_s5_im"][c])
        in_maps.append(m)
    res = run_bass_kernel_spmd(nc, in_maps, core_ids=list(range(n)))
    R = res.results
    y = np.stack([r["y"] for r in R])
    y_sample = np.ascontiguousarray(y[:, :LS])
    y_prompt = np.ascontiguousarray(y[:, LS:].reshape(n * 2, LP, D))
    cat = lambda k: np.ascontiguousarray(np.concatenate([r[k] for r in R], axis=0))
    return (y_prompt, y_sample, cat("nsg"), cat("nsr"), cat("ns5re"), cat("ns5im"))
```
